# Optimizing a Trainium2 kernel written in Bass

```python
import math
import jax, jax.numpy as jnp
from jax import lax
import numpy as np

D_MODEL = 1024
BATCH = 8
SEQ = 4096
DEPTH = 2

HEAD_DIM = 64
GRID_W = 64
ROPE_THETA = 10000.0
NORM_EPS = 1e-6
NEG_INF = -1e30

A_Q_HEADS = 8
A_KV_HEADS = 2
A_RADIUS = 128
B_PAIRS = ((128, 1), (512, 4), (2048, 16))
B_HEADS_PER_GROUP = 2
B_HEADS = B_HEADS_PER_GROUP * len(B_PAIRS)
C_HEADS = 4
C_QK_DIM = 32
C_V_DIM = 2 * C_QK_DIM
C_Q_BLOCK = 128
D_HEADS = 4
NA_ROWS = 8
NA_COLS = 16
D_MLP = 4 * D_MODEL
N_BRANCHES = 4

A_Q_COLS = A_Q_HEADS * HEAD_DIM
A_KV_COLS = A_KV_HEADS * HEAD_DIM
B_COLS = B_HEADS * HEAD_DIM
C_QK_COLS = C_HEADS * 2 * C_QK_DIM
C_V_COLS = C_HEADS * C_V_DIM
D_COLS = D_HEADS * HEAD_DIM
GATE_COLS = N_BRANCHES * D_MODEL
IN_SPLITS = (A_Q_COLS, A_KV_COLS, A_KV_COLS, B_COLS, B_COLS, B_COLS,
             C_QK_COLS, C_QK_COLS, C_V_COLS, D_COLS, D_COLS, D_COLS, GATE_COLS)
IN_COLS = sum(IN_SPLITS)
A_OUT = A_Q_HEADS * HEAD_DIM
B_OUT = B_HEADS_PER_GROUP * HEAD_DIM
C_OUT = C_HEADS * C_V_DIM
D_OUT = D_HEADS * HEAD_DIM

kernel_name = "hybrid_gated_parallel_attention_encoder"


def rms_norm(x, g):
    xf = x.astype(jnp.float32)
    y = xf * lax.rsqrt(jnp.mean(xf * xf, axis=-1, keepdims=True) + NORM_EPS)
    return (y * g.astype(jnp.float32)).astype(x.dtype)


def rope_tables(n, dim):
    inv = ROPE_THETA ** (-jnp.arange(0, dim, 2, dtype=jnp.float32) / dim)
    ang = jnp.arange(n, dtype=jnp.float32)[:, None] * inv[None, :]
    return jnp.cos(ang), jnp.sin(ang)


def apply_rope(x, cos, sin):
    half = x.shape[-1] // 2
    x1, x2 = x[..., :half], x[..., half:]
    c, s = cos.astype(x.dtype), sin.astype(x.dtype)
    return jnp.concatenate([x1 * c - x2 * s, x2 * c + x1 * s], axis=-1)


def split_heads(t, n):
    b, s, _ = t.shape
    return t.reshape(b, s, n, -1).transpose(0, 2, 1, 3)


def merge_heads(t):
    b, h, s, d = t.shape
    return t.transpose(0, 2, 1, 3).reshape(b, s, h * d)


def banded_attention(q, k, v, radius, sink=None):
    b, g, r, l, dh = q.shape
    dv = v.shape[-1]
    blk = radius
    nb = -(-l // blk)
    lp = nb * blk
    pad = lp - l
    qb = jnp.pad(q, ((0, 0), (0, 0), (0, 0), (0, pad), (0, 0))).reshape(b, g, r, nb, blk, dh)
    kp = jnp.pad(k, ((0, 0), (0, 0), (blk, pad + blk), (0, 0)))
    vp = jnp.pad(v, ((0, 0), (0, 0), (blk, pad + blk), (0, 0)))
    kb = jnp.concatenate([kp[:, :, o * blk:o * blk + lp].reshape(b, g, nb, blk, dh) for o in range(3)], axis=3)
    vb = jnp.concatenate([vp[:, :, o * blk:o * blk + lp].reshape(b, g, nb, blk, dv) for o in range(3)], axis=3)
    qpos = jnp.arange(lp).reshape(nb, blk)
    kpos = jnp.arange(nb)[:, None] * blk - blk + jnp.arange(3 * blk)[None, :]
    valid = ((kpos[:, None, :] >= 0) & (kpos[:, None, :] < l)
             & (jnp.abs(kpos[:, None, :] - qpos[:, :, None]) <= radius))
    s = jnp.einsum('bgrnqd,bgnkd->bgrnqk', qb, kb).astype(jnp.float32) * (dh ** -0.5)
    s = jnp.where(valid, s, NEG_INF)
    m = jnp.max(s, axis=-1, keepdims=True)
    if sink is not None:
        sk = sink.astype(jnp.float32)[None, :, :, None, None, None]
        m = jnp.maximum(m, sk)
        p = jnp.exp(s - m)
        den = jnp.sum(p, axis=-1, keepdims=True) + jnp.exp(sk - m)
    else:
        p = jnp.exp(s - m)
        den = jnp.sum(p, axis=-1, keepdims=True)
    out = jnp.einsum('bgrnqk,bgnkd->bgrnqd', (p / den).astype(v.dtype), vb)
    lse = (jnp.log(den) + m)[..., 0]
    out = out.reshape(b, g, r, lp, dv)[:, :, :, :l]
    lse = lse.reshape(b, g, r, lp)[:, :, :, :l]
    return out, lse


def mixer_window_gqa(q, k, v, qn_g, kn_g, sink, cos, sin):
    b, s, _ = q.shape
    q = apply_rope(rms_norm(split_heads(q, A_Q_HEADS), qn_g), cos, sin)
    k = apply_rope(rms_norm(split_heads(k, A_KV_HEADS), kn_g), cos, sin)
    v = split_heads(v, A_KV_HEADS)
    rep = A_Q_HEADS // A_KV_HEADS
    q = q.reshape(b, A_KV_HEADS, rep, s, HEAD_DIM)
    out, _ = banded_attention(q, k, v, A_RADIUS, sink.reshape(A_KV_HEADS, rep))
    return merge_heads(out.reshape(b, A_Q_HEADS, s, HEAD_DIM))


def to_residue_classes(t, d):
    b, h, s, dh = t.shape
    return t.reshape(b, h, s // d, d, dh).transpose(0, 1, 3, 2, 4).reshape(b, h * d, s // d, dh)


def from_residue_classes(t, h, d):
    b, _, sd = t.shape[:3]
    rest = t.shape[3:]
    t = t.reshape((b, h, d, sd) + rest)
    perm = (0, 1, 3, 2) + tuple(range(4, t.ndim))
    return t.transpose(perm).reshape((b, h, sd * d) + rest)


def mixer_dilated(q, k, v, qn_g, kn_g, cos, sin):
    q = apply_rope(rms_norm(split_heads(q, B_HEADS), qn_g), cos, sin)
    k = apply_rope(rms_norm(split_heads(k, B_HEADS), kn_g), cos, sin)
    v = split_heads(v, B_HEADS)
    hpg = B_HEADS_PER_GROUP
    outs, lses = [], []
    for gi, (window, dil) in enumerate(B_PAIRS):
        hs = slice(gi * hpg, (gi + 1) * hpg)
        qd = to_residue_classes(q[:, hs], dil)[:, :, None]
        kd = to_residue_classes(k[:, hs], dil)
        vd = to_residue_classes(v[:, hs], dil)
        o, lse = banded_attention(qd, kd, vd, window // (2 * dil))
        outs.append(from_residue_classes(o[:, :, 0], hpg, dil))
        lses.append(from_residue_classes(lse[:, :, 0], hpg, dil))
    wts = jax.nn.softmax(jnp.stack(lses, axis=0), axis=0)
    out = jnp.sum(wts[..., None] * jnp.stack(outs, axis=0).astype(jnp.float32), axis=0)
    return merge_heads(out.astype(q.dtype))


def mixer_differential(q, k, v, qn_g, kn_g, lam_p, subln_g, cos, sin, lambda_init):
    b, s, _ = q.shape
    q = q.reshape(b, s, C_HEADS, 2, C_QK_DIM).transpose(0, 2, 3, 1, 4)
    k = k.reshape(b, s, C_HEADS, 2, C_QK_DIM).transpose(0, 2, 3, 1, 4)
    q = apply_rope(rms_norm(q, qn_g), cos, sin)
    k = apply_rope(rms_norm(k, kn_g), cos, sin)
    v = split_heads(v, C_HEADS)
    lp = lam_p.astype(jnp.float32)
    lam = jnp.exp(jnp.sum(lp[0] * lp[1])) - jnp.exp(jnp.sum(lp[2] * lp[3])) + lambda_init
    nq = s // C_Q_BLOCK
    qb = q.reshape(b, C_HEADS, 2, nq, C_Q_BLOCK, C_QK_DIM).transpose(3, 0, 1, 2, 4, 5)
    scale = C_QK_DIM ** -0.5

    def block(qi):
        sc = jnp.einsum('bhcqd,bhckd->bhcqk', qi, k).astype(jnp.float32) * scale
        p = jax.nn.softmax(sc, axis=-1)
        a = p[:, :, 0] - lam * p[:, :, 1]
        return jnp.einsum('bhqk,bhkd->bhqd', a.astype(v.dtype), v)

    o = lax.map(block, qb)
    o = o.transpose(1, 2, 0, 3, 4).reshape(b, C_HEADS, s, C_V_DIM)
    o = rms_norm(o, subln_g) * (1.0 - lambda_init)
    return merge_heads(o)


def mixer_neighbourhood(q, k, v, qn_g, kn_g, rpb):
    b, s, _ = q.shape
    rows = s // GRID_W
    kr = min(NA_ROWS, rows)
    q = rms_norm(split_heads(q, D_HEADS), qn_g).reshape(b, D_HEADS, rows, GRID_W, HEAD_DIM)
    k = rms_norm(split_heads(k, D_HEADS), kn_g).reshape(b, D_HEADS, rows, GRID_W, HEAD_DIM)
    v = split_heads(v, D_HEADS).reshape(b, D_HEADS, rows, GRID_W, HEAD_DIM)
    qi = jnp.arange(rows)
    r0 = jnp.clip(qi - kr // 2, 0, rows - kr)
    ridx = r0[:, None] + jnp.arange(kr)[None, :]
    cj = jnp.arange(GRID_W)
    c0 = jnp.clip(cj - NA_COLS // 2, 0, GRID_W - NA_COLS)
    col_ok = (cj[None, :] >= c0[:, None]) & (cj[None, :] < c0[:, None] + NA_COLS)
    mask = jnp.tile(col_ok, (1, kr))
    kg = k[:, :, ridx].reshape(b, D_HEADS, rows, kr * GRID_W, HEAD_DIM)
    vg = v[:, :, ridx].reshape(b, D_HEADS, rows, kr * GRID_W, HEAD_DIM)
    dr = ridx - qi[:, None] + (NA_ROWS - 1)
    dc = jnp.clip(cj[None, :] - cj[:, None], -(NA_COLS - 1), NA_COLS - 1) + (NA_COLS - 1)
    bias = rpb[:, dr[:, None, :, None], dc[None, :, None, :]]
    bias = bias.reshape(D_HEADS, rows, GRID_W, kr * GRID_W).astype(jnp.float32)
    sc = jnp.einsum('bhiqd,bhikd->bhiqk', q, kg).astype(jnp.float32) * (HEAD_DIM ** -0.5) + bias
    sc = jnp.where(mask, sc, NEG_INF)
    p = jax.nn.softmax(sc, axis=-1)
    o = jnp.einsum('bhiqk,bhikd->bhiqd', p.astype(v.dtype), vg)
    return merge_heads(o.reshape(b, D_HEADS, s, HEAD_DIM))


def setup_inputs(seed: int = 0) -> dict:
    key = jax.random.key(seed)
    ks = jax.random.split(key, 19)
    f32 = jnp.float32
    nrm = lambda kk, shape, sc: jax.random.normal(kk, shape, f32) * sc
    gain = lambda kk, shape: 1.0 + 0.02 * jax.random.normal(kk, shape, f32)
    return {
        "x": nrm(ks[0], (BATCH, SEQ, D_MODEL), 1.0),
        "attn_norm_g": gain(ks[1], (DEPTH, D_MODEL)),
        "w_in": nrm(ks[2], (DEPTH, D_MODEL, IN_COLS), D_MODEL ** -0.5),
        "a_qk_norm_g": gain(ks[3], (DEPTH, 2, HEAD_DIM)),
        "a_sink": nrm(ks[4], (DEPTH, A_Q_HEADS), 0.5),
        "b_qk_norm_g": gain(ks[5], (DEPTH, 2, HEAD_DIM)),
        "c_qk_norm_g": gain(ks[6], (DEPTH, 2, C_QK_DIM)),
        "c_lambda": nrm(ks[7], (DEPTH, 4, C_QK_DIM), 0.1),
        "c_subln_g": gain(ks[8], (DEPTH, C_V_DIM)),
        "d_qk_norm_g": gain(ks[9], (DEPTH, 2, HEAD_DIM)),
        "d_rel_bias": nrm(ks[10], (DEPTH, D_HEADS, 2 * NA_ROWS - 1, 2 * NA_COLS - 1), 0.1),
        "w_branch_a": nrm(ks[11], (DEPTH, A_OUT, D_MODEL), A_OUT ** -0.5),
        "w_branch_b": nrm(ks[12], (DEPTH, B_OUT, D_MODEL), B_OUT ** -0.5),
        "w_branch_c": nrm(ks[13], (DEPTH, C_OUT, D_MODEL), C_OUT ** -0.5),
        "w_branch_d": nrm(ks[14], (DEPTH, D_OUT, D_MODEL), D_OUT ** -0.5),
        "w_out": nrm(ks[15], (DEPTH, D_MODEL, D_MODEL), D_MODEL ** -0.5),
        "mlp_norm_g": gain(ks[16], (DEPTH, D_MODEL)),
        "w_up": nrm(ks[17], (DEPTH, D_MODEL, D_MLP), D_MODEL ** -0.5),
        "w_down": nrm(ks[18], (DEPTH, D_MLP, D_MODEL), D_MLP ** -0.5),
    }


def reference(x, attn_norm_g, w_in, a_qk_norm_g, a_sink, b_qk_norm_g, c_qk_norm_g, c_lambda,
              c_subln_g, d_qk_norm_g, d_rel_bias, w_branch_a, w_branch_b, w_branch_c, w_branch_d,
              w_out, mlp_norm_g, w_up, w_down):
    b, s, _ = x.shape
    cos64, sin64 = rope_tables(s, HEAD_DIM)
    cos32, sin32 = rope_tables(s, C_QK_DIM)
    offsets = []
    acc = 0
    for w in IN_SPLITS[:-1]:
        acc += w
        offsets.append(acc)
    for l in range(DEPTH):
        h = rms_norm(x, attn_norm_g[l])
        proj = jnp.einsum('bsd,de->bse', h, w_in[l])
        aq, ak, av, bq, bk, bv, cq, ck, cv, dq, dk, dv, gate = jnp.split(proj, offsets, axis=-1)
        ya = mixer_window_gqa(aq, ak, av, a_qk_norm_g[l, 0], a_qk_norm_g[l, 1], a_sink[l], cos64, sin64)
        yb = mixer_dilated(bq, bk, bv, b_qk_norm_g[l, 0], b_qk_norm_g[l, 1], cos64, sin64)
        lambda_init = 0.8 - 0.6 * math.exp(-0.3 * l)
        yc = mixer_differential(cq, ck, cv, c_qk_norm_g[l, 0], c_qk_norm_g[l, 1], c_lambda[l],
                                c_subln_g[l], cos32, sin32, lambda_init)
        yd = mixer_neighbourhood(dq, dk, dv, d_qk_norm_g[l, 0], d_qk_norm_g[l, 1], d_rel_bias[l])
        g = jax.nn.sigmoid(gate.astype(jnp.float32)).astype(x.dtype).reshape(b, s, N_BRANCHES, D_MODEL)
        merged = (g[:, :, 0] * (ya @ w_branch_a[l]) + g[:, :, 1] * (yb @ w_branch_b[l])
                  + g[:, :, 2] * (yc @ w_branch_c[l]) + g[:, :, 3] * (yd @ w_branch_d[l]))
        x = x + merged @ w_out[l]
        hm = rms_norm(x, mlp_norm_g[l])
        u = jnp.square(jax.nn.relu(hm @ w_up[l]))
        x = x + u @ w_down[l]
    return x
```

```python
import math
from contextlib import ExitStack
import numpy as np
import concourse.bass as bass
import concourse.mybir as mybir
from concourse.bass_utils import run_bass_kernel_spmd

F32 = mybir.dt.float32
BF16 = mybir.dt.bfloat16
U8 = mybir.dt.uint8
AF = mybir.ActivationFunctionType
ALU = mybir.AluOpType

S = 4096
DM = 1024
NT = 32
EPS = 1e-6
INC = 7552
ENGS = ['pe', 'act', 'dve', 'pool', 'sp']
SAME_ENG_SYNC = ('act', 'dve', 'pool')


class Buf:
    __slots__ = ('name', 'w', 'rs', 'sem', 'cnt')

    def __init__(self, name):
        self.name = name
        self.w = None
        self.rs = {}
        self.sem = None
        self.cnt = 0


class Op:
    __slots__ = ('eng', 'fn', 'deps', 'dwaits', 'needs_inc', 'semval', 'dma_sem')

    def __init__(self, eng, fn):
        self.eng = eng
        self.fn = fn
        self.deps = set()
        self.dwaits = {}
        self.needs_inc = False
        self.semval = 0
        self.dma_sem = None


class Prog:
    def __init__(self, nc, stack):
        self.nc = nc
        self.stack = stack
        self.ops = {e: [] for e in ENGS}
        self.bufs = []
        self.esem = {e: stack.enter_context(nc.semaphore("s_" + e)) for e in ENGS}
        self.nsem = len(ENGS)

    def buf(self, name):
        b = Buf(name)
        self.bufs.append(b)
        return b

    def bufs_n(self, name, n):
        return [self.buf(f"{name}{i}") for i in range(n)]

    def _add_ev(self, o, ev):
        if ev is None:
            return
        if ev[0] == 'op':
            d = ev[1]
            if d.eng == o.eng and d.eng not in SAME_ENG_SYNC:
                return
            o.deps.add(d)
            d.needs_inc = True
        else:
            _, sem, val = ev
            cur = o.dwaits.get(id(sem))
            if cur is None or cur[1] < val:
                o.dwaits[id(sem)] = (sem, val)

    def _track(self, o, ev, key, R, W):
        for b in R:
            self._add_ev(o, b.w)
        for b in W:
            self._add_ev(o, b.w)
            for e2 in b.rs.values():
                self._add_ev(o, e2)
        for b in R:
            b.rs[key] = ev
        for b in W:
            b.w = ev
            b.rs = {}

    def op(self, eng, fn, R=(), W=()):
        o = Op(eng, fn)
        self._track(o, ('op', o), eng, R, W)
        self.ops[eng].append(o)
        return o

    def dma(self, eng, out, in_, owner, R=(), W=()):
        if owner.sem is None:
            owner.sem = self.stack.enter_context(self.nc.semaphore("d_" + owner.name))
            self.nsem += 1
        owner.cnt += 16
        o = Op(eng, lambda e: e.dma_start(out=out, in_=in_))
        o.dma_sem = owner.sem
        self._track(o, ('dma', owner.sem, owner.cnt), 'dma', R, W)
        self.ops[eng].append(o)
        return o

    def load(self, out, in_, owner, eng='sp'):
        return self.dma(eng, out, in_, owner, R=(), W=(owner,))

    def store(self, out, in_, owner, eng='sp'):
        return self.dma(eng, out, in_, owner, R=(owner,), W=())

    def barrier(self):
        o = Op('sp', lambda e: e.nop())
        for E in ENGS:
            if self.ops[E]:
                last = None
                for c in reversed(self.ops[E]):
                    if c.fn is not None and c.dma_sem is None:
                        last = c
                        break
                if last is not None and E != 'sp':
                    o.deps.add(last)
                    last.needs_inc = True
        for b in self.bufs:
            if b.sem is not None and b.cnt > 0:
                o.dwaits[id(b.sem)] = (b.sem, b.cnt)
        o.needs_inc = True
        self.ops['sp'].append(o)
        for E in ENGS:
            if E != 'sp':
                w = Op(E, None)
                w.deps.add(o)
                self.ops[E].append(w)
        for b in self.bufs:
            b.w = None
            b.rs = {}

    def mm(self, out, lhsT, rhs, start=True, stop=True, R=(), W=()):
        return self.op('pe', lambda e: e.matmul(out, lhsT, rhs, start=start, stop=stop), R, W)

    def tr(self, out, in_, ident, R=(), W=()):
        return self.op('pe', lambda e: e.transpose(out, in_, ident), R, W)

    def act(self, out, in_, func, R=(), W=(), **kw):
        return self.op('act', lambda e: e.activation(out=out, in_=in_, func=func, **kw), R, W)

    def tt(self, eng, out, in0, in1, op, R=(), W=()):
        return self.op(eng, lambda e: e.tensor_tensor(out=out, in0=in0, in1=in1, op=op), R, W)

    def ts(self, eng, out, in0, s1, s2, op0, op1=None, R=(), W=()):
        if op1 is None:
            return self.op(eng, lambda e: e.tensor_scalar(out=out, in0=in0, scalar1=s1, scalar2=None, op0=op0), R, W)
        return self.op(eng, lambda e: e.tensor_scalar(out=out, in0=in0, scalar1=s1, scalar2=s2, op0=op0, op1=op1), R, W)

    def stt(self, out, in0, scalar, in1, op0, op1, R=(), W=()):
        return self.op('dve', lambda e: e.scalar_tensor_tensor(out=out, in0=in0, scalar=scalar, in1=in1, op0=op0, op1=op1), R, W)

    def cp(self, eng, out, in_, R=(), W=()):
        if eng == 'act':
            return self.op('act', lambda e: e.copy(out=out, in_=in_), R, W)
        return self.op(eng, lambda e: e.tensor_copy(out=out, in_=in_), R, W)

    def recip(self, out, in_, R=(), W=()):
        return self.op('dve', lambda e: e.reciprocal(out=out, in_=in_), R, W)

    def memset(self, eng, ap, val, W=()):
        return self.op(eng, lambda e: e.memset(ap, val), (), W)

    def emit(self, block):
        for E in ENGS:
            c = 0
            for o in self.ops[E]:
                if o.needs_inc and o.dma_sem is None:
                    c += 1
                    o.semval = c
        esem = self.esem

        def run(E, eng):
            waited = {}
            for o in self.ops[E]:
                needs = []
                for d in o.deps:
                    needs.append((esem[d.eng], d.semval))
                for sem, val in o.dwaits.values():
                    needs.append((sem, val))
                for sem, val in needs:
                    k = id(sem)
                    if waited.get(k, 0) < val:
                        eng.wait_ge(sem, val)
                        waited[k] = val
                if o.fn is not None:
                    ins = o.fn(eng)
                    if o.dma_sem is not None:
                        ins.then_inc(o.dma_sem, 16)
                    elif o.needs_inc:
                        ins.then_inc(esem[E], 1)

        @block.tensor
        def _(e):
            run('pe', e)

        @block.scalar
        def _(e):
            run('act', e)

        @block.vector
        def _(e):
            run('dve', e)

        @block.gpsimd
        def _(e):
            run('pool', e)

        @block.sync
        def _(e):
            run('sp', e)


class Arena:
    def __init__(self, t, size):
        self.t = t
        self.size = size
        self.off = 0

    def alloc(self, free_shape, dtype):
        es = 4 if dtype == F32 else (2 if dtype == BF16 else 1)
        n = 1
        for s_ in free_shape:
            n *= s_
        nb = (n * es + 31) // 32 * 32
        assert self.off + nb <= self.size, f"arena overflow {self.off}+{nb}>{self.size}"
        ap = self.t[:, self.off:self.off + n * es].bitcast(dtype)
        self.off += nb
        if len(free_shape) == 2:
            ap = ap.rearrange("p (a b) -> p a b", a=free_shape[0])
        elif len(free_shape) == 3:
            ap = ap.rearrange("p (a b c) -> p a b c", a=free_shape[0], b=free_shape[1])
        return ap

    def mark(self):
        return self.off

    def release(self, m):
        self.off = m


QK_BLOCKS = []
for i in range(4):
    QK_BLOCKS.append((i * 128, 'n64'))
QK_BLOCKS.append((512, 'n64'))
for g, kind in enumerate(['n64', 'p4', 'p16']):
    QK_BLOCKS.append((768 + g * 128, kind))
for g, kind in enumerate(['n64', 'p4', 'p16']):
    QK_BLOCKS.append((1152 + g * 128, kind))
for i in range(2):
    QK_BLOCKS.append((1920 + i * 128, 'n32'))
for i in range(2):
    QK_BLOCKS.append((2176 + i * 128, 'n32'))
for i in range(2):
    QK_BLOCKS.append((2688 + i * 128, 'd'))
for i in range(2):
    QK_BLOCKS.append((2944 + i * 128, 'd'))
NQKB = len(QK_BLOCKS)
VNAT_COLS = [(640, 128), (2432, 256), (3200, 256), (1536, 128)]
GATE_OFF = 3456
B_DIL = [1, 4, 16]


def perm_tokens(d):
    L = S // d
    j = np.arange(S)
    return (j % L) * d + (j // L)


def rope_tabs(dim):
    half = dim // 2
    inv = np.power(np.float32(10000.0), -(np.arange(0, dim, 2, dtype=np.float32) / np.float32(dim))).astype(np.float32)
    ang = (np.arange(S, dtype=np.float32)[:, None] * inv[None, :]).astype(np.float32)
    c = np.cos(ang).astype(np.float32)
    s_ = np.sin(ang).astype(np.float32)
    p = np.arange(128) % dim
    cosT = c[:, p % half].T.copy()
    sgn = np.where(p < half, -1.0, 1.0).astype(np.float32)
    sinT = (s_[:, p % half] * sgn[None, :]).T.copy()
    return cosT, sinT


def d_tables():
    rows = 64
    r0 = np.clip(np.arange(rows) - 4, 0, rows - 8)
    cj = np.arange(64)
    c0 = np.clip(cj - 8, 0, 48)
    col_ok = (cj[None, :] >= c0[:, None]) & (cj[None, :] < c0[:, None] + 16)
    dc = np.clip(cj[None, :] - cj[:, None], -15, 15) + 15
    tabs = {}
    tab_list = []
    pairs = []
    for n in range(32):
        lo = r0[2 * n] // 2
        hi = (r0[2 * n + 1] + 7) // 2
        pl = []
        for m in range(lo, hi + 1):
            valid = np.zeros((128, 128), dtype=bool)
            dr = np.zeros((128, 128), dtype=np.int64)
            for a in range(2):
                for b in range(2):
                    rho = 2 * m + a
                    i = 2 * n + b
                    ok = (r0[i] <= rho) and (rho <= r0[i] + 7)
                    if ok:
                        valid[a * 64:(a + 1) * 64, b * 64:(b + 1) * 64] = col_ok.T
                        dr[a * 64:(a + 1) * 64, b * 64:(b + 1) * 64] = rho - i + 7
            key = (m - n, valid.tobytes())
            if key not in tabs:
                tabs[key] = len(tab_list)
                tab_list.append((dr, valid))
            pl.append((m, tabs[key]))
        pairs.append(pl)
    dcidx = np.zeros((128, 128), dtype=np.int64)
    for a in range(2):
        for b in range(2):
            dcidx[a * 64:(a + 1) * 64, b * 64:(b + 1) * 64] = dc.T
    return tab_list, pairs, dcidx


D_TABS, D_PAIRS, D_DC = d_tables()
NTAB = len(D_TABS)

C_IDENT = 0
C_BD64 = 128
C_BD32 = 256
C_PSW64 = 384
C_PSW32 = 512
C_ONES = 640
C_MLO = 768
C_MHI = 896
C_DVALID = 1024
NCONST = C_DVALID + NTAB * 128


def make_consts():
    c = np.zeros((128, NCONST), dtype=np.float32)
    p = np.arange(128)
    c[:, C_IDENT:C_IDENT + 128] = np.eye(128, dtype=np.float32)
    c[:, C_BD64:C_BD64 + 128] = (p[:, None] // 64 == p[None, :] // 64)
    c[:, C_BD32:C_BD32 + 128] = (p[:, None] // 32 == p[None, :] // 32)
    part64 = (p // 64) * 64 + (p % 64 + 32) % 64
    part32 = (p // 32) * 32 + (p % 32 + 16) % 32
    c[:, C_PSW64:C_PSW64 + 128] = (p[:, None] == part64[None, :])
    c[:, C_PSW32:C_PSW32 + 128] = (p[:, None] == part32[None, :])
    c[:, C_ONES:C_ONES + 128] = 1.0
    c[:, C_MLO:C_MLO + 128] = (p[:, None] >= p[None, :])
    c[:, C_MHI:C_MHI + 128] = (p[:, None] <= p[None, :])
    for t, (dr, valid) in enumerate(D_TABS):
        c[:, C_DVALID + t * 128:C_DVALID + (t + 1) * 128] = valid
    return c


def host_prep(inp):
    f = lambda a: np.ascontiguousarray(np.asarray(a, dtype=np.float32))
    out = {}
    out['w_in'] = f(inp['w_in'])
    out['w_ba'] = f(inp['w_branch_a'])
    out['w_bb'] = f(inp['w_branch_b'])
    out['w_bc'] = f(inp['w_branch_c'])
    out['w_bd'] = f(inp['w_branch_d'])
    out['w_out'] = f(inp['w_out'])
    out['w_up'] = f(inp['w_up'])
    out['w_down'] = f(inp['w_down'])
    out['gb_attn'] = f(np.broadcast_to(f(inp['attn_norm_g'])[:, None, :], (2, 128, DM)))
    out['gb_mlp'] = f(np.broadcast_to(f(inp['mlp_norm_g'])[:, None, :], (2, 128, DM)))
    gcol = np.zeros((2, 128, NQKB), dtype=np.float32)
    aq, bq, cq, dq = f(inp['a_qk_norm_g']), f(inp['b_qk_norm_g']), f(inp['c_qk_norm_g']), f(inp['d_qk_norm_g'])
    for l in range(2):
        for b in range(4):
            gcol[l, :, b] = np.tile(aq[l, 0], 2)
        gcol[l, :, 4] = np.tile(aq[l, 1], 2)
        for b in range(5, 8):
            gcol[l, :, b] = np.tile(bq[l, 0], 2)
        for b in range(8, 11):
            gcol[l, :, b] = np.tile(bq[l, 1], 2)
        for b in range(11, 13):
            gcol[l, :, b] = np.tile(cq[l, 0], 4)
        for b in range(13, 15):
            gcol[l, :, b] = np.tile(cq[l, 1], 4)
        for b in range(15, 17):
            gcol[l, :, b] = np.tile(dq[l, 0], 2)
        for b in range(17, 19):
            gcol[l, :, b] = np.tile(dq[l, 1], 2)
    out['gcol'] = gcol
    out['sinkb'] = f(np.broadcast_to(f(inp['a_sink'])[:, None, :], (2, 128, 8)))
    out['lamb'] = f(np.broadcast_to(f(inp['c_lambda']).reshape(2, 1, 128), (2, 128, 128)))
    out['gsub'] = f(np.tile(f(inp['c_subln_g']), (1, 2)).reshape(2, 128, 1))
    rpb = f(inp['d_rel_bias'])
    db = np.zeros((2, 4, 128, NTAB, 128), dtype=np.float32)
    for t, (dr, valid) in enumerate(D_TABS):
        db[:, :, :, t, :] = rpb[:, :, dr, D_DC]
    out['dbias'] = db.reshape(2, 4, 128, NTAB * 128)
    c64, s64 = rope_tabs(64)
    c32, s32 = rope_tabs(32)
    p4, p16 = perm_tokens(4), perm_tokens(16)
    out['rope'] = np.ascontiguousarray(np.stack([c64, s64, c64[:, p4], s64[:, p4], c64[:, p16], s64[:, p16], c32, s32], 0))
    out['consts'] = make_consts()
    return out


PARAM_SHAPES = {
    'w_in': [2, DM, INC], 'w_ba': [2, 512, DM], 'w_bb': [2, 128, DM], 'w_bc': [2, 256, DM], 'w_bd': [2, 256, DM],
    'w_out': [2, DM, DM], 'w_up': [2, DM, 4096], 'w_down': [2, 4096, DM],
    'gb_attn': [2, 128, DM], 'gb_mlp': [2, 128, DM], 'gcol': [2, 128, NQKB], 'sinkb': [2, 128, 8],
    'lamb': [2, 128, 128], 'gsub': [2, 128, 1], 'dbias': [2, 4, 128, NTAB * 128],
    'rope': [8, 128, S], 'consts': [128, NCONST],
}


class Ctx:
    pass


def sl(start, n, step):
    return slice(start, start + (n - 1) * step + 1, step)


def ring(lst, i):
    return lst[i % len(lst)]


def phase_p1(C, l, xin):
    P, ar, D = C.P, C.ar, C.D
    nb0 = len(P.bufs)
    m0 = ar.mark()
    hT = ar.alloc([8, S], BF16)
    b_hT = P.buf('hT')
    gb = ar.alloc([DM], F32)
    b_gb = P.buf('gb')
    gcol = ar.alloc([NQKB], F32)
    b_gcol = P.buf('gcol')
    P.load(gb, D['gb_attn'][l], b_gb)
    P.load(gcol, D['gcol'][l], b_gcol)
    m1 = ar.mark()
    xt = [ar.alloc([DM], F32) for _ in range(3)]
    bx = P.bufs_n('xt', 3)
    junk = ar.alloc([DM], BF16)
    b_junk = P.buf('junk')
    ss = [ar.alloc([1], F32) for _ in range(2)]
    bss = P.bufs_n('ss', 2)
    hb = [ar.alloc([DM], BF16) for _ in range(2)]
    bhb = P.bufs_n('hb', 2)
    for tt in range(NT):
        i, j = tt % 3, tt % 2
        P.load(xt[i], xin[tt * 128:(tt + 1) * 128, :], bx[i])
        P.act(junk, xt[i], AF.Square, R=[bx[i]], W=[b_junk, bss[j]], accum_out=ss[j])
        P.act(ss[j], ss[j], AF.Ln, R=[bss[j]], W=[bss[j]], scale=1.0 / DM, bias=EPS)
        P.act(ss[j], ss[j], AF.Exp, R=[bss[j]], W=[bss[j]], scale=-0.5)
        P.stt(hb[j], xt[i], ss[j], gb, ALU.mult, ALU.mult, R=[bx[i], bss[j], b_gb], W=[bhb[j]])
        pbv = C.pb[j].bitcast(BF16)
        for kc in range(8):
            P.tr(pbv[:, kc * 128:(kc + 1) * 128], hb[j][:, kc * 128:(kc + 1) * 128], C.ident, R=[bhb[j], C.b_const], W=[C.bpb[j]])
        P.cp('act' if tt % 2 else 'dve', hT[:, :, tt * 128:(tt + 1) * 128], pbv.rearrange("p (k t) -> p k t", k=8), R=[C.bpb[j]], W=[b_hT])
    for kc in range(8):
        P.store(D['hT_d'][kc * 128:(kc + 1) * 128, :], hT[:, kc, :], b_hT)
    ar.release(m1)
    tabs = ar.alloc([2, S], F32)
    b_tabs = P.buf('ropetab')
    wq = [ar.alloc([8, 128], BF16) for _ in range(2)]
    bwq = P.bufs_n('wq', 2)
    NB = 3
    sq = [ar.alloc([512], BF16) for _ in range(NB)]
    bsq = P.bufs_n('sq', NB)
    xg = [ar.alloc([512], BF16) for _ in range(NB)]
    bxg = P.bufs_n('xg', NB)
    rs = [ar.alloc([512], F32) for _ in range(NB)]
    brs = P.bufs_n('rs', NB)
    ta = [ar.alloc([512], F32) for _ in range(NB)]
    bta = P.bufs_n('ta', NB)
    tb = [ar.alloc([512], F32) for _ in range(NB)]
    btb = P.bufs_n('tb', NB)
    ob = [ar.alloc([512], BF16) for _ in range(NB)]
    bob = P.bufs_n('ob', NB)
    w_in = D['w_in'][l].rearrange("(kc p) c -> p kc c", p=128)
    cur_tab = None
    it = 0
    for blk, (coff, kind) in enumerate(QK_BLOCKS):
        wi = blk % 2
        P.dma('pool', wq[wi], w_in[:, :, coff:coff + 128], bwq[wi], W=(bwq[wi],))
        tabkind = {'n64': 0, 'p4': 2, 'p16': 4, 'n32': 6, 'd': None}[kind]
        if tabkind is not None and tabkind != cur_tab:
            P.load(tabs[:, 0, :], D['rope'][tabkind], b_tabs)
            P.load(tabs[:, 1, :], D['rope'][tabkind + 1], b_tabs)
            cur_tab = tabkind
        dh = 32 if kind == 'n32' else 64
        bd = C.bd32 if dh == 32 else C.bd64
        psw = C.psw32 if dh == 32 else C.psw64
        for tc in range(8):
            k = it % NB
            it += 1
            if kind == 'p4':
                r, h0 = tc // 2, (tc % 2) * 512
                tok = lambda kc: hT[:, kc, sl(r + 4 * h0, 512, 4)]
            elif kind == 'p16':
                tok = lambda kc: hT[:, kc, :].rearrange("p (m r) -> p r m", r=16)[:, 2 * tc:2 * tc + 2, :]
            else:
                tok = lambda kc: hT[:, kc, tc * 512:(tc + 1) * 512]
            pA, pS, pR = C.pb[2 + (it % 2)], C.pb[4 + (it % 2)], C.pb[6 + (it % 2)]
            bA, bS, bR = C.bpb[2 + (it % 2)], C.bpb[4 + (it % 2)], C.bpb[6 + (it % 2)]
            for kc in range(8):
                P.mm(pA[:, :], wq[wi][:, kc, :], tok(kc), start=(kc == 0), stop=(kc == 7), R=[bwq[wi], b_hT], W=[bA])
            P.act(sq[k], pA[:, :], AF.Square, R=[bA], W=[bsq[k]])
            P.act(xg[k], pA[:, :], AF.Copy, R=[bA, b_gcol], W=[bxg[k]], scale=gcol[:, blk:blk + 1])
            P.mm(pS[:, :], bd, sq[k], R=[bsq[k], C.b_const], W=[bS])
            P.act(rs[k], pS[:, :], AF.Ln, R=[bS], W=[brs[k]], scale=1.0 / dh, bias=EPS)
            P.act(rs[k], rs[k], AF.Exp, R=[brs[k]], W=[brs[k]], scale=-0.5)
            csl = slice(tc * 512, (tc + 1) * 512)
            if kind == 'd':
                P.tt('dve', ob[k], xg[k], rs[k], ALU.mult, R=[bxg[k], brs[k]], W=[bob[k]])
            else:
                P.mm(pR[:, :], psw, xg[k], R=[bxg[k], C.b_const], W=[bR])
                P.tt('pool', ta[k], xg[k], tabs[:, 0, csl], ALU.mult, R=[bxg[k], b_tabs], W=[bta[k]])
                P.tt('dve', tb[k], pR[:, :], tabs[:, 1, csl], ALU.mult, R=[bR, b_tabs], W=[btb[k]])
                P.tt('pool', ta[k], ta[k], tb[k], ALU.add, R=[bta[k], btb[k]], W=[bta[k]])
                P.tt('dve', ob[k], ta[k], rs[k], ALU.mult, R=[bta[k], brs[k]], W=[bob[k]])
            P.store(D['QKT_d'][blk * 128:(blk + 1) * 128, csl], ob[k], bob[k])
    ar.release(m1)
    wv = ar.alloc([8, 1024], BF16)
    b_wv = P.buf('wv')
    o = 0
    for (coff, n) in VNAT_COLS + [(1536 + 128, 256)]:
        P.dma('pool', wv[:, :, o:o + n], w_in[:, :, coff:coff + n], b_wv, W=(b_wv,))
        o += n
    vn = [ar.alloc([12, 65], BF16) for _ in range(2)]
    bvn = P.bufs_n('vn', 2)
    vb = [ar.alloc([2, 2, 65], BF16) for _ in range(2)]
    bvb = P.bufs_n('vbp', 2)
    for j in range(2):
        P.memset('pool', vn[j][:, :, 64:65], 1.0, W=[bvn[j]])
        P.memset('pool', vb[j][:, :, :, 64:65], 1.0, W=[bvb[j]])
    p4, p16 = perm_tokens(4), perm_tokens(16)
    for tt in range(NT):
        j = tt % 2
        pa, pbk, pc = C.pb[j * 3], C.pb[j * 3 + 1], C.pb[j * 3 + 2]
        ba, bb_, bc = C.bpb[j * 3], C.bpb[j * 3 + 1], C.bpb[j * 3 + 2]
        for kc in range(8):
            P.mm(pa[:, :], hT[:, kc, tt * 128:(tt + 1) * 128], wv[:, kc, 0:512], start=(kc == 0), stop=(kc == 7), R=[b_hT, b_wv], W=[ba])
        for kc in range(8):
            P.mm(pbk[:, 0:256], hT[:, kc, tt * 128:(tt + 1) * 128], wv[:, kc, 512:768], start=(kc == 0), stop=(kc == 7), R=[b_hT, b_wv], W=[bb_])
        t4 = int(p4[tt * 128])
        t16 = int(p16[tt * 128])
        for kc in range(8):
            P.mm(pc[:, 0:128], hT[:, kc, sl(t4, 128, 4)], wv[:, kc, 768:896], start=(kc == 0), stop=(kc == 7), R=[b_hT, b_wv], W=[bc])
        for kc in range(8):
            P.mm(pc[:, 128:256], hT[:, kc, sl(t16, 128, 16)], wv[:, kc, 896:1024], start=(kc == 0), stop=(kc == 7), R=[b_hT, b_wv], W=[bc])
        P.cp('act', vn[j][:, 0:8, 0:64], pa[:, :].rearrange("p (h d) -> p h d", h=8), R=[ba], W=[bvn[j]])
        P.cp('dve', vn[j][:, 8:12, 0:64], pbk[:, 0:256].rearrange("p (h d) -> p h d", h=4), R=[bb_], W=[bvn[j]])
        P.cp('dve', vb[j][:, :, :, 0:64], pc[:, 0:256].rearrange("p (g h d) -> p g h d", g=2, h=2), R=[bc], W=[bvb[j]])
        P.store(D['Vnat_d'][tt * 128:(tt + 1) * 128, :], vn[j].rearrange("p h d -> p (h d)"), bvn[j])
        for g in range(2):
            P.store(D['Vb_d'][g, tt * 128:(tt + 1) * 128, :], vb[j][:, g].rearrange("p h d -> p (h d)"), bvb[j])
    ar.release(m0)
    P.barrier()
    P.retire(nb0)


def finalize_norm(C, acc, bacc, n, dst, bdst, shape3=None, esink=None, tagk=0):
    P = C.P
    k = tagk % 2
    r, rhi, rlo = C.fr[k], C.frhi[k], C.frlo[k]
    br = C.bfr[k]
    bc, bbc = C.pb[6 + k], C.bpb[6 + k]
    src = acc[64:65, 0:n]
    if esink is not None:
        j, q = shape3
        P.tt('dve', r[64:65, 0:n].rearrange("p (j q) -> p j q", j=j), src.rearrange("p (j q) -> p j q", j=j),
             esink, ALU.add, R=[bacc, C.b_esk], W=[br])
        P.recip(r[64:65, 0:n], r[64:65, 0:n], R=[br], W=[br])
    else:
        P.recip(r[64:65, 0:n], src, R=[bacc], W=[br])
    P.cp('dve', rhi[64:65, 0:n], r[64:65, 0:n], R=[br], W=[br])
    P.tt('dve', rlo[64:65, 0:n], r[64:65, 0:n], rhi[64:65, 0:n], ALU.subtract, R=[br], W=[br])
    P.mm(bc[0:64, 0:n], C.ones_bf[64:65, 0:64], rhi[64:65, 0:n], start=True, stop=False, R=[br, C.b_const], W=[bbc])
    P.mm(bc[0:64, 0:n], C.ones_bf[64:65, 0:64], rlo[64:65, 0:n], start=False, stop=True, R=[br, C.b_const], W=[bbc])
    bcs, bbcs = C.fbcs[k], C.bfbcs[k]
    P.cp('act', bcs[0:64, 0:n], bc[0:64, 0:n], R=[bbc], W=[bbcs])
    a0 = acc[0:64, 0:n]
    b0 = bcs[0:64, 0:n]
    if shape3 is not None:
        j, q = shape3
        a0 = a0.rearrange("p (j q) -> p j q", j=j)
        b0 = b0.rearrange("p (j q) -> p j q", j=j)
    P.tt('dve', dst, a0, b0, ALU.mult, R=[bacc, bbcs], W=[bdst])


def alloc_fin(C):
    ar, P = C.ar, C.P
    C.fr = [ar.alloc([512], F32) for _ in range(2)]
    C.frhi = [ar.alloc([512], BF16) for _ in range(2)]
    C.frlo = [ar.alloc([512], BF16) for _ in range(2)]
    C.bfr = P.bufs_n('fr', 2)
    C.fbcs = [ar.alloc([512], F32) for _ in range(2)]
    C.bfbcs = P.bufs_n('fbcs', 2)


def mixer_a(C, l):
    P, ar, D = C.P, C.ar, C.D
    nb0 = len(P.bufs)
    m0 = ar.mark()
    alloc_fin(C)
    QT = ar.alloc([4, S], BF16)
    bQT = P.buf('aQT')
    KT = ar.alloc([S], BF16)
    bKT = P.buf('aKT')
    V = ar.alloc([NT, 130], BF16)
    bV = P.buf('aV')
    yst = ar.alloc([4, S], BF16)
    byst = P.buf('ayst')
    esk = ar.alloc([8], F32)
    C.b_esk = P.buf('esk')
    P.load(esk, D['sinkb'][l], C.b_esk)
    P.act(esk, esk, AF.Exp, R=[C.b_esk], W=[C.b_esk])
    for g in range(2):
        for j in range(4):
            h = 4 * g + j
            P.load(QT[g * 64:(g + 1) * 64, j, :], D['QKT_d'][h * 64:(h + 1) * 64, :], bQT)
    P.load(KT, D['QKT_d'][512:640, :], bKT)
    P.load(V, D['Vnat_d'].rearrange("(t p) c -> p t c", p=128)[:, :, 0:130], bV)
    NP = 3
    pt = [ar.alloc([512], BF16) for _ in range(NP)]
    bpt = P.bufs_n('apt', NP)
    mlo = C.cbf[:, C_MLO:C_MLO + 128].unsqueeze(1).broadcast_to([128, 4, 128])
    mhi = C.cbf[:, C_MHI:C_MHI + 128].unsqueeze(1).broadcast_to([128, 4, 128])
    it = 0
    fi = 0
    for g in range(2):
        ps = slice(g * 64, (g + 1) * 64)
        for n in range(NT):
            ms = [m for m in (n - 1, n, n + 1) if 0 <= m < NT]
            acc, bacc = C.pb[3 + (fi % 2)], C.bpb[3 + (fi % 2)]
            for idx, m in enumerate(ms):
                k = it % NP
                st, bst = C.pb[it % 3], C.bpb[it % 3]
                it += 1
                P.mm(st[:, :].rearrange("p (j q) -> p j q", j=4), KT[ps, m * 128:(m + 1) * 128], QT[ps, :, n * 128:(n + 1) * 128],
                     R=[bKT, bQT], W=[bst])
                P.act(pt[k], st[:, :], AF.Exp, R=[bst], W=[bpt[k]], scale=0.125)
                if m != n:
                    v3 = pt[k].rearrange("p (j q) -> p j q", j=4)
                    P.tt('pool' if it % 2 else 'dve', v3, v3, mlo if m < n else mhi, ALU.mult, R=[bpt[k], C.b_const], W=[bpt[k]])
                P.mm(acc[0:65, :], V[:, m, g * 65:(g + 1) * 65], pt[k], start=(idx == 0), stop=(idx == len(ms) - 1),
                     R=[bV, bpt[k]], W=[bacc])
            es = esk[64:65, 4 * g:4 * g + 4].unsqueeze(2).broadcast_to([1, 4, 128])
            finalize_norm(C, acc, bacc, 512, yst[0:64, :, n * 128:(n + 1) * 128], byst, shape3=(4, 128), esink=es, tagk=fi)
            fi += 1
        for j in range(4):
            h = 4 * g + j
            P.store(D['YT_d'][h * 64:(h + 1) * 64, :], yst[0:64, j, :], byst)
    ar.release(m0)
    P.barrier()
    P.retire(nb0)


def mixer_b(C, l):
    P, ar, D = C.P, C.ar, C.D
    nb0 = len(P.bufs)
    m0 = ar.mark()
    alloc_fin(C)
    accN = ar.alloc([2, S], F32)
    baccN = P.buf('baccN')
    yst = ar.alloc([2, S], BF16)
    byst = P.buf('byst')
    QTs = [ar.alloc([S], BF16) for _ in range(2)]
    KTs = [ar.alloc([S], BF16) for _ in range(2)]
    Vs = [ar.alloc([NT, 130], BF16) for _ in range(2)]
    bQ = P.bufs_n('bQT', 2)
    bK = P.bufs_n('bKT', 2)
    bVv = P.bufs_n('bV', 2)
    NP = 3
    pt = [ar.alloc([512], BF16) for _ in range(NP)]
    bpt = P.bufs_n('bpt', NP)
    it = 0
    ai = 0
    for gi, d in enumerate(B_DIL):
        L = S // d
        nb = L // 128
        QT, KT, V = QTs[gi % 2], KTs[gi % 2], Vs[gi % 2]
        bq, bk, bv = bQ[gi % 2], bK[gi % 2], bVv[gi % 2]
        P.load(QT, D['QKT_d'][(5 + gi) * 128:(6 + gi) * 128, :], bq)
        P.load(KT, D['QKT_d'][(8 + gi) * 128:(9 + gi) * 128, :], bk)
        if gi == 0:
            P.load(V, D['Vnat_d'].rearrange("(t p) c -> p t c", p=128)[:, :, 650:780], bv)
        else:
            P.load(V, D['Vb_d'][gi - 1].rearrange("(t p) c -> p t c", p=128), bv)
        for jh in range(2):
            ps = slice(jh * 64, (jh + 1) * 64)
            for r in range(d):
                base = r * L
                qblocks = []
                for n_ in range(-1, nb):
                    q0 = 128 * n_ + 64
                    qa, qb = max(q0, 0), min(q0 + 128, L)
                    tiles = []
                    if n_ >= 0:
                        tiles.append((n_, C_MLO))
                    if n_ + 1 < nb:
                        tiles.append((n_ + 1, C_MHI))
                    qblocks.append((qa, qb - qa, qa - q0, tiles))
                for g0 in range(0, len(qblocks), 4):
                    grp = qblocks[g0:g0 + 4]
                    acc, bacc = C.pb[3 + (ai % 3)], C.bpb[3 + (ai % 3)]
                    ai += 1
                    ncols = sum(q[1] for q in grp)
                    pstart = grp[0][0]
                    col = 0
                    for bi in range(0, len(grp), 2):
                        sub = grp[bi:bi + 2]
                        k = it % NP
                        st, bst = C.pb[it % 3], C.bpb[it % 3]
                        it += 1
                        plist = []
                        sc = 0
                        for (qa, nq, aoff, tiles) in sub:
                            for ti, (m, mk) in enumerate(tiles):
                                tg = base // 128 + m
                                P.mm(st[:, sc:sc + nq], KT[ps, tg * 128:(tg + 1) * 128], QT[ps, base + qa:base + qa + nq],
                                     R=[bk, bq], W=[bst])
                                plist.append((sc, nq, aoff, mk, tg, col, ti == 0, ti == len(tiles) - 1))
                                sc += nq
                            col += nq
                        P.act(pt[k][:, 0:sc], st[:, 0:sc], AF.Exp, R=[bst], W=[bpt[k]], scale=0.125)
                        for pi, (s0, nq, aoff, mk, tg, c0, _, _) in enumerate(plist):
                            P.tt('pool' if pi % 2 else 'dve', pt[k][:, s0:s0 + nq], pt[k][:, s0:s0 + nq],
                                 C.cbf[:, mk + aoff:mk + aoff + nq], ALU.mult, R=[bpt[k], C.b_const], W=[bpt[k]])
                        for (s0, nq, aoff, mk, tg, c0, first, last) in plist:
                            cc = c0
                            P.mm(acc[0:65, cc:cc + nq], V[:, tg, jh * 65:(jh + 1) * 65], pt[k][:, s0:s0 + nq], start=first, stop=last,
                                 R=[bv, bpt[k]], W=[bacc])
                    if gi == 0:
                        P.cp('act', accN[0:65, jh, pstart:pstart + ncols], acc[0:65, 0:ncols], R=[bacc], W=[baccN])
                    else:
                        t0 = r + d * pstart
                        view = accN[0:65, jh, sl(t0, ncols, d)]
                        P.tt('dve', view, view, acc[0:65, 0:ncols], ALU.add, R=[bacc, baccN], W=[baccN])
    fi = 0
    for jh in range(2):
        for ch in range(8):
            cs = slice(ch * 512, (ch + 1) * 512)
            k = fi % 2
            r, rhi, rlo, br = C.fr[k], C.frhi[k], C.frlo[k], C.bfr[k]
            bc, bbc = C.pb[6 + k], C.bpb[6 + k]
            P.recip(r[64:65, :], accN[64:65, jh, cs], R=[baccN], W=[br])
            P.cp('dve', rhi[64:65, :], r[64:65, :], R=[br], W=[br])
            P.tt('dve', rlo[64:65, :], r[64:65, :], rhi[64:65, :], ALU.subtract, R=[br], W=[br])
            P.mm(bc[0:64, :], C.ones_bf[64:65, 0:64], rhi[64:65, :], start=True, stop=False, R=[br, C.b_const], W=[bbc])
            P.mm(bc[0:64, :], C.ones_bf[64:65, 0:64], rlo[64:65, :], start=False, stop=True, R=[br, C.b_const], W=[bbc])
            P.tt('dve', yst[0:64, jh, cs], accN[0:64, jh, cs], bc[0:64, :], ALU.mult, R=[baccN, bbc], W=[byst])
            fi += 1
        P.store(D['YT_d'][512 + jh * 64:512 + (jh + 1) * 64, :], yst[0:64, jh, :], byst)
    ar.release(m0)
    P.barrier()
    P.retire(nb0)


def mixer_d(C, l):
    P, ar, D = C.P, C.ar, C.D
    nb0 = len(P.bufs)
    m0 = ar.mark()
    alloc_fin(C)
    QT = ar.alloc([2, S], BF16)
    KT = ar.alloc([2, S], BF16)
    V = ar.alloc([NT, 260], BF16)
    bQT, bKT, bV = P.buf('dQT'), P.buf('dKT'), P.buf('dV')
    yst = ar.alloc([4, S], BF16)
    byst = P.buf('dyst')
    EB = ar.alloc([4, NTAB * 128], BF16)
    bEB = P.buf('dEB')
    tmpb = [ar.alloc([NTAB * 128], F32) for _ in range(2)]
    btmp = P.bufs_n('dtmp', 2)
    for i in range(2):
        P.load(QT[:, i, :], D['QKT_d'][(15 + i) * 128:(16 + i) * 128, :], bQT)
        P.load(KT[:, i, :], D['QKT_d'][(17 + i) * 128:(18 + i) * 128, :], bKT)
    P.load(V, D['Vnat_d'].rearrange("(t p) c -> p t c", p=128)[:, :, 390:650], bV)
    for h in range(4):
        P.load(tmpb[h % 2], D['dbias'][l, h], btmp[h % 2])
        P.act(tmpb[h % 2], tmpb[h % 2], AF.Exp, R=[btmp[h % 2]], W=[btmp[h % 2]])
        P.tt('dve', EB[:, h, :], tmpb[h % 2], C.cbf[:, C_DVALID:C_DVALID + NTAB * 128], ALU.mult, R=[btmp[h % 2], C.b_const], W=[bEB])
    NP = 3
    pt = [ar.alloc([512], BF16) for _ in range(NP)]
    bpt = P.bufs_n('dpt', NP)
    it = 0
    fi = 0
    for h in range(4):
        bq = h // 2
        ps = slice((h % 2) * 64, (h % 2 + 1) * 64)
        for n4 in range(8):
            acc, bacc = C.pb[3 + (fi % 2)], C.bpb[3 + (fi % 2)]
            plist = []
            for n in range(n4 * 4, n4 * 4 + 4):
                pl = D_PAIRS[n]
                for pi, (m, tab) in enumerate(pl):
                    plist.append((n, m, tab, pi == 0, pi == len(pl) - 1))
            for c0 in range(0, len(plist), 4):
                chunk = plist[c0:c0 + 4]
                k = it % NP
                st, bst = C.pb[it % 3], C.bpb[it % 3]
                it += 1
                for i, (n, m, tab, first, last) in enumerate(chunk):
                    P.mm(st[:, i * 128:(i + 1) * 128], KT[ps, bq, m * 128:(m + 1) * 128], QT[ps, bq, n * 128:(n + 1) * 128],
                         R=[bKT, bQT], W=[bst])
                used = len(chunk) * 128
                P.act(pt[k][:, 0:used], st[:, 0:used], AF.Exp, R=[bst], W=[bpt[k]], scale=0.125)
                for i, (n, m, tab, first, last) in enumerate(chunk):
                    P.tt('pool' if i % 2 else 'dve', pt[k][:, i * 128:(i + 1) * 128], pt[k][:, i * 128:(i + 1) * 128],
                         EB[:, h, tab * 128:(tab + 1) * 128], ALU.mult, R=[bpt[k], bEB], W=[bpt[k]])
                for i, (n, m, tab, first, last) in enumerate(chunk):
                    qc = (n % 4) * 128
                    P.mm(acc[0:65, qc:qc + 128], V[:, m, h * 65:(h + 1) * 65], pt[k][:, i * 128:(i + 1) * 128], start=first, stop=last,
                         R=[bV, bpt[k]], W=[bacc])
            finalize_norm(C, acc, bacc, 512, yst[0:64, h, n4 * 512:(n4 + 1) * 512], byst, tagk=fi)
            fi += 1
        P.store(D['YT_d'][896 + h * 64:896 + (h + 1) * 64, :], yst[0:64, h, :], byst)
    ar.release(m0)
    P.barrier()
    P.retire(nb0)


def mixer_c(C, l):
    P, ar, D = C.P, C.ar, C.D
    lam_init = 0.8 - 0.6 * math.exp(-0.3 * l)
    nb0 = len(P.bufs)
    m0 = ar.mark()
    alloc_fin(C)
    QT = ar.alloc([3, S], BF16)
    KT = ar.alloc([3, S], BF16)
    V = ar.alloc([NT, 260], BF16)
    bQT, bKT, bV = P.buf('cQT'), P.buf('cKT'), P.buf('cV')
    yst = ar.alloc([4, S], BF16)
    byst = P.buf('cyst')
    for b in range(8):
        grp, slot = b // 3, b % 3
        h, c = b // 2, b % 2
        row = (h // 2) * 128 + ((h % 2) * 2 + c) * 32
        P.load(QT[slot * 32:(slot + 1) * 32, grp, :], D['QKT_d'][11 * 128 + row:11 * 128 + row + 32, :], bQT)
        P.load(KT[slot * 32:(slot + 1) * 32, grp, :], D['QKT_d'][13 * 128 + row:13 * 128 + row + 32, :], bKT)
    P.load(V, D['Vnat_d'].rearrange("(t p) c -> p t c", p=128)[:, :, 130:390], bV)
    lamb = ar.alloc([128], F32)
    blam = P.buf('lam')
    lt = ar.alloc([2, 32], F32)
    l2 = ar.alloc([2], F32)
    nlam = ar.alloc([1], F32)
    gsc = ar.alloc([1], F32)
    bgsc = P.buf('gsc')
    P.load(lamb, D['lamb'][l], blam)
    P.load(gsc, D['gsub'][l], bgsc)
    lv = lamb.rearrange("p (a b c) -> p a b c", a=2, b=2)
    P.tt('dve', lt, lv[:, :, 0, :], lv[:, :, 1, :], ALU.mult, R=[blam], W=[blam])
    P.op('dve', lambda e: e.reduce_sum(out=l2, in_=lt, axis=mybir.AxisListType.X), R=[blam], W=[blam])
    P.act(l2, l2, AF.Exp, R=[blam], W=[blam])
    P.tt('dve', nlam, l2[:, 0:1], l2[:, 1:2], ALU.subtract, R=[blam], W=[blam])
    P.ts('dve', nlam, nlam, lam_init, -1.0, ALU.add, ALU.mult, R=[blam], W=[blam])
    P.ts('dve', gsc, gsc, 1.0 - lam_init, None, ALU.mult, R=[bgsc], W=[bgsc])
    NP = 4
    pt = [ar.alloc([512], BF16) for _ in range(NP)]
    bpt = P.bufs_n('cpt', NP)
    to = [ar.alloc([512], F32) for _ in range(2)]
    t1 = [ar.alloc([512], F32) for _ in range(2)]
    sqb = [ar.alloc([512], BF16) for _ in range(2)]
    rsd = [ar.alloc([512], F32) for _ in range(2)]
    bto = P.bufs_n('cto', 2)
    scale = 32.0 ** -0.5
    it = 0
    fi = 0
    for h in range(4):
        for Q in range(8):
            qs = slice(Q * 512, (Q + 1) * 512)
            accs = [C.pb[4], C.pb[5]]
            baccs = [C.bpb[4], C.bpb[5]]
            for c in range(2):
                b = 2 * h + c
                grp, slot = b // 3, b % 3
                ps = slice(slot * 32, (slot + 1) * 32)
                for kt in range(NT):
                    k = it % NP
                    st, bst = C.pb[it % 4], C.bpb[it % 4]
                    it += 1
                    P.mm(st[:, :], KT[ps, grp, kt * 128:(kt + 1) * 128], QT[ps, grp, qs], R=[bKT, bQT], W=[bst])
                    P.act(pt[k], st[:, :], AF.Exp, R=[bst], W=[bpt[k]], scale=scale)
                    P.mm(accs[c][0:65, :], V[:, kt, h * 65:(h + 1) * 65], pt[k], start=(kt == 0), stop=(kt == NT - 1),
                         R=[bV, bpt[k]], W=[baccs[c]])
            k2 = fi % 2
            bt = bto[k2]
            for c in range(2):
                r, rhi, rlo, br = C.fr[c], C.frhi[c], C.frlo[c], C.bfr[c]
                bc, bbc = C.pb[6 + c], C.bpb[6 + c]
                P.recip(r[64:65, :], accs[c][64:65, :], R=[baccs[c]], W=[br])
                if c == 1:
                    P.ts('dve', r[64:65, :], r[64:65, :], nlam[64:65, 0:1], None, ALU.mult, R=[br, blam], W=[br])
                P.cp('dve', rhi[64:65, :], r[64:65, :], R=[br], W=[br])
                P.tt('dve', rlo[64:65, :], r[64:65, :], rhi[64:65, :], ALU.subtract, R=[br], W=[br])
                P.mm(bc[0:64, :], C.ones_bf[64:65, 0:64], rhi[64:65, :], start=True, stop=False, R=[br, C.b_const], W=[bbc])
                P.mm(bc[0:64, :], C.ones_bf[64:65, 0:64], rlo[64:65, :], start=False, stop=True, R=[br, C.b_const], W=[bbc])
                P.cp('act', C.fbcs[c][0:64, :], bc[0:64, :], R=[bbc], W=[C.bfbcs[c]])
            P.tt('dve', to[k2][0:64, :], accs[0][0:64, :], C.fbcs[0][0:64, :], ALU.mult, R=[baccs[0], C.bfbcs[0]], W=[bt])
            P.tt('dve', t1[k2][0:64, :], accs[1][0:64, :], C.fbcs[1][0:64, :], ALU.mult, R=[baccs[1], C.bfbcs[1], bt], W=[bt])
            P.tt('pool', to[k2][0:64, :], to[k2][0:64, :], t1[k2][0:64, :], ALU.add, R=[bt], W=[bt])
            P.act(sqb[k2][0:64, :], to[k2][0:64, :], AF.Square, R=[bt], W=[bt])
            ssb, bssb = C.pb[6], C.bpb[6]
            P.mm(ssb[0:64, :], C.ones_bf[0:64, 0:64], sqb[k2][0:64, :], R=[bt, C.b_const], W=[bssb])
            P.act(rsd[k2][0:64, :], ssb[0:64, :], AF.Ln, R=[bssb], W=[bt], scale=1.0 / 64, bias=EPS)
            P.act(rsd[k2][0:64, :], rsd[k2][0:64, :], AF.Exp, R=[bt], W=[bt], scale=-0.5)
            P.stt(yst[0:64, h, qs], to[k2][0:64, :], gsc[0:64, 0:1], rsd[k2][0:64, :], ALU.mult, ALU.mult, R=[bt, bgsc], W=[byst])
            fi += 1
        P.store(D['YT_d'][640 + h * 64:640 + (h + 1) * 64, :], yst[0:64, h, :], byst)
    ar.release(m0)
    P.barrier()
    P.retire(nb0)


def phase_p3a(C, l, xin, xout):
    P, ar, D = C.P, C.ar, C.D
    nb0 = len(P.bufs)
    m0 = ar.mark()
    Wg = ar.alloc([8, 4096], BF16)
    Wb = ar.alloc([9, DM], BF16)
    Wo = ar.alloc([8, DM], BF16)
    bWg, bWb, bWo = P.buf('Wg'), P.buf('Wb'), P.buf('Wo')
    w_in = D['w_in'][l].rearrange("(kc p) c -> p kc c", p=128)
    for kc in range(8):
        P.dma('pool', Wg[:, kc, :], w_in[:, kc, GATE_OFF:GATE_OFF + 4096], bWg, W=(bWg,))
    P.dma('pool', Wb[:, 0:4, :], D['w_ba'][l].rearrange("(kc p) c -> p kc c", p=128), bWb, W=(bWb,))
    P.dma('pool', Wb[:, 4, :], D['w_bb'][l], bWb, W=(bWb,))
    P.dma('pool', Wb[:, 5:7, :], D['w_bc'][l].rearrange("(kc p) c -> p kc c", p=128), bWb, W=(bWb,))
    P.dma('pool', Wb[:, 7:9, :], D['w_bd'][l].rearrange("(kc p) c -> p kc c", p=128), bWb, W=(bWb,))
    P.dma('pool', Wo, D['w_out'][l].rearrange("(kc p) c -> p kc c", p=128), bWo, W=(bWo,))
    hTc = [ar.alloc([8, 512], BF16) for _ in range(2)]
    YTc = [ar.alloc([9, 512], BF16) for _ in range(2)]
    bh = P.bufs_n('hTc', 2)
    by = P.bufs_n('YTc', 2)
    mT = [ar.alloc([8, 512], BF16) for _ in range(2)]
    bmT = P.bufs_n('mT', 2)
    sig = [ar.alloc([512], F32) for _ in range(3)]
    bsig = P.bufs_n('sig', 3)
    tmp = [ar.alloc([512], F32) for _ in range(2)]
    btmp = P.bufs_n('mtmp', 2)
    macc = [ar.alloc([512], F32) for _ in range(2)]
    bmacc = P.bufs_n('macc', 2)
    xt = [ar.alloc([DM], F32) for _ in range(2)]
    bxt = P.bufs_n('x3', 2)
    xo = [ar.alloc([DM], F32) for _ in range(2)]
    bxo = P.bufs_n('xo3', 2)
    hT_v = D['hT_d'].rearrange("(kc p) t -> p kc t", p=128)
    YT_v = D['YT_d'].rearrange("(kc p) t -> p kc t", p=128)
    branches = [(0, 4), (4, 5), (5, 7), (7, 9)]
    it = 0
    si = 0
    ti = 0
    for tg in range(8):
        g2 = tg % 2
        ts_ = slice(tg * 512, (tg + 1) * 512)
        P.load(hTc[g2], hT_v[:, :, ts_], bh[g2])
        P.load(YTc[g2], YT_v[:, :, ts_], by[g2])
        for ct in range(8):
            ma, bma = macc[ct % 2], bmacc[ct % 2]
            for br in range(4):
                pg, bpg = C.pb[(it % 2) * 2], C.bpb[(it % 2) * 2]
                py, bpy = C.pb[(it % 2) * 2 + 1], C.bpb[(it % 2) * 2 + 1]
                it += 1
                c0 = br * 1024 + ct * 128
                for kc in range(8):
                    P.mm(pg[:, :], Wg[:, kc, c0:c0 + 128], hTc[g2][:, kc, :], start=(kc == 0), stop=(kc == 7), R=[bWg, bh[g2]], W=[bpg])
                b0, b1 = branches[br]
                for bi in range(b0, b1):
                    P.mm(py[:, :], Wb[:, bi, ct * 128:(ct + 1) * 128], YTc[g2][:, bi, :], start=(bi == b0), stop=(bi == b1 - 1),
                         R=[bWb, by[g2]], W=[bpy])
                s_, bs_ = sig[si % 3], bsig[si % 3]
                si += 1
                P.act(s_, pg[:, :], AF.Sigmoid, R=[bpg], W=[bs_])
                if br == 0:
                    P.tt('dve', ma, s_, py[:, :], ALU.mult, R=[bs_, bpy], W=[bma])
                else:
                    t_, bt_ = tmp[ti % 2], btmp[ti % 2]
                    ti += 1
                    P.tt('dve', t_, s_, py[:, :], ALU.mult, R=[bs_, bpy], W=[bt_])
                    if br < 3:
                        P.tt('pool', ma, ma, t_, ALU.add, R=[bma, bt_], W=[bma])
                    else:
                        P.tt('pool', mT[g2][:, ct, :], ma, t_, ALU.add, R=[bma, bt_], W=[bmT[g2]])
        for tt in range(4):
            tok0 = tg * 512 + tt * 128
            xi = (tg * 4 + tt) % 2
            P.load(xt[xi], xin[tok0:tok0 + 128, :], bxt[xi])
            for cg in range(2):
                po, bpo = C.pb[4 + (cg + 2 * tt) % 4], C.bpb[4 + (cg + 2 * tt) % 4]
                for kc in range(8):
                    P.mm(po[:, :], mT[g2][:, kc, tt * 128:(tt + 1) * 128], Wo[:, kc, cg * 512:(cg + 1) * 512], start=(kc == 0), stop=(kc == 7),
                         R=[bmT[g2], bWo], W=[bpo])
                P.tt('dve', xo[xi][:, cg * 512:(cg + 1) * 512], po[:, :], xt[xi][:, cg * 512:(cg + 1) * 512], ALU.add,
                     R=[bpo, bxt[xi]], W=[bxo[xi]])
            P.store(xout[tok0:tok0 + 128, :], xo[xi], bxo[xi])
    ar.release(m0)
    P.barrier()
    P.retire(nb0)


def phase_p3b(C, l, xin, xout):
    P, ar, D = C.P, C.ar, C.D
    nb0 = len(P.bufs)
    m0 = ar.mark()
    Wu = ar.alloc([8, 4096], BF16)
    Wd = ar.alloc([32, DM], BF16)
    gb = ar.alloc([DM], F32)
    bWu, bWd, bgb = P.buf('Wu'), P.buf('Wd'), P.buf('gbm')
    wu_v = D['w_up'][l].rearrange("(kc p) c -> p kc c", p=128)
    wd_v = D['w_down'][l].rearrange("(kc p) c -> p kc c", p=128)
    P.load(gb, D['gb_mlp'][l], bgb)
    for kc in range(8):
        P.dma('pool', Wu[:, kc, :], wu_v[:, kc, :], bWu, W=(bWu,))
    for k4 in range(8):
        P.dma('pool', Wd[:, k4 * 4:(k4 + 1) * 4, :], wd_v[:, k4 * 4:(k4 + 1) * 4, :], bWd, W=(bWd,))
    xt = [ar.alloc([DM], F32) for _ in range(2)]
    bxt = P.bufs_n('x4', 2)
    xo = [ar.alloc([DM], F32) for _ in range(1)]
    bxo = P.bufs_n('xo4', 1)
    ss = [ar.alloc([1], F32) for _ in range(2)]
    bss = P.bufs_n('ss4', 2)
    hb = [ar.alloc([DM], BF16) for _ in range(2)]
    bhb = P.bufs_n('hb4', 2)
    hmT = [ar.alloc([8, 512], BF16) for _ in range(2)]
    bhm = P.bufs_n('hmT', 2)
    uT = ar.alloc([32, 512], BF16)
    buT = P.buf('uT')
    rl = [ar.alloc([512], F32) for _ in range(2)]
    brl = P.bufs_n('rl', 2)
    xi = 0
    it = 0
    for tg in range(8):
        g2 = tg % 2
        for tt in range(4):
            tok0 = tg * 512 + tt * 128
            i, j = xi % 2, xi % 2
            xi += 1
            P.load(xt[i], xin[tok0:tok0 + 128, :], bxt[i])
            P.act(hb[j], xt[i], AF.Square, R=[bxt[i]], W=[bhb[j], bss[j]], accum_out=ss[j])
            P.act(ss[j], ss[j], AF.Ln, R=[bss[j]], W=[bss[j]], scale=1.0 / DM, bias=EPS)
            P.act(ss[j], ss[j], AF.Exp, R=[bss[j]], W=[bss[j]], scale=-0.5)
            P.stt(hb[j], xt[i], ss[j], gb, ALU.mult, ALU.mult, R=[bxt[i], bss[j], bgb], W=[bhb[j]])
            pbv = C.pb[6 + j].bitcast(BF16)
            for kc in range(8):
                P.tr(pbv[:, kc * 128:(kc + 1) * 128], hb[j][:, kc * 128:(kc + 1) * 128], C.ident, R=[bhb[j], C.b_const], W=[C.bpb[6 + j]])
            P.cp('dve', hmT[g2][:, :, tt * 128:(tt + 1) * 128], pbv.rearrange("p (k t) -> p k t", k=8), R=[C.bpb[6 + j]], W=[bhm[g2]])
        for mt in range(32):
            pu, bpu = C.pb[it % 3], C.bpb[it % 3]
            r_, br_ = rl[it % 2], brl[it % 2]
            it += 1
            for kc in range(8):
                P.mm(pu[:, :], Wu[:, kc, mt * 128:(mt + 1) * 128], hmT[g2][:, kc, :], start=(kc == 0), stop=(kc == 7), R=[bWu, bhm[g2]], W=[bpu])
            P.act(r_, pu[:, :], AF.Relu, R=[bpu], W=[br_])
            P.tt('pool' if mt % 2 else 'dve', uT[:, mt, :], r_, r_, ALU.mult, R=[br_], W=[buT])
        for tt in range(4):
            tok0 = tg * 512 + tt * 128
            i, j = xi % 2, 0
            xi += 1
            P.load(xt[i], xin[tok0:tok0 + 128, :], bxt[i])
            for cg in range(2):
                pd, bpd = C.pb[3 + (cg + 2 * tt) % 3], C.bpb[3 + (cg + 2 * tt) % 3]
                for mt in range(32):
                    P.mm(pd[:, :], uT[:, mt, tt * 128:(tt + 1) * 128], Wd[:, mt, cg * 512:(cg + 1) * 512], start=(mt == 0), stop=(mt == 31),
                         R=[buT, bWd], W=[bpd])
                P.tt('dve', xo[j][:, cg * 512:(cg + 1) * 512], pd[:, :], xt[i][:, cg * 512:(cg + 1) * 512], ALU.add,
                     R=[bpd, bxt[i]], W=[bxo[j]])
            P.store(xout[tok0:tok0 + 128, :], xo[j], bxo[j])
    ar.release(m0)
    P.barrier()
    P.retire(nb0)


ARENA = 206 * 1024
NSEM_POOL = 96


class SemPool:
    def __init__(self, nc, stack, n):
        self.sems = [stack.enter_context(nc.semaphore(f"sm{i}")) for i in range(n)]
        self.i = 0

        self.free = []

    def get(self):
        if self.free:
            return self.free.pop()
        s_ = self.sems[self.i]
        self.i += 1
        return (s_, 0)

    def put(self, sem, cnt):
        self.free.append((sem, cnt))


def build(n_layers=2, phases=None, dbg=False):
    nc = bass.Bass("TRN2", target_bir_lowering=False)
    D = {}
    x = nc.dram_tensor("x", [S, DM], F32, kind="ExternalInput").ap()
    for name, shp in PARAM_SHAPES.items():
        D[name] = nc.dram_tensor(name, shp, F32, kind="ExternalInput").ap()
    y = nc.dram_tensor("y", [S, DM], F32, kind="ExternalOutput").ap()
    sk = "ExternalOutput" if dbg else "Internal"
    D['hT_d'] = nc.dram_tensor("hT_d", [DM, S], BF16, kind=sk).ap()
    D['QKT_d'] = nc.dram_tensor("QKT_d", [NQKB * 128, S], BF16, kind=sk).ap()
    D['Vnat_d'] = nc.dram_tensor("Vnat_d", [S, 780], BF16, kind=sk).ap()
    D['Vb_d'] = nc.dram_tensor("Vb_d", [2, S, 130], BF16, kind=sk).ap()
    D['YT_d'] = nc.dram_tensor("YT_d", [1152, S], BF16, kind=sk).ap()
    x1_d = nc.dram_tensor("x1_d", [S, DM], F32, kind=sk).ap()
    x2_d = nc.dram_tensor("x2_d", [S, DM], F32, kind=sk).ap()
    with ExitStack() as stack:
        arena_t = stack.enter_context(nc.sbuf_tensor("arena", [128, ARENA], U8))
        pbs = [stack.enter_context(nc.psum_tensor(f"pb{i}", [128, 512], F32)) for i in range(8)]
        sp_ = SemPool(nc, stack, NSEM_POOL)

        class _St:
            def enter_context(self, cm):
                raise RuntimeError

        P = Prog.__new__(Prog)
        P.nc = nc
        P.ops = {e: [] for e in ENGS}
        P.bufs = []
        P.esem = {e: sp_.get()[0] for e in ENGS}
        P.nsem = len(ENGS)

        def dma(eng, out, in_, owner, R=(), W=()):
            if owner.sem is None:
                owner.sem, owner.cnt = sp_.get()
            owner.cnt += 16
            o = Op(eng, lambda e: e.dma_start(out=out, in_=in_))
            o.dma_sem = owner.sem
            P._track(o, ('dma', owner.sem, owner.cnt), 'dma', R, W)
            P.ops[eng].append(o)
            return o
        P.dma = dma

        def retire(nb0):
            for b in P.bufs[nb0:]:
                if b.sem is not None:
                    sp_.put(b.sem, b.cnt)
                    b.sem = None
            del P.bufs[nb0:]
        P.retire = retire
        block = stack.enter_context(nc.Block())
        C = Ctx()
        C.P, C.D = P, D
        C.ar = Arena(arena_t, ARENA)
        C.pb = [p[:, :] for p in pbs]
        C.bpb = P.bufs_n('pb', 8)
        C.cbf = C.ar.alloc([NCONST], BF16)
        C.b_const = P.buf('consts')
        P.dma('pool', C.cbf, D['consts'], C.b_const, W=(C.b_const,))
        C.ident = C.cbf[:, C_IDENT:C_IDENT + 128]
        C.bd64 = C.cbf[:, C_BD64:C_BD64 + 128]
        C.bd32 = C.cbf[:, C_BD32:C_BD32 + 128]
        C.psw64 = C.cbf[:, C_PSW64:C_PSW64 + 128]
        C.psw32 = C.cbf[:, C_PSW32:C_PSW32 + 128]
        C.ones_bf = C.cbf[:, C_ONES:C_ONES + 128]
        all_ph = ['p1', 'a', 'b', 'c', 'd', 'p3a', 'p3b']
        phases = phases or all_ph
        for l in range(n_layers):
            xin = x if l == 0 else x2_d
            xfin = y if l == n_layers - 1 else x2_d
            if 'p1' in phases:
                phase_p1(C, l, xin)
            if 'a' in phases:
                mixer_a(C, l)
            if 'b' in phases:
                mixer_b(C, l)
            if 'c' in phases:
                mixer_c(C, l)
            if 'd' in phases:
                mixer_d(C, l)
            if 'p3a' in phases:
                phase_p3a(C, l, xin, x1_d)
            if 'p3b' in phases:
                phase_p3b(C, l, x1_d, xfin)
        P.barrier()
        P.emit(block)
        C.nsem = sp_.i
    return nc, C


_CACHE = {}


def kernel(**inputs):
    x = np.ascontiguousarray(np.asarray(inputs['x'], dtype=np.float32))
    params = host_prep(inputs)
    if 'nc' not in _CACHE:
        _CACHE['nc'] = build()[0]
    nc = _CACHE['nc']
    in_maps = []
    for b in range(8):
        m = {'x': x[b]}
        m.update(params)
        in_maps.append(m)
    res = run_bass_kernel_spmd(nc, in_maps, core_ids=list(range(8)))
    return np.stack([np.asarray(r['y'], dtype=np.float32) for r in res.results], axis=0)
```

```python
import math
from contextlib import ExitStack
import numpy as np
import concourse.bass as bass
import concourse.mybir as mybir
from concourse.bass_utils import run_bass_kernel_spmd

F32 = mybir.dt.float32
BF16 = mybir.dt.bfloat16
U8 = mybir.dt.uint8
AF = mybir.ActivationFunctionType
ALU = mybir.AluOpType

S = 4096
DM = 1024
NT = 32
EPS = 1e-6
INC = 7552
ENGS = ['pe', 'act', 'dve', 'pool', 'sp']
SAME_ENG_SYNC = ('act', 'dve', 'pool')


class Buf:
    __slots__ = ('name', 'w', 'rs', 'sem', 'cnt')

    def __init__(self, name):
        self.name = name
        self.w = None
        self.rs = {}
        self.sem = None
        self.cnt = 0


class Op:
    __slots__ = ('eng', 'fn', 'deps', 'dwaits', 'needs_inc', 'semval', 'dma_sem')

    def __init__(self, eng, fn):
        self.eng = eng
        self.fn = fn
        self.deps = set()
        self.dwaits = {}
        self.needs_inc = False
        self.semval = 0
        self.dma_sem = None


class Prog:
    def __init__(self, nc, stack):
        self.nc = nc
        self.stack = stack
        self.ops = {e: [] for e in ENGS}
        self.bufs = []
        self.esem = {e: stack.enter_context(nc.semaphore("s_" + e)) for e in ENGS}
        self.nsem = len(ENGS)

    def buf(self, name):
        b = Buf(name)
        self.bufs.append(b)
        return b

    def bufs_n(self, name, n):
        return [self.buf(f"{name}{i}") for i in range(n)]

    def _add_ev(self, o, ev):
        if ev is None:
            return
        if ev[0] == 'op':
            d = ev[1]
            if d.eng == o.eng and d.eng not in SAME_ENG_SYNC:
                return
            o.deps.add(d)
            d.needs_inc = True
        else:
            _, sem, val = ev
            cur = o.dwaits.get(id(sem))
            if cur is None or cur[1] < val:
                o.dwaits[id(sem)] = (sem, val)

    def _track(self, o, ev, key, R, W):
        for b in R:
            self._add_ev(o, b.w)
        for b in W:
            self._add_ev(o, b.w)
            for e2 in b.rs.values():
                self._add_ev(o, e2)
        for b in R:
            b.rs[key] = ev
        for b in W:
            b.w = ev
            b.rs = {}

    def op(self, eng, fn, R=(), W=()):
        o = Op(eng, fn)
        self._track(o, ('op', o), eng, R, W)
        self.ops[eng].append(o)
        return o

    def dma(self, eng, out, in_, owner, R=(), W=()):
        if owner.sem is None:
            owner.sem = self.stack.enter_context(self.nc.semaphore("d_" + owner.name))
            self.nsem += 1
        owner.cnt += 16
        o = Op(eng, lambda e: e.dma_start(out=out, in_=in_))
        o.dma_sem = owner.sem
        self._track(o, ('dma', owner.sem, owner.cnt), 'dma', R, W)
        self.ops[eng].append(o)
        return o

    def load(self, out, in_, owner, eng='sp'):
        return self.dma(eng, out, in_, owner, R=(), W=(owner,))

    def store(self, out, in_, owner, eng='sp'):
        return self.dma(eng, out, in_, owner, R=(owner,), W=())

    def barrier(self):
        o = Op('sp', lambda e: e.nop())
        for E in ENGS:
            if self.ops[E]:
                last = None
                for c in reversed(self.ops[E]):
                    if c.fn is not None and c.dma_sem is None:
                        last = c
                        break
                if last is not None and E != 'sp':
                    o.deps.add(last)
                    last.needs_inc = True
        for b in self.bufs:
            if b.sem is not None and b.cnt > 0:
                o.dwaits[id(b.sem)] = (b.sem, b.cnt)
        o.needs_inc = True
        self.ops['sp'].append(o)
        for E in ENGS:
            if E != 'sp':
                w = Op(E, None)
                w.deps.add(o)
                self.ops[E].append(w)
        for b in self.bufs:
            b.w = None
            b.rs = {}

    def mm(self, out, lhsT, rhs, start=True, stop=True, R=(), W=()):
        return self.op('pe', lambda e: e.matmul(out, lhsT, rhs, start=start, stop=stop), R, W)

    def tr(self, out, in_, ident, R=(), W=()):
        return self.op('pe', lambda e: e.transpose(out, in_, ident), R, W)

    def act(self, out, in_, func, R=(), W=(), **kw):
        return self.op('act', lambda e: e.activation(out=out, in_=in_, func=func, **kw), R, W)

    def tt(self, eng, out, in0, in1, op, R=(), W=()):
        return self.op(eng, lambda e: e.tensor_tensor(out=out, in0=in0, in1=in1, op=op), R, W)

    def ts(self, eng, out, in0, s1, s2, op0, op1=None, R=(), W=()):
        if op1 is None:
            return self.op(eng, lambda e: e.tensor_scalar(out=out, in0=in0, scalar1=s1, scalar2=None, op0=op0), R, W)
        return self.op(eng, lambda e: e.tensor_scalar(out=out, in0=in0, scalar1=s1, scalar2=s2, op0=op0, op1=op1), R, W)

    def stt(self, out, in0, scalar, in1, op0, op1, R=(), W=()):
        return self.op('dve', lambda e: e.scalar_tensor_tensor(out=out, in0=in0, scalar=scalar, in1=in1, op0=op0, op1=op1), R, W)

    def cp(self, eng, out, in_, R=(), W=()):
        if eng == 'act':
            return self.op('act', lambda e: e.copy(out=out, in_=in_), R, W)
        return self.op(eng, lambda e: e.tensor_copy(out=out, in_=in_), R, W)

    def recip(self, out, in_, R=(), W=()):
        return self.op('dve', lambda e: e.reciprocal(out=out, in_=in_), R, W)

    def memset(self, eng, ap, val, W=()):
        return self.op(eng, lambda e: e.memset(ap, val), (), W)

    def emit(self, block):
        for E in ENGS:
            c = 0
            for o in self.ops[E]:
                if o.needs_inc and o.dma_sem is None:
                    c += 1
                    o.semval = c
        esem = self.esem

        def run(E, eng):
            waited = {}
            for o in self.ops[E]:
                needs = []
                for d in o.deps:
                    needs.append((esem[d.eng], d.semval))
                for sem, val in o.dwaits.values():
                    needs.append((sem, val))
                for sem, val in needs:
                    k = id(sem)
                    if waited.get(k, 0) < val:
                        eng.wait_ge(sem, val)
                        waited[k] = val
                if o.fn is not None:
                    ins = o.fn(eng)
                    if o.dma_sem is not None:
                        ins.then_inc(o.dma_sem, 16)
                    elif o.needs_inc:
                        ins.then_inc(esem[E], 1)

        @block.tensor
        def _(e):
            run('pe', e)

        @block.scalar
        def _(e):
            run('act', e)

        @block.vector
        def _(e):
            run('dve', e)

        @block.gpsimd
        def _(e):
            run('pool', e)

        @block.sync
        def _(e):
            run('sp', e)


class Arena:
    def __init__(self, t, size):
        self.t = t
        self.size = size
        self.off = 0

    def alloc(self, free_shape, dtype):
        es = 4 if dtype == F32 else (2 if dtype == BF16 else 1)
        n = 1
        for s_ in free_shape:
            n *= s_
        nb = (n * es + 31) // 32 * 32
        assert self.off + nb <= self.size, f"arena overflow {self.off}+{nb}>{self.size}"
        ap = self.t[:, self.off:self.off + n * es].bitcast(dtype)
        self.off += nb
        if len(free_shape) == 2:
            ap = ap.rearrange("p (a b) -> p a b", a=free_shape[0])
        elif len(free_shape) == 3:
            ap = ap.rearrange("p (a b c) -> p a b c", a=free_shape[0], b=free_shape[1])
        return ap

    def mark(self):
        return self.off

    def release(self, m):
        self.off = m


QK_BLOCKS = []
for i in range(4):
    QK_BLOCKS.append((i * 128, 'n64'))
QK_BLOCKS.append((512, 'n64'))
for g, kind in enumerate(['n64', 'p4', 'p16']):
    QK_BLOCKS.append((768 + g * 128, kind))
for g, kind in enumerate(['n64', 'p4', 'p16']):
    QK_BLOCKS.append((1152 + g * 128, kind))
for i in range(2):
    QK_BLOCKS.append((1920 + i * 128, 'n32'))
for i in range(2):
    QK_BLOCKS.append((2176 + i * 128, 'n32'))
for i in range(2):
    QK_BLOCKS.append((2688 + i * 128, 'd'))
for i in range(2):
    QK_BLOCKS.append((2944 + i * 128, 'd'))
NQKB = len(QK_BLOCKS)
VNAT_COLS = [(640, 128), (2432, 256), (3200, 256), (1536, 128)]
GATE_OFF = 3456
B_DIL = [1, 4, 16]


def perm_tokens(d):
    L = S // d
    j = np.arange(S)
    return (j % L) * d + (j // L)


def rope_tabs(dim):
    half = dim // 2
    inv = np.power(np.float32(10000.0), -(np.arange(0, dim, 2, dtype=np.float32) / np.float32(dim))).astype(np.float32)
    ang = (np.arange(S, dtype=np.float32)[:, None] * inv[None, :]).astype(np.float32)
    c = np.cos(ang).astype(np.float32)
    s_ = np.sin(ang).astype(np.float32)
    p = np.arange(128) % dim
    cosT = c[:, p % half].T.copy()
    sgn = np.where(p < half, -1.0, 1.0).astype(np.float32)
    sinT = (s_[:, p % half] * sgn[None, :]).T.copy()
    return cosT, sinT


def d_tables():
    rows = 64
    r0 = np.clip(np.arange(rows) - 4, 0, rows - 8)
    cj = np.arange(64)
    c0 = np.clip(cj - 8, 0, 48)
    col_ok = (cj[None, :] >= c0[:, None]) & (cj[None, :] < c0[:, None] + 16)
    dc = np.clip(cj[None, :] - cj[:, None], -15, 15) + 15
    tabs = {}
    tab_list = []
    pairs = []
    for n in range(32):
        lo = r0[2 * n] // 2
        hi = (r0[2 * n + 1] + 7) // 2
        pl = []
        for m in range(lo, hi + 1):
            valid = np.zeros((128, 128), dtype=bool)
            dr = np.zeros((128, 128), dtype=np.int64)
            for a in range(2):
                for b in range(2):
                    rho = 2 * m + a
                    i = 2 * n + b
                    ok = (r0[i] <= rho) and (rho <= r0[i] + 7)
                    if ok:
                        valid[a * 64:(a + 1) * 64, b * 64:(b + 1) * 64] = col_ok.T
                        dr[a * 64:(a + 1) * 64, b * 64:(b + 1) * 64] = rho - i + 7
            key = (m - n, valid.tobytes())
            if key not in tabs:
                tabs[key] = len(tab_list)
                tab_list.append((dr, valid))
            pl.append((m, tabs[key]))
        pairs.append(pl)
    dcidx = np.zeros((128, 128), dtype=np.int64)
    for a in range(2):
        for b in range(2):
            dcidx[a * 64:(a + 1) * 64, b * 64:(b + 1) * 64] = dc.T
    return tab_list, pairs, dcidx


D_TABS, D_PAIRS, D_DC = d_tables()
NTAB = len(D_TABS)

C_IDENT = 0
C_BD64 = 128
C_BD32 = 256
C_PSW64 = 384
C_PSW32 = 512
C_ONES = 640
C_MLO = 768
C_MHI = 896
C_DVALID = 1024
NCONST = C_DVALID + NTAB * 128


def make_consts():
    c = np.zeros((128, NCONST), dtype=np.float32)
    p = np.arange(128)
    c[:, C_IDENT:C_IDENT + 128] = np.eye(128, dtype=np.float32)
    c[:, C_BD64:C_BD64 + 128] = (p[:, None] // 64 == p[None, :] // 64)
    c[:, C_BD32:C_BD32 + 128] = (p[:, None] // 32 == p[None, :] // 32)
    part64 = (p // 64) * 64 + (p % 64 + 32) % 64
    part32 = (p // 32) * 32 + (p % 32 + 16) % 32
    c[:, C_PSW64:C_PSW64 + 128] = (p[:, None] == part64[None, :])
    c[:, C_PSW32:C_PSW32 + 128] = (p[:, None] == part32[None, :])
    c[:, C_ONES:C_ONES + 128] = 1.0
    c[:, C_MLO:C_MLO + 128] = (p[:, None] >= p[None, :])
    c[:, C_MHI:C_MHI + 128] = (p[:, None] <= p[None, :])
    for t, (dr, valid) in enumerate(D_TABS):
        c[:, C_DVALID + t * 128:C_DVALID + (t + 1) * 128] = valid
    return c


def host_prep(inp):
    f = lambda a: np.ascontiguousarray(np.asarray(a, dtype=np.float32))
    out = {}
    out['w_in'] = f(inp['w_in'])
    out['w_ba'] = f(inp['w_branch_a'])
    out['w_bb'] = f(inp['w_branch_b'])
    out['w_bc'] = f(inp['w_branch_c'])
    out['w_bd'] = f(inp['w_branch_d'])
    out['w_out'] = f(inp['w_out'])
    out['w_up'] = f(inp['w_up'])
    out['w_down'] = f(inp['w_down'])
    out['gb_attn'] = f(np.broadcast_to(f(inp['attn_norm_g'])[:, None, :], (2, 128, DM)))
    out['gb_mlp'] = f(np.broadcast_to(f(inp['mlp_norm_g'])[:, None, :], (2, 128, DM)))
    gcol = np.zeros((2, 128, NQKB), dtype=np.float32)
    aq, bq, cq, dq = f(inp['a_qk_norm_g']), f(inp['b_qk_norm_g']), f(inp['c_qk_norm_g']), f(inp['d_qk_norm_g'])
    for l in range(2):
        for b in range(4):
            gcol[l, :, b] = np.tile(aq[l, 0], 2)
        gcol[l, :, 4] = np.tile(aq[l, 1], 2)
        for b in range(5, 8):
            gcol[l, :, b] = np.tile(bq[l, 0], 2)
        for b in range(8, 11):
            gcol[l, :, b] = np.tile(bq[l, 1], 2)
        for b in range(11, 13):
            gcol[l, :, b] = np.tile(cq[l, 0], 4)
        for b in range(13, 15):
            gcol[l, :, b] = np.tile(cq[l, 1], 4)
        for b in range(15, 17):
            gcol[l, :, b] = np.tile(dq[l, 0], 2)
        for b in range(17, 19):
            gcol[l, :, b] = np.tile(dq[l, 1], 2)
    out['gcol'] = gcol
    out['sinkb'] = f(np.broadcast_to(f(inp['a_sink'])[:, None, :], (2, 128, 8)))
    out['lamb'] = f(np.broadcast_to(f(inp['c_lambda']).reshape(2, 1, 128), (2, 128, 128)))
    out['gsub'] = f(np.tile(f(inp['c_subln_g']), (1, 2)).reshape(2, 128, 1))
    rpb = f(inp['d_rel_bias'])
    db = np.zeros((2, 4, 128, NTAB, 128), dtype=np.float32)
    for t, (dr, valid) in enumerate(D_TABS):
        db[:, :, :, t, :] = rpb[:, :, dr, D_DC]
    out['dbias'] = db.reshape(2, 4, 128, NTAB * 128)
    c64, s64 = rope_tabs(64)
    c32, s32 = rope_tabs(32)
    p4, p16 = perm_tokens(4), perm_tokens(16)
    out['rope'] = np.ascontiguousarray(np.stack([c64, s64, c64[:, p4], s64[:, p4], c64[:, p16], s64[:, p16], c32, s32], 0))
    out['consts'] = make_consts()
    return out


PARAM_SHAPES = {
    'w_in': [2, DM, INC], 'w_ba': [2, 512, DM], 'w_bb': [2, 128, DM], 'w_bc': [2, 256, DM], 'w_bd': [2, 256, DM],
    'w_out': [2, DM, DM], 'w_up': [2, DM, 4096], 'w_down': [2, 4096, DM],
    'gb_attn': [2, 128, DM], 'gb_mlp': [2, 128, DM], 'gcol': [2, 128, NQKB], 'sinkb': [2, 128, 8],
    'lamb': [2, 128, 128], 'gsub': [2, 128, 1], 'dbias': [2, 4, 128, NTAB * 128],
    'rope': [8, 128, S], 'consts': [128, NCONST],
}


class Ctx:
    pass


def sl(start, n, step):
    return slice(start, start + (n - 1) * step + 1, step)


def ring(lst, i):
    return lst[i % len(lst)]


def phase_p1(C, l, xin):
    P, ar, D = C.P, C.ar, C.D
    nb0 = len(P.bufs)
    m0 = ar.mark()
    hT = ar.alloc([8, S], BF16)
    b_hT = P.buf('hT')
    gb = ar.alloc([DM], F32)
    b_gb = P.buf('gb')
    gcol = ar.alloc([NQKB], F32)
    b_gcol = P.buf('gcol')
    P.load(gb, D['gb_attn'][l], b_gb)
    P.load(gcol, D['gcol'][l], b_gcol)
    m1 = ar.mark()
    xt = [ar.alloc([DM], F32) for _ in range(3)]
    bx = P.bufs_n('xt', 3)
    junk = ar.alloc([DM], BF16)
    b_junk = P.buf('junk')
    ss = [ar.alloc([1], F32) for _ in range(2)]
    bss = P.bufs_n('ss', 2)
    hb = [ar.alloc([DM], BF16) for _ in range(2)]
    bhb = P.bufs_n('hb', 2)
    for tt in range(NT):
        i, j = tt % 3, tt % 2
        P.load(xt[i], xin[tt * 128:(tt + 1) * 128, :], bx[i])
        P.act(junk, xt[i], AF.Square, R=[bx[i]], W=[b_junk, bss[j]], accum_out=ss[j])
        P.act(ss[j], ss[j], AF.Ln, R=[bss[j]], W=[bss[j]], scale=1.0 / DM, bias=EPS)
        P.act(ss[j], ss[j], AF.Exp, R=[bss[j]], W=[bss[j]], scale=-0.5)
        P.stt(hb[j], xt[i], ss[j], gb, ALU.mult, ALU.mult, R=[bx[i], bss[j], b_gb], W=[bhb[j]])
        pbv = C.pb[j].bitcast(BF16)
        for kc in range(8):
            P.tr(pbv[:, kc * 128:(kc + 1) * 128], hb[j][:, kc * 128:(kc + 1) * 128], C.ident, R=[bhb[j], C.b_const], W=[C.bpb[j]])
        P.cp('act' if tt % 2 else 'dve', hT[:, :, tt * 128:(tt + 1) * 128], pbv.rearrange("p (k t) -> p k t", k=8), R=[C.bpb[j]], W=[b_hT])
    for kc in range(8):
        P.store(D['hT_d'][kc * 128:(kc + 1) * 128, :], hT[:, kc, :], b_hT)
    ar.release(m1)
    tabs = ar.alloc([2, S], F32)
    b_tabs = P.buf('ropetab')
    wq = [ar.alloc([8, 128], BF16) for _ in range(2)]
    bwq = P.bufs_n('wq', 2)
    NB = 3
    sq = [ar.alloc([512], BF16) for _ in range(NB)]
    bsq = P.bufs_n('sq', NB)
    xg = [ar.alloc([512], BF16) for _ in range(NB)]
    bxg = P.bufs_n('xg', NB)
    rs = [ar.alloc([512], F32) for _ in range(NB)]
    brs = P.bufs_n('rs', NB)
    ta = [ar.alloc([512], F32) for _ in range(NB)]
    bta = P.bufs_n('ta', NB)
    tb = [ar.alloc([512], F32) for _ in range(NB)]
    btb = P.bufs_n('tb', NB)
    ob = [ar.alloc([512], BF16) for _ in range(NB)]
    bob = P.bufs_n('ob', NB)
    w_in = D['w_in'][l].rearrange("(kc p) c -> p kc c", p=128)
    items = [(blk, tc) for blk in range(NQKB) for tc in range(8)]
    state = {'tab': None}

    def tokf(kind, tc):
        if kind == 'p4':
            r, h0 = tc // 2, (tc % 2) * 512
            return lambda kc: hT[:, kc, sl(r + 4 * h0, 512, 4)]
        if kind == 'p16':
            return lambda kc: hT[:, kc, :].rearrange("p (m r) -> p r m", r=16)[:, 2 * tc:2 * tc + 2, :]
        return lambda kc: hT[:, kc, tc * 512:(tc + 1) * 512]

    def s0(itm, t):
        blk, tc = itm
        coff, kind = QK_BLOCKS[blk]
        wi = blk % 2
        if tc == 0:
            P.dma('pool', wq[wi], w_in[:, :, coff:coff + 128], bwq[wi], W=(bwq[wi],))
        tok = tokf(kind, tc)
        pA, bA = C.pb[2 + (t % 2)], C.bpb[2 + (t % 2)]
        for kc in range(8):
            P.mm(pA[:, :], wq[wi][:, kc, :], tok(kc), start=(kc == 0), stop=(kc == 7), R=[bwq[wi], b_hT], W=[bA])

    def s1(itm, t):
        blk, tc = itm
        coff, kind = QK_BLOCKS[blk]
        tabkind = {'n64': 0, 'p4': 2, 'p16': 4, 'n32': 6, 'd': None}[kind]
        if tabkind is not None and tabkind != state['tab']:
            P.load(tabs[:, 0, :], D['rope'][tabkind], b_tabs)
            P.load(tabs[:, 1, :], D['rope'][tabkind + 1], b_tabs)
            state['tab'] = tabkind
        dh = 32 if kind == 'n32' else 64
        bd = C.bd32 if dh == 32 else C.bd64
        psw = C.psw32 if dh == 32 else C.psw64
        k = t % NB
        pA, pS, pR = C.pb[2 + (t % 2)], C.pb[4 + (t % 2)], C.pb[6 + (t % 2)]
        bA, bS, bR = C.bpb[2 + (t % 2)], C.bpb[4 + (t % 2)], C.bpb[6 + (t % 2)]
        P.act(sq[k], pA[:, :], AF.Square, R=[bA], W=[bsq[k]])
        P.act(xg[k], pA[:, :], AF.Copy, R=[bA, b_gcol], W=[bxg[k]], scale=gcol[:, blk:blk + 1])
        P.mm(pS[:, :], bd, sq[k], R=[bsq[k], C.b_const], W=[bS])
        if kind != 'd':
            P.mm(pR[:, :], psw, xg[k], R=[bxg[k], C.b_const], W=[bR])
        P.act(rs[k], pS[:, :], AF.Ln, R=[bS], W=[brs[k]], scale=1.0 / dh, bias=EPS)
        P.act(rs[k], rs[k], AF.Exp, R=[brs[k]], W=[brs[k]], scale=-0.5)
        csl = slice(tc * 512, (tc + 1) * 512)
        if kind == 'd':
            P.tt('dve', ob[k], xg[k], rs[k], ALU.mult, R=[bxg[k], brs[k]], W=[bob[k]])
        else:
            P.tt('pool', ta[k], xg[k], tabs[:, 0, csl], ALU.mult, R=[bxg[k], b_tabs], W=[bta[k]])
            P.tt('dve', tb[k], pR[:, :], tabs[:, 1, csl], ALU.mult, R=[bR, b_tabs], W=[btb[k]])
            P.tt('pool', ta[k], ta[k], tb[k], ALU.add, R=[bta[k], btb[k]], W=[bta[k]])
            P.tt('dve', ob[k], ta[k], rs[k], ALU.mult, R=[bta[k], brs[k]], W=[bob[k]])
        P.store(D['QKT_d'][blk * 128:(blk + 1) * 128, csl], ob[k], bob[k])

    run_pipeline(items, 1, s0, s1)
    ar.release(m1)
    wv = ar.alloc([8, 1024], BF16)
    b_wv = P.buf('wv')
    o = 0
    for (coff, n) in VNAT_COLS + [(1536 + 128, 256)]:
        P.dma('pool', wv[:, :, o:o + n], w_in[:, :, coff:coff + n], b_wv, W=(b_wv,))
        o += n
    vn = [ar.alloc([12, 65], BF16) for _ in range(2)]
    bvn = P.bufs_n('vn', 2)
    vb = [ar.alloc([2, 2, 65], BF16) for _ in range(2)]
    bvb = P.bufs_n('vbp', 2)
    for j in range(2):
        P.memset('pool', vn[j][:, :, 64:65], 1.0, W=[bvn[j]])
        P.memset('pool', vb[j][:, :, :, 64:65], 1.0, W=[bvb[j]])
    p4, p16 = perm_tokens(4), perm_tokens(16)
    for tt in range(NT):
        j = tt % 2
        pa, pbk, pc = C.pb[j * 3], C.pb[j * 3 + 1], C.pb[j * 3 + 2]
        ba, bb_, bc = C.bpb[j * 3], C.bpb[j * 3 + 1], C.bpb[j * 3 + 2]
        for kc in range(8):
            P.mm(pa[:, :], hT[:, kc, tt * 128:(tt + 1) * 128], wv[:, kc, 0:512], start=(kc == 0), stop=(kc == 7), R=[b_hT, b_wv], W=[ba])
        for kc in range(8):
            P.mm(pbk[:, 0:256], hT[:, kc, tt * 128:(tt + 1) * 128], wv[:, kc, 512:768], start=(kc == 0), stop=(kc == 7), R=[b_hT, b_wv], W=[bb_])
        t4 = int(p4[tt * 128])
        t16 = int(p16[tt * 128])
        for kc in range(8):
            P.mm(pc[:, 0:128], hT[:, kc, sl(t4, 128, 4)], wv[:, kc, 768:896], start=(kc == 0), stop=(kc == 7), R=[b_hT, b_wv], W=[bc])
        for kc in range(8):
            P.mm(pc[:, 128:256], hT[:, kc, sl(t16, 128, 16)], wv[:, kc, 896:1024], start=(kc == 0), stop=(kc == 7), R=[b_hT, b_wv], W=[bc])
        P.cp('act', vn[j][:, 0:8, 0:64], pa[:, :].rearrange("p (h d) -> p h d", h=8), R=[ba], W=[bvn[j]])
        P.cp('dve', vn[j][:, 8:12, 0:64], pbk[:, 0:256].rearrange("p (h d) -> p h d", h=4), R=[bb_], W=[bvn[j]])
        P.cp('dve', vb[j][:, :, :, 0:64], pc[:, 0:256].rearrange("p (g h d) -> p g h d", g=2, h=2), R=[bc], W=[bvb[j]])
        P.store(D['Vnat_d'][tt * 128:(tt + 1) * 128, :], vn[j].rearrange("p h d -> p (h d)"), bvn[j])
        for g in range(2):
            P.store(D['Vb_d'][g, tt * 128:(tt + 1) * 128, :], vb[j][:, g].rearrange("p h d -> p (h d)"), bvb[j])
    ar.release(m0)
    P.barrier()
    P.retire(nb0)


def run_pipeline(items, LA, s0, s1):
    n = len(items)
    for t in range(n + LA):
        if t < n:
            s0(items[t], t)
        if t >= LA:
            s1(items[t - LA], t - LA)


def finalize_norm(C, acc, bacc, n, dst, bdst, shape3=None, esink=None, tagk=0):
    P = C.P
    k = tagk % 2
    r, rhi, rlo = C.fr[k], C.frhi[k], C.frlo[k]
    br = C.bfr[k]
    bc, bbc = C.pb[6 + k], C.bpb[6 + k]
    src = acc[64:65, 0:n]
    if esink is not None:
        j, q = shape3
        P.tt('dve', r[64:65, 0:n].rearrange("p (j q) -> p j q", j=j), src.rearrange("p (j q) -> p j q", j=j),
             esink, ALU.add, R=[bacc, C.b_esk], W=[br])
        P.recip(r[64:65, 0:n], r[64:65, 0:n], R=[br], W=[br])
    else:
        P.recip(r[64:65, 0:n], src, R=[bacc], W=[br])
    P.cp('dve', rhi[64:65, 0:n], r[64:65, 0:n], R=[br], W=[br])
    P.tt('dve', rlo[64:65, 0:n], r[64:65, 0:n], rhi[64:65, 0:n], ALU.subtract, R=[br], W=[br])
    P.mm(bc[0:64, 0:n], C.ones_bf[64:65, 0:64], rhi[64:65, 0:n], start=True, stop=False, R=[br, C.b_const], W=[bbc])
    P.mm(bc[0:64, 0:n], C.ones_bf[64:65, 0:64], rlo[64:65, 0:n], start=False, stop=True, R=[br, C.b_const], W=[bbc])
    bcs, bbcs = C.fbcs[k], C.bfbcs[k]
    P.cp('act', bcs[0:64, 0:n], bc[0:64, 0:n], R=[bbc], W=[bbcs])
    a0 = acc[0:64, 0:n]
    b0 = bcs[0:64, 0:n]
    if shape3 is not None:
        j, q = shape3
        a0 = a0.rearrange("p (j q) -> p j q", j=j)
        b0 = b0.rearrange("p (j q) -> p j q", j=j)
    P.tt('dve', dst, a0, b0, ALU.mult, R=[bacc, bbcs], W=[bdst])


def alloc_fin(C):
    ar, P = C.ar, C.P
    C.fr = [ar.alloc([512], F32) for _ in range(2)]
    C.frhi = [ar.alloc([512], BF16) for _ in range(2)]
    C.frlo = [ar.alloc([512], BF16) for _ in range(2)]
    C.bfr = P.bufs_n('fr', 2)
    C.fbcs = [ar.alloc([512], F32) for _ in range(2)]
    C.bfbcs = P.bufs_n('fbcs', 2)


def mixer_a(C, l):
    P, ar, D = C.P, C.ar, C.D
    nb0 = len(P.bufs)
    m0 = ar.mark()
    alloc_fin(C)
    QT = ar.alloc([4, S], BF16)
    bQT = P.buf('aQT')
    KT = ar.alloc([S], BF16)
    bKT = P.buf('aKT')
    V = ar.alloc([NT, 130], BF16)
    bV = P.buf('aV')
    yst = ar.alloc([4, S], BF16)
    byst = P.buf('ayst')
    esk = ar.alloc([8], F32)
    C.b_esk = P.buf('esk')
    P.load(esk, D['sinkb'][l], C.b_esk)
    P.act(esk, esk, AF.Exp, R=[C.b_esk], W=[C.b_esk])
    for g in range(2):
        for j in range(4):
            h = 4 * g + j
            P.load(QT[g * 64:(g + 1) * 64, j, :], D['QKT_d'][h * 64:(h + 1) * 64, :], bQT)
    P.load(KT, D['QKT_d'][512:640, :], bKT)
    P.load(V, D['Vnat_d'].rearrange("(t p) c -> p t c", p=128)[:, :, 0:130], bV)
    NP = 3
    pt = [ar.alloc([512], BF16) for _ in range(NP)]
    bpt = P.bufs_n('apt', NP)
    mlo = C.cbf[:, C_MLO:C_MLO + 128].unsqueeze(1).broadcast_to([128, 4, 128])
    mhi = C.cbf[:, C_MHI:C_MHI + 128].unsqueeze(1).broadcast_to([128, 4, 128])
    items = []
    fi = 0
    for g in range(2):
        for n in range(NT):
            ms = [m for m in (n - 1, n, n + 1) if 0 <= m < NT]
            for idx, m in enumerate(ms):
                items.append((g, n, m, idx == 0, idx == len(ms) - 1, fi))
            fi += 1

    def s0(itm, t):
        g, n, m, first, last, f = itm
        ps = slice(g * 64, (g + 1) * 64)
        st, bst = C.pb[t % 3], C.bpb[t % 3]
        P.mm(st[:, :].rearrange("p (j q) -> p j q", j=4), KT[ps, m * 128:(m + 1) * 128], QT[ps, :, n * 128:(n + 1) * 128],
             R=[bKT, bQT], W=[bst])

    def s1(itm, t):
        g, n, m, first, last, f = itm
        k = t % NP
        st, bst = C.pb[t % 3], C.bpb[t % 3]
        acc, bacc = C.pb[3 + (f % 2)], C.bpb[3 + (f % 2)]
        P.act(pt[k], st[:, :], AF.Exp, R=[bst], W=[bpt[k]], scale=0.125)
        if m != n:
            v3 = pt[k].rearrange("p (j q) -> p j q", j=4)
            P.tt('pool' if t % 2 else 'dve', v3, v3, mlo if m < n else mhi, ALU.mult, R=[bpt[k], C.b_const], W=[bpt[k]])
        P.mm(acc[0:65, :], V[:, m, g * 65:(g + 1) * 65], pt[k], start=first, stop=last, R=[bV, bpt[k]], W=[bacc])
        if last:
            es = esk[64:65, 4 * g:4 * g + 4].unsqueeze(2).broadcast_to([1, 4, 128])
            finalize_norm(C, acc, bacc, 512, yst[0:64, :, n * 128:(n + 1) * 128], byst, shape3=(4, 128), esink=es, tagk=f)
            if n == NT - 1:
                for j in range(4):
                    h = 4 * g + j
                    P.store(D['YT_d'][h * 64:(h + 1) * 64, :], yst[0:64, j, :], byst)

    run_pipeline(items, 2, s0, s1)
    ar.release(m0)
    P.barrier()
    P.retire(nb0)


def mixer_b(C, l):
    P, ar, D = C.P, C.ar, C.D
    nb0 = len(P.bufs)
    m0 = ar.mark()
    alloc_fin(C)
    accN = ar.alloc([2, S], F32)
    baccN = P.buf('baccN')
    yst = ar.alloc([2, S], BF16)
    byst = P.buf('byst')
    QTs = [ar.alloc([S], BF16) for _ in range(3)]
    KTs = [ar.alloc([S], BF16) for _ in range(3)]
    Vs = [ar.alloc([NT, 130], BF16) for _ in range(3)]
    bQ = P.bufs_n('bQT', 3)
    bK = P.bufs_n('bKT', 3)
    bVv = P.bufs_n('bV', 3)
    NP = 3
    pt = [ar.alloc([512], BF16) for _ in range(NP)]
    bpt = P.bufs_n('bpt', NP)
    items = []
    ai = 0
    for gi, d in enumerate(B_DIL):
        L = S // d
        nb = L // 128
        QT, KT, V = QTs[gi], KTs[gi], Vs[gi]
        bq, bk, bv = bQ[gi], bK[gi], bVv[gi]
        P.load(QT, D['QKT_d'][(5 + gi) * 128:(6 + gi) * 128, :], bq)
        P.load(KT, D['QKT_d'][(8 + gi) * 128:(9 + gi) * 128, :], bk)
        if gi == 0:
            P.load(V, D['Vnat_d'].rearrange("(t p) c -> p t c", p=128)[:, :, 650:780], bv)
        else:
            P.load(V, D['Vb_d'][gi - 1].rearrange("(t p) c -> p t c", p=128), bv)
        for jh in range(2):
            for r in range(d):
                base = r * L
                qblocks = []
                for n_ in range(-1, nb):
                    q0 = 128 * n_ + 64
                    qa, qb = max(q0, 0), min(q0 + 128, L)
                    tiles = []
                    if n_ >= 0:
                        tiles.append((n_, C_MLO))
                    if n_ + 1 < nb:
                        tiles.append((n_ + 1, C_MHI))
                    qblocks.append((qa, qb - qa, qa - q0, tiles))
                for g0 in range(0, len(qblocks), 4):
                    grp = qblocks[g0:g0 + 4]
                    ncols = sum(q[1] for q in grp)
                    pstart = grp[0][0]
                    col = 0
                    nsub = (len(grp) + 1) // 2
                    for bi in range(0, len(grp), 2):
                        sub = grp[bi:bi + 2]
                        plist = []
                        sc = 0
                        for (qa, nq, aoff, tiles) in sub:
                            for ti, (m, mk) in enumerate(tiles):
                                plist.append((sc, nq, aoff, mk, base // 128 + m, col, ti == 0, ti == len(tiles) - 1, base + qa))
                                sc += nq
                            col += nq
                        endinfo = None
                        if bi // 2 == nsub - 1:
                            endinfo = (gi, d, r, pstart, ncols)
                        items.append((QT, KT, V, bq, bk, bv, jh, plist, sc, ai, endinfo))
                    ai += 1

    def s0(itm, t):
        QT, KT, V, bq, bk, bv, jh, plist, sc, a_, endinfo = itm
        ps = slice(jh * 64, (jh + 1) * 64)
        st, bst = C.pb[t % 3], C.bpb[t % 3]
        for (s0_, nq, aoff, mk, tg, c0, first, last, qpos) in plist:
            P.mm(st[:, s0_:s0_ + nq], KT[ps, tg * 128:(tg + 1) * 128], QT[ps, qpos:qpos + nq], R=[bk, bq], W=[bst])

    def s1(itm, t):
        QT, KT, V, bq, bk, bv, jh, plist, sc, a_, endinfo = itm
        k = t % NP
        st, bst = C.pb[t % 3], C.bpb[t % 3]
        acc, bacc = C.pb[3 + (a_ % 3)], C.bpb[3 + (a_ % 3)]
        P.act(pt[k][:, 0:sc], st[:, 0:sc], AF.Exp, R=[bst], W=[bpt[k]], scale=0.125)
        for pi, (s0_, nq, aoff, mk, tg, c0, _, _, _) in enumerate(plist):
            P.tt('pool' if pi % 2 else 'dve', pt[k][:, s0_:s0_ + nq], pt[k][:, s0_:s0_ + nq],
                 C.cbf[:, mk + aoff:mk + aoff + nq], ALU.mult, R=[bpt[k], C.b_const], W=[bpt[k]])
        for (s0_, nq, aoff, mk, tg, c0, first, last, _) in plist:
            P.mm(acc[0:65, c0:c0 + nq], V[:, tg, jh * 65:(jh + 1) * 65], pt[k][:, s0_:s0_ + nq], start=first, stop=last,
                 R=[bv, bpt[k]], W=[bacc])
        if endinfo is not None:
            gi, d, r, pstart, ncols = endinfo
            if gi == 0:
                P.cp('act', accN[0:65, jh, pstart:pstart + ncols], acc[0:65, 0:ncols], R=[bacc], W=[baccN])
            else:
                t0 = r + d * pstart
                view = accN[0:65, jh, sl(t0, ncols, d)]
                P.tt('dve', view, view, acc[0:65, 0:ncols], ALU.add, R=[bacc, baccN], W=[baccN])

    run_pipeline(items, 2, s0, s1)
    fi = 0
    for jh in range(2):
        for ch in range(8):
            cs = slice(ch * 512, (ch + 1) * 512)
            k = fi % 2
            r, rhi, rlo, br = C.fr[k], C.frhi[k], C.frlo[k], C.bfr[k]
            bc, bbc = C.pb[6 + k], C.bpb[6 + k]
            P.recip(r[64:65, :], accN[64:65, jh, cs], R=[baccN], W=[br])
            P.cp('dve', rhi[64:65, :], r[64:65, :], R=[br], W=[br])
            P.tt('dve', rlo[64:65, :], r[64:65, :], rhi[64:65, :], ALU.subtract, R=[br], W=[br])
            P.mm(bc[0:64, :], C.ones_bf[64:65, 0:64], rhi[64:65, :], start=True, stop=False, R=[br, C.b_const], W=[bbc])
            P.mm(bc[0:64, :], C.ones_bf[64:65, 0:64], rlo[64:65, :], start=False, stop=True, R=[br, C.b_const], W=[bbc])
            P.tt('dve', yst[0:64, jh, cs], accN[0:64, jh, cs], bc[0:64, :], ALU.mult, R=[baccN, bbc], W=[byst])
            fi += 1
        P.store(D['YT_d'][512 + jh * 64:512 + (jh + 1) * 64, :], yst[0:64, jh, :], byst)
    ar.release(m0)
    P.barrier()
    P.retire(nb0)


def mixer_d(C, l):
    P, ar, D = C.P, C.ar, C.D
    nb0 = len(P.bufs)
    m0 = ar.mark()
    alloc_fin(C)
    QT = ar.alloc([2, S], BF16)
    KT = ar.alloc([2, S], BF16)
    V = ar.alloc([NT, 260], BF16)
    bQT, bKT, bV = P.buf('dQT'), P.buf('dKT'), P.buf('dV')
    yst = ar.alloc([4, S], BF16)
    byst = P.buf('dyst')
    EB = ar.alloc([4, NTAB * 128], BF16)
    bEB = P.buf('dEB')
    tmpb = [ar.alloc([NTAB * 128], F32) for _ in range(2)]
    btmp = P.bufs_n('dtmp', 2)
    for i in range(2):
        P.load(QT[:, i, :], D['QKT_d'][(15 + i) * 128:(16 + i) * 128, :], bQT)
        P.load(KT[:, i, :], D['QKT_d'][(17 + i) * 128:(18 + i) * 128, :], bKT)
    P.load(V, D['Vnat_d'].rearrange("(t p) c -> p t c", p=128)[:, :, 390:650], bV)
    for h in range(4):
        P.load(tmpb[h % 2], D['dbias'][l, h], btmp[h % 2])
        P.act(tmpb[h % 2], tmpb[h % 2], AF.Exp, R=[btmp[h % 2]], W=[btmp[h % 2]])
        P.tt('dve', EB[:, h, :], tmpb[h % 2], C.cbf[:, C_DVALID:C_DVALID + NTAB * 128], ALU.mult, R=[btmp[h % 2], C.b_const], W=[bEB])
    NP = 3
    pt = [ar.alloc([512], BF16) for _ in range(NP)]
    bpt = P.bufs_n('dpt', NP)
    items = []
    fi = 0
    for h in range(4):
        for n4 in range(8):
            plist = []
            for n in range(n4 * 4, n4 * 4 + 4):
                pl = D_PAIRS[n]
                for pi, (m, tab) in enumerate(pl):
                    plist.append((n, m, tab, pi == 0, pi == len(pl) - 1))
            nch = (len(plist) + 3) // 4
            for ci in range(nch):
                items.append((h, n4, plist[ci * 4:ci * 4 + 4], fi, ci == nch - 1))
            fi += 1

    def s0(itm, t):
        h, n4, chunk, f, endg = itm
        bq = h // 2
        ps = slice((h % 2) * 64, (h % 2 + 1) * 64)
        st, bst = C.pb[t % 3], C.bpb[t % 3]
        for i, (n, m, tab, first, last) in enumerate(chunk):
            P.mm(st[:, i * 128:(i + 1) * 128], KT[ps, bq, m * 128:(m + 1) * 128], QT[ps, bq, n * 128:(n + 1) * 128],
                 R=[bKT, bQT], W=[bst])

    def s1(itm, t):
        h, n4, chunk, f, endg = itm
        k = t % NP
        st, bst = C.pb[t % 3], C.bpb[t % 3]
        acc, bacc = C.pb[3 + (f % 2)], C.bpb[3 + (f % 2)]
        used = len(chunk) * 128
        P.act(pt[k][:, 0:used], st[:, 0:used], AF.Exp, R=[bst], W=[bpt[k]], scale=0.125)
        for i, (n, m, tab, first, last) in enumerate(chunk):
            P.tt('pool' if i % 2 else 'dve', pt[k][:, i * 128:(i + 1) * 128], pt[k][:, i * 128:(i + 1) * 128],
                 EB[:, h, tab * 128:(tab + 1) * 128], ALU.mult, R=[bpt[k], bEB], W=[bpt[k]])
        for i, (n, m, tab, first, last) in enumerate(chunk):
            qc = (n % 4) * 128
            P.mm(acc[0:65, qc:qc + 128], V[:, m, h * 65:(h + 1) * 65], pt[k][:, i * 128:(i + 1) * 128], start=first, stop=last,
                 R=[bV, bpt[k]], W=[bacc])
        if endg:
            finalize_norm(C, acc, bacc, 512, yst[0:64, h, n4 * 512:(n4 + 1) * 512], byst, tagk=f)
            if n4 == 7:
                P.store(D['YT_d'][896 + h * 64:896 + (h + 1) * 64, :], yst[0:64, h, :], byst)

    run_pipeline(items, 2, s0, s1)
    ar.release(m0)
    P.barrier()
    P.retire(nb0)


def mixer_c(C, l):
    P, ar, D = C.P, C.ar, C.D
    lam_init = 0.8 - 0.6 * math.exp(-0.3 * l)
    nb0 = len(P.bufs)
    m0 = ar.mark()
    alloc_fin(C)
    QT = ar.alloc([3, S], BF16)
    KT = ar.alloc([3, S], BF16)
    V = ar.alloc([NT, 260], BF16)
    bQT, bKT, bV = P.buf('cQT'), P.buf('cKT'), P.buf('cV')
    yst = ar.alloc([4, S], BF16)
    byst = P.buf('cyst')
    for b in range(8):
        grp, slot = b // 3, b % 3
        h, c = b // 2, b % 2
        row = (h // 2) * 128 + ((h % 2) * 2 + c) * 32
        P.load(QT[slot * 32:(slot + 1) * 32, grp, :], D['QKT_d'][11 * 128 + row:11 * 128 + row + 32, :], bQT)
        P.load(KT[slot * 32:(slot + 1) * 32, grp, :], D['QKT_d'][13 * 128 + row:13 * 128 + row + 32, :], bKT)
    P.load(V, D['Vnat_d'].rearrange("(t p) c -> p t c", p=128)[:, :, 130:390], bV)
    lamb = ar.alloc([128], F32)
    blam = P.buf('lam')
    lt = ar.alloc([2, 32], F32)
    l2 = ar.alloc([2], F32)
    nlam = ar.alloc([1], F32)
    gsc = ar.alloc([1], F32)
    bgsc = P.buf('gsc')
    P.load(lamb, D['lamb'][l], blam)
    P.load(gsc, D['gsub'][l], bgsc)
    lv = lamb.rearrange("p (a b c) -> p a b c", a=2, b=2)
    P.tt('dve', lt, lv[:, :, 0, :], lv[:, :, 1, :], ALU.mult, R=[blam], W=[blam])
    P.op('dve', lambda e: e.reduce_sum(out=l2, in_=lt, axis=mybir.AxisListType.X), R=[blam], W=[blam])
    P.act(l2, l2, AF.Exp, R=[blam], W=[blam])
    P.tt('dve', nlam, l2[:, 0:1], l2[:, 1:2], ALU.subtract, R=[blam], W=[blam])
    P.ts('dve', nlam, nlam, lam_init, -1.0, ALU.add, ALU.mult, R=[blam], W=[blam])
    P.ts('dve', gsc, gsc, 1.0 - lam_init, None, ALU.mult, R=[bgsc], W=[bgsc])
    NP = 4
    pt = [ar.alloc([512], BF16) for _ in range(NP)]
    bpt = P.bufs_n('cpt', NP)
    to = [ar.alloc([512], F32) for _ in range(2)]
    t1 = [ar.alloc([512], F32) for _ in range(2)]
    sqb = [ar.alloc([512], BF16) for _ in range(2)]
    rsd = [ar.alloc([512], F32) for _ in range(2)]
    bto = P.bufs_n('cto', 2)
    scale = 32.0 ** -0.5
    items = []
    fi = 0
    for h in range(4):
        for Q in range(8):
            for c in range(2):
                for kt in range(NT):
                    items.append((h, Q, c, kt, fi))
            fi += 1

    def s0(itm, t):
        h, Q, c, kt, f = itm
        b = 2 * h + c
        grp, slot = b // 3, b % 3
        ps = slice(slot * 32, (slot + 1) * 32)
        st, bst = C.pb[t % 3], C.bpb[t % 3]
        P.mm(st[:, :], KT[ps, grp, kt * 128:(kt + 1) * 128], QT[ps, grp, Q * 512:(Q + 1) * 512], R=[bKT, bQT], W=[bst])

    def s1(itm, t):
        h, Q, c, kt, f = itm
        qs = slice(Q * 512, (Q + 1) * 512)
        k = t % NP
        st, bst = C.pb[t % 3], C.bpb[t % 3]
        a0 = 3 + 2 * (f % 2)
        accs = [C.pb[a0], C.pb[a0 + 1]]
        baccs = [C.bpb[a0], C.bpb[a0 + 1]]
        P.act(pt[k], st[:, :], AF.Exp, R=[bst], W=[bpt[k]], scale=scale)
        P.mm(accs[c][0:65, :], V[:, kt, h * 65:(h + 1) * 65], pt[k], start=(kt == 0), stop=(kt == NT - 1),
             R=[bV, bpt[k]], W=[baccs[c]])
        if not (c == 1 and kt == NT - 1):
            return
        k2 = f % 2
        bt = bto[k2]
        bc, bbc = C.pb[7], C.bpb[7]
        for c_ in range(2):
            r, rhi, rlo, br = C.fr[c_], C.frhi[c_], C.frlo[c_], C.bfr[c_]
            P.recip(r[64:65, :], accs[c_][64:65, :], R=[baccs[c_]], W=[br])
            if c_ == 1:
                P.ts('dve', r[64:65, :], r[64:65, :], nlam[64:65, 0:1], None, ALU.mult, R=[br, blam], W=[br])
            P.cp('dve', rhi[64:65, :], r[64:65, :], R=[br], W=[br])
            P.tt('dve', rlo[64:65, :], r[64:65, :], rhi[64:65, :], ALU.subtract, R=[br], W=[br])
        for c_ in range(2):
            r, rhi, rlo, br = C.fr[c_], C.frhi[c_], C.frlo[c_], C.bfr[c_]
            P.mm(bc[0:64, :], C.ones_bf[64:65, 0:64], rhi[64:65, :], start=True, stop=False, R=[br, C.b_const], W=[bbc])
            P.mm(bc[0:64, :], C.ones_bf[64:65, 0:64], rlo[64:65, :], start=False, stop=True, R=[br, C.b_const], W=[bbc])
            P.cp('act', C.fbcs[c_][0:64, :], bc[0:64, :], R=[bbc], W=[C.bfbcs[c_]])
        P.tt('dve', to[k2][0:64, :], accs[0][0:64, :], C.fbcs[0][0:64, :], ALU.mult, R=[baccs[0], C.bfbcs[0]], W=[bt])
        P.tt('dve', t1[k2][0:64, :], accs[1][0:64, :], C.fbcs[1][0:64, :], ALU.mult, R=[baccs[1], C.bfbcs[1], bt], W=[bt])
        P.tt('pool', to[k2][0:64, :], to[k2][0:64, :], t1[k2][0:64, :], ALU.add, R=[bt], W=[bt])
        P.act(sqb[k2][0:64, :], to[k2][0:64, :], AF.Square, R=[bt], W=[bt])
        P.mm(bc[0:64, :], C.ones_bf[0:64, 0:64], sqb[k2][0:64, :], R=[bt, C.b_const], W=[bbc])
        P.act(rsd[k2][0:64, :], bc[0:64, :], AF.Ln, R=[bbc], W=[bt], scale=1.0 / 64, bias=EPS)
        P.act(rsd[k2][0:64, :], rsd[k2][0:64, :], AF.Exp, R=[bt], W=[bt], scale=-0.5)
        P.stt(yst[0:64, h, qs], to[k2][0:64, :], gsc[0:64, 0:1], rsd[k2][0:64, :], ALU.mult, ALU.mult, R=[bt, bgsc], W=[byst])
        if Q == 7:
            P.store(D['YT_d'][640 + h * 64:640 + (h + 1) * 64, :], yst[0:64, h, :], byst)

    run_pipeline(items, 2, s0, s1)
    ar.release(m0)
    P.barrier()
    P.retire(nb0)


def phase_p3a(C, l, xin, xout):
    P, ar, D = C.P, C.ar, C.D
    nb0 = len(P.bufs)
    m0 = ar.mark()
    Wg = ar.alloc([8, 4096], BF16)
    Wb = ar.alloc([9, DM], BF16)
    Wo = ar.alloc([8, DM], BF16)
    bWg, bWb, bWo = P.buf('Wg'), P.buf('Wb'), P.buf('Wo')
    w_in = D['w_in'][l].rearrange("(kc p) c -> p kc c", p=128)
    for kc in range(8):
        P.dma('pool', Wg[:, kc, :], w_in[:, kc, GATE_OFF:GATE_OFF + 4096], bWg, W=(bWg,))
    P.dma('pool', Wb[:, 0:4, :], D['w_ba'][l].rearrange("(kc p) c -> p kc c", p=128), bWb, W=(bWb,))
    P.dma('pool', Wb[:, 4, :], D['w_bb'][l], bWb, W=(bWb,))
    P.dma('pool', Wb[:, 5:7, :], D['w_bc'][l].rearrange("(kc p) c -> p kc c", p=128), bWb, W=(bWb,))
    P.dma('pool', Wb[:, 7:9, :], D['w_bd'][l].rearrange("(kc p) c -> p kc c", p=128), bWb, W=(bWb,))
    P.dma('pool', Wo, D['w_out'][l].rearrange("(kc p) c -> p kc c", p=128), bWo, W=(bWo,))
    hTc = [ar.alloc([8, 512], BF16) for _ in range(2)]
    YTc = [ar.alloc([9, 512], BF16) for _ in range(2)]
    bh = P.bufs_n('hTc', 2)
    by = P.bufs_n('YTc', 2)
    mT = [ar.alloc([8, 512], BF16) for _ in range(2)]
    bmT = P.bufs_n('mT', 2)
    sig = [ar.alloc([512], F32) for _ in range(3)]
    bsig = P.bufs_n('sig', 3)
    tmp = [ar.alloc([512], F32) for _ in range(2)]
    btmp = P.bufs_n('mtmp', 2)
    macc = [ar.alloc([512], F32) for _ in range(2)]
    bmacc = P.bufs_n('macc', 2)
    xt = [ar.alloc([DM], F32) for _ in range(2)]
    bxt = P.bufs_n('x3', 2)
    xo = [ar.alloc([DM], F32) for _ in range(2)]
    bxo = P.bufs_n('xo3', 2)
    hT_v = D['hT_d'].rearrange("(kc p) t -> p kc t", p=128)
    YT_v = D['YT_d'].rearrange("(kc p) t -> p kc t", p=128)
    branches = [(0, 4), (4, 5), (5, 7), (7, 9)]
    it = 0
    si = 0
    ti = 0
    for tg in range(8):
        g2 = tg % 2
        ts_ = slice(tg * 512, (tg + 1) * 512)
        P.load(hTc[g2], hT_v[:, :, ts_], bh[g2])
        P.load(YTc[g2], YT_v[:, :, ts_], by[g2])
        for ct in range(8):
            ma, bma = macc[ct % 2], bmacc[ct % 2]
            for br in range(4):
                pg, bpg = C.pb[(it % 2) * 2], C.bpb[(it % 2) * 2]
                py, bpy = C.pb[(it % 2) * 2 + 1], C.bpb[(it % 2) * 2 + 1]
                it += 1
                c0 = br * 1024 + ct * 128
                for kc in range(8):
                    P.mm(pg[:, :], Wg[:, kc, c0:c0 + 128], hTc[g2][:, kc, :], start=(kc == 0), stop=(kc == 7), R=[bWg, bh[g2]], W=[bpg])
                b0, b1 = branches[br]
                for bi in range(b0, b1):
                    P.mm(py[:, :], Wb[:, bi, ct * 128:(ct + 1) * 128], YTc[g2][:, bi, :], start=(bi == b0), stop=(bi == b1 - 1),
                         R=[bWb, by[g2]], W=[bpy])
                s_, bs_ = sig[si % 3], bsig[si % 3]
                si += 1
                P.act(s_, pg[:, :], AF.Sigmoid, R=[bpg], W=[bs_])
                if br == 0:
                    P.tt('dve', ma, s_, py[:, :], ALU.mult, R=[bs_, bpy], W=[bma])
                else:
                    t_, bt_ = tmp[ti % 2], btmp[ti % 2]
                    ti += 1
                    P.tt('dve', t_, s_, py[:, :], ALU.mult, R=[bs_, bpy], W=[bt_])
                    if br < 3:
                        P.tt('pool', ma, ma, t_, ALU.add, R=[bma, bt_], W=[bma])
                    else:
                        P.tt('pool', mT[g2][:, ct, :], ma, t_, ALU.add, R=[bma, bt_], W=[bmT[g2]])
        for tt in range(4):
            tok0 = tg * 512 + tt * 128
            xi = (tg * 4 + tt) % 2
            P.load(xt[xi], xin[tok0:tok0 + 128, :], bxt[xi])
            for cg in range(2):
                po, bpo = C.pb[4 + (cg + 2 * tt) % 4], C.bpb[4 + (cg + 2 * tt) % 4]
                for kc in range(8):
                    P.mm(po[:, :], mT[g2][:, kc, tt * 128:(tt + 1) * 128], Wo[:, kc, cg * 512:(cg + 1) * 512], start=(kc == 0), stop=(kc == 7),
                         R=[bmT[g2], bWo], W=[bpo])
                P.tt('dve', xo[xi][:, cg * 512:(cg + 1) * 512], po[:, :], xt[xi][:, cg * 512:(cg + 1) * 512], ALU.add,
                     R=[bpo, bxt[xi]], W=[bxo[xi]])
            P.store(xout[tok0:tok0 + 128, :], xo[xi], bxo[xi])
    ar.release(m0)
    P.barrier()
    P.retire(nb0)


def phase_p3b(C, l, xin, xout):
    P, ar, D = C.P, C.ar, C.D
    nb0 = len(P.bufs)
    m0 = ar.mark()
    Wu = ar.alloc([8, 4096], BF16)
    Wd = ar.alloc([32, DM], BF16)
    gb = ar.alloc([DM], F32)
    bWu, bWd, bgb = P.buf('Wu'), P.buf('Wd'), P.buf('gbm')
    wu_v = D['w_up'][l].rearrange("(kc p) c -> p kc c", p=128)
    wd_v = D['w_down'][l].rearrange("(kc p) c -> p kc c", p=128)
    P.load(gb, D['gb_mlp'][l], bgb)
    for kc in range(8):
        P.dma('pool', Wu[:, kc, :], wu_v[:, kc, :], bWu, W=(bWu,))
    for k4 in range(8):
        P.dma('pool', Wd[:, k4 * 4:(k4 + 1) * 4, :], wd_v[:, k4 * 4:(k4 + 1) * 4, :], bWd, W=(bWd,))
    xt = [ar.alloc([DM], F32) for _ in range(2)]
    bxt = P.bufs_n('x4', 2)
    xo = [ar.alloc([DM], F32) for _ in range(1)]
    bxo = P.bufs_n('xo4', 1)
    ss = [ar.alloc([1], F32) for _ in range(2)]
    bss = P.bufs_n('ss4', 2)
    hb = [ar.alloc([DM], BF16) for _ in range(2)]
    bhb = P.bufs_n('hb4', 2)
    hmT = [ar.alloc([8, 512], BF16) for _ in range(2)]
    bhm = P.bufs_n('hmT', 2)
    uT = ar.alloc([32, 512], BF16)
    buT = P.buf('uT')
    rl = [ar.alloc([512], F32) for _ in range(2)]
    brl = P.bufs_n('rl', 2)
    xi = 0
    it = 0
    for tg in range(8):
        g2 = tg % 2
        for tt in range(4):
            tok0 = tg * 512 + tt * 128
            i, j = xi % 2, xi % 2
            xi += 1
            P.load(xt[i], xin[tok0:tok0 + 128, :], bxt[i])
            P.act(hb[j], xt[i], AF.Square, R=[bxt[i]], W=[bhb[j], bss[j]], accum_out=ss[j])
            P.act(ss[j], ss[j], AF.Ln, R=[bss[j]], W=[bss[j]], scale=1.0 / DM, bias=EPS)
            P.act(ss[j], ss[j], AF.Exp, R=[bss[j]], W=[bss[j]], scale=-0.5)
            P.stt(hb[j], xt[i], ss[j], gb, ALU.mult, ALU.mult, R=[bxt[i], bss[j], bgb], W=[bhb[j]])
            pbv = C.pb[6 + j].bitcast(BF16)
            for kc in range(8):
                P.tr(pbv[:, kc * 128:(kc + 1) * 128], hb[j][:, kc * 128:(kc + 1) * 128], C.ident, R=[bhb[j], C.b_const], W=[C.bpb[6 + j]])
            P.cp('dve', hmT[g2][:, :, tt * 128:(tt + 1) * 128], pbv.rearrange("p (k t) -> p k t", k=8), R=[C.bpb[6 + j]], W=[bhm[g2]])
        for mt in range(32):
            pu, bpu = C.pb[it % 3], C.bpb[it % 3]
            r_, br_ = rl[it % 2], brl[it % 2]
            it += 1
            for kc in range(8):
                P.mm(pu[:, :], Wu[:, kc, mt * 128:(mt + 1) * 128], hmT[g2][:, kc, :], start=(kc == 0), stop=(kc == 7), R=[bWu, bhm[g2]], W=[bpu])
            P.act(r_, pu[:, :], AF.Relu, R=[bpu], W=[br_])
            P.tt('pool' if mt % 2 else 'dve', uT[:, mt, :], r_, r_, ALU.mult, R=[br_], W=[buT])
        for tt in range(4):
            tok0 = tg * 512 + tt * 128
            i, j = xi % 2, 0
            xi += 1
            P.load(xt[i], xin[tok0:tok0 + 128, :], bxt[i])
            for cg in range(2):
                pd, bpd = C.pb[3 + (cg + 2 * tt) % 3], C.bpb[3 + (cg + 2 * tt) % 3]
                for mt in range(32):
                    P.mm(pd[:, :], uT[:, mt, tt * 128:(tt + 1) * 128], Wd[:, mt, cg * 512:(cg + 1) * 512], start=(mt == 0), stop=(mt == 31),
                         R=[buT, bWd], W=[bpd])
                P.tt('dve', xo[j][:, cg * 512:(cg + 1) * 512], pd[:, :], xt[i][:, cg * 512:(cg + 1) * 512], ALU.add,
                     R=[bpd, bxt[i]], W=[bxo[j]])
            P.store(xout[tok0:tok0 + 128, :], xo[j], bxo[j])
    ar.release(m0)
    P.barrier()
    P.retire(nb0)


ARENA = 206 * 1024
NSEM_POOL = 96


class SemPool:
    def __init__(self, nc, stack, n):
        self.sems = [stack.enter_context(nc.semaphore(f"sm{i}")) for i in range(n)]
        self.i = 0

        self.free = []

    def get(self):
        if self.free:
            return self.free.pop()
        s_ = self.sems[self.i]
        self.i += 1
        return (s_, 0)

    def put(self, sem, cnt):
        self.free.append((sem, cnt))


def build(n_layers=2, phases=None, dbg=False):
    nc = bass.Bass("TRN2", target_bir_lowering=False)
    D = {}
    x = nc.dram_tensor("x", [S, DM], F32, kind="ExternalInput").ap()
    for name, shp in PARAM_SHAPES.items():
        D[name] = nc.dram_tensor(name, shp, F32, kind="ExternalInput").ap()
    y = nc.dram_tensor("y", [S, DM], F32, kind="ExternalOutput").ap()
    sk = "ExternalOutput" if dbg else "Internal"
    D['hT_d'] = nc.dram_tensor("hT_d", [DM, S], BF16, kind=sk).ap()
    D['QKT_d'] = nc.dram_tensor("QKT_d", [NQKB * 128, S], BF16, kind=sk).ap()
    D['Vnat_d'] = nc.dram_tensor("Vnat_d", [S, 780], BF16, kind=sk).ap()
    D['Vb_d'] = nc.dram_tensor("Vb_d", [2, S, 130], BF16, kind=sk).ap()
    D['YT_d'] = nc.dram_tensor("YT_d", [1152, S], BF16, kind=sk).ap()
    x1_d = nc.dram_tensor("x1_d", [S, DM], F32, kind=sk).ap()
    x2_d = nc.dram_tensor("x2_d", [S, DM], F32, kind=sk).ap()
    with ExitStack() as stack:
        arena_t = stack.enter_context(nc.sbuf_tensor("arena", [128, ARENA], U8))
        pbs = [stack.enter_context(nc.psum_tensor(f"pb{i}", [128, 512], F32)) for i in range(8)]
        sp_ = SemPool(nc, stack, NSEM_POOL)

        class _St:
            def enter_context(self, cm):
                raise RuntimeError

        P = Prog.__new__(Prog)
        P.nc = nc
        P.ops = {e: [] for e in ENGS}
        P.bufs = []
        P.esem = {e: sp_.get()[0] for e in ENGS}
        P.nsem = len(ENGS)

        def dma(eng, out, in_, owner, R=(), W=()):
            if owner.sem is None:
                owner.sem, owner.cnt = sp_.get()
            owner.cnt += 16
            o = Op(eng, lambda e: e.dma_start(out=out, in_=in_))
            o.dma_sem = owner.sem
            P._track(o, ('dma', owner.sem, owner.cnt), 'dma', R, W)
            P.ops[eng].append(o)
            return o
        P.dma = dma

        def retire(nb0):
            for b in P.bufs[nb0:]:
                if b.sem is not None:
                    sp_.put(b.sem, b.cnt)
                    b.sem = None
            del P.bufs[nb0:]
        P.retire = retire
        block = stack.enter_context(nc.Block())
        C = Ctx()
        C.P, C.D = P, D
        C.ar = Arena(arena_t, ARENA)
        C.pb = [p[:, :] for p in pbs]
        C.bpb = P.bufs_n('pb', 8)
        C.cbf = C.ar.alloc([NCONST], BF16)
        C.b_const = P.buf('consts')
        P.dma('pool', C.cbf, D['consts'], C.b_const, W=(C.b_const,))
        C.ident = C.cbf[:, C_IDENT:C_IDENT + 128]
        C.bd64 = C.cbf[:, C_BD64:C_BD64 + 128]
        C.bd32 = C.cbf[:, C_BD32:C_BD32 + 128]
        C.psw64 = C.cbf[:, C_PSW64:C_PSW64 + 128]
        C.psw32 = C.cbf[:, C_PSW32:C_PSW32 + 128]
        C.ones_bf = C.cbf[:, C_ONES:C_ONES + 128]
        all_ph = ['p1', 'a', 'b', 'c', 'd', 'p3a', 'p3b']
        phases = phases or all_ph
        for l in range(n_layers):
            xin = x if l == 0 else x2_d
            xfin = y if l == n_layers - 1 else x2_d
            if 'p1' in phases:
                phase_p1(C, l, xin)
            if 'a' in phases:
                mixer_a(C, l)
            if 'b' in phases:
                mixer_b(C, l)
            if 'c' in phases:
                mixer_c(C, l)
            if 'd' in phases:
                mixer_d(C, l)
            if 'p3a' in phases:
                phase_p3a(C, l, xin, x1_d)
            if 'p3b' in phases:
                phase_p3b(C, l, x1_d, xfin)
        P.barrier()
        P.emit(block)
        C.nsem = sp_.i
    return nc, C


_CACHE = {}


def kernel(**inputs):
    x = np.ascontiguousarray(np.asarray(inputs['x'], dtype=np.float32))
    params = host_prep(inputs)
    if 'nc' not in _CACHE:
        _CACHE['nc'] = build()[0]
    nc = _CACHE['nc']
    in_maps = []
    for b in range(8):
        m = {'x': x[b]}
        m.update(params)
        in_maps.append(m)
    res = run_bass_kernel_spmd(nc, in_maps, core_ids=list(range(8)))
    return np.stack([np.asarray(r['y'], dtype=np.float32) for r in res.results], axis=0)
```

```python
import math
from contextlib import ExitStack
import numpy as np
import concourse.bass as bass
import concourse.mybir as mybir
from concourse.bass_utils import run_bass_kernel_spmd

F32 = mybir.dt.float32
BF16 = mybir.dt.bfloat16
U8 = mybir.dt.uint8
AF = mybir.ActivationFunctionType
ALU = mybir.AluOpType

S = 4096
DM = 1024
NT = 32
EPS = 1e-6
INC = 7552
ENGS = ['pe', 'act', 'dve', 'pool', 'sp']
SAME_ENG_SYNC = ('act', 'dve', 'pool')


class Buf:
    __slots__ = ('name', 'w', 'rs', 'sem', 'cnt')

    def __init__(self, name):
        self.name = name
        self.w = None
        self.rs = {}
        self.sem = None
        self.cnt = 0


class Op:
    __slots__ = ('eng', 'fn', 'deps', 'dwaits', 'needs_inc', 'semval', 'dma_sem')

    def __init__(self, eng, fn):
        self.eng = eng
        self.fn = fn
        self.deps = set()
        self.dwaits = {}
        self.needs_inc = False
        self.semval = 0
        self.dma_sem = None


class Prog:
    def __init__(self, nc, stack):
        self.nc = nc
        self.stack = stack
        self.ops = {e: [] for e in ENGS}
        self.bufs = []
        self.esem = {e: stack.enter_context(nc.semaphore("s_" + e)) for e in ENGS}
        self.nsem = len(ENGS)

    def buf(self, name):
        b = Buf(name)
        self.bufs.append(b)
        return b

    def bufs_n(self, name, n):
        return [self.buf(f"{name}{i}") for i in range(n)]

    def _add_ev(self, o, ev):
        if ev is None:
            return
        if ev[0] == 'op':
            d = ev[1]
            if d.eng == o.eng and d.eng not in SAME_ENG_SYNC:
                return
            o.deps.add(d)
            d.needs_inc = True
        else:
            _, sem, val = ev
            cur = o.dwaits.get(id(sem))
            if cur is None or cur[1] < val:
                o.dwaits[id(sem)] = (sem, val)

    def _track(self, o, ev, key, R, W):
        for b in R:
            self._add_ev(o, b.w)
        for b in W:
            self._add_ev(o, b.w)
            for e2 in b.rs.values():
                self._add_ev(o, e2)
        for b in R:
            b.rs[key] = ev
        for b in W:
            b.w = ev
            b.rs = {}

    def op(self, eng, fn, R=(), W=()):
        o = Op(eng, fn)
        self._track(o, ('op', o), eng, R, W)
        self.ops[eng].append(o)
        return o

    def dma(self, eng, out, in_, owner, R=(), W=()):
        if owner.sem is None:
            owner.sem = self.stack.enter_context(self.nc.semaphore("d_" + owner.name))
            self.nsem += 1
        owner.cnt += 16
        o = Op(eng, lambda e: e.dma_start(out=out, in_=in_))
        o.dma_sem = owner.sem
        self._track(o, ('dma', owner.sem, owner.cnt), 'dma', R, W)
        self.ops[eng].append(o)
        return o

    def load(self, out, in_, owner, eng='sp'):
        return self.dma(eng, out, in_, owner, R=(), W=(owner,))

    def store(self, out, in_, owner, eng='sp'):
        return self.dma(eng, out, in_, owner, R=(owner,), W=())

    def barrier(self):
        o = Op('sp', lambda e: e.nop())
        for E in ENGS:
            if self.ops[E]:
                last = None
                for c in reversed(self.ops[E]):
                    if c.fn is not None and c.dma_sem is None:
                        last = c
                        break
                if last is not None and E != 'sp':
                    o.deps.add(last)
                    last.needs_inc = True
        for b in self.bufs:
            if b.sem is not None and b.cnt > 0:
                o.dwaits[id(b.sem)] = (b.sem, b.cnt)
        o.needs_inc = True
        self.ops['sp'].append(o)
        for E in ENGS:
            if E != 'sp':
                w = Op(E, None)
                w.deps.add(o)
                self.ops[E].append(w)
        for b in self.bufs:
            b.w = None
            b.rs = {}

    def mm(self, out, lhsT, rhs, start=True, stop=True, R=(), W=()):
        return self.op('pe', lambda e: e.matmul(out, lhsT, rhs, start=start, stop=stop), R, W)

    def tr(self, out, in_, ident, R=(), W=()):
        return self.op('pe', lambda e: e.transpose(out, in_, ident), R, W)

    def act(self, out, in_, func, R=(), W=(), **kw):
        return self.op('act', lambda e: e.activation(out=out, in_=in_, func=func, **kw), R, W)

    def tt(self, eng, out, in0, in1, op, R=(), W=()):
        return self.op(eng, lambda e: e.tensor_tensor(out=out, in0=in0, in1=in1, op=op), R, W)

    def ts(self, eng, out, in0, s1, s2, op0, op1=None, R=(), W=()):
        if op1 is None:
            return self.op(eng, lambda e: e.tensor_scalar(out=out, in0=in0, scalar1=s1, scalar2=None, op0=op0), R, W)
        return self.op(eng, lambda e: e.tensor_scalar(out=out, in0=in0, scalar1=s1, scalar2=s2, op0=op0, op1=op1), R, W)

    def stt(self, out, in0, scalar, in1, op0, op1, R=(), W=()):
        return self.op('dve', lambda e: e.scalar_tensor_tensor(out=out, in0=in0, scalar=scalar, in1=in1, op0=op0, op1=op1), R, W)

    def cp(self, eng, out, in_, R=(), W=()):
        if eng == 'act':
            return self.op('act', lambda e: e.copy(out=out, in_=in_), R, W)
        return self.op(eng, lambda e: e.tensor_copy(out=out, in_=in_), R, W)

    def recip(self, out, in_, R=(), W=()):
        return self.op('dve', lambda e: e.reciprocal(out=out, in_=in_), R, W)

    def memset(self, eng, ap, val, W=()):
        return self.op(eng, lambda e: e.memset(ap, val), (), W)

    def emit(self, block):
        for E in ENGS:
            c = 0
            for o in self.ops[E]:
                if o.needs_inc and o.dma_sem is None:
                    c += 1
                    o.semval = c
        esem = self.esem

        def run(E, eng):
            waited = {}
            for o in self.ops[E]:
                needs = []
                for d in o.deps:
                    needs.append((esem[d.eng], d.semval))
                for sem, val in o.dwaits.values():
                    needs.append((sem, val))
                for sem, val in needs:
                    k = id(sem)
                    if waited.get(k, 0) < val:
                        eng.wait_ge(sem, val)
                        waited[k] = val
                if o.fn is not None:
                    ins = o.fn(eng)
                    if o.dma_sem is not None:
                        ins.then_inc(o.dma_sem, 16)
                    elif o.needs_inc:
                        ins.then_inc(esem[E], 1)

        @block.tensor
        def _(e):
            run('pe', e)

        @block.scalar
        def _(e):
            run('act', e)

        @block.vector
        def _(e):
            run('dve', e)

        @block.gpsimd
        def _(e):
            run('pool', e)

        @block.sync
        def _(e):
            run('sp', e)


class Arena:
    def __init__(self, t, size):
        self.t = t
        self.size = size
        self.off = 0

    def alloc(self, free_shape, dtype):
        es = 4 if dtype == F32 else (2 if dtype == BF16 else 1)
        n = 1
        for s_ in free_shape:
            n *= s_
        nb = (n * es + 31) // 32 * 32
        assert self.off + nb <= self.size, f"arena overflow {self.off}+{nb}>{self.size}"
        ap = self.t[:, self.off:self.off + n * es].bitcast(dtype)
        self.off += nb
        if len(free_shape) == 2:
            ap = ap.rearrange("p (a b) -> p a b", a=free_shape[0])
        elif len(free_shape) == 3:
            ap = ap.rearrange("p (a b c) -> p a b c", a=free_shape[0], b=free_shape[1])
        return ap

    def mark(self):
        return self.off

    def release(self, m):
        self.off = m


QK_BLOCKS = []
for i in range(4):
    QK_BLOCKS.append((i * 128, 'n64'))
QK_BLOCKS.append((512, 'n64'))
for g, kind in enumerate(['n64', 'p4', 'p16']):
    QK_BLOCKS.append((768 + g * 128, kind))
for g, kind in enumerate(['n64', 'p4', 'p16']):
    QK_BLOCKS.append((1152 + g * 128, kind))
for i in range(2):
    QK_BLOCKS.append((1920 + i * 128, 'n32'))
for i in range(2):
    QK_BLOCKS.append((2176 + i * 128, 'n32'))
for i in range(2):
    QK_BLOCKS.append((2688 + i * 128, 'd'))
for i in range(2):
    QK_BLOCKS.append((2944 + i * 128, 'd'))
NQKB = len(QK_BLOCKS)
VNAT_COLS = [(640, 128), (2432, 256), (3200, 256), (1536, 128)]
GATE_OFF = 3456
B_DIL = [1, 4, 16]


def perm_tokens(d):
    L = S // d
    j = np.arange(S)
    return (j % L) * d + (j // L)


def rope_tabs(dim):
    half = dim // 2
    inv = np.power(np.float32(10000.0), -(np.arange(0, dim, 2, dtype=np.float32) / np.float32(dim))).astype(np.float32)
    ang = (np.arange(S, dtype=np.float32)[:, None] * inv[None, :]).astype(np.float32)
    c = np.cos(ang).astype(np.float32)
    s_ = np.sin(ang).astype(np.float32)
    p = np.arange(128) % dim
    cosT = c[:, p % half].T.copy()
    sgn = np.where(p < half, -1.0, 1.0).astype(np.float32)
    sinT = (s_[:, p % half] * sgn[None, :]).T.copy()
    return cosT, sinT


def d_tables():
    rows = 64
    r0 = np.clip(np.arange(rows) - 4, 0, rows - 8)
    cj = np.arange(64)
    c0 = np.clip(cj - 8, 0, 48)
    col_ok = (cj[None, :] >= c0[:, None]) & (cj[None, :] < c0[:, None] + 16)
    dc = np.clip(cj[None, :] - cj[:, None], -15, 15) + 15
    tabs = {}
    tab_list = []
    pairs = []
    for n in range(32):
        lo = r0[2 * n] // 2
        hi = (r0[2 * n + 1] + 7) // 2
        pl = []
        for m in range(lo, hi + 1):
            valid = np.zeros((128, 128), dtype=bool)
            dr = np.zeros((128, 128), dtype=np.int64)
            for a in range(2):
                for b in range(2):
                    rho = 2 * m + a
                    i = 2 * n + b
                    ok = (r0[i] <= rho) and (rho <= r0[i] + 7)
                    if ok:
                        valid[a * 64:(a + 1) * 64, b * 64:(b + 1) * 64] = col_ok.T
                        dr[a * 64:(a + 1) * 64, b * 64:(b + 1) * 64] = rho - i + 7
            key = (m - n, valid.tobytes())
            if key not in tabs:
                tabs[key] = len(tab_list)
                tab_list.append((dr, valid))
            pl.append((m, tabs[key]))
        pairs.append(pl)
    dcidx = np.zeros((128, 128), dtype=np.int64)
    for a in range(2):
        for b in range(2):
            dcidx[a * 64:(a + 1) * 64, b * 64:(b + 1) * 64] = dc.T
    return tab_list, pairs, dcidx


D_TABS, D_PAIRS, D_DC = d_tables()
NTAB = len(D_TABS)

C_IDENT = 0
C_BD64 = 128
C_BD32 = 256
C_PSW64 = 384
C_PSW32 = 512
C_ONES = 640
C_MLO = 768
C_MHI = 896
C_DVALID = 1024
NCONST = C_DVALID + NTAB * 128


def make_consts():
    c = np.zeros((128, NCONST), dtype=np.float32)
    p = np.arange(128)
    c[:, C_IDENT:C_IDENT + 128] = np.eye(128, dtype=np.float32)
    c[:, C_BD64:C_BD64 + 128] = (p[:, None] // 64 == p[None, :] // 64)
    c[:, C_BD32:C_BD32 + 128] = (p[:, None] // 32 == p[None, :] // 32)
    part64 = (p // 64) * 64 + (p % 64 + 32) % 64
    part32 = (p // 32) * 32 + (p % 32 + 16) % 32
    c[:, C_PSW64:C_PSW64 + 128] = (p[:, None] == part64[None, :])
    c[:, C_PSW32:C_PSW32 + 128] = (p[:, None] == part32[None, :])
    c[:, C_ONES:C_ONES + 128] = 1.0
    c[:, C_MLO:C_MLO + 128] = (p[:, None] >= p[None, :])
    c[:, C_MHI:C_MHI + 128] = (p[:, None] <= p[None, :])
    for t, (dr, valid) in enumerate(D_TABS):
        c[:, C_DVALID + t * 128:C_DVALID + (t + 1) * 128] = valid
    return c


def host_prep(inp):
    f = lambda a: np.ascontiguousarray(np.asarray(a, dtype=np.float32))
    out = {}
    out['w_in'] = f(inp['w_in'])
    out['w_ba'] = f(inp['w_branch_a'])
    out['w_bb'] = f(inp['w_branch_b'])
    out['w_bc'] = f(inp['w_branch_c'])
    out['w_bd'] = f(inp['w_branch_d'])
    out['w_out'] = f(inp['w_out'])
    out['w_up'] = f(inp['w_up'])
    out['w_down'] = f(inp['w_down'])
    out['gb_attn'] = f(np.broadcast_to(f(inp['attn_norm_g'])[:, None, :], (2, 128, DM)))
    out['gb_mlp'] = f(np.broadcast_to(f(inp['mlp_norm_g'])[:, None, :], (2, 128, DM)))
    gcol = np.zeros((2, 128, NQKB), dtype=np.float32)
    aq, bq, cq, dq = f(inp['a_qk_norm_g']), f(inp['b_qk_norm_g']), f(inp['c_qk_norm_g']), f(inp['d_qk_norm_g'])
    for l in range(2):
        for b in range(4):
            gcol[l, :, b] = np.tile(aq[l, 0], 2)
        gcol[l, :, 4] = np.tile(aq[l, 1], 2)
        for b in range(5, 8):
            gcol[l, :, b] = np.tile(bq[l, 0], 2)
        for b in range(8, 11):
            gcol[l, :, b] = np.tile(bq[l, 1], 2)
        for b in range(11, 13):
            gcol[l, :, b] = np.tile(cq[l, 0], 4)
        for b in range(13, 15):
            gcol[l, :, b] = np.tile(cq[l, 1], 4)
        for b in range(15, 17):
            gcol[l, :, b] = np.tile(dq[l, 0], 2)
        for b in range(17, 19):
            gcol[l, :, b] = np.tile(dq[l, 1], 2)
    out['gcol'] = gcol
    out['sinkb'] = f(np.broadcast_to(f(inp['a_sink'])[:, None, :], (2, 128, 8)))
    out['lamb'] = f(np.broadcast_to(f(inp['c_lambda']).reshape(2, 1, 128), (2, 128, 128)))
    out['gsub'] = f(np.tile(f(inp['c_subln_g']), (1, 2)).reshape(2, 128, 1))
    rpb = f(inp['d_rel_bias'])
    db = np.zeros((2, 4, 128, NTAB, 128), dtype=np.float32)
    for t, (dr, valid) in enumerate(D_TABS):
        db[:, :, :, t, :] = rpb[:, :, dr, D_DC]
    out['dbias'] = db.reshape(2, 4, 128, NTAB * 128)
    c64, s64 = rope_tabs(64)
    c32, s32 = rope_tabs(32)
    p4, p16 = perm_tokens(4), perm_tokens(16)
    out['rope'] = np.ascontiguousarray(np.stack([c64, s64, c64[:, p4], s64[:, p4], c64[:, p16], s64[:, p16], c32, s32], 0))
    out['consts'] = make_consts()
    return out


PARAM_SHAPES = {
    'w_in': [2, DM, INC], 'w_ba': [2, 512, DM], 'w_bb': [2, 128, DM], 'w_bc': [2, 256, DM], 'w_bd': [2, 256, DM],
    'w_out': [2, DM, DM], 'w_up': [2, DM, 4096], 'w_down': [2, 4096, DM],
    'gb_attn': [2, 128, DM], 'gb_mlp': [2, 128, DM], 'gcol': [2, 128, NQKB], 'sinkb': [2, 128, 8],
    'lamb': [2, 128, 128], 'gsub': [2, 128, 1], 'dbias': [2, 4, 128, NTAB * 128],
    'rope': [8, 128, S], 'consts': [128, NCONST],
}


class Ctx:
    pass


def sl(start, n, step):
    return slice(start, start + (n - 1) * step + 1, step)


def ring(lst, i):
    return lst[i % len(lst)]


def phase_p1(C, l, xin):
    P, ar, D = C.P, C.ar, C.D
    nb0 = len(P.bufs)
    m0 = ar.mark()
    hT = ar.alloc([8, S], BF16)
    b_hT = P.buf('hT')
    gb = ar.alloc([DM], F32)
    b_gb = P.buf('gb')
    gcol = ar.alloc([NQKB], F32)
    b_gcol = P.buf('gcol')
    P.load(gb, D['gb_attn'][l], b_gb)
    P.load(gcol, D['gcol'][l], b_gcol)
    m1 = ar.mark()
    xt = [ar.alloc([DM], F32) for _ in range(3)]
    bx = P.bufs_n('xt', 3)
    junk = ar.alloc([DM], BF16)
    b_junk = P.buf('junk')
    ss = [ar.alloc([1], F32) for _ in range(2)]
    bss = P.bufs_n('ss', 2)
    hb = [ar.alloc([DM], BF16) for _ in range(2)]
    bhb = P.bufs_n('hb', 2)
    for tt in range(NT):
        i, j = tt % 3, tt % 2
        P.load(xt[i], xin[tt * 128:(tt + 1) * 128, :], bx[i])
        P.act(junk, xt[i], AF.Square, R=[bx[i]], W=[b_junk, bss[j]], accum_out=ss[j])
        P.act(ss[j], ss[j], AF.Ln, R=[bss[j]], W=[bss[j]], scale=1.0 / DM, bias=EPS)
        P.act(ss[j], ss[j], AF.Exp, R=[bss[j]], W=[bss[j]], scale=-0.5)
        P.stt(hb[j], xt[i], ss[j], gb, ALU.mult, ALU.mult, R=[bx[i], bss[j], b_gb], W=[bhb[j]])
        pbv = C.pb[j].bitcast(BF16)
        for kc in range(8):
            P.tr(pbv[:, kc * 128:(kc + 1) * 128], hb[j][:, kc * 128:(kc + 1) * 128], C.ident, R=[bhb[j], C.b_const], W=[C.bpb[j]])
        P.cp('act' if tt % 2 else 'dve', hT[:, :, tt * 128:(tt + 1) * 128], pbv.rearrange("p (k t) -> p k t", k=8), R=[C.bpb[j]], W=[b_hT])
    for kc in range(8):
        P.store(D['hT_d'][kc * 128:(kc + 1) * 128, :], hT[:, kc, :], b_hT)
    ar.release(m1)
    tabs = ar.alloc([2, S], F32)
    b_tabs = P.buf('ropetab')
    wq = [ar.alloc([8, 128], BF16) for _ in range(2)]
    bwq = P.bufs_n('wq', 2)
    NB = 3
    sq = [ar.alloc([512], BF16) for _ in range(NB)]
    bsq = P.bufs_n('sq', NB)
    xg = [ar.alloc([512], BF16) for _ in range(NB)]
    bxg = P.bufs_n('xg', NB)
    rs = [ar.alloc([512], F32) for _ in range(NB)]
    brs = P.bufs_n('rs', NB)
    ta = [ar.alloc([512], F32) for _ in range(NB)]
    bta = P.bufs_n('ta', NB)
    tb = [ar.alloc([512], F32) for _ in range(NB)]
    btb = P.bufs_n('tb', NB)
    ob = [ar.alloc([512], BF16) for _ in range(NB)]
    bob = P.bufs_n('ob', NB)
    w_in = D['w_in'][l].rearrange("(kc p) c -> p kc c", p=128)
    items = [(blk, tc) for blk in range(NQKB) for tc in range(8)]
    state = {'tab': None}

    def tokf(kind, tc):
        if kind == 'p4':
            r, h0 = tc // 2, (tc % 2) * 512
            return lambda kc: hT[:, kc, sl(r + 4 * h0, 512, 4)]
        if kind == 'p16':
            return lambda kc: hT[:, kc, :].rearrange("p (m r) -> p r m", r=16)[:, 2 * tc:2 * tc + 2, :]
        return lambda kc: hT[:, kc, tc * 512:(tc + 1) * 512]

    def s0(itm, t):
        blk, tc = itm
        coff, kind = QK_BLOCKS[blk]
        wi = blk % 2
        if tc == 0:
            P.dma('pool', wq[wi], w_in[:, :, coff:coff + 128], bwq[wi], W=(bwq[wi],))
        tok = tokf(kind, tc)
        pA, bA = C.pb[2 + (t % 2)], C.bpb[2 + (t % 2)]
        for kc in range(8):
            P.mm(pA[:, :], wq[wi][:, kc, :], tok(kc), start=(kc == 0), stop=(kc == 7), R=[bwq[wi], b_hT], W=[bA])

    def s1(itm, t, defer):
        blk, tc = itm
        coff, kind = QK_BLOCKS[blk]
        tabkind = {'n64': 0, 'p4': 2, 'p16': 4, 'n32': 6, 'd': None}[kind]
        if tabkind is not None and tabkind != state['tab']:
            P.load(tabs[:, 0, :], D['rope'][tabkind], b_tabs)
            P.load(tabs[:, 1, :], D['rope'][tabkind + 1], b_tabs)
            state['tab'] = tabkind
        dh = 32 if kind == 'n32' else 64
        bd = C.bd32 if dh == 32 else C.bd64
        psw = C.psw32 if dh == 32 else C.psw64
        k = t % NB
        pA, pS, pR = C.pb[2 + (t % 2)], C.pb[4 + (t % 2)], C.pb[6 + (t % 2)]
        bA, bS, bR = C.bpb[2 + (t % 2)], C.bpb[4 + (t % 2)], C.bpb[6 + (t % 2)]
        P.act(sq[k], pA[:, :], AF.Square, R=[bA], W=[bsq[k]])
        P.act(xg[k], pA[:, :], AF.Copy, R=[bA, b_gcol], W=[bxg[k]], scale=gcol[:, blk:blk + 1])
        P.mm(pS[:, :], bd, sq[k], R=[bsq[k], C.b_const], W=[bS])
        if kind != 'd':
            P.mm(pR[:, :], psw, xg[k], R=[bxg[k], C.b_const], W=[bR])
        P.act(rs[k], pS[:, :], AF.Ln, R=[bS], W=[brs[k]], scale=1.0 / dh, bias=EPS)
        P.act(rs[k], rs[k], AF.Exp, R=[brs[k]], W=[brs[k]], scale=-0.5)
        csl = slice(tc * 512, (tc + 1) * 512)
        if kind == 'd':
            P.tt('dve', ob[k], xg[k], rs[k], ALU.mult, R=[bxg[k], brs[k]], W=[bob[k]])
        else:
            P.tt('pool', ta[k], xg[k], tabs[:, 0, csl], ALU.mult, R=[bxg[k], b_tabs], W=[bta[k]])
            P.tt('dve', tb[k], pR[:, :], tabs[:, 1, csl], ALU.mult, R=[bR, b_tabs], W=[btb[k]])
            P.tt('pool', ta[k], ta[k], tb[k], ALU.add, R=[bta[k], btb[k]], W=[bta[k]])
            P.tt('dve', ob[k], ta[k], rs[k], ALU.mult, R=[bta[k], brs[k]], W=[bob[k]])
        P.store(D['QKT_d'][blk * 128:(blk + 1) * 128, csl], ob[k], bob[k])

    run_pipeline(items, 1, s0, s1)
    ar.release(m1)
    wv = ar.alloc([8, 1024], BF16)
    b_wv = P.buf('wv')
    o = 0
    for (coff, n) in VNAT_COLS + [(1536 + 128, 256)]:
        P.dma('pool', wv[:, :, o:o + n], w_in[:, :, coff:coff + n], b_wv, W=(b_wv,))
        o += n
    vn = [ar.alloc([12, 65], BF16) for _ in range(2)]
    bvn = P.bufs_n('vn', 2)
    vb = [ar.alloc([2, 2, 65], BF16) for _ in range(2)]
    bvb = P.bufs_n('vbp', 2)
    for j in range(2):
        P.memset('pool', vn[j][:, :, 64:65], 1.0, W=[bvn[j]])
        P.memset('pool', vb[j][:, :, :, 64:65], 1.0, W=[bvb[j]])
    p4, p16 = perm_tokens(4), perm_tokens(16)
    for tt in range(NT):
        j = tt % 2
        pa, pbk, pc = C.pb[j * 3], C.pb[j * 3 + 1], C.pb[j * 3 + 2]
        ba, bb_, bc = C.bpb[j * 3], C.bpb[j * 3 + 1], C.bpb[j * 3 + 2]
        for kc in range(8):
            P.mm(pa[:, :], hT[:, kc, tt * 128:(tt + 1) * 128], wv[:, kc, 0:512], start=(kc == 0), stop=(kc == 7), R=[b_hT, b_wv], W=[ba])
        for kc in range(8):
            P.mm(pbk[:, 0:256], hT[:, kc, tt * 128:(tt + 1) * 128], wv[:, kc, 512:768], start=(kc == 0), stop=(kc == 7), R=[b_hT, b_wv], W=[bb_])
        t4 = int(p4[tt * 128])
        t16 = int(p16[tt * 128])
        for kc in range(8):
            P.mm(pc[:, 0:128], hT[:, kc, sl(t4, 128, 4)], wv[:, kc, 768:896], start=(kc == 0), stop=(kc == 7), R=[b_hT, b_wv], W=[bc])
        for kc in range(8):
            P.mm(pc[:, 128:256], hT[:, kc, sl(t16, 128, 16)], wv[:, kc, 896:1024], start=(kc == 0), stop=(kc == 7), R=[b_hT, b_wv], W=[bc])
        P.cp('act', vn[j][:, 0:8, 0:64], pa[:, :].rearrange("p (h d) -> p h d", h=8), R=[ba], W=[bvn[j]])
        P.cp('dve', vn[j][:, 8:12, 0:64], pbk[:, 0:256].rearrange("p (h d) -> p h d", h=4), R=[bb_], W=[bvn[j]])
        P.cp('dve', vb[j][:, :, :, 0:64], pc[:, 0:256].rearrange("p (g h d) -> p g h d", g=2, h=2), R=[bc], W=[bvb[j]])
        P.store(D['Vnat_d'][tt * 128:(tt + 1) * 128, :], vn[j].rearrange("p h d -> p (h d)"), bvn[j])
        for g in range(2):
            P.store(D['Vb_d'][g, tt * 128:(tt + 1) * 128, :], vb[j][:, g].rearrange("p h d -> p (h d)"), bvb[j])
    ar.release(m0)
    P.barrier()
    P.retire(nb0)


def run_pipeline(items, LA, s0, s1):
    n = len(items)
    pend = {}
    cur = [0]

    def defer(delay, fn):
        pend.setdefault(cur[0] + delay, []).append(fn)

    for t in range(n + LA):
        cur[0] = t
        if t < n:
            s0(items[t], t)
        if t >= LA:
            s1(items[t - LA], t - LA, defer)
        for fn in pend.pop(t, []):
            fn()
    while pend:
        t2 = min(pend)
        cur[0] = t2
        for fn in pend.pop(t2):
            fn()


def finalize_norm(C, acc, bacc, n, dst, bdst, shape3=None, esink=None, tagk=0, defer=None, after=None, dl=(1, 3, 5, 6)):
    P = C.P
    k = tagk % 2
    r, rhi, rlo = C.fr[k], C.frhi[k], C.frlo[k]
    br = C.bfr[k]
    bc, bbc = C.pb[6 + k], C.bpb[6 + k]
    bcs, bbcs = C.fbcs[k], C.bfbcs[k]
    src = acc[64:65, 0:n]

    def g0():
        if esink is not None:
            j, q = shape3
            P.tt('dve', r[64:65, 0:n].rearrange("p (j q) -> p j q", j=j), src.rearrange("p (j q) -> p j q", j=j),
                 esink, ALU.add, R=[bacc, C.b_esk], W=[br])
            P.recip(r[64:65, 0:n], r[64:65, 0:n], R=[br], W=[br])
        else:
            P.recip(r[64:65, 0:n], src, R=[bacc], W=[br])
        P.cp('dve', rhi[64:65, 0:n], r[64:65, 0:n], R=[br], W=[br])
        P.tt('dve', rlo[64:65, 0:n], r[64:65, 0:n], rhi[64:65, 0:n], ALU.subtract, R=[br], W=[br])

    def g1():
        P.mm(bc[0:64, 0:n], C.ones_bf[64:65, 0:64], rhi[64:65, 0:n], start=True, stop=False, R=[br, C.b_const], W=[bbc])
        P.mm(bc[0:64, 0:n], C.ones_bf[64:65, 0:64], rlo[64:65, 0:n], start=False, stop=True, R=[br, C.b_const], W=[bbc])

    def g2():
        P.cp('act', bcs[0:64, 0:n], bc[0:64, 0:n], R=[bbc], W=[bbcs])

    def g3():
        a0 = acc[0:64, 0:n]
        b0 = bcs[0:64, 0:n]
        if shape3 is not None:
            j, q = shape3
            a0 = a0.rearrange("p (j q) -> p j q", j=j)
            b0 = b0.rearrange("p (j q) -> p j q", j=j)
        P.tt('dve', dst, a0, b0, ALU.mult, R=[bacc, bbcs], W=[bdst])
        if after is not None:
            after()

    if defer is None:
        g0(); g1(); g2(); g3()
    else:
        defer(dl[0], g0); defer(dl[1], g1); defer(dl[2], g2); defer(dl[3], g3)


def alloc_fin(C):
    ar, P = C.ar, C.P
    C.fr = [ar.alloc([512], F32) for _ in range(2)]
    C.frhi = [ar.alloc([512], BF16) for _ in range(2)]
    C.frlo = [ar.alloc([512], BF16) for _ in range(2)]
    C.bfr = P.bufs_n('fr', 2)
    C.fbcs = [ar.alloc([512], F32) for _ in range(2)]
    C.bfbcs = P.bufs_n('fbcs', 2)


def mixer_a(C, l):
    P, ar, D = C.P, C.ar, C.D
    nb0 = len(P.bufs)
    m0 = ar.mark()
    alloc_fin(C)
    QT = ar.alloc([4, S], BF16)
    bQT = P.buf('aQT')
    KT = ar.alloc([2, S], BF16)
    bKT = P.buf('aKT')
    V = ar.alloc([NT, 130], BF16)
    bV = P.buf('aV')
    yst = ar.alloc([4, S], BF16)
    byst = P.buf('ayst')
    esk = ar.alloc([8], F32)
    C.b_esk = P.buf('esk')
    P.load(esk, D['sinkb'][l], C.b_esk)
    P.act(esk, esk, AF.Exp, R=[C.b_esk], W=[C.b_esk])
    for g in range(2):
        for j in range(4):
            h = 4 * g + j
            P.load(QT[g * 64:(g + 1) * 64, j, :], D['QKT_d'][h * 64:(h + 1) * 64, :], bQT)
    P.memset('pool', KT, 0.0, W=[bKT])
    for g in range(2):
        P.load(KT[g * 64:(g + 1) * 64, g, :], D['QKT_d'][512 + g * 64:512 + (g + 1) * 64, :], bKT)
    P.load(V, D['Vnat_d'].rearrange("(t p) c -> p t c", p=128)[:, :, 0:130], bV)
    NP = 3
    pt = [ar.alloc([512], BF16) for _ in range(NP)]
    bpt = P.bufs_n('apt', NP)
    mlo = C.cbf[:, C_MLO:C_MLO + 128].unsqueeze(1).broadcast_to([128, 4, 128])
    mhi = C.cbf[:, C_MHI:C_MHI + 128].unsqueeze(1).broadcast_to([128, 4, 128])
    items = []
    fi = 0
    for g in range(2):
        for n in range(NT):
            ms = [m for m in (n - 1, n, n + 1) if 0 <= m < NT]
            for idx, m in enumerate(ms):
                items.append((g, n, m, idx == 0, idx == len(ms) - 1, fi))
            fi += 1

    def s0(itm, t):
        g, n, m, first, last, f = itm
        ps = slice(g * 64, (g + 1) * 64)
        st, bst = C.pb[t % 3], C.bpb[t % 3]
        P.mm(st[:, :].rearrange("p (j q) -> p j q", j=4), KT[:, g, m * 128:(m + 1) * 128], QT[:, :, n * 128:(n + 1) * 128],
             R=[bKT, bQT], W=[bst])

    def s1(itm, t, defer):
        g, n, m, first, last, f = itm
        k = t % NP
        st, bst = C.pb[t % 3], C.bpb[t % 3]
        acc, bacc = C.pb[3 + (f % 3)], C.bpb[3 + (f % 3)]
        P.act(pt[k], st[:, :], AF.Exp, R=[bst], W=[bpt[k]], scale=0.125)
        if m != n:
            v3 = pt[k].rearrange("p (j q) -> p j q", j=4)
            P.tt('pool' if t % 2 else 'dve', v3, v3, mlo if m < n else mhi, ALU.mult, R=[bpt[k], C.b_const], W=[bpt[k]])
        P.mm(acc[0:65, :], V[:, m, g * 65:(g + 1) * 65], pt[k], start=first, stop=last, R=[bV, bpt[k]], W=[bacc])
        if last:
            es = esk[64:65, 4 * g:4 * g + 4].unsqueeze(2).broadcast_to([1, 4, 128])
            def after(g=g, n=n):
                if n == NT - 1:
                    for j in range(4):
                        h = 4 * g + j
                        P.store(D['YT_d'][h * 64:(h + 1) * 64, :], yst[0:64, j, :], byst)
            finalize_norm(C, acc, bacc, 512, yst[0:64, :, n * 128:(n + 1) * 128], byst, shape3=(4, 128), esink=es, tagk=f,
                          defer=defer, after=after, dl=(1, 2, 3, 4))

    run_pipeline(items, 2, s0, s1)
    ar.release(m0)
    P.barrier()
    P.retire(nb0)


def mixer_b(C, l):
    P, ar, D = C.P, C.ar, C.D
    nb0 = len(P.bufs)
    m0 = ar.mark()
    alloc_fin(C)
    accN = ar.alloc([2, S], F32)
    baccN = P.buf('baccN')
    yst = ar.alloc([2, S], BF16)
    byst = P.buf('byst')
    QTs = [ar.alloc([S], BF16) for _ in range(3)]
    KTs = [ar.alloc([2, S], BF16) for _ in range(3)]
    Vs = [ar.alloc([NT, 130], BF16) for _ in range(3)]
    bQ = P.bufs_n('bQT', 3)
    bK = P.bufs_n('bKT', 3)
    bVv = P.bufs_n('bV', 3)
    NP = 3
    pt = [ar.alloc([512], BF16) for _ in range(NP)]
    bpt = P.bufs_n('bpt', NP)
    items = []
    ai = 0
    for gi, d in enumerate(B_DIL):
        L = S // d
        nb = L // 128
        QT, KT, V = QTs[gi], KTs[gi], Vs[gi]
        bq, bk, bv = bQ[gi], bK[gi], bVv[gi]
        P.load(QT, D['QKT_d'][(5 + gi) * 128:(6 + gi) * 128, :], bq)
        P.memset('pool' if gi % 2 else 'dve', KT, 0.0, W=[bk])
        for jh_ in range(2):
            P.load(KT[jh_ * 64:(jh_ + 1) * 64, jh_, :], D['QKT_d'][(8 + gi) * 128 + jh_ * 64:(8 + gi) * 128 + (jh_ + 1) * 64, :], bk)
        if gi == 0:
            P.load(V, D['Vnat_d'].rearrange("(t p) c -> p t c", p=128)[:, :, 650:780], bv)
        else:
            P.load(V, D['Vb_d'][gi - 1].rearrange("(t p) c -> p t c", p=128), bv)
        for jh in range(2):
            for r in range(d):
                base = r * L
                qblocks = []
                for n_ in range(-1, nb):
                    q0 = 128 * n_ + 64
                    qa, qb = max(q0, 0), min(q0 + 128, L)
                    tiles = []
                    if n_ >= 0:
                        tiles.append((n_, C_MLO))
                    if n_ + 1 < nb:
                        tiles.append((n_ + 1, C_MHI))
                    qblocks.append((qa, qb - qa, qa - q0, tiles))
                for g0 in range(0, len(qblocks), 4):
                    grp = qblocks[g0:g0 + 4]
                    ncols = sum(q[1] for q in grp)
                    pstart = grp[0][0]
                    col = 0
                    nsub = (len(grp) + 1) // 2
                    for bi in range(0, len(grp), 2):
                        sub = grp[bi:bi + 2]
                        plist = []
                        sc = 0
                        for (qa, nq, aoff, tiles) in sub:
                            for ti, (m, mk) in enumerate(tiles):
                                plist.append((sc, nq, aoff, mk, base // 128 + m, col, ti == 0, ti == len(tiles) - 1, base + qa))
                                sc += nq
                            col += nq
                        endinfo = None
                        if bi // 2 == nsub - 1:
                            endinfo = (gi, d, r, pstart, ncols)
                        items.append((QT, KT, V, bq, bk, bv, jh, plist, sc, ai, endinfo))
                    ai += 1

    def s0(itm, t):
        QT, KT, V, bq, bk, bv, jh, plist, sc, a_, endinfo = itm
        ps = slice(jh * 64, (jh + 1) * 64)
        st, bst = C.pb[t % 3], C.bpb[t % 3]
        for (s0_, nq, aoff, mk, tg, c0, first, last, qpos) in plist:
            P.mm(st[:, s0_:s0_ + nq], KT[:, jh, tg * 128:(tg + 1) * 128], QT[:, qpos:qpos + nq], R=[bk, bq], W=[bst])

    def s1(itm, t, defer):
        QT, KT, V, bq, bk, bv, jh, plist, sc, a_, endinfo = itm
        k = t % NP
        st, bst = C.pb[t % 3], C.bpb[t % 3]
        acc, bacc = C.pb[3 + (a_ % 3)], C.bpb[3 + (a_ % 3)]
        P.act(pt[k][:, 0:sc], st[:, 0:sc], AF.Exp, R=[bst], W=[bpt[k]], scale=0.125)
        for pi, (s0_, nq, aoff, mk, tg, c0, _, _, _) in enumerate(plist):
            P.tt('pool' if pi % 2 else 'dve', pt[k][:, s0_:s0_ + nq], pt[k][:, s0_:s0_ + nq],
                 C.cbf[:, mk + aoff:mk + aoff + nq], ALU.mult, R=[bpt[k], C.b_const], W=[bpt[k]])
        for (s0_, nq, aoff, mk, tg, c0, first, last, _) in plist:
            P.mm(acc[0:65, c0:c0 + nq], V[:, tg, jh * 65:(jh + 1) * 65], pt[k][:, s0_:s0_ + nq], start=first, stop=last,
                 R=[bv, bpt[k]], W=[bacc])
        if endinfo is not None:
            gi, d, r, pstart, ncols = endinfo
            if gi == 0:
                P.cp('act', accN[0:65, jh, pstart:pstart + ncols], acc[0:65, 0:ncols], R=[bacc], W=[baccN])
            else:
                t0 = r + d * pstart
                view = accN[0:65, jh, sl(t0, ncols, d)]
                P.tt('dve', view, view, acc[0:65, 0:ncols], ALU.add, R=[bacc, baccN], W=[baccN])

    run_pipeline(items, 2, s0, s1)
    fi = 0
    for jh in range(2):
        for ch in range(8):
            cs = slice(ch * 512, (ch + 1) * 512)
            k = fi % 2
            r, rhi, rlo, br = C.fr[k], C.frhi[k], C.frlo[k], C.bfr[k]
            bc, bbc = C.pb[6 + k], C.bpb[6 + k]
            P.recip(r[64:65, :], accN[64:65, jh, cs], R=[baccN], W=[br])
            P.cp('dve', rhi[64:65, :], r[64:65, :], R=[br], W=[br])
            P.tt('dve', rlo[64:65, :], r[64:65, :], rhi[64:65, :], ALU.subtract, R=[br], W=[br])
            P.mm(bc[0:64, :], C.ones_bf[64:65, 0:64], rhi[64:65, :], start=True, stop=False, R=[br, C.b_const], W=[bbc])
            P.mm(bc[0:64, :], C.ones_bf[64:65, 0:64], rlo[64:65, :], start=False, stop=True, R=[br, C.b_const], W=[bbc])
            P.tt('dve', yst[0:64, jh, cs], accN[0:64, jh, cs], bc[0:64, :], ALU.mult, R=[baccN, bbc], W=[byst])
            fi += 1
        P.store(D['YT_d'][512 + jh * 64:512 + (jh + 1) * 64, :], yst[0:64, jh, :], byst)
    ar.release(m0)
    P.barrier()
    P.retire(nb0)


def mixer_d(C, l):
    P, ar, D = C.P, C.ar, C.D
    nb0 = len(P.bufs)
    m0 = ar.mark()
    alloc_fin(C)
    QT = ar.alloc([2, S], BF16)
    KT = ar.alloc([4, S], BF16)
    V = ar.alloc([NT, 260], BF16)
    bQT, bKT, bV = P.buf('dQT'), P.buf('dKT'), P.buf('dV')
    yst = ar.alloc([4, S], BF16)
    byst = P.buf('dyst')
    EB = ar.alloc([4, NTAB * 128], BF16)
    bEB = P.buf('dEB')
    tmpb = [ar.alloc([NTAB * 128], F32) for _ in range(2)]
    btmp = P.bufs_n('dtmp', 2)
    P.memset('pool', KT[:, 0:2, :], 0.0, W=[bKT])
    P.memset('dve', KT[:, 2:4, :], 0.0, W=[bKT])
    for i in range(2):
        P.load(QT[:, i, :], D['QKT_d'][(15 + i) * 128:(16 + i) * 128, :], bQT)
    for h_ in range(4):
        r0_ = (h_ % 2) * 64
        P.load(KT[r0_:r0_ + 64, h_, :], D['QKT_d'][17 * 128 + h_ * 64:17 * 128 + (h_ + 1) * 64, :], bKT)
    P.load(V, D['Vnat_d'].rearrange("(t p) c -> p t c", p=128)[:, :, 390:650], bV)
    for h in range(4):
        P.load(tmpb[h % 2], D['dbias'][l, h], btmp[h % 2])
        P.act(tmpb[h % 2], tmpb[h % 2], AF.Exp, R=[btmp[h % 2]], W=[btmp[h % 2]])
        P.tt('dve', EB[:, h, :], tmpb[h % 2], C.cbf[:, C_DVALID:C_DVALID + NTAB * 128], ALU.mult, R=[btmp[h % 2], C.b_const], W=[bEB])
    NP = 3
    pt = [ar.alloc([512], BF16) for _ in range(NP)]
    bpt = P.bufs_n('dpt', NP)
    items = []
    fi = 0
    for h in range(4):
        for n4 in range(8):
            plist = []
            for n in range(n4 * 4, n4 * 4 + 4):
                pl = D_PAIRS[n]
                for pi, (m, tab) in enumerate(pl):
                    plist.append((n, m, tab, pi == 0, pi == len(pl) - 1))
            nch = (len(plist) + 3) // 4
            for ci in range(nch):
                items.append((h, n4, plist[ci * 4:ci * 4 + 4], fi, ci == nch - 1))
            fi += 1

    def s0(itm, t):
        h, n4, chunk, f, endg = itm
        bq = h // 2
        ps = slice((h % 2) * 64, (h % 2 + 1) * 64)
        st, bst = C.pb[t % 3], C.bpb[t % 3]
        for i, (n, m, tab, first, last) in enumerate(chunk):
            P.mm(st[:, i * 128:(i + 1) * 128], KT[:, h, m * 128:(m + 1) * 128], QT[:, bq, n * 128:(n + 1) * 128],
                 R=[bKT, bQT], W=[bst])

    def s1(itm, t, defer):
        h, n4, chunk, f, endg = itm
        k = t % NP
        st, bst = C.pb[t % 3], C.bpb[t % 3]
        acc, bacc = C.pb[3 + (f % 3)], C.bpb[3 + (f % 3)]
        used = len(chunk) * 128
        P.act(pt[k][:, 0:used], st[:, 0:used], AF.Exp, R=[bst], W=[bpt[k]], scale=0.125)
        for i, (n, m, tab, first, last) in enumerate(chunk):
            P.tt('pool' if i % 2 else 'dve', pt[k][:, i * 128:(i + 1) * 128], pt[k][:, i * 128:(i + 1) * 128],
                 EB[:, h, tab * 128:(tab + 1) * 128], ALU.mult, R=[bpt[k], bEB], W=[bpt[k]])
        for i, (n, m, tab, first, last) in enumerate(chunk):
            qc = (n % 4) * 128
            P.mm(acc[0:65, qc:qc + 128], V[:, m, h * 65:(h + 1) * 65], pt[k][:, i * 128:(i + 1) * 128], start=first, stop=last,
                 R=[bV, bpt[k]], W=[bacc])
        if endg:
            def after(h=h, n4=n4):
                if n4 == 7:
                    P.store(D['YT_d'][896 + h * 64:896 + (h + 1) * 64, :], yst[0:64, h, :], byst)
            finalize_norm(C, acc, bacc, 512, yst[0:64, h, n4 * 512:(n4 + 1) * 512], byst, tagk=f, defer=defer, after=after, dl=(1, 2, 3, 4))

    run_pipeline(items, 2, s0, s1)
    ar.release(m0)
    P.barrier()
    P.retire(nb0)


def mixer_c(C, l):
    P, ar, D = C.P, C.ar, C.D
    lam_init = 0.8 - 0.6 * math.exp(-0.3 * l)
    nb0 = len(P.bufs)
    m0 = ar.mark()
    alloc_fin(C)
    QT = ar.alloc([2, S], BF16)
    KT = ar.alloc([8, S], BF16)
    V = ar.alloc([NT, 260], BF16)
    bQT, bKT, bV = P.buf('cQT'), P.buf('cKT'), P.buf('cV')
    yst = ar.alloc([4, S], BF16)
    byst = P.buf('cyst')
    P.memset('pool', KT[:, 0:4, :], 0.0, W=[bKT])
    P.memset('dve', KT[:, 4:8, :], 0.0, W=[bKT])
    for g2 in range(2):
        P.load(QT[:, g2, :], D['QKT_d'][(11 + g2) * 128:(12 + g2) * 128, :], bQT)
    for b in range(8):
        sl_ = (b % 4) * 32
        row = (b // 4) * 128 + sl_
        P.load(KT[sl_:sl_ + 32, b, :], D['QKT_d'][13 * 128 + row:13 * 128 + row + 32, :], bKT)
    P.load(V, D['Vnat_d'].rearrange("(t p) c -> p t c", p=128)[:, :, 130:390], bV)
    lamb = ar.alloc([128], F32)
    blam = P.buf('lam')
    lt = ar.alloc([2, 32], F32)
    l2 = ar.alloc([2], F32)
    nlam = ar.alloc([1], F32)
    gsc = ar.alloc([1], F32)
    bgsc = P.buf('gsc')
    P.load(lamb, D['lamb'][l], blam)
    P.load(gsc, D['gsub'][l], bgsc)
    lv = lamb.rearrange("p (a b c) -> p a b c", a=2, b=2)
    P.tt('dve', lt, lv[:, :, 0, :], lv[:, :, 1, :], ALU.mult, R=[blam], W=[blam])
    P.op('dve', lambda e: e.reduce_sum(out=l2, in_=lt, axis=mybir.AxisListType.X), R=[blam], W=[blam])
    P.act(l2, l2, AF.Exp, R=[blam], W=[blam])
    P.tt('dve', nlam, l2[:, 0:1], l2[:, 1:2], ALU.subtract, R=[blam], W=[blam])
    P.ts('dve', nlam, nlam, lam_init, -1.0, ALU.add, ALU.mult, R=[blam], W=[blam])
    P.ts('dve', gsc, gsc, 1.0 - lam_init, None, ALU.mult, R=[bgsc], W=[bgsc])
    NP = 4
    pt = [ar.alloc([512], BF16) for _ in range(NP)]
    bpt = P.bufs_n('cpt', NP)
    to = [ar.alloc([512], F32) for _ in range(2)]
    t1 = [ar.alloc([512], F32) for _ in range(2)]
    sqb = [ar.alloc([512], BF16) for _ in range(2)]
    rsd = [ar.alloc([512], F32) for _ in range(2)]
    bto = P.bufs_n('cto', 2)
    scale = 32.0 ** -0.5
    items = []
    fi = 0
    for h in range(4):
        for Q in range(8):
            for c in range(2):
                for kt in range(NT):
                    items.append((h, Q, c, kt, fi))
            fi += 1

    def s0(itm, t):
        h, Q, c, kt, f = itm
        b = 2 * h + c
        st, bst = C.pb[t % 3], C.bpb[t % 3]
        P.mm(st[:, :], KT[:, b, kt * 128:(kt + 1) * 128], QT[:, b // 4, Q * 512:(Q + 1) * 512], R=[bKT, bQT], W=[bst])

    def s1(itm, t, defer):
        h, Q, c, kt, f = itm
        qs = slice(Q * 512, (Q + 1) * 512)
        k = t % NP
        st, bst = C.pb[t % 3], C.bpb[t % 3]
        a0 = 3 + 2 * (f % 2)
        accs = [C.pb[a0], C.pb[a0 + 1]]
        baccs = [C.bpb[a0], C.bpb[a0 + 1]]
        P.act(pt[k], st[:, :], AF.Exp, R=[bst], W=[bpt[k]], scale=scale)
        P.mm(accs[c][0:65, :], V[:, kt, h * 65:(h + 1) * 65], pt[k], start=(kt == 0), stop=(kt == NT - 1),
             R=[bV, bpt[k]], W=[baccs[c]])
        if not (c == 1 and kt == NT - 1):
            return
        k2 = f % 2
        bt = bto[k2]
        bc, bbc = C.pb[7], C.bpb[7]

        def f0():
            for c_ in range(2):
                r, rhi, rlo, br = C.fr[c_], C.frhi[c_], C.frlo[c_], C.bfr[c_]
                P.recip(r[64:65, :], accs[c_][64:65, :], R=[baccs[c_]], W=[br])
                if c_ == 1:
                    P.ts('dve', r[64:65, :], r[64:65, :], nlam[64:65, 0:1], None, ALU.mult, R=[br, blam], W=[br])
                P.cp('dve', rhi[64:65, :], r[64:65, :], R=[br], W=[br])
                P.tt('dve', rlo[64:65, :], r[64:65, :], rhi[64:65, :], ALU.subtract, R=[br], W=[br])

        def f1(c_):
            def fn():
                r, rhi, rlo, br = C.fr[c_], C.frhi[c_], C.frlo[c_], C.bfr[c_]
                P.mm(bc[0:64, :], C.ones_bf[64:65, 0:64], rhi[64:65, :], start=True, stop=False, R=[br, C.b_const], W=[bbc])
                P.mm(bc[0:64, :], C.ones_bf[64:65, 0:64], rlo[64:65, :], start=False, stop=True, R=[br, C.b_const], W=[bbc])
            return fn

        def f2(c_):
            def fn():
                P.cp('act', C.fbcs[c_][0:64, :], bc[0:64, :], R=[bbc], W=[C.bfbcs[c_]])
            return fn

        def f3():
            P.tt('dve', to[k2][0:64, :], accs[0][0:64, :], C.fbcs[0][0:64, :], ALU.mult, R=[baccs[0], C.bfbcs[0]], W=[bt])
            P.tt('dve', t1[k2][0:64, :], accs[1][0:64, :], C.fbcs[1][0:64, :], ALU.mult, R=[baccs[1], C.bfbcs[1], bt], W=[bt])
            P.tt('pool', to[k2][0:64, :], to[k2][0:64, :], t1[k2][0:64, :], ALU.add, R=[bt], W=[bt])

        def f4():
            P.act(sqb[k2][0:64, :], to[k2][0:64, :], AF.Square, R=[bt], W=[bt])

        def f5():
            P.mm(bc[0:64, :], C.ones_bf[0:64, 0:64], sqb[k2][0:64, :], R=[bt, C.b_const], W=[bbc])

        def f6():
            P.act(rsd[k2][0:64, :], bc[0:64, :], AF.Ln, R=[bbc], W=[bt], scale=1.0 / 64, bias=EPS)
            P.act(rsd[k2][0:64, :], rsd[k2][0:64, :], AF.Exp, R=[bt], W=[bt], scale=-0.5)

        def f7():
            P.stt(yst[0:64, h, qs], to[k2][0:64, :], gsc[0:64, 0:1], rsd[k2][0:64, :], ALU.mult, ALU.mult, R=[bt, bgsc], W=[byst])
            if Q == 7:
                P.store(D['YT_d'][640 + h * 64:640 + (h + 1) * 64, :], yst[0:64, h, :], byst)

        defer(1, f0)
        defer(4, f1(0))
        defer(5, f2(0))
        defer(6, f1(1))
        defer(7, f2(1))
        defer(9, f3)
        defer(12, f4)
        defer(13, f5)
        defer(15, f6)
        defer(17, f7)

    run_pipeline(items, 2, s0, s1)
    ar.release(m0)
    P.barrier()
    P.retire(nb0)


def phase_p3a(C, l, xin, xout):
    P, ar, D = C.P, C.ar, C.D
    nb0 = len(P.bufs)
    m0 = ar.mark()
    Wg = ar.alloc([8, 4096], BF16)
    Wb = ar.alloc([9, DM], BF16)
    Wo = ar.alloc([8, DM], BF16)
    bWg, bWb, bWo = P.buf('Wg'), P.buf('Wb'), P.buf('Wo')
    w_in = D['w_in'][l].rearrange("(kc p) c -> p kc c", p=128)
    for kc in range(8):
        P.dma('pool', Wg[:, kc, :], w_in[:, kc, GATE_OFF:GATE_OFF + 4096], bWg, W=(bWg,))
    P.dma('pool', Wb[:, 0:4, :], D['w_ba'][l].rearrange("(kc p) c -> p kc c", p=128), bWb, W=(bWb,))
    P.dma('pool', Wb[:, 4, :], D['w_bb'][l], bWb, W=(bWb,))
    P.dma('pool', Wb[:, 5:7, :], D['w_bc'][l].rearrange("(kc p) c -> p kc c", p=128), bWb, W=(bWb,))
    P.dma('pool', Wb[:, 7:9, :], D['w_bd'][l].rearrange("(kc p) c -> p kc c", p=128), bWb, W=(bWb,))
    P.dma('pool', Wo, D['w_out'][l].rearrange("(kc p) c -> p kc c", p=128), bWo, W=(bWo,))
    hTc = [ar.alloc([8, 512], BF16) for _ in range(2)]
    YTc = [ar.alloc([9, 512], BF16) for _ in range(2)]
    bh = P.bufs_n('hTc', 2)
    by = P.bufs_n('YTc', 2)
    mT = [ar.alloc([8, 512], BF16) for _ in range(2)]
    bmT = P.bufs_n('mT', 2)
    sig = [ar.alloc([512], F32) for _ in range(3)]
    bsig = P.bufs_n('sig', 3)
    tmp = [ar.alloc([512], F32) for _ in range(2)]
    btmp = P.bufs_n('mtmp', 2)
    macc = [ar.alloc([512], F32) for _ in range(2)]
    bmacc = P.bufs_n('macc', 2)
    xt = [ar.alloc([DM], F32) for _ in range(2)]
    bxt = P.bufs_n('x3', 2)
    xo = [ar.alloc([DM], F32) for _ in range(2)]
    bxo = P.bufs_n('xo3', 2)
    hT_v = D['hT_d'].rearrange("(kc p) t -> p kc t", p=128)
    YT_v = D['YT_d'].rearrange("(kc p) t -> p kc t", p=128)
    branches = [(0, 4), (4, 5), (5, 7), (7, 9)]
    it = 0
    si = 0
    ti = 0
    for tg in range(8):
        g2 = tg % 2
        ts_ = slice(tg * 512, (tg + 1) * 512)
        P.load(hTc[g2], hT_v[:, :, ts_], bh[g2])
        P.load(YTc[g2], YT_v[:, :, ts_], by[g2])
        for ct in range(8):
            ma, bma = macc[ct % 2], bmacc[ct % 2]
            for br in range(4):
                pg, bpg = C.pb[(it % 2) * 2], C.bpb[(it % 2) * 2]
                py, bpy = C.pb[(it % 2) * 2 + 1], C.bpb[(it % 2) * 2 + 1]
                it += 1
                c0 = br * 1024 + ct * 128
                for kc in range(8):
                    P.mm(pg[:, :], Wg[:, kc, c0:c0 + 128], hTc[g2][:, kc, :], start=(kc == 0), stop=(kc == 7), R=[bWg, bh[g2]], W=[bpg])
                b0, b1 = branches[br]
                for bi in range(b0, b1):
                    P.mm(py[:, :], Wb[:, bi, ct * 128:(ct + 1) * 128], YTc[g2][:, bi, :], start=(bi == b0), stop=(bi == b1 - 1),
                         R=[bWb, by[g2]], W=[bpy])
                s_, bs_ = sig[si % 3], bsig[si % 3]
                si += 1
                P.act(s_, pg[:, :], AF.Sigmoid, R=[bpg], W=[bs_])
                if br == 0:
                    P.tt('dve', ma, s_, py[:, :], ALU.mult, R=[bs_, bpy], W=[bma])
                else:
                    t_, bt_ = tmp[ti % 2], btmp[ti % 2]
                    ti += 1
                    P.tt('dve', t_, s_, py[:, :], ALU.mult, R=[bs_, bpy], W=[bt_])
                    if br < 3:
                        P.tt('pool', ma, ma, t_, ALU.add, R=[bma, bt_], W=[bma])
                    else:
                        P.tt('pool', mT[g2][:, ct, :], ma, t_, ALU.add, R=[bma, bt_], W=[bmT[g2]])
        for tt in range(4):
            tok0 = tg * 512 + tt * 128
            xi = (tg * 4 + tt) % 2
            P.load(xt[xi], xin[tok0:tok0 + 128, :], bxt[xi])
            for cg in range(2):
                po, bpo = C.pb[4 + (cg + 2 * tt) % 4], C.bpb[4 + (cg + 2 * tt) % 4]
                for kc in range(8):
                    P.mm(po[:, :], mT[g2][:, kc, tt * 128:(tt + 1) * 128], Wo[:, kc, cg * 512:(cg + 1) * 512], start=(kc == 0), stop=(kc == 7),
                         R=[bmT[g2], bWo], W=[bpo])
                P.tt('dve', xo[xi][:, cg * 512:(cg + 1) * 512], po[:, :], xt[xi][:, cg * 512:(cg + 1) * 512], ALU.add,
                     R=[bpo, bxt[xi]], W=[bxo[xi]])
            P.store(xout[tok0:tok0 + 128, :], xo[xi], bxo[xi])
    ar.release(m0)
    P.barrier()
    P.retire(nb0)


def phase_p3b(C, l, xin, xout):
    P, ar, D = C.P, C.ar, C.D
    nb0 = len(P.bufs)
    m0 = ar.mark()
    Wu = ar.alloc([8, 4096], BF16)
    Wd = ar.alloc([32, DM], BF16)
    gb = ar.alloc([DM], F32)
    bWu, bWd, bgb = P.buf('Wu'), P.buf('Wd'), P.buf('gbm')
    wu_v = D['w_up'][l].rearrange("(kc p) c -> p kc c", p=128)
    wd_v = D['w_down'][l].rearrange("(kc p) c -> p kc c", p=128)
    P.load(gb, D['gb_mlp'][l], bgb)
    for kc in range(8):
        P.dma('pool', Wu[:, kc, :], wu_v[:, kc, :], bWu, W=(bWu,))
    for k4 in range(8):
        P.dma('pool', Wd[:, k4 * 4:(k4 + 1) * 4, :], wd_v[:, k4 * 4:(k4 + 1) * 4, :], bWd, W=(bWd,))
    xt = [ar.alloc([DM], F32) for _ in range(2)]
    bxt = P.bufs_n('x4', 2)
    xo = [ar.alloc([DM], F32) for _ in range(1)]
    bxo = P.bufs_n('xo4', 1)
    ss = [ar.alloc([1], F32) for _ in range(2)]
    bss = P.bufs_n('ss4', 2)
    hb = [ar.alloc([DM], BF16) for _ in range(2)]
    bhb = P.bufs_n('hb4', 2)
    hmT = [ar.alloc([8, 512], BF16) for _ in range(2)]
    bhm = P.bufs_n('hmT', 2)
    uT = ar.alloc([32, 512], BF16)
    buT = P.buf('uT')
    rl = [ar.alloc([512], F32) for _ in range(2)]
    brl = P.bufs_n('rl', 2)
    xi = 0
    it = 0
    for tg in range(8):
        g2 = tg % 2
        for tt in range(4):
            tok0 = tg * 512 + tt * 128
            i, j = xi % 2, xi % 2
            xi += 1
            P.load(xt[i], xin[tok0:tok0 + 128, :], bxt[i])
            P.act(hb[j], xt[i], AF.Square, R=[bxt[i]], W=[bhb[j], bss[j]], accum_out=ss[j])
            P.act(ss[j], ss[j], AF.Ln, R=[bss[j]], W=[bss[j]], scale=1.0 / DM, bias=EPS)
            P.act(ss[j], ss[j], AF.Exp, R=[bss[j]], W=[bss[j]], scale=-0.5)
            P.stt(hb[j], xt[i], ss[j], gb, ALU.mult, ALU.mult, R=[bxt[i], bss[j], bgb], W=[bhb[j]])
            pbv = C.pb[6 + j].bitcast(BF16)
            for kc in range(8):
                P.tr(pbv[:, kc * 128:(kc + 1) * 128], hb[j][:, kc * 128:(kc + 1) * 128], C.ident, R=[bhb[j], C.b_const], W=[C.bpb[6 + j]])
            P.cp('dve', hmT[g2][:, :, tt * 128:(tt + 1) * 128], pbv.rearrange("p (k t) -> p k t", k=8), R=[C.bpb[6 + j]], W=[bhm[g2]])
        for mt in range(32):
            pu, bpu = C.pb[it % 3], C.bpb[it % 3]
            r_, br_ = rl[it % 2], brl[it % 2]
            it += 1
            for kc in range(8):
                P.mm(pu[:, :], Wu[:, kc, mt * 128:(mt + 1) * 128], hmT[g2][:, kc, :], start=(kc == 0), stop=(kc == 7), R=[bWu, bhm[g2]], W=[bpu])
            P.act(r_, pu[:, :], AF.Relu, R=[bpu], W=[br_])
            P.tt('pool' if mt % 2 else 'dve', uT[:, mt, :], r_, r_, ALU.mult, R=[br_], W=[buT])
        for tt in range(4):
            tok0 = tg * 512 + tt * 128
            i, j = xi % 2, 0
            xi += 1
            P.load(xt[i], xin[tok0:tok0 + 128, :], bxt[i])
            for cg in range(2):
                pd, bpd = C.pb[3 + (cg + 2 * tt) % 3], C.bpb[3 + (cg + 2 * tt) % 3]
                for mt in range(32):
                    P.mm(pd[:, :], uT[:, mt, tt * 128:(tt + 1) * 128], Wd[:, mt, cg * 512:(cg + 1) * 512], start=(mt == 0), stop=(mt == 31),
                         R=[buT, bWd], W=[bpd])
                P.tt('dve', xo[j][:, cg * 512:(cg + 1) * 512], pd[:, :], xt[i][:, cg * 512:(cg + 1) * 512], ALU.add,
                     R=[bpd, bxt[i]], W=[bxo[j]])
            P.store(xout[tok0:tok0 + 128, :], xo[j], bxo[j])
    ar.release(m0)
    P.barrier()
    P.retire(nb0)


ARENA = 206 * 1024
NSEM_POOL = 96


class SemPool:
    def __init__(self, nc, stack, n):
        self.sems = [stack.enter_context(nc.semaphore(f"sm{i}")) for i in range(n)]
        self.i = 0

        self.free = []

    def get(self):
        if self.free:
            return self.free.pop()
        s_ = self.sems[self.i]
        self.i += 1
        return (s_, 0)

    def put(self, sem, cnt):
        self.free.append((sem, cnt))


def build(n_layers=2, phases=None, dbg=False):
    nc = bass.Bass("TRN2", target_bir_lowering=False)
    D = {}
    x = nc.dram_tensor("x", [S, DM], F32, kind="ExternalInput").ap()
    for name, shp in PARAM_SHAPES.items():
        D[name] = nc.dram_tensor(name, shp, F32, kind="ExternalInput").ap()
    y = nc.dram_tensor("y", [S, DM], F32, kind="ExternalOutput").ap()
    sk = "ExternalOutput" if dbg else "Internal"
    D['hT_d'] = nc.dram_tensor("hT_d", [DM, S], BF16, kind=sk).ap()
    D['QKT_d'] = nc.dram_tensor("QKT_d", [NQKB * 128, S], BF16, kind=sk).ap()
    D['Vnat_d'] = nc.dram_tensor("Vnat_d", [S, 780], BF16, kind=sk).ap()
    D['Vb_d'] = nc.dram_tensor("Vb_d", [2, S, 130], BF16, kind=sk).ap()
    D['YT_d'] = nc.dram_tensor("YT_d", [1152, S], BF16, kind=sk).ap()
    x1_d = nc.dram_tensor("x1_d", [S, DM], F32, kind=sk).ap()
    x2_d = nc.dram_tensor("x2_d", [S, DM], F32, kind=sk).ap()
    with ExitStack() as stack:
        arena_t = stack.enter_context(nc.sbuf_tensor("arena", [128, ARENA], U8))
        pbs = [stack.enter_context(nc.psum_tensor(f"pb{i}", [128, 512], F32)) for i in range(8)]
        sp_ = SemPool(nc, stack, NSEM_POOL)

        class _St:
            def enter_context(self, cm):
                raise RuntimeError

        P = Prog.__new__(Prog)
        P.nc = nc
        P.ops = {e: [] for e in ENGS}
        P.bufs = []
        P.esem = {e: sp_.get()[0] for e in ENGS}
        P.nsem = len(ENGS)

        def dma(eng, out, in_, owner, R=(), W=()):
            if owner.sem is None:
                owner.sem, owner.cnt = sp_.get()
            owner.cnt += 16
            o = Op(eng, lambda e: e.dma_start(out=out, in_=in_))
            o.dma_sem = owner.sem
            P._track(o, ('dma', owner.sem, owner.cnt), 'dma', R, W)
            P.ops[eng].append(o)
            return o
        P.dma = dma

        def retire(nb0):
            for b in P.bufs[nb0:]:
                if b.sem is not None:
                    sp_.put(b.sem, b.cnt)
                    b.sem = None
            del P.bufs[nb0:]
        P.retire = retire
        block = stack.enter_context(nc.Block())
        C = Ctx()
        C.P, C.D = P, D
        C.ar = Arena(arena_t, ARENA)
        C.pb = [p[:, :] for p in pbs]
        C.bpb = P.bufs_n('pb', 8)
        C.cbf = C.ar.alloc([NCONST], BF16)
        C.b_const = P.buf('consts')
        P.dma('pool', C.cbf, D['consts'], C.b_const, W=(C.b_const,))
        C.ident = C.cbf[:, C_IDENT:C_IDENT + 128]
        C.bd64 = C.cbf[:, C_BD64:C_BD64 + 128]
        C.bd32 = C.cbf[:, C_BD32:C_BD32 + 128]
        C.psw64 = C.cbf[:, C_PSW64:C_PSW64 + 128]
        C.psw32 = C.cbf[:, C_PSW32:C_PSW32 + 128]
        C.ones_bf = C.cbf[:, C_ONES:C_ONES + 128]
        all_ph = ['p1', 'a', 'b', 'c', 'd', 'p3a', 'p3b']
        phases = phases or all_ph
        for l in range(n_layers):
            xin = x if l == 0 else x2_d
            xfin = y if l == n_layers - 1 else x2_d
            if 'p1' in phases:
                phase_p1(C, l, xin)
            if 'a' in phases:
                mixer_a(C, l)
            if 'b' in phases:
                mixer_b(C, l)
            if 'c' in phases:
                mixer_c(C, l)
            if 'd' in phases:
                mixer_d(C, l)
            if 'p3a' in phases:
                phase_p3a(C, l, xin, x1_d)
            if 'p3b' in phases:
                phase_p3b(C, l, x1_d, xfin)
        P.barrier()
        P.emit(block)
        C.nsem = sp_.i
    return nc, C


_CACHE = {}


def kernel(**inputs):
    x = np.ascontiguousarray(np.asarray(inputs['x'], dtype=np.float32))
    params = host_prep(inputs)
    if 'nc' not in _CACHE:
        _CACHE['nc'] = build()[0]
    nc = _CACHE['nc']
    in_maps = []
    for b in range(8):
        m = {'x': x[b]}
        m.update(params)
        in_maps.append(m)
    res = run_bass_kernel_spmd(nc, in_maps, core_ids=list(range(8)))
    return np.stack([np.asarray(r['y'], dtype=np.float32) for r in res.results], axis=0)
```

```python
import math
from contextlib import ExitStack
import numpy as np
import concourse.bass as bass
import concourse.mybir as mybir
from concourse.bass_utils import run_bass_kernel_spmd

F32 = mybir.dt.float32
BF16 = mybir.dt.bfloat16
U8 = mybir.dt.uint8
AF = mybir.ActivationFunctionType
ALU = mybir.AluOpType

S = 4096
DM = 1024
NT = 32
EPS = 1e-6
INC = 7552
ENGS = ['pe', 'act', 'dve', 'pool', 'sp']
SAME_ENG_SYNC = ('act', 'dve', 'pool')


class Buf:
    __slots__ = ('name', 'w', 'rs', 'sem', 'cnt')

    def __init__(self, name):
        self.name = name
        self.w = None
        self.rs = {}
        self.sem = None
        self.cnt = 0


class Op:
    __slots__ = ('eng', 'fn', 'deps', 'dwaits', 'needs_inc', 'semval', 'dma_sem')

    def __init__(self, eng, fn):
        self.eng = eng
        self.fn = fn
        self.deps = set()
        self.dwaits = {}
        self.needs_inc = False
        self.semval = 0
        self.dma_sem = None


class Prog:
    def __init__(self, nc, stack):
        self.nc = nc
        self.stack = stack
        self.ops = {e: [] for e in ENGS}
        self.bufs = []
        self.esem = {e: stack.enter_context(nc.semaphore("s_" + e)) for e in ENGS}
        self.nsem = len(ENGS)

    def buf(self, name):
        b = Buf(name)
        self.bufs.append(b)
        return b

    def bufs_n(self, name, n):
        return [self.buf(f"{name}{i}") for i in range(n)]

    def _add_ev(self, o, ev):
        if ev is None:
            return
        if ev[0] == 'op':
            d = ev[1]
            if d.eng == o.eng and d.eng not in SAME_ENG_SYNC:
                return
            o.deps.add(d)
            d.needs_inc = True
        else:
            _, sem, val = ev
            cur = o.dwaits.get(id(sem))
            if cur is None or cur[1] < val:
                o.dwaits[id(sem)] = (sem, val)

    def _track(self, o, ev, key, R, W):
        for b in R:
            self._add_ev(o, b.w)
        for b in W:
            self._add_ev(o, b.w)
            for e2 in b.rs.values():
                self._add_ev(o, e2)
        for b in R:
            b.rs[key] = ev
        for b in W:
            b.w = ev
            b.rs = {}

    def op(self, eng, fn, R=(), W=()):
        o = Op(eng, fn)
        self._track(o, ('op', o), eng, R, W)
        self.ops[eng].append(o)
        return o

    def dma(self, eng, out, in_, owner, R=(), W=()):
        if owner.sem is None:
            owner.sem = self.stack.enter_context(self.nc.semaphore("d_" + owner.name))
            self.nsem += 1
        owner.cnt += 16
        o = Op(eng, lambda e: e.dma_start(out=out, in_=in_))
        o.dma_sem = owner.sem
        self._track(o, ('dma', owner.sem, owner.cnt), 'dma', R, W)
        self.ops[eng].append(o)
        return o

    def load(self, out, in_, owner, eng='sp'):
        return self.dma(eng, out, in_, owner, R=(), W=(owner,))

    def store(self, out, in_, owner, eng='sp'):
        return self.dma(eng, out, in_, owner, R=(owner,), W=())

    def barrier(self):
        o = Op('sp', lambda e: e.nop())
        for E in ENGS:
            if self.ops[E]:
                last = None
                for c in reversed(self.ops[E]):
                    if c.fn is not None and c.dma_sem is None:
                        last = c
                        break
                if last is not None and E != 'sp':
                    o.deps.add(last)
                    last.needs_inc = True
        for b in self.bufs:
            if b.sem is not None and b.cnt > 0:
                o.dwaits[id(b.sem)] = (b.sem, b.cnt)
        o.needs_inc = True
        self.ops['sp'].append(o)
        for E in ENGS:
            if E != 'sp':
                w = Op(E, None)
                w.deps.add(o)
                self.ops[E].append(w)
        for b in self.bufs:
            b.w = None
            b.rs = {}

    def mm(self, out, lhsT, rhs, start=True, stop=True, R=(), W=()):
        return self.op('pe', lambda e: e.matmul(out, lhsT, rhs, start=start, stop=stop), R, W)

    def tr(self, out, in_, ident, R=(), W=()):
        return self.op('pe', lambda e: e.transpose(out, in_, ident), R, W)

    def act(self, out, in_, func, R=(), W=(), **kw):
        return self.op('act', lambda e: e.activation(out=out, in_=in_, func=func, **kw), R, W)

    def tt(self, eng, out, in0, in1, op, R=(), W=()):
        return self.op(eng, lambda e: e.tensor_tensor(out=out, in0=in0, in1=in1, op=op), R, W)

    def ts(self, eng, out, in0, s1, s2, op0, op1=None, R=(), W=()):
        if op1 is None:
            return self.op(eng, lambda e: e.tensor_scalar(out=out, in0=in0, scalar1=s1, scalar2=None, op0=op0), R, W)
        return self.op(eng, lambda e: e.tensor_scalar(out=out, in0=in0, scalar1=s1, scalar2=s2, op0=op0, op1=op1), R, W)

    def stt(self, out, in0, scalar, in1, op0, op1, R=(), W=()):
        return self.op('dve', lambda e: e.scalar_tensor_tensor(out=out, in0=in0, scalar=scalar, in1=in1, op0=op0, op1=op1), R, W)

    def cp(self, eng, out, in_, R=(), W=()):
        if eng == 'act':
            return self.op('act', lambda e: e.copy(out=out, in_=in_), R, W)
        return self.op(eng, lambda e: e.tensor_copy(out=out, in_=in_), R, W)

    def recip(self, out, in_, R=(), W=()):
        return self.op('dve', lambda e: e.reciprocal(out=out, in_=in_), R, W)

    def memset(self, eng, ap, val, W=()):
        return self.op(eng, lambda e: e.memset(ap, val), (), W)

    def emit(self, block):
        for E in ENGS:
            c = 0
            for o in self.ops[E]:
                if o.needs_inc and o.dma_sem is None:
                    c += 1
                    o.semval = c
        esem = self.esem

        def run(E, eng):
            waited = {}
            for o in self.ops[E]:
                needs = []
                for d in o.deps:
                    needs.append((esem[d.eng], d.semval))
                for sem, val in o.dwaits.values():
                    needs.append((sem, val))
                for sem, val in needs:
                    k = id(sem)
                    if waited.get(k, 0) < val:
                        eng.wait_ge(sem, val)
                        waited[k] = val
                if o.fn is not None:
                    ins = o.fn(eng)
                    if o.dma_sem is not None:
                        ins.then_inc(o.dma_sem, 16)
                    elif o.needs_inc:
                        ins.then_inc(esem[E], 1)

        @block.tensor
        def _(e):
            run('pe', e)

        @block.scalar
        def _(e):
            run('act', e)

        @block.vector
        def _(e):
            run('dve', e)

        @block.gpsimd
        def _(e):
            run('pool', e)

        @block.sync
        def _(e):
            run('sp', e)


class Arena:
    def __init__(self, t, size):
        self.t = t
        self.size = size
        self.off = 0

    def alloc(self, free_shape, dtype):
        es = 4 if dtype == F32 else (2 if dtype == BF16 else 1)
        n = 1
        for s_ in free_shape:
            n *= s_
        nb = (n * es + 31) // 32 * 32
        assert self.off + nb <= self.size, f"arena overflow {self.off}+{nb}>{self.size}"
        ap = self.t[:, self.off:self.off + n * es].bitcast(dtype)
        self.off += nb
        if len(free_shape) == 2:
            ap = ap.rearrange("p (a b) -> p a b", a=free_shape[0])
        elif len(free_shape) == 3:
            ap = ap.rearrange("p (a b c) -> p a b c", a=free_shape[0], b=free_shape[1])
        return ap

    def mark(self):
        return self.off

    def release(self, m):
        self.off = m


QK_BLOCKS = []
for i in range(4):
    QK_BLOCKS.append((i * 128, 'n64'))
QK_BLOCKS.append((512, 'n64'))
for g, kind in enumerate(['n64', 'p4', 'p16']):
    QK_BLOCKS.append((768 + g * 128, kind))
for g, kind in enumerate(['n64', 'p4', 'p16']):
    QK_BLOCKS.append((1152 + g * 128, kind))
for i in range(2):
    QK_BLOCKS.append((1920 + i * 128, 'n32'))
for i in range(2):
    QK_BLOCKS.append((2176 + i * 128, 'n32'))
for i in range(2):
    QK_BLOCKS.append((2688 + i * 128, 'd'))
for i in range(2):
    QK_BLOCKS.append((2944 + i * 128, 'd'))
NQKB = len(QK_BLOCKS)
VNAT_COLS = [(640, 128), (2432, 256), (3200, 256), (1536, 128)]
GATE_OFF = 3456
B_DIL = [1, 4, 16]


def perm_tokens(d):
    L = S // d
    j = np.arange(S)
    return (j % L) * d + (j // L)


def rope_tabs(dim):
    half = dim // 2
    inv = np.power(np.float32(10000.0), -(np.arange(0, dim, 2, dtype=np.float32) / np.float32(dim))).astype(np.float32)
    ang = (np.arange(S, dtype=np.float32)[:, None] * inv[None, :]).astype(np.float32)
    c = np.cos(ang).astype(np.float32)
    s_ = np.sin(ang).astype(np.float32)
    p = np.arange(128) % dim
    cosT = c[:, p % half].T.copy()
    sgn = np.where(p < half, -1.0, 1.0).astype(np.float32)
    sinT = (s_[:, p % half] * sgn[None, :]).T.copy()
    return cosT, sinT


def d_tables():
    rows = 64
    r0 = np.clip(np.arange(rows) - 4, 0, rows - 8)
    cj = np.arange(64)
    c0 = np.clip(cj - 8, 0, 48)
    col_ok = (cj[None, :] >= c0[:, None]) & (cj[None, :] < c0[:, None] + 16)
    dc = np.clip(cj[None, :] - cj[:, None], -15, 15) + 15
    tabs = {}
    tab_list = []
    pairs = []
    for n in range(32):
        lo = r0[2 * n] // 2
        hi = (r0[2 * n + 1] + 7) // 2
        pl = []
        for m in range(lo, hi + 1):
            valid = np.zeros((128, 128), dtype=bool)
            dr = np.zeros((128, 128), dtype=np.int64)
            for a in range(2):
                for b in range(2):
                    rho = 2 * m + a
                    i = 2 * n + b
                    ok = (r0[i] <= rho) and (rho <= r0[i] + 7)
                    if ok:
                        valid[a * 64:(a + 1) * 64, b * 64:(b + 1) * 64] = col_ok.T
                        dr[a * 64:(a + 1) * 64, b * 64:(b + 1) * 64] = rho - i + 7
            key = (m - n, valid.tobytes())
            if key not in tabs:
                tabs[key] = len(tab_list)
                tab_list.append((dr, valid))
            pl.append((m, tabs[key]))
        pairs.append(pl)
    dcidx = np.zeros((128, 128), dtype=np.int64)
    for a in range(2):
        for b in range(2):
            dcidx[a * 64:(a + 1) * 64, b * 64:(b + 1) * 64] = dc.T
    return tab_list, pairs, dcidx


D_TABS, D_PAIRS, D_DC = d_tables()
NTAB = len(D_TABS)

C_IDENT = 0
C_BD64 = 128
C_BD32 = 256
C_PSW64 = 384
C_PSW32 = 512
C_ONES = 640
C_MLO = 768
C_MHI = 896
C_NLO4 = 1024
C_NHI4 = 1536
C_DVALID = 2048
NCONST = C_DVALID + NTAB * 128


def make_consts():
    c = np.zeros((128, NCONST), dtype=np.float32)
    p = np.arange(128)
    c[:, C_IDENT:C_IDENT + 128] = np.eye(128, dtype=np.float32)
    c[:, C_BD64:C_BD64 + 128] = (p[:, None] // 64 == p[None, :] // 64)
    c[:, C_BD32:C_BD32 + 128] = (p[:, None] // 32 == p[None, :] // 32)
    part64 = (p // 64) * 64 + (p % 64 + 32) % 64
    part32 = (p // 32) * 32 + (p % 32 + 16) % 32
    c[:, C_PSW64:C_PSW64 + 128] = (p[:, None] == part64[None, :])
    c[:, C_PSW32:C_PSW32 + 128] = (p[:, None] == part32[None, :])
    c[:, C_ONES:C_ONES + 128] = 1.0
    c[:, C_MLO:C_MLO + 128] = (p[:, None] >= p[None, :])
    c[:, C_MHI:C_MHI + 128] = (p[:, None] <= p[None, :])
    for r_ in range(4):
        c[:, C_NLO4 + r_ * 128:C_NLO4 + (r_ + 1) * 128] = (c[:, C_MLO:C_MLO + 128] - 1.0) * 30000.0
        c[:, C_NHI4 + r_ * 128:C_NHI4 + (r_ + 1) * 128] = (c[:, C_MHI:C_MHI + 128] - 1.0) * 30000.0
    for t, (dr, valid) in enumerate(D_TABS):
        c[:, C_DVALID + t * 128:C_DVALID + (t + 1) * 128] = valid
    return c


def host_prep(inp):
    f = lambda a: np.ascontiguousarray(np.asarray(a, dtype=np.float32))
    out = {}
    out['w_in'] = f(inp['w_in'])
    out['w_ba'] = f(inp['w_branch_a'])
    out['w_bb'] = f(inp['w_branch_b'])
    out['w_bc'] = f(inp['w_branch_c'])
    out['w_bd'] = f(inp['w_branch_d'])
    out['w_out'] = f(inp['w_out'])
    out['w_up'] = f(inp['w_up'])
    out['w_down'] = f(inp['w_down'])
    out['gb_attn'] = f(np.broadcast_to(f(inp['attn_norm_g'])[:, None, :], (2, 128, DM)))
    out['gb_mlp'] = f(np.broadcast_to(f(inp['mlp_norm_g'])[:, None, :], (2, 128, DM)))
    gcol = np.zeros((2, 128, NQKB), dtype=np.float32)
    aq, bq, cq, dq = f(inp['a_qk_norm_g']), f(inp['b_qk_norm_g']), f(inp['c_qk_norm_g']), f(inp['d_qk_norm_g'])
    for l in range(2):
        for b in range(4):
            gcol[l, :, b] = np.tile(aq[l, 0], 2)
        gcol[l, :, 4] = np.tile(aq[l, 1], 2)
        for b in range(5, 8):
            gcol[l, :, b] = np.tile(bq[l, 0], 2)
        for b in range(8, 11):
            gcol[l, :, b] = np.tile(bq[l, 1], 2)
        for b in range(11, 13):
            gcol[l, :, b] = np.tile(cq[l, 0], 4)
        for b in range(13, 15):
            gcol[l, :, b] = np.tile(cq[l, 1], 4)
        for b in range(15, 17):
            gcol[l, :, b] = np.tile(dq[l, 0], 2)
        for b in range(17, 19):
            gcol[l, :, b] = np.tile(dq[l, 1], 2)
    out['gcol'] = gcol
    out['sinkb'] = f(np.broadcast_to(f(inp['a_sink'])[:, None, :], (2, 128, 8)))
    out['lamb'] = f(np.broadcast_to(f(inp['c_lambda']).reshape(2, 1, 128), (2, 128, 128)))
    out['gsub'] = f(np.tile(f(inp['c_subln_g']), (1, 2)).reshape(2, 128, 1))
    rpb = f(inp['d_rel_bias'])
    db = np.zeros((2, 4, 128, NTAB, 128), dtype=np.float32)
    for t, (dr, valid) in enumerate(D_TABS):
        db[:, :, :, t, :] = rpb[:, :, dr, D_DC]
    out['dbias'] = db.reshape(2, 4, 128, NTAB * 128)
    c64, s64 = rope_tabs(64)
    c32, s32 = rope_tabs(32)
    p4, p16 = perm_tokens(4), perm_tokens(16)
    out['rope'] = np.ascontiguousarray(np.stack([c64, s64, c64[:, p4], s64[:, p4], c64[:, p16], s64[:, p16], c32, s32], 0))
    out['consts'] = make_consts()
    return out


PARAM_SHAPES = {
    'w_in': [2, DM, INC], 'w_ba': [2, 512, DM], 'w_bb': [2, 128, DM], 'w_bc': [2, 256, DM], 'w_bd': [2, 256, DM],
    'w_out': [2, DM, DM], 'w_up': [2, DM, 4096], 'w_down': [2, 4096, DM],
    'gb_attn': [2, 128, DM], 'gb_mlp': [2, 128, DM], 'gcol': [2, 128, NQKB], 'sinkb': [2, 128, 8],
    'lamb': [2, 128, 128], 'gsub': [2, 128, 1], 'dbias': [2, 4, 128, NTAB * 128],
    'rope': [8, 128, S], 'consts': [128, NCONST],
}


class Ctx:
    pass


def sl(start, n, step):
    return slice(start, start + (n - 1) * step + 1, step)


def ring(lst, i):
    return lst[i % len(lst)]


def phase_p1(C, l, xin):
    P, ar, D = C.P, C.ar, C.D
    nb0 = len(P.bufs)
    m0 = ar.mark()
    hT = ar.alloc([8, S], BF16)
    b_hT = P.buf('hT')
    gb = ar.alloc([DM], F32)
    b_gb = P.buf('gb')
    gcol = ar.alloc([NQKB], F32)
    b_gcol = P.buf('gcol')
    P.load(gb, D['gb_attn'][l], b_gb)
    P.load(gcol, D['gcol'][l], b_gcol)
    m1 = ar.mark()
    xt = [ar.alloc([DM], F32) for _ in range(3)]
    bx = P.bufs_n('xt', 3)
    junk = ar.alloc([DM], BF16)
    b_junk = P.buf('junk')
    ss = [ar.alloc([1], F32) for _ in range(4)]
    bss = P.bufs_n('ss', 4)
    hb = [ar.alloc([DM], BF16) for _ in range(4)]
    bhb = P.bufs_n('hb', 4)
    for tt in range(NT):
        i, j = tt % 3, tt % 4
        P.load(xt[i], xin[tt * 128:(tt + 1) * 128, :], bx[i])
        P.act(junk, xt[i], AF.Square, R=[bx[i]], W=[b_junk, bss[j]], accum_out=ss[j])
        P.act(ss[j], ss[j], AF.Ln, R=[bss[j]], W=[bss[j]], scale=1.0 / DM, bias=EPS)
        P.act(ss[j], ss[j], AF.Exp, R=[bss[j]], W=[bss[j]], scale=-0.5)
        P.stt(hb[j], xt[i], ss[j], gb, ALU.mult, ALU.mult, R=[bx[i], bss[j], b_gb], W=[bhb[j]])
        pbv = C.pb[j].bitcast(BF16)
        for kc in range(8):
            P.tr(pbv[:, kc * 128:(kc + 1) * 128], hb[j][:, kc * 128:(kc + 1) * 128], C.ident, R=[bhb[j], C.b_const], W=[C.bpb[j]])
        P.cp('act' if tt % 2 else 'dve', hT[:, :, tt * 128:(tt + 1) * 128], pbv.rearrange("p (k t) -> p k t", k=8), R=[C.bpb[j]], W=[b_hT])
    for kc in range(8):
        P.store(D['hT_d'][kc * 128:(kc + 1) * 128, :], hT[:, kc, :], b_hT)
    ar.release(m1)
    tabs2 = [ar.alloc([2, S], F32) for _ in range(2)]
    b_tabs2 = P.bufs_n('ropetab', 2)
    wq = [ar.alloc([8, 128], BF16) for _ in range(2)]
    bwq = P.bufs_n('wq', 2)
    NB = 4
    sq = [ar.alloc([512], BF16) for _ in range(NB)]
    bsq = P.bufs_n('sq', NB)
    xg = [ar.alloc([512], BF16) for _ in range(NB)]
    bxg = P.bufs_n('xg', NB)
    rs = [ar.alloc([512], F32) for _ in range(NB)]
    brs = P.bufs_n('rs', NB)
    ta = [ar.alloc([512], F32) for _ in range(NB)]
    bta = P.bufs_n('ta', NB)
    tb = [ar.alloc([512], F32) for _ in range(NB)]
    btb = P.bufs_n('tb', NB)
    ob = [ar.alloc([512], BF16) for _ in range(NB)]
    bob = P.bufs_n('ob', NB)
    w_in = D['w_in'][l].rearrange("(kc p) c -> p kc c", p=128)
    blk_order = [0, 1, 2, 3, 4, 5, 8, 6, 9, 7, 10, 11, 12, 13, 14, 15, 16, 17, 18]
    items = [(blk, tc) for blk in blk_order for tc in range(8)]
    TABK = {'n64': 0, 'p4': 2, 'p16': 4, 'n32': 6, 'd': None}
    variants = [0, 2, 4, 6]
    state = {'vi': -1}

    def load_tab(vi):
        if vi < len(variants):
            tb_, bt_ = tabs2[vi % 2], b_tabs2[vi % 2]
            P.load(tb_[:, 0, :], D['rope'][variants[vi]], bt_)
            P.load(tb_[:, 1, :], D['rope'][variants[vi] + 1], bt_)
    load_tab(0)

    def tokf(kind, tc):
        if kind == 'p4':
            r, h0 = tc // 2, (tc % 2) * 512
            return lambda kc: hT[:, kc, sl(r + 4 * h0, 512, 4)]
        if kind == 'p16':
            return lambda kc: hT[:, kc, :].rearrange("p (m r) -> p r m", r=16)[:, 2 * tc:2 * tc + 2, :]
        return lambda kc: hT[:, kc, tc * 512:(tc + 1) * 512]

    def s0(itm, t):
        blk, tc = itm
        coff, kind = QK_BLOCKS[blk]
        wi = blk % 2
        if tc == 0:
            P.dma('pool', wq[wi], w_in[:, :, coff:coff + 128], bwq[wi], W=(bwq[wi],))
        tok = tokf(kind, tc)
        pA, bA = C.pb[t % 3], C.bpb[t % 3]
        for kc in range(8):
            P.mm(pA[:, :], wq[wi][:, kc, :], tok(kc), start=(kc == 0), stop=(kc == 7), R=[bwq[wi], b_hT], W=[bA])

    def s1(itm, t, defer):
        blk, tc = itm
        coff, kind = QK_BLOCKS[blk]
        tabkind = TABK[kind]
        if tabkind is not None:
            vi = variants.index(tabkind)
            if vi != state['vi']:
                state['vi'] = vi
                load_tab(vi + 1)
            tabs, b_tabs = tabs2[vi % 2], b_tabs2[vi % 2]
        dh = 32 if kind == 'n32' else 64
        bd = C.bd32 if dh == 32 else C.bd64
        psw = C.psw32 if dh == 32 else C.psw64
        k = t % NB
        pA, pS, pR = C.pb[t % 3], C.pb[3 + (t % 2)], C.pb[5 + (t % 3)]
        bA, bS, bR = C.bpb[t % 3], C.bpb[3 + (t % 2)], C.bpb[5 + (t % 3)]
        P.act(sq[k], pA[:, :], AF.Square, R=[bA], W=[bsq[k]])
        P.act(xg[k], pA[:, :], AF.Copy, R=[bA, b_gcol], W=[bxg[k]], scale=gcol[:, blk:blk + 1])
        P.mm(pS[:, :], bd, sq[k], R=[bsq[k], C.b_const], W=[bS])
        if kind != 'd':
            P.mm(pR[:, :], psw, xg[k], R=[bxg[k], C.b_const], W=[bR])
        csl = slice(tc * 512, (tc + 1) * 512)
        if kind != 'd':
            P.tt('pool', ta[k], xg[k], tabs[:, 0, csl], ALU.mult, R=[bxg[k], b_tabs], W=[bta[k]])
            P.tt('dve', tb[k], pR[:, :], tabs[:, 1, csl], ALU.mult, R=[bR, b_tabs], W=[btb[k]])

        def s1b():
            P.act(rs[k], pS[:, :], AF.Ln, R=[bS], W=[brs[k]], scale=1.0 / dh, bias=EPS)
            P.act(rs[k], rs[k], AF.Exp, R=[brs[k]], W=[brs[k]], scale=-0.5)
            if kind == 'd':
                P.tt('dve', ob[k], xg[k], rs[k], ALU.mult, R=[bxg[k], brs[k]], W=[bob[k]])
            else:
                P.tt('pool', ta[k], ta[k], tb[k], ALU.add, R=[bta[k], btb[k]], W=[bta[k]])
                P.tt('dve', ob[k], ta[k], rs[k], ALU.mult, R=[bta[k], brs[k]], W=[bob[k]])
            P.store(D['QKT_d'][blk * 128:(blk + 1) * 128, csl], ob[k], bob[k])
        defer(1, s1b)

    run_pipeline(items, 2, s0, s1)
    ar.release(m1)
    wv = ar.alloc([8, 1024], BF16)
    b_wv = P.buf('wv')
    o = 0
    for (coff, n) in VNAT_COLS + [(1536 + 128, 256)]:
        P.dma('pool', wv[:, :, o:o + n], w_in[:, :, coff:coff + n], b_wv, W=(b_wv,))
        o += n
    vn = [ar.alloc([12, 65], BF16) for _ in range(2)]
    bvn = P.bufs_n('vn', 2)
    vb = [ar.alloc([2, 2, 65], BF16) for _ in range(2)]
    bvb = P.bufs_n('vbp', 2)
    for j in range(2):
        P.memset('pool', vn[j][:, :, 64:65], 1.0, W=[bvn[j]])
        P.memset('pool', vb[j][:, :, :, 64:65], 1.0, W=[bvb[j]])
    p4, p16 = perm_tokens(4), perm_tokens(16)
    for tt in range(NT):
        j = tt % 2
        pa, pbk, pc = C.pb[j * 3], C.pb[j * 3 + 1], C.pb[j * 3 + 2]
        ba, bb_, bc = C.bpb[j * 3], C.bpb[j * 3 + 1], C.bpb[j * 3 + 2]
        for kc in range(8):
            P.mm(pa[:, :], hT[:, kc, tt * 128:(tt + 1) * 128], wv[:, kc, 0:512], start=(kc == 0), stop=(kc == 7), R=[b_hT, b_wv], W=[ba])
        for kc in range(8):
            P.mm(pbk[:, 0:256], hT[:, kc, tt * 128:(tt + 1) * 128], wv[:, kc, 512:768], start=(kc == 0), stop=(kc == 7), R=[b_hT, b_wv], W=[bb_])
        t4 = int(p4[tt * 128])
        t16 = int(p16[tt * 128])
        for kc in range(8):
            P.mm(pc[:, 0:128], hT[:, kc, sl(t4, 128, 4)], wv[:, kc, 768:896], start=(kc == 0), stop=(kc == 7), R=[b_hT, b_wv], W=[bc])
        for kc in range(8):
            P.mm(pc[:, 128:256], hT[:, kc, sl(t16, 128, 16)], wv[:, kc, 896:1024], start=(kc == 0), stop=(kc == 7), R=[b_hT, b_wv], W=[bc])
        P.cp('act', vn[j][:, 0:8, 0:64], pa[:, :].rearrange("p (h d) -> p h d", h=8), R=[ba], W=[bvn[j]])
        P.cp('dve', vn[j][:, 8:12, 0:64], pbk[:, 0:256].rearrange("p (h d) -> p h d", h=4), R=[bb_], W=[bvn[j]])
        P.cp('dve', vb[j][:, :, :, 0:64], pc[:, 0:256].rearrange("p (g h d) -> p g h d", g=2, h=2), R=[bc], W=[bvb[j]])
        P.store(D['Vnat_d'][tt * 128:(tt + 1) * 128, :], vn[j].rearrange("p h d -> p (h d)"), bvn[j])
        for g in range(2):
            P.store(D['Vb_d'][g, tt * 128:(tt + 1) * 128, :], vb[j][:, g].rearrange("p h d -> p (h d)"), bvb[j])
    ar.release(m0)
    P.barrier()
    P.retire(nb0)


def run_pipeline(items, LA, s0, s1):
    n = len(items)
    pend = {}
    cur = [0]

    def defer(delay, fn):
        pend.setdefault(cur[0] + delay, []).append(fn)

    for t in range(n + LA):
        cur[0] = t
        if t < n:
            s0(items[t], t)
        if t >= LA:
            s1(items[t - LA], t - LA, defer)
        for fn in pend.pop(t, []):
            fn()
    while pend:
        t2 = min(pend)
        cur[0] = t2
        for fn in pend.pop(t2):
            fn()


def finalize_norm(C, acc, bacc, n, dst, bdst, shape3=None, esink=None, tagk=0, defer=None, after=None, dl=(1, 3, 5, 6)):
    P = C.P
    k = tagk % 2
    r, rhi, rlo = C.fr[k], C.frhi[k], C.frlo[k]
    br = C.bfr[k]
    bc, bbc = C.pb[7], C.bpb[7]
    bcs, bbcs = C.fbcs[k], C.bfbcs[k]
    src = acc[64:65, 0:n]

    def g0():
        if esink is not None:
            j, q = shape3
            P.tt('dve', r[64:65, 0:n].rearrange("p (j q) -> p j q", j=j), src.rearrange("p (j q) -> p j q", j=j),
                 esink, ALU.add, R=[bacc, C.b_esk], W=[br])
            P.act(r[64:65, 0:n], r[64:65, 0:n], AF.Ln, R=[br], W=[br])
        else:
            P.act(r[64:65, 0:n], src, AF.Ln, R=[bacc], W=[br])
        P.act(r[64:65, 0:n], r[64:65, 0:n], AF.Exp, R=[br], W=[br], scale=-1.0)
        P.cp('dve', rhi[64:65, 0:n], r[64:65, 0:n], R=[br], W=[br])
        P.tt('dve', rlo[64:65, 0:n], r[64:65, 0:n], rhi[64:65, 0:n], ALU.subtract, R=[br], W=[br])

    def g1():
        P.mm(bc[0:64, 0:n], C.ones_bf[64:65, 0:64], rhi[64:65, 0:n], start=True, stop=False, R=[br, C.b_const], W=[bbc])
        P.mm(bc[0:64, 0:n], C.ones_bf[64:65, 0:64], rlo[64:65, 0:n], start=False, stop=True, R=[br, C.b_const], W=[bbc])

    def g2():
        P.cp('act', bcs[0:64, 0:n], bc[0:64, 0:n], R=[bbc], W=[bbcs])

    def g3():
        a0 = acc[0:64, 0:n]
        b0 = bcs[0:64, 0:n]
        if shape3 is not None:
            j, q = shape3
            a0 = a0.rearrange("p (j q) -> p j q", j=j)
            b0 = b0.rearrange("p (j q) -> p j q", j=j)
        P.tt('dve', dst, a0, b0, ALU.mult, R=[bacc, bbcs], W=[bdst])
        if after is not None:
            after()

    if defer is None:
        g0(); g1(); g2(); g3()
    else:
        defer(dl[0], g0); defer(dl[1], g1); defer(dl[2], g2); defer(dl[3], g3)


def alloc_fin(C):
    ar, P = C.ar, C.P
    C.fr = [ar.alloc([512], F32) for _ in range(2)]
    C.frhi = [ar.alloc([512], BF16) for _ in range(2)]
    C.frlo = [ar.alloc([512], BF16) for _ in range(2)]
    C.bfr = P.bufs_n('fr', 2)
    C.fbcs = [ar.alloc([512], F32) for _ in range(2)]
    C.bfbcs = P.bufs_n('fbcs', 2)


def mixer_a(C, l):
    P, ar, D = C.P, C.ar, C.D
    nb0 = len(P.bufs)
    m0 = ar.mark()
    alloc_fin(C)
    QT = ar.alloc([4, S], BF16)
    bQT = P.buf('aQT')
    KT = ar.alloc([2, S], BF16)
    bKT = P.buf('aKT')
    V = ar.alloc([NT, 130], BF16)
    bV = P.buf('aV')
    yst = ar.alloc([4, S], BF16)
    byst = P.buf('ayst')
    esk = ar.alloc([8], F32)
    C.b_esk = P.buf('esk')
    P.load(esk, D['sinkb'][l], C.b_esk)
    P.act(esk, esk, AF.Exp, R=[C.b_esk], W=[C.b_esk])
    for g in range(2):
        for j in range(4):
            h = 4 * g + j
            P.load(QT[g * 64:(g + 1) * 64, j, :], D['QKT_d'][h * 64:(h + 1) * 64, :], bQT)
    P.memset('pool', KT, 0.0, W=[bKT])
    for g in range(2):
        P.load(KT[g * 64:(g + 1) * 64, g, :], D['QKT_d'][512 + g * 64:512 + (g + 1) * 64, :], bKT)
    P.load(V, D['Vnat_d'].rearrange("(t p) c -> p t c", p=128)[:, :, 0:130], bV)
    NP = 4
    pt = [ar.alloc([512], BF16) for _ in range(NP)]
    bpt = P.bufs_n('apt', NP)
    mlo = C.cbf[:, C_MLO:C_MLO + 128].unsqueeze(1).broadcast_to([128, 4, 128])
    mhi = C.cbf[:, C_MHI:C_MHI + 128].unsqueeze(1).broadcast_to([128, 4, 128])
    items = []
    fi = 0
    for g in range(2):
        for n in range(NT):
            ms = [m for m in (n - 1, n, n + 1) if 0 <= m < NT]
            for idx, m in enumerate(ms):
                items.append((g, n, m, idx == 0, idx == len(ms) - 1, fi))
            fi += 1

    def s0(itm, t):
        g, n, m, first, last, f = itm
        ps = slice(g * 64, (g + 1) * 64)
        st, bst = C.pb[t % 4], C.bpb[t % 4]
        P.mm(st[:, :].rearrange("p (j q) -> p j q", j=4), KT[:, g, m * 128:(m + 1) * 128], QT[:, :, n * 128:(n + 1) * 128],
             start=True, stop=(m == n), R=[bKT, bQT], W=[bst])
        if m != n:
            nk = C_NLO4 if m < n else C_NHI4
            P.mm(st[:, :], C.ident, C.cbf[:, nk:nk + 512], start=False, stop=True, R=[C.b_const], W=[bst])

    def s1(itm, t, defer):
        g, n, m, first, last, f = itm
        k = t % NP
        st, bst = C.pb[t % 4], C.bpb[t % 4]
        acc, bacc = C.pb[4 + (f % 3)], C.bpb[4 + (f % 3)]
        P.act(pt[k], st[:, :], AF.Exp, R=[bst], W=[bpt[k]], scale=0.125)
        P.mm(acc[0:65, :], V[:, m, g * 65:(g + 1) * 65], pt[k], start=first, stop=last, R=[bV, bpt[k]], W=[bacc])
        if last:
            es = esk[64:65, 4 * g:4 * g + 4].unsqueeze(2).broadcast_to([1, 4, 128])
            def after(g=g, n=n):
                if n == NT - 1:
                    for j in range(4):
                        h = 4 * g + j
                        P.store(D['YT_d'][h * 64:(h + 1) * 64, :], yst[0:64, j, :], byst)
            finalize_norm(C, acc, bacc, 512, yst[0:64, :, n * 128:(n + 1) * 128], byst, shape3=(4, 128), esink=es, tagk=f,
                          defer=defer, after=after, dl=(1, 2, 3, 4))

    run_pipeline(items, 3, s0, s1)
    ar.release(m0)
    P.barrier()
    P.retire(nb0)


def mixer_b(C, l):
    P, ar, D = C.P, C.ar, C.D
    nb0 = len(P.bufs)
    m0 = ar.mark()
    alloc_fin(C)
    accN = ar.alloc([2, S], F32)
    baccN = P.buf('baccN')
    yst = ar.alloc([2, S], BF16)
    byst = P.buf('byst')
    QTs = [ar.alloc([S], BF16) for _ in range(3)]
    KTs = [ar.alloc([2, S], BF16) for _ in range(3)]
    Vs = [ar.alloc([NT, 130], BF16) for _ in range(3)]
    bQ = P.bufs_n('bQT', 3)
    bK = P.bufs_n('bKT', 3)
    bVv = P.bufs_n('bV', 3)
    NP = 4
    pt = [ar.alloc([512], BF16) for _ in range(NP)]
    bpt = P.bufs_n('bpt', NP)
    items = []
    ai = 0
    for gi, d in enumerate(B_DIL):
        L = S // d
        nb = L // 128
        QT, KT, V = QTs[gi], KTs[gi], Vs[gi]
        bq, bk, bv = bQ[gi], bK[gi], bVv[gi]
        P.load(QT, D['QKT_d'][(5 + gi) * 128:(6 + gi) * 128, :], bq)
        P.memset('pool' if gi % 2 else 'dve', KT, 0.0, W=[bk])
        for jh_ in range(2):
            P.load(KT[jh_ * 64:(jh_ + 1) * 64, jh_, :], D['QKT_d'][(8 + gi) * 128 + jh_ * 64:(8 + gi) * 128 + (jh_ + 1) * 64, :], bk)
        if gi == 0:
            P.load(V, D['Vnat_d'].rearrange("(t p) c -> p t c", p=128)[:, :, 650:780], bv)
        else:
            P.load(V, D['Vb_d'][gi - 1].rearrange("(t p) c -> p t c", p=128), bv)
        for jh in range(2):
            for r in range(d):
                base = r * L
                qblocks = []
                for n_ in range(-1, nb):
                    q0 = 128 * n_ + 64
                    qa, qb = max(q0, 0), min(q0 + 128, L)
                    tiles = []
                    if n_ >= 0:
                        tiles.append((n_, C_MLO))
                    if n_ + 1 < nb:
                        tiles.append((n_ + 1, C_MHI))
                    qblocks.append((qa, qb - qa, qa - q0, tiles))
                for g0 in range(0, len(qblocks), 4):
                    grp = qblocks[g0:g0 + 4]
                    ncols = sum(q[1] for q in grp)
                    pstart = grp[0][0]
                    col = 0
                    nsub = (len(grp) + 1) // 2
                    for bi in range(0, len(grp), 2):
                        sub = grp[bi:bi + 2]
                        plist = []
                        sc = 0
                        for (qa, nq, aoff, tiles) in sub:
                            for ti, (m, mk) in enumerate(tiles):
                                plist.append((sc, nq, aoff, mk, base // 128 + m, col, ti == 0, ti == len(tiles) - 1, base + qa))
                                sc += nq
                            col += nq
                        endinfo = None
                        if bi // 2 == nsub - 1:
                            endinfo = (gi, d, r, pstart, ncols)
                        items.append((QT, KT, V, bq, bk, bv, jh, plist, sc, ai, endinfo))
                    ai += 1

    def s0(itm, t):
        QT, KT, V, bq, bk, bv, jh, plist, sc, a_, endinfo = itm
        ps = slice(jh * 64, (jh + 1) * 64)
        st, bst = C.pb[t % 4], C.bpb[t % 4]
        for (s0_, nq, aoff, mk, tg, c0, first, last, qpos) in plist:
            P.mm(st[:, s0_:s0_ + nq], KT[:, jh, tg * 128:(tg + 1) * 128], QT[:, qpos:qpos + nq], start=True, stop=False, R=[bk, bq], W=[bst])
            nk = C_NLO4 if mk == C_MLO else C_NHI4
            P.mm(st[:, s0_:s0_ + nq], C.ident, C.cbf[:, nk + aoff:nk + aoff + nq], start=False, stop=True, R=[C.b_const], W=[bst])

    def s1(itm, t, defer):
        QT, KT, V, bq, bk, bv, jh, plist, sc, a_, endinfo = itm
        k = t % NP
        st, bst = C.pb[t % 4], C.bpb[t % 4]
        acc, bacc = C.pb[4 + (a_ % 3)], C.bpb[4 + (a_ % 3)]
        P.act(pt[k][:, 0:sc], st[:, 0:sc], AF.Exp, R=[bst], W=[bpt[k]], scale=0.125)
        for (s0_, nq, aoff, mk, tg, c0, first, last, _) in plist:
            P.mm(acc[0:65, c0:c0 + nq], V[:, tg, jh * 65:(jh + 1) * 65], pt[k][:, s0_:s0_ + nq], start=first, stop=last,
                 R=[bv, bpt[k]], W=[bacc])
        if endinfo is not None:
            gi, d, r, pstart, ncols = endinfo
            if gi == 0:
                P.cp('act', accN[0:65, jh, pstart:pstart + ncols], acc[0:65, 0:ncols], R=[bacc], W=[baccN])
            else:
                t0 = r + d * pstart
                view = accN[0:65, jh, sl(t0, ncols, d)]
                P.tt('dve', view, view, acc[0:65, 0:ncols], ALU.add, R=[bacc, baccN], W=[baccN])

    run_pipeline(items, 3, s0, s1)
    fi = 0
    for jh in range(2):
        for ch in range(8):
            cs = slice(ch * 512, (ch + 1) * 512)
            k = fi % 2
            r, rhi, rlo, br = C.fr[k], C.frhi[k], C.frlo[k], C.bfr[k]
            bc, bbc = C.pb[7], C.bpb[7]
            P.act(r[64:65, :], accN[64:65, jh, cs], AF.Ln, R=[baccN], W=[br])
            P.act(r[64:65, :], r[64:65, :], AF.Exp, R=[br], W=[br], scale=-1.0)
            P.cp('dve', rhi[64:65, :], r[64:65, :], R=[br], W=[br])
            P.tt('dve', rlo[64:65, :], r[64:65, :], rhi[64:65, :], ALU.subtract, R=[br], W=[br])
            P.mm(bc[0:64, :], C.ones_bf[64:65, 0:64], rhi[64:65, :], start=True, stop=False, R=[br, C.b_const], W=[bbc])
            P.mm(bc[0:64, :], C.ones_bf[64:65, 0:64], rlo[64:65, :], start=False, stop=True, R=[br, C.b_const], W=[bbc])
            P.tt('dve', yst[0:64, jh, cs], accN[0:64, jh, cs], bc[0:64, :], ALU.mult, R=[baccN, bbc], W=[byst])
            fi += 1
        P.store(D['YT_d'][512 + jh * 64:512 + (jh + 1) * 64, :], yst[0:64, jh, :], byst)
    ar.release(m0)
    P.barrier()
    P.retire(nb0)


def mixer_d(C, l):
    P, ar, D = C.P, C.ar, C.D
    nb0 = len(P.bufs)
    m0 = ar.mark()
    alloc_fin(C)
    QT = ar.alloc([2, S], BF16)
    KT = ar.alloc([4, S], BF16)
    V = ar.alloc([NT, 260], BF16)
    bQT, bKT, bV = P.buf('dQT'), P.buf('dKT'), P.buf('dV')
    yst = ar.alloc([4, S], BF16)
    byst = P.buf('dyst')
    EB = ar.alloc([4, NTAB * 128], BF16)
    bEB = P.buf('dEB')
    tmpb = [ar.alloc([NTAB * 128], F32) for _ in range(2)]
    btmp = P.bufs_n('dtmp', 2)
    P.memset('pool', KT[:, 0:2, :], 0.0, W=[bKT])
    P.memset('dve', KT[:, 2:4, :], 0.0, W=[bKT])
    for i in range(2):
        P.load(QT[:, i, :], D['QKT_d'][(15 + i) * 128:(16 + i) * 128, :], bQT)
    for h_ in range(4):
        r0_ = (h_ % 2) * 64
        P.load(KT[r0_:r0_ + 64, h_, :], D['QKT_d'][17 * 128 + h_ * 64:17 * 128 + (h_ + 1) * 64, :], bKT)
    P.load(V, D['Vnat_d'].rearrange("(t p) c -> p t c", p=128)[:, :, 390:650], bV)
    for h in range(4):
        P.load(tmpb[h % 2], D['dbias'][l, h], btmp[h % 2])
        P.act(tmpb[h % 2], tmpb[h % 2], AF.Exp, R=[btmp[h % 2]], W=[btmp[h % 2]])
        P.tt('dve', EB[:, h, :], tmpb[h % 2], C.cbf[:, C_DVALID:C_DVALID + NTAB * 128], ALU.mult, R=[btmp[h % 2], C.b_const], W=[bEB])
    NP = 4
    pt = [ar.alloc([512], BF16) for _ in range(NP)]
    bpt = P.bufs_n('dpt', NP)
    items = []
    fi = 0
    for h in range(4):
        for n4 in range(8):
            plist = []
            for n in range(n4 * 4, n4 * 4 + 4):
                pl = D_PAIRS[n]
                for pi, (m, tab) in enumerate(pl):
                    plist.append((n, m, tab, pi == 0, pi == len(pl) - 1))
            nch = (len(plist) + 3) // 4
            for ci in range(nch):
                items.append((h, n4, plist[ci * 4:ci * 4 + 4], fi, ci == nch - 1))
            fi += 1

    def s0(itm, t):
        h, n4, chunk, f, endg = itm
        bq = h // 2
        ps = slice((h % 2) * 64, (h % 2 + 1) * 64)
        st, bst = C.pb[t % 4], C.bpb[t % 4]
        for i, (n, m, tab, first, last) in enumerate(chunk):
            P.mm(st[:, i * 128:(i + 1) * 128], KT[:, h, m * 128:(m + 1) * 128], QT[:, bq, n * 128:(n + 1) * 128],
                 R=[bKT, bQT], W=[bst])

    def s1(itm, t, defer):
        h, n4, chunk, f, endg = itm
        k = t % NP
        st, bst = C.pb[t % 4], C.bpb[t % 4]
        acc, bacc = C.pb[4 + (f % 3)], C.bpb[4 + (f % 3)]
        used = len(chunk) * 128
        P.act(pt[k][:, 0:used], st[:, 0:used], AF.Exp, R=[bst], W=[bpt[k]], scale=0.125)
        for i, (n, m, tab, first, last) in enumerate(chunk):
            P.tt('pool' if i % 2 else 'dve', pt[k][:, i * 128:(i + 1) * 128], pt[k][:, i * 128:(i + 1) * 128],
                 EB[:, h, tab * 128:(tab + 1) * 128], ALU.mult, R=[bpt[k], bEB], W=[bpt[k]])
        for i, (n, m, tab, first, last) in enumerate(chunk):
            qc = (n % 4) * 128
            P.mm(acc[0:65, qc:qc + 128], V[:, m, h * 65:(h + 1) * 65], pt[k][:, i * 128:(i + 1) * 128], start=first, stop=last,
                 R=[bV, bpt[k]], W=[bacc])
        if endg:
            def after(h=h, n4=n4):
                if n4 == 7:
                    P.store(D['YT_d'][896 + h * 64:896 + (h + 1) * 64, :], yst[0:64, h, :], byst)
            finalize_norm(C, acc, bacc, 512, yst[0:64, h, n4 * 512:(n4 + 1) * 512], byst, tagk=f, defer=defer, after=after, dl=(1, 2, 3, 4))

    run_pipeline(items, 3, s0, s1)
    ar.release(m0)
    P.barrier()
    P.retire(nb0)


def mixer_c(C, l):
    P, ar, D = C.P, C.ar, C.D
    lam_init = 0.8 - 0.6 * math.exp(-0.3 * l)
    nb0 = len(P.bufs)
    m0 = ar.mark()
    alloc_fin(C)
    QT = ar.alloc([2, S], BF16)
    KT = ar.alloc([8, S], BF16)
    V = ar.alloc([NT, 260], BF16)
    bQT, bKT, bV = P.buf('cQT'), P.buf('cKT'), P.buf('cV')
    yst = ar.alloc([4, S], BF16)
    byst = P.buf('cyst')
    P.memset('pool', KT[:, 0:4, :], 0.0, W=[bKT])
    P.memset('dve', KT[:, 4:8, :], 0.0, W=[bKT])
    for g2 in range(2):
        P.load(QT[:, g2, :], D['QKT_d'][(11 + g2) * 128:(12 + g2) * 128, :], bQT)
    for b in range(8):
        sl_ = (b % 4) * 32
        row = (b // 4) * 128 + sl_
        P.load(KT[sl_:sl_ + 32, b, :], D['QKT_d'][13 * 128 + row:13 * 128 + row + 32, :], bKT)
    P.load(V, D['Vnat_d'].rearrange("(t p) c -> p t c", p=128)[:, :, 130:390], bV)
    lamb = ar.alloc([128], F32)
    blam = P.buf('lam')
    lt = ar.alloc([2, 32], F32)
    l2 = ar.alloc([2], F32)
    nlam = ar.alloc([1], F32)
    gsc = ar.alloc([1], F32)
    bgsc = P.buf('gsc')
    P.load(lamb, D['lamb'][l], blam)
    P.load(gsc, D['gsub'][l], bgsc)
    lv = lamb.rearrange("p (a b c) -> p a b c", a=2, b=2)
    P.tt('dve', lt, lv[:, :, 0, :], lv[:, :, 1, :], ALU.mult, R=[blam], W=[blam])
    P.op('dve', lambda e: e.reduce_sum(out=l2, in_=lt, axis=mybir.AxisListType.X), R=[blam], W=[blam])
    P.act(l2, l2, AF.Exp, R=[blam], W=[blam])
    P.tt('dve', nlam, l2[:, 0:1], l2[:, 1:2], ALU.subtract, R=[blam], W=[blam])
    P.ts('dve', nlam, nlam, lam_init, -1.0, ALU.add, ALU.mult, R=[blam], W=[blam])
    P.ts('dve', gsc, gsc, 1.0 - lam_init, None, ALU.mult, R=[bgsc], W=[bgsc])
    NP = 4
    pt = [ar.alloc([512], BF16) for _ in range(NP)]
    bpt = P.bufs_n('cpt', NP)
    to = [ar.alloc([512], F32) for _ in range(2)]
    t1 = [ar.alloc([512], F32) for _ in range(2)]
    sqb = [ar.alloc([512], BF16) for _ in range(2)]
    rsd = [ar.alloc([512], F32) for _ in range(2)]
    bto = P.bufs_n('cto', 2)
    scale = 32.0 ** -0.5
    items = []
    fi = 0
    for h in range(4):
        for Q in range(8):
            for c in range(2):
                for kt in range(NT):
                    items.append((h, Q, c, kt, fi))
            fi += 1

    def s0(itm, t):
        h, Q, c, kt, f = itm
        b = 2 * h + c
        st, bst = C.pb[t % 3], C.bpb[t % 3]
        P.mm(st[:, :], KT[:, b, kt * 128:(kt + 1) * 128], QT[:, b // 4, Q * 512:(Q + 1) * 512], R=[bKT, bQT], W=[bst])

    def s1(itm, t, defer):
        h, Q, c, kt, f = itm
        qs = slice(Q * 512, (Q + 1) * 512)
        k = t % NP
        st, bst = C.pb[t % 3], C.bpb[t % 3]
        a0 = 3 + 2 * (f % 2)
        accs = [C.pb[a0], C.pb[a0 + 1]]
        baccs = [C.bpb[a0], C.bpb[a0 + 1]]
        P.act(pt[k], st[:, :], AF.Exp, R=[bst], W=[bpt[k]], scale=scale)
        P.mm(accs[c][0:65, :], V[:, kt, h * 65:(h + 1) * 65], pt[k], start=(kt == 0), stop=(kt == NT - 1),
             R=[bV, bpt[k]], W=[baccs[c]])
        if not (c == 1 and kt == NT - 1):
            return
        k2 = f % 2
        bt = bto[k2]
        bc, bbc = C.pb[7], C.bpb[7]

        def f0():
            for c_ in range(2):
                r, rhi, rlo, br = C.fr[c_], C.frhi[c_], C.frlo[c_], C.bfr[c_]
                P.recip(r[64:65, :], accs[c_][64:65, :], R=[baccs[c_]], W=[br])
                if c_ == 1:
                    P.ts('dve', r[64:65, :], r[64:65, :], nlam[64:65, 0:1], None, ALU.mult, R=[br, blam], W=[br])
                P.cp('dve', rhi[64:65, :], r[64:65, :], R=[br], W=[br])
                P.tt('dve', rlo[64:65, :], r[64:65, :], rhi[64:65, :], ALU.subtract, R=[br], W=[br])

        def f1(c_):
            def fn():
                r, rhi, rlo, br = C.fr[c_], C.frhi[c_], C.frlo[c_], C.bfr[c_]
                P.mm(bc[0:64, :], C.ones_bf[64:65, 0:64], rhi[64:65, :], start=True, stop=False, R=[br, C.b_const], W=[bbc])
                P.mm(bc[0:64, :], C.ones_bf[64:65, 0:64], rlo[64:65, :], start=False, stop=True, R=[br, C.b_const], W=[bbc])
            return fn

        def f2(c_):
            def fn():
                P.cp('act', C.fbcs[c_][0:64, :], bc[0:64, :], R=[bbc], W=[C.bfbcs[c_]])
            return fn

        def f3():
            P.tt('dve', to[k2][0:64, :], accs[0][0:64, :], C.fbcs[0][0:64, :], ALU.mult, R=[baccs[0], C.bfbcs[0]], W=[bt])
            P.tt('dve', t1[k2][0:64, :], accs[1][0:64, :], C.fbcs[1][0:64, :], ALU.mult, R=[baccs[1], C.bfbcs[1], bt], W=[bt])
            P.tt('pool', to[k2][0:64, :], to[k2][0:64, :], t1[k2][0:64, :], ALU.add, R=[bt], W=[bt])

        def f4():
            P.act(sqb[k2][0:64, :], to[k2][0:64, :], AF.Square, R=[bt], W=[bt])

        def f5():
            P.mm(bc[0:64, :], C.ones_bf[0:64, 0:64], sqb[k2][0:64, :], R=[bt, C.b_const], W=[bbc])

        def f6():
            P.act(rsd[k2][0:64, :], bc[0:64, :], AF.Ln, R=[bbc], W=[bt], scale=1.0 / 64, bias=EPS)
            P.act(rsd[k2][0:64, :], rsd[k2][0:64, :], AF.Exp, R=[bt], W=[bt], scale=-0.5)

        def f7():
            P.stt(yst[0:64, h, qs], to[k2][0:64, :], gsc[0:64, 0:1], rsd[k2][0:64, :], ALU.mult, ALU.mult, R=[bt, bgsc], W=[byst])
            if Q == 7:
                P.store(D['YT_d'][640 + h * 64:640 + (h + 1) * 64, :], yst[0:64, h, :], byst)

        defer(1, f0)
        defer(4, f1(0))
        defer(5, f2(0))
        defer(6, f1(1))
        defer(7, f2(1))
        defer(9, f3)
        defer(12, f4)
        defer(13, f5)
        defer(15, f6)
        defer(17, f7)

    run_pipeline(items, 2, s0, s1)
    ar.release(m0)
    P.barrier()
    P.retire(nb0)


def phase_p3a(C, l, xin, xout):
    P, ar, D = C.P, C.ar, C.D
    nb0 = len(P.bufs)
    m0 = ar.mark()
    Wg = ar.alloc([8, 4096], BF16)
    Wb = ar.alloc([9, DM], BF16)
    Wo = ar.alloc([8, DM], BF16)
    bWg, bWb, bWo = P.buf('Wg'), P.buf('Wb'), P.buf('Wo')
    w_in = D['w_in'][l].rearrange("(kc p) c -> p kc c", p=128)
    for kc in range(8):
        P.dma('pool', Wg[:, kc, :], w_in[:, kc, GATE_OFF:GATE_OFF + 4096], bWg, W=(bWg,))
    P.dma('pool', Wb[:, 0:4, :], D['w_ba'][l].rearrange("(kc p) c -> p kc c", p=128), bWb, W=(bWb,))
    P.dma('pool', Wb[:, 4, :], D['w_bb'][l], bWb, W=(bWb,))
    P.dma('pool', Wb[:, 5:7, :], D['w_bc'][l].rearrange("(kc p) c -> p kc c", p=128), bWb, W=(bWb,))
    P.dma('pool', Wb[:, 7:9, :], D['w_bd'][l].rearrange("(kc p) c -> p kc c", p=128), bWb, W=(bWb,))
    P.dma('pool', Wo, D['w_out'][l].rearrange("(kc p) c -> p kc c", p=128), bWo, W=(bWo,))
    hTc = [ar.alloc([8, 512], BF16) for _ in range(2)]
    YTc = [ar.alloc([9, 512], BF16) for _ in range(2)]
    bh = P.bufs_n('hTc', 2)
    by = P.bufs_n('YTc', 2)
    mT = [ar.alloc([8, 512], BF16) for _ in range(2)]
    bmT = P.bufs_n('mT', 2)
    sig = [ar.alloc([512], F32) for _ in range(3)]
    bsig = P.bufs_n('sig', 3)
    tmp = [ar.alloc([512], F32) for _ in range(2)]
    btmp = P.bufs_n('mtmp', 2)
    macc = [ar.alloc([512], F32) for _ in range(2)]
    bmacc = P.bufs_n('macc', 2)
    xt = [ar.alloc([DM], F32) for _ in range(2)]
    bxt = P.bufs_n('x3', 2)
    xo = [ar.alloc([DM], F32) for _ in range(2)]
    bxo = P.bufs_n('xo3', 2)
    hT_v = D['hT_d'].rearrange("(kc p) t -> p kc t", p=128)
    YT_v = D['YT_d'].rearrange("(kc p) t -> p kc t", p=128)
    branches = [(0, 4), (4, 5), (5, 7), (7, 9)]
    it = 0
    si = 0
    ti = 0
    for tg in range(8):
        g2 = tg % 2
        ts_ = slice(tg * 512, (tg + 1) * 512)
        P.load(hTc[g2], hT_v[:, :, ts_], bh[g2])
        P.load(YTc[g2], YT_v[:, :, ts_], by[g2])
        for ct in range(8):
            ma, bma = macc[ct % 2], bmacc[ct % 2]
            for br in range(4):
                pg, bpg = C.pb[(it % 2) * 2], C.bpb[(it % 2) * 2]
                py, bpy = C.pb[(it % 2) * 2 + 1], C.bpb[(it % 2) * 2 + 1]
                it += 1
                c0 = br * 1024 + ct * 128
                for kc in range(8):
                    P.mm(pg[:, :], Wg[:, kc, c0:c0 + 128], hTc[g2][:, kc, :], start=(kc == 0), stop=(kc == 7), R=[bWg, bh[g2]], W=[bpg])
                b0, b1 = branches[br]
                for bi in range(b0, b1):
                    P.mm(py[:, :], Wb[:, bi, ct * 128:(ct + 1) * 128], YTc[g2][:, bi, :], start=(bi == b0), stop=(bi == b1 - 1),
                         R=[bWb, by[g2]], W=[bpy])
                s_, bs_ = sig[si % 3], bsig[si % 3]
                si += 1
                P.act(s_, pg[:, :], AF.Sigmoid, R=[bpg], W=[bs_])
                if br == 0:
                    P.tt('dve', ma, s_, py[:, :], ALU.mult, R=[bs_, bpy], W=[bma])
                else:
                    t_, bt_ = tmp[ti % 2], btmp[ti % 2]
                    ti += 1
                    P.tt('dve', t_, s_, py[:, :], ALU.mult, R=[bs_, bpy], W=[bt_])
                    if br < 3:
                        P.tt('pool', ma, ma, t_, ALU.add, R=[bma, bt_], W=[bma])
                    else:
                        P.tt('pool', mT[g2][:, ct, :], ma, t_, ALU.add, R=[bma, bt_], W=[bmT[g2]])
        for tt in range(4):
            tok0 = tg * 512 + tt * 128
            xi = (tg * 4 + tt) % 2
            P.load(xt[xi], xin[tok0:tok0 + 128, :], bxt[xi])
            for cg in range(2):
                po, bpo = C.pb[4 + (cg + 2 * tt) % 4], C.bpb[4 + (cg + 2 * tt) % 4]
                for kc in range(8):
                    P.mm(po[:, :], mT[g2][:, kc, tt * 128:(tt + 1) * 128], Wo[:, kc, cg * 512:(cg + 1) * 512], start=(kc == 0), stop=(kc == 7),
                         R=[bmT[g2], bWo], W=[bpo])
                P.tt('dve', xo[xi][:, cg * 512:(cg + 1) * 512], po[:, :], xt[xi][:, cg * 512:(cg + 1) * 512], ALU.add,
                     R=[bpo, bxt[xi]], W=[bxo[xi]])
            P.store(xout[tok0:tok0 + 128, :], xo[xi], bxo[xi])
    ar.release(m0)
    P.barrier()
    P.retire(nb0)


def phase_p3b(C, l, xin, xout):
    P, ar, D = C.P, C.ar, C.D
    nb0 = len(P.bufs)
    m0 = ar.mark()
    Wu = ar.alloc([8, 4096], BF16)
    Wd = ar.alloc([32, DM], BF16)
    gb = ar.alloc([DM], F32)
    bWu, bWd, bgb = P.buf('Wu'), P.buf('Wd'), P.buf('gbm')
    wu_v = D['w_up'][l].rearrange("(kc p) c -> p kc c", p=128)
    wd_v = D['w_down'][l].rearrange("(kc p) c -> p kc c", p=128)
    P.load(gb, D['gb_mlp'][l], bgb)
    for kc in range(8):
        P.dma('pool', Wu[:, kc, :], wu_v[:, kc, :], bWu, W=(bWu,))
    for k4 in range(8):
        P.dma('pool', Wd[:, k4 * 4:(k4 + 1) * 4, :], wd_v[:, k4 * 4:(k4 + 1) * 4, :], bWd, W=(bWd,))
    xt = [ar.alloc([DM], F32) for _ in range(2)]
    bxt = P.bufs_n('x4', 2)
    xo = [ar.alloc([DM], F32) for _ in range(1)]
    bxo = P.bufs_n('xo4', 1)
    ss = [ar.alloc([1], F32) for _ in range(2)]
    bss = P.bufs_n('ss4', 2)
    hb = [ar.alloc([DM], BF16) for _ in range(2)]
    bhb = P.bufs_n('hb4', 2)
    hmT = [ar.alloc([8, 512], BF16) for _ in range(2)]
    bhm = P.bufs_n('hmT', 2)
    uT = ar.alloc([32, 512], BF16)
    buT = P.buf('uT')
    rl = [ar.alloc([512], F32) for _ in range(2)]
    brl = P.bufs_n('rl', 2)
    xi = 0
    it = 0
    for tg in range(8):
        g2 = tg % 2
        for tt in range(4):
            tok0 = tg * 512 + tt * 128
            i, j = xi % 2, xi % 2
            xi += 1
            P.load(xt[i], xin[tok0:tok0 + 128, :], bxt[i])
            P.act(hb[j], xt[i], AF.Square, R=[bxt[i]], W=[bhb[j], bss[j]], accum_out=ss[j])
            P.act(ss[j], ss[j], AF.Ln, R=[bss[j]], W=[bss[j]], scale=1.0 / DM, bias=EPS)
            P.act(ss[j], ss[j], AF.Exp, R=[bss[j]], W=[bss[j]], scale=-0.5)
            P.stt(hb[j], xt[i], ss[j], gb, ALU.mult, ALU.mult, R=[bxt[i], bss[j], bgb], W=[bhb[j]])
            pbv = C.pb[6 + j].bitcast(BF16)
            for kc in range(8):
                P.tr(pbv[:, kc * 128:(kc + 1) * 128], hb[j][:, kc * 128:(kc + 1) * 128], C.ident, R=[bhb[j], C.b_const], W=[C.bpb[6 + j]])
            P.cp('dve', hmT[g2][:, :, tt * 128:(tt + 1) * 128], pbv.rearrange("p (k t) -> p k t", k=8), R=[C.bpb[6 + j]], W=[bhm[g2]])
        for mt in range(32):
            pu, bpu = C.pb[it % 3], C.bpb[it % 3]
            r_, br_ = rl[it % 2], brl[it % 2]
            it += 1
            for kc in range(8):
                P.mm(pu[:, :], Wu[:, kc, mt * 128:(mt + 1) * 128], hmT[g2][:, kc, :], start=(kc == 0), stop=(kc == 7), R=[bWu, bhm[g2]], W=[bpu])
            P.act(r_, pu[:, :], AF.Relu, R=[bpu], W=[br_])
            P.tt('pool' if mt % 2 else 'dve', uT[:, mt, :], r_, r_, ALU.mult, R=[br_], W=[buT])
        for tt in range(4):
            tok0 = tg * 512 + tt * 128
            i, j = xi % 2, 0
            xi += 1
            P.load(xt[i], xin[tok0:tok0 + 128, :], bxt[i])
            for cg in range(2):
                pd, bpd = C.pb[3 + (cg + 2 * tt) % 3], C.bpb[3 + (cg + 2 * tt) % 3]
                for mt in range(32):
                    P.mm(pd[:, :], uT[:, mt, tt * 128:(tt + 1) * 128], Wd[:, mt, cg * 512:(cg + 1) * 512], start=(mt == 0), stop=(mt == 31),
                         R=[buT, bWd], W=[bpd])
                P.tt('dve', xo[j][:, cg * 512:(cg + 1) * 512], pd[:, :], xt[i][:, cg * 512:(cg + 1) * 512], ALU.add,
                     R=[bpd, bxt[i]], W=[bxo[j]])
            P.store(xout[tok0:tok0 + 128, :], xo[j], bxo[j])
    ar.release(m0)
    P.barrier()
    P.retire(nb0)


ARENA = 207 * 1024 + 512
NSEM_POOL = 96


class SemPool:
    def __init__(self, nc, stack, n):
        self.sems = [stack.enter_context(nc.semaphore(f"sm{i}")) for i in range(n)]
        self.i = 0

        self.free = []

    def get(self):
        if self.free:
            return self.free.pop()
        s_ = self.sems[self.i]
        self.i += 1
        return (s_, 0)

    def put(self, sem, cnt):
        self.free.append((sem, cnt))


def build(n_layers=2, phases=None, dbg=False):
    nc = bass.Bass("TRN2", target_bir_lowering=False)
    D = {}
    x = nc.dram_tensor("x", [S, DM], F32, kind="ExternalInput").ap()
    for name, shp in PARAM_SHAPES.items():
        D[name] = nc.dram_tensor(name, shp, F32, kind="ExternalInput").ap()
    y = nc.dram_tensor("y", [S, DM], F32, kind="ExternalOutput").ap()
    sk = "ExternalOutput" if dbg else "Internal"
    D['hT_d'] = nc.dram_tensor("hT_d", [DM, S], BF16, kind=sk).ap()
    D['QKT_d'] = nc.dram_tensor("QKT_d", [NQKB * 128, S], BF16, kind=sk).ap()
    D['Vnat_d'] = nc.dram_tensor("Vnat_d", [S, 780], BF16, kind=sk).ap()
    D['Vb_d'] = nc.dram_tensor("Vb_d", [2, S, 130], BF16, kind=sk).ap()
    D['YT_d'] = nc.dram_tensor("YT_d", [1152, S], BF16, kind=sk).ap()
    x1_d = nc.dram_tensor("x1_d", [S, DM], F32, kind=sk).ap()
    x2_d = nc.dram_tensor("x2_d", [S, DM], F32, kind=sk).ap()
    with ExitStack() as stack:
        arena_t = stack.enter_context(nc.sbuf_tensor("arena", [128, ARENA], U8))
        pbs = [stack.enter_context(nc.psum_tensor(f"pb{i}", [128, 512], F32)) for i in range(8)]
        sp_ = SemPool(nc, stack, NSEM_POOL)

        class _St:
            def enter_context(self, cm):
                raise RuntimeError

        P = Prog.__new__(Prog)
        P.nc = nc
        P.ops = {e: [] for e in ENGS}
        P.bufs = []
        P.esem = {e: sp_.get()[0] for e in ENGS}
        P.nsem = len(ENGS)

        def dma(eng, out, in_, owner, R=(), W=()):
            if owner.sem is None:
                owner.sem, owner.cnt = sp_.get()
            owner.cnt += 16
            o = Op(eng, lambda e: e.dma_start(out=out, in_=in_))
            o.dma_sem = owner.sem
            P._track(o, ('dma', owner.sem, owner.cnt), 'dma', R, W)
            P.ops[eng].append(o)
            return o
        P.dma = dma

        def retire(nb0):
            for b in P.bufs[nb0:]:
                if b.sem is not None:
                    sp_.put(b.sem, b.cnt)
                    b.sem = None
            del P.bufs[nb0:]
        P.retire = retire
        block = stack.enter_context(nc.Block())
        C = Ctx()
        C.P, C.D = P, D
        C.ar = Arena(arena_t, ARENA)
        C.pb = [p[:, :] for p in pbs]
        C.bpb = P.bufs_n('pb', 8)
        C.cbf = C.ar.alloc([NCONST], BF16)
        C.b_const = P.buf('consts')
        P.dma('pool', C.cbf, D['consts'], C.b_const, W=(C.b_const,))
        C.ident = C.cbf[:, C_IDENT:C_IDENT + 128]
        C.bd64 = C.cbf[:, C_BD64:C_BD64 + 128]
        C.bd32 = C.cbf[:, C_BD32:C_BD32 + 128]
        C.psw64 = C.cbf[:, C_PSW64:C_PSW64 + 128]
        C.psw32 = C.cbf[:, C_PSW32:C_PSW32 + 128]
        C.ones_bf = C.cbf[:, C_ONES:C_ONES + 128]
        all_ph = ['p1', 'a', 'b', 'c', 'd', 'p3a', 'p3b']
        phases = phases or all_ph
        for l in range(n_layers):
            xin = x if l == 0 else x2_d
            xfin = y if l == n_layers - 1 else x2_d
            if 'p1' in phases:
                phase_p1(C, l, xin)
            if 'a' in phases:
                mixer_a(C, l)
            if 'b' in phases:
                mixer_b(C, l)
            if 'c' in phases:
                mixer_c(C, l)
            if 'd' in phases:
                mixer_d(C, l)
            if 'p3a' in phases:
                phase_p3a(C, l, xin, x1_d)
            if 'p3b' in phases:
                phase_p3b(C, l, x1_d, xfin)
        P.barrier()
        P.emit(block)
        C.nsem = sp_.i
    return nc, C


_CACHE = {}


def kernel(**inputs):
    x = np.ascontiguousarray(np.asarray(inputs['x'], dtype=np.float32))
    params = host_prep(inputs)
    if 'nc' not in _CACHE:
        _CACHE['nc'] = build()[0]
    nc = _CACHE['nc']
    in_maps = []
    for b in range(8):
        m = {'x': x[b]}
        m.update(params)
        in_maps.append(m)
    res = run_bass_kernel_spmd(nc, in_maps, core_ids=list(range(8)))
    return np.stack([np.asarray(r['y'], dtype=np.float32) for r in res.results], axis=0)
```

```python
import math
from contextlib import ExitStack
import numpy as np
import concourse.bass as bass
import concourse.mybir as mybir
from concourse.bass_utils import run_bass_kernel_spmd

F32 = mybir.dt.float32
BF16 = mybir.dt.bfloat16
U8 = mybir.dt.uint8
AF = mybir.ActivationFunctionType
ALU = mybir.AluOpType

S = 4096
DM = 1024
NT = 32
EPS = 1e-6
INC = 7552
ENGS = ['pe', 'act', 'dve', 'pool', 'sp']
SAME_ENG_SYNC = ('act', 'dve', 'pool')


class Buf:
    __slots__ = ('name', 'w', 'rs', 'sem', 'cnt')

    def __init__(self, name):
        self.name = name
        self.w = None
        self.rs = {}
        self.sem = None
        self.cnt = 0


class Op:
    __slots__ = ('eng', 'fn', 'deps', 'dwaits', 'needs_inc', 'semval', 'dma_sem')

    def __init__(self, eng, fn):
        self.eng = eng
        self.fn = fn
        self.deps = set()
        self.dwaits = {}
        self.needs_inc = False
        self.semval = 0
        self.dma_sem = None


class Prog:
    def __init__(self, nc, stack):
        self.nc = nc
        self.stack = stack
        self.ops = {e: [] for e in ENGS}
        self.bufs = []
        self.esem = {e: stack.enter_context(nc.semaphore("s_" + e)) for e in ENGS}
        self.nsem = len(ENGS)

    def buf(self, name):
        b = Buf(name)
        self.bufs.append(b)
        return b

    def bufs_n(self, name, n):
        return [self.buf(f"{name}{i}") for i in range(n)]

    def _add_ev(self, o, ev):
        if ev is None:
            return
        if ev[0] == 'op':
            d = ev[1]
            if d.eng == o.eng and d.eng not in SAME_ENG_SYNC:
                return
            o.deps.add(d)
            d.needs_inc = True
        else:
            _, sem, val = ev
            cur = o.dwaits.get(id(sem))
            if cur is None or cur[1] < val:
                o.dwaits[id(sem)] = (sem, val)

    def _track(self, o, ev, key, R, W):
        for b in R:
            self._add_ev(o, b.w)
        for b in W:
            self._add_ev(o, b.w)
            for e2 in b.rs.values():
                self._add_ev(o, e2)
        for b in R:
            b.rs[key] = ev
        for b in W:
            b.w = ev
            b.rs = {}

    def op(self, eng, fn, R=(), W=()):
        o = Op(eng, fn)
        self._track(o, ('op', o), eng, R, W)
        self.ops[eng].append(o)
        return o

    def dma(self, eng, out, in_, owner, R=(), W=()):
        if owner.sem is None:
            owner.sem = self.stack.enter_context(self.nc.semaphore("d_" + owner.name))
            self.nsem += 1
        owner.cnt += 16
        o = Op(eng, lambda e: e.dma_start(out=out, in_=in_))
        o.dma_sem = owner.sem
        self._track(o, ('dma', owner.sem, owner.cnt), 'dma', R, W)
        self.ops[eng].append(o)
        return o

    def load(self, out, in_, owner, eng='sp'):
        return self.dma(eng, out, in_, owner, R=(), W=(owner,))

    def store(self, out, in_, owner, eng='sp'):
        return self.dma(eng, out, in_, owner, R=(owner,), W=())

    def barrier(self):
        o = Op('sp', lambda e: e.nop())
        for E in ENGS:
            if self.ops[E]:
                last = None
                for c in reversed(self.ops[E]):
                    if c.fn is not None and c.dma_sem is None:
                        last = c
                        break
                if last is not None and E != 'sp':
                    o.deps.add(last)
                    last.needs_inc = True
        for b in self.bufs:
            if b.sem is not None and b.cnt > 0:
                o.dwaits[id(b.sem)] = (b.sem, b.cnt)
        o.needs_inc = True
        self.ops['sp'].append(o)
        for E in ENGS:
            if E != 'sp':
                w = Op(E, None)
                w.deps.add(o)
                self.ops[E].append(w)
        for b in self.bufs:
            b.w = None
            b.rs = {}

    def mm(self, out, lhsT, rhs, start=True, stop=True, R=(), W=()):
        return self.op('pe', lambda e: e.matmul(out, lhsT, rhs, start=start, stop=stop), R, W)

    def tr(self, out, in_, ident, R=(), W=()):
        return self.op('pe', lambda e: e.transpose(out, in_, ident), R, W)

    def act(self, out, in_, func, R=(), W=(), **kw):
        return self.op('act', lambda e: e.activation(out=out, in_=in_, func=func, **kw), R, W)

    def tt(self, eng, out, in0, in1, op, R=(), W=()):
        return self.op(eng, lambda e: e.tensor_tensor(out=out, in0=in0, in1=in1, op=op), R, W)

    def ts(self, eng, out, in0, s1, s2, op0, op1=None, R=(), W=()):
        if op1 is None:
            return self.op(eng, lambda e: e.tensor_scalar(out=out, in0=in0, scalar1=s1, scalar2=None, op0=op0), R, W)
        return self.op(eng, lambda e: e.tensor_scalar(out=out, in0=in0, scalar1=s1, scalar2=s2, op0=op0, op1=op1), R, W)

    def stt(self, out, in0, scalar, in1, op0, op1, R=(), W=()):
        return self.op('dve', lambda e: e.scalar_tensor_tensor(out=out, in0=in0, scalar=scalar, in1=in1, op0=op0, op1=op1), R, W)

    def cp(self, eng, out, in_, R=(), W=()):
        if eng == 'act':
            return self.op('act', lambda e: e.copy(out=out, in_=in_), R, W)
        return self.op(eng, lambda e: e.tensor_copy(out=out, in_=in_), R, W)

    def recip(self, out, in_, R=(), W=()):
        return self.op('dve', lambda e: e.reciprocal(out=out, in_=in_), R, W)

    def memset(self, eng, ap, val, W=()):
        return self.op(eng, lambda e: e.memset(ap, val), (), W)

    def emit(self, block):
        for E in ENGS:
            c = 0
            for o in self.ops[E]:
                if o.needs_inc and o.dma_sem is None:
                    c += 1
                    o.semval = c
        esem = self.esem

        def run(E, eng):
            waited = {}
            for o in self.ops[E]:
                needs = []
                for d in o.deps:
                    needs.append((esem[d.eng], d.semval))
                for sem, val in o.dwaits.values():
                    needs.append((sem, val))
                for sem, val in needs:
                    k = id(sem)
                    if waited.get(k, 0) < val:
                        eng.wait_ge(sem, val)
                        waited[k] = val
                if o.fn is not None:
                    ins = o.fn(eng)
                    if o.dma_sem is not None:
                        ins.then_inc(o.dma_sem, 16)
                    elif o.needs_inc:
                        ins.then_inc(esem[E], 1)

        @block.tensor
        def _(e):
            run('pe', e)

        @block.scalar
        def _(e):
            run('act', e)

        @block.vector
        def _(e):
            run('dve', e)

        @block.gpsimd
        def _(e):
            run('pool', e)

        @block.sync
        def _(e):
            run('sp', e)


class Arena:
    def __init__(self, t, size):
        self.t = t
        self.size = size
        self.off = 0

    def alloc(self, free_shape, dtype):
        es = 4 if dtype == F32 else (2 if dtype == BF16 else 1)
        n = 1
        for s_ in free_shape:
            n *= s_
        nb = (n * es + 31) // 32 * 32
        assert self.off + nb <= self.size, f"arena overflow {self.off}+{nb}>{self.size}"
        ap = self.t[:, self.off:self.off + n * es].bitcast(dtype)
        self.off += nb
        if len(free_shape) == 2:
            ap = ap.rearrange("p (a b) -> p a b", a=free_shape[0])
        elif len(free_shape) == 3:
            ap = ap.rearrange("p (a b c) -> p a b c", a=free_shape[0], b=free_shape[1])
        return ap

    def mark(self):
        return self.off

    def release(self, m):
        self.off = m


QK_BLOCKS = []
for i in range(4):
    QK_BLOCKS.append((i * 128, 'n64'))
QK_BLOCKS.append((512, 'n64'))
for g, kind in enumerate(['n64', 'p4', 'p16']):
    QK_BLOCKS.append((768 + g * 128, kind))
for g, kind in enumerate(['n64', 'p4', 'p16']):
    QK_BLOCKS.append((1152 + g * 128, kind))
for i in range(2):
    QK_BLOCKS.append((1920 + i * 128, 'n32'))
for i in range(2):
    QK_BLOCKS.append((2176 + i * 128, 'n32'))
for i in range(2):
    QK_BLOCKS.append((2688 + i * 128, 'd'))
for i in range(2):
    QK_BLOCKS.append((2944 + i * 128, 'd'))
NQKB = len(QK_BLOCKS)
VNAT_COLS = [(640, 128), (2432, 256), (3200, 256), (1536, 128)]
GATE_OFF = 3456
B_DIL = [1, 4, 16]


def perm_tokens(d):
    L = S // d
    j = np.arange(S)
    return (j % L) * d + (j // L)


def rope_tabs(dim):
    half = dim // 2
    inv = np.power(np.float32(10000.0), -(np.arange(0, dim, 2, dtype=np.float32) / np.float32(dim))).astype(np.float32)
    ang = (np.arange(S, dtype=np.float32)[:, None] * inv[None, :]).astype(np.float32)
    c = np.cos(ang).astype(np.float32)
    s_ = np.sin(ang).astype(np.float32)
    p = np.arange(128) % dim
    cosT = c[:, p % half].T.copy()
    sgn = np.where(p < half, -1.0, 1.0).astype(np.float32)
    sinT = (s_[:, p % half] * sgn[None, :]).T.copy()
    return cosT, sinT


def d_tables():
    rows = 64
    r0 = np.clip(np.arange(rows) - 4, 0, rows - 8)
    cj = np.arange(64)
    c0 = np.clip(cj - 8, 0, 48)
    col_ok = (cj[None, :] >= c0[:, None]) & (cj[None, :] < c0[:, None] + 16)
    dc = np.clip(cj[None, :] - cj[:, None], -15, 15) + 15
    tabs = {}
    tab_list = []
    pairs = []
    for n in range(32):
        lo = r0[2 * n] // 2
        hi = (r0[2 * n + 1] + 7) // 2
        pl = []
        for m in range(lo, hi + 1):
            valid = np.zeros((128, 128), dtype=bool)
            dr = np.zeros((128, 128), dtype=np.int64)
            for a in range(2):
                for b in range(2):
                    rho = 2 * m + a
                    i = 2 * n + b
                    ok = (r0[i] <= rho) and (rho <= r0[i] + 7)
                    if ok:
                        valid[a * 64:(a + 1) * 64, b * 64:(b + 1) * 64] = col_ok.T
                        dr[a * 64:(a + 1) * 64, b * 64:(b + 1) * 64] = rho - i + 7
            key = (m - n, valid.tobytes())
            if key not in tabs:
                tabs[key] = len(tab_list)
                tab_list.append((dr, valid))
            pl.append((m, tabs[key]))
        pairs.append(pl)
    dcidx = np.zeros((128, 128), dtype=np.int64)
    for a in range(2):
        for b in range(2):
            dcidx[a * 64:(a + 1) * 64, b * 64:(b + 1) * 64] = dc.T
    return tab_list, pairs, dcidx


D_TABS, D_PAIRS, D_DC = d_tables()
NTAB = len(D_TABS)

C_IDENT = 0
C_BD64 = 128
C_BD32 = 256
C_PSW64 = 384
C_PSW32 = 512
C_ONES = 640
C_MLO = 768
C_MHI = 896
C_NLO4 = 1024
C_NHI4 = 1536
C_DVALID = 2048
NCONST = C_DVALID + NTAB * 128


def make_consts():
    c = np.zeros((128, NCONST), dtype=np.float32)
    p = np.arange(128)
    c[:, C_IDENT:C_IDENT + 128] = np.eye(128, dtype=np.float32)
    c[:, C_BD64:C_BD64 + 128] = (p[:, None] // 64 == p[None, :] // 64)
    c[:, C_BD32:C_BD32 + 128] = (p[:, None] // 32 == p[None, :] // 32)
    part64 = (p // 64) * 64 + (p % 64 + 32) % 64
    part32 = (p // 32) * 32 + (p % 32 + 16) % 32
    c[:, C_PSW64:C_PSW64 + 128] = (p[:, None] == part64[None, :])
    c[:, C_PSW32:C_PSW32 + 128] = (p[:, None] == part32[None, :])
    c[:, C_ONES:C_ONES + 128] = 1.0
    c[:, C_MLO:C_MLO + 128] = (p[:, None] >= p[None, :])
    c[:, C_MHI:C_MHI + 128] = (p[:, None] <= p[None, :])
    for r_ in range(4):
        c[:, C_NLO4 + r_ * 128:C_NLO4 + (r_ + 1) * 128] = (c[:, C_MLO:C_MLO + 128] - 1.0) * 30000.0
        c[:, C_NHI4 + r_ * 128:C_NHI4 + (r_ + 1) * 128] = (c[:, C_MHI:C_MHI + 128] - 1.0) * 30000.0
    for t, (dr, valid) in enumerate(D_TABS):
        c[:, C_DVALID + t * 128:C_DVALID + (t + 1) * 128] = valid
    return c


def host_prep(inp):
    f = lambda a: np.ascontiguousarray(np.asarray(a, dtype=np.float32))
    out = {}
    out['w_in'] = f(inp['w_in'])
    out['w_ba'] = f(inp['w_branch_a'])
    out['w_bb'] = f(inp['w_branch_b'])
    out['w_bc'] = f(inp['w_branch_c'])
    out['w_bd'] = f(inp['w_branch_d'])
    out['w_out'] = f(inp['w_out'])
    out['w_up'] = f(inp['w_up'])
    out['w_down'] = f(inp['w_down'])
    out['gb_attn'] = f(np.broadcast_to(f(inp['attn_norm_g'])[:, None, :], (2, 128, DM)))
    out['gb_mlp'] = f(np.broadcast_to(f(inp['mlp_norm_g'])[:, None, :], (2, 128, DM)))
    gcol = np.zeros((2, 128, NQKB), dtype=np.float32)
    aq, bq, cq, dq = f(inp['a_qk_norm_g']), f(inp['b_qk_norm_g']), f(inp['c_qk_norm_g']), f(inp['d_qk_norm_g'])
    for l in range(2):
        for b in range(4):
            gcol[l, :, b] = np.tile(aq[l, 0], 2)
        gcol[l, :, 4] = np.tile(aq[l, 1], 2)
        for b in range(5, 8):
            gcol[l, :, b] = np.tile(bq[l, 0], 2)
        for b in range(8, 11):
            gcol[l, :, b] = np.tile(bq[l, 1], 2)
        for b in range(11, 13):
            gcol[l, :, b] = np.tile(cq[l, 0], 4)
        for b in range(13, 15):
            gcol[l, :, b] = np.tile(cq[l, 1], 4)
        for b in range(15, 17):
            gcol[l, :, b] = np.tile(dq[l, 0], 2)
        for b in range(17, 19):
            gcol[l, :, b] = np.tile(dq[l, 1], 2)
    out['gcol'] = gcol
    out['sinkb'] = f(np.broadcast_to(f(inp['a_sink'])[:, None, :], (2, 128, 8)))
    out['lamb'] = f(np.broadcast_to(f(inp['c_lambda']).reshape(2, 1, 128), (2, 128, 128)))
    out['gsub'] = f(np.tile(f(inp['c_subln_g']), (1, 2)).reshape(2, 128, 1))
    rpb = f(inp['d_rel_bias'])
    db = np.zeros((2, 4, 128, NTAB, 128), dtype=np.float32)
    for t, (dr, valid) in enumerate(D_TABS):
        db[:, :, :, t, :] = rpb[:, :, dr, D_DC]
    out['dbias'] = db.reshape(2, 4, 128, NTAB * 128)
    c64, s64 = rope_tabs(64)
    c32, s32 = rope_tabs(32)
    p4, p16 = perm_tokens(4), perm_tokens(16)
    out['rope'] = np.ascontiguousarray(np.stack([c64, s64, c64[:, p4], s64[:, p4], c64[:, p16], s64[:, p16], c32, s32], 0))
    out['consts'] = make_consts()
    return out


PARAM_SHAPES = {
    'w_in': [2, DM, INC], 'w_ba': [2, 512, DM], 'w_bb': [2, 128, DM], 'w_bc': [2, 256, DM], 'w_bd': [2, 256, DM],
    'w_out': [2, DM, DM], 'w_up': [2, DM, 4096], 'w_down': [2, 4096, DM],
    'gb_attn': [2, 128, DM], 'gb_mlp': [2, 128, DM], 'gcol': [2, 128, NQKB], 'sinkb': [2, 128, 8],
    'lamb': [2, 128, 128], 'gsub': [2, 128, 1], 'dbias': [2, 4, 128, NTAB * 128],
    'rope': [8, 128, S], 'consts': [128, NCONST],
}


class Ctx:
    pass


def sl(start, n, step):
    return slice(start, start + (n - 1) * step + 1, step)


def ring(lst, i):
    return lst[i % len(lst)]


def phase_p1(C, l, xin):
    P, ar, D = C.P, C.ar, C.D
    nb0 = len(P.bufs)
    m0 = ar.mark()
    hT = ar.alloc([8, S], BF16)
    b_hT = P.buf('hT')
    gb = ar.alloc([DM], F32)
    b_gb = P.buf('gb')
    gcol = ar.alloc([NQKB], F32)
    b_gcol = P.buf('gcol')
    P.load(gb, D['gb_attn'][l], b_gb)
    P.load(gcol, D['gcol'][l], b_gcol)
    m1 = ar.mark()
    xt = [ar.alloc([DM], F32) for _ in range(3)]
    bx = P.bufs_n('xt', 3)
    junk = ar.alloc([DM], BF16)
    b_junk = P.buf('junk')
    ss = [ar.alloc([1], F32) for _ in range(4)]
    bss = P.bufs_n('ss', 4)
    hb = [ar.alloc([DM], BF16) for _ in range(4)]
    bhb = P.bufs_n('hb', 4)
    for tt in range(NT):
        i, j = tt % 3, tt % 4
        P.load(xt[i], xin[tt * 128:(tt + 1) * 128, :], bx[i])
        P.act(junk, xt[i], AF.Square, R=[bx[i]], W=[b_junk, bss[j]], accum_out=ss[j])
        P.act(ss[j], ss[j], AF.Ln, R=[bss[j]], W=[bss[j]], scale=1.0 / DM, bias=EPS)
        P.act(ss[j], ss[j], AF.Exp, R=[bss[j]], W=[bss[j]], scale=-0.5)
        P.stt(hb[j], xt[i], ss[j], gb, ALU.mult, ALU.mult, R=[bx[i], bss[j], b_gb], W=[bhb[j]])
        pbv = C.pb[j].bitcast(BF16)
        for kc in range(8):
            P.tr(pbv[:, kc * 128:(kc + 1) * 128], hb[j][:, kc * 128:(kc + 1) * 128], C.ident, R=[bhb[j], C.b_const], W=[C.bpb[j]])
        P.cp('act' if tt % 2 else 'dve', hT[:, :, tt * 128:(tt + 1) * 128], pbv.rearrange("p (k t) -> p k t", k=8), R=[C.bpb[j]], W=[b_hT])
    for kc in range(8):
        P.store(D['hT_d'][kc * 128:(kc + 1) * 128, :], hT[:, kc, :], b_hT)
    ar.release(m1)
    tabs2 = [ar.alloc([2, S], F32) for _ in range(2)]
    b_tabs2 = P.bufs_n('ropetab', 2)
    wq = [ar.alloc([8, 128], BF16) for _ in range(2)]
    bwq = P.bufs_n('wq', 2)
    NB = 4
    sq = [ar.alloc([512], BF16) for _ in range(NB)]
    bsq = P.bufs_n('sq', NB)
    xg = [ar.alloc([512], BF16) for _ in range(NB)]
    bxg = P.bufs_n('xg', NB)
    rs = [ar.alloc([512], F32) for _ in range(NB)]
    brs = P.bufs_n('rs', NB)
    ta = [ar.alloc([512], F32) for _ in range(NB)]
    bta = P.bufs_n('ta', NB)
    tb = [ar.alloc([512], F32) for _ in range(NB)]
    btb = P.bufs_n('tb', NB)
    ob = [ar.alloc([512], BF16) for _ in range(NB)]
    bob = P.bufs_n('ob', NB)
    w_in = D['w_in'][l].rearrange("(kc p) c -> p kc c", p=128)
    blk_order = [0, 1, 2, 3, 4, 5, 8, 6, 9, 7, 10, 11, 12, 13, 14, 15, 16, 17, 18]
    items = [(blk, tc) for blk in blk_order for tc in range(8)]
    TABK = {'n64': 0, 'p4': 2, 'p16': 4, 'n32': 6, 'd': None}
    variants = [0, 2, 4, 6]
    state = {'vi': -1}

    def load_tab(vi):
        if vi < len(variants):
            tb_, bt_ = tabs2[vi % 2], b_tabs2[vi % 2]
            P.load(tb_[:, 0, :], D['rope'][variants[vi]], bt_)
            P.load(tb_[:, 1, :], D['rope'][variants[vi] + 1], bt_)
    load_tab(0)

    def tokf(kind, tc):
        if kind == 'p4':
            r, h0 = tc // 2, (tc % 2) * 512
            return lambda kc: hT[:, kc, sl(r + 4 * h0, 512, 4)]
        if kind == 'p16':
            return lambda kc: hT[:, kc, :].rearrange("p (m r) -> p r m", r=16)[:, 2 * tc:2 * tc + 2, :]
        return lambda kc: hT[:, kc, tc * 512:(tc + 1) * 512]

    def s0(itm, t):
        blk, tc = itm
        coff, kind = QK_BLOCKS[blk]
        wi = blk % 2
        if tc == 0:
            P.dma('pool', wq[wi], w_in[:, :, coff:coff + 128], bwq[wi], W=(bwq[wi],))
        tok = tokf(kind, tc)
        pA, bA = C.pb[t % 3], C.bpb[t % 3]
        for kc in range(8):
            P.mm(pA[:, :], wq[wi][:, kc, :], tok(kc), start=(kc == 0), stop=(kc == 7), R=[bwq[wi], b_hT], W=[bA])

    def s1(itm, t, defer):
        blk, tc = itm
        coff, kind = QK_BLOCKS[blk]
        tabkind = TABK[kind]
        if tabkind is not None:
            vi = variants.index(tabkind)
            if vi != state['vi']:
                state['vi'] = vi
                load_tab(vi + 1)
            tabs, b_tabs = tabs2[vi % 2], b_tabs2[vi % 2]
        dh = 32 if kind == 'n32' else 64
        bd = C.bd32 if dh == 32 else C.bd64
        psw = C.psw32 if dh == 32 else C.psw64
        k = t % NB
        pA, pS, pR = C.pb[t % 3], C.pb[3 + (t % 2)], C.pb[5 + (t % 3)]
        bA, bS, bR = C.bpb[t % 3], C.bpb[3 + (t % 2)], C.bpb[5 + (t % 3)]
        P.act(sq[k], pA[:, :], AF.Square, R=[bA], W=[bsq[k]])
        P.act(xg[k], pA[:, :], AF.Copy, R=[bA, b_gcol], W=[bxg[k]], scale=gcol[:, blk:blk + 1])
        P.mm(pS[:, :], bd, sq[k], R=[bsq[k], C.b_const], W=[bS])
        if kind != 'd':
            P.mm(pR[:, :], psw, xg[k], R=[bxg[k], C.b_const], W=[bR])
        csl = slice(tc * 512, (tc + 1) * 512)
        if kind != 'd':
            P.tt('pool', ta[k], xg[k], tabs[:, 0, csl], ALU.mult, R=[bxg[k], b_tabs], W=[bta[k]])
            P.tt('dve', tb[k], pR[:, :], tabs[:, 1, csl], ALU.mult, R=[bR, b_tabs], W=[btb[k]])

        def s1b():
            P.act(rs[k], pS[:, :], AF.Ln, R=[bS], W=[brs[k]], scale=1.0 / dh, bias=EPS)
            P.act(rs[k], rs[k], AF.Exp, R=[brs[k]], W=[brs[k]], scale=-0.5)
            if kind == 'd':
                P.tt('dve', ob[k], xg[k], rs[k], ALU.mult, R=[bxg[k], brs[k]], W=[bob[k]])
            else:
                P.tt('pool', ta[k], ta[k], tb[k], ALU.add, R=[bta[k], btb[k]], W=[bta[k]])
                P.tt('dve', ob[k], ta[k], rs[k], ALU.mult, R=[bta[k], brs[k]], W=[bob[k]])
            P.store(D['QKT_d'][blk * 128:(blk + 1) * 128, csl], ob[k], bob[k])
        defer(1, s1b)

    run_pipeline(items, 2, s0, s1)
    ar.release(m1)
    wv = ar.alloc([8, 1024], BF16)
    b_wv = P.buf('wv')
    o = 0
    for (coff, n) in VNAT_COLS + [(1536 + 128, 256)]:
        P.dma('pool', wv[:, :, o:o + n], w_in[:, :, coff:coff + n], b_wv, W=(b_wv,))
        o += n
    vn = [ar.alloc([12, 65], BF16) for _ in range(2)]
    bvn = P.bufs_n('vn', 2)
    vb = [ar.alloc([2, 2, 65], BF16) for _ in range(2)]
    bvb = P.bufs_n('vbp', 2)
    for j in range(2):
        P.memset('pool', vn[j][:, :, 64:65], 1.0, W=[bvn[j]])
        P.memset('pool', vb[j][:, :, :, 64:65], 1.0, W=[bvb[j]])
    p4, p16 = perm_tokens(4), perm_tokens(16)
    for tt in range(NT):
        j = tt % 2
        pa, pbk, pc = C.pb[j * 3], C.pb[j * 3 + 1], C.pb[j * 3 + 2]
        ba, bb_, bc = C.bpb[j * 3], C.bpb[j * 3 + 1], C.bpb[j * 3 + 2]
        for kc in range(8):
            P.mm(pa[:, :], hT[:, kc, tt * 128:(tt + 1) * 128], wv[:, kc, 0:512], start=(kc == 0), stop=(kc == 7), R=[b_hT, b_wv], W=[ba])
        for kc in range(8):
            P.mm(pbk[:, 0:256], hT[:, kc, tt * 128:(tt + 1) * 128], wv[:, kc, 512:768], start=(kc == 0), stop=(kc == 7), R=[b_hT, b_wv], W=[bb_])
        t4 = int(p4[tt * 128])
        t16 = int(p16[tt * 128])
        for kc in range(8):
            P.mm(pc[:, 0:128], hT[:, kc, sl(t4, 128, 4)], wv[:, kc, 768:896], start=(kc == 0), stop=(kc == 7), R=[b_hT, b_wv], W=[bc])
        for kc in range(8):
            P.mm(pc[:, 128:256], hT[:, kc, sl(t16, 128, 16)], wv[:, kc, 896:1024], start=(kc == 0), stop=(kc == 7), R=[b_hT, b_wv], W=[bc])
        P.cp('act', vn[j][:, 0:8, 0:64], pa[:, :].rearrange("p (h d) -> p h d", h=8), R=[ba], W=[bvn[j]])
        P.cp('dve', vn[j][:, 8:12, 0:64], pbk[:, 0:256].rearrange("p (h d) -> p h d", h=4), R=[bb_], W=[bvn[j]])
        P.cp('dve', vb[j][:, :, :, 0:64], pc[:, 0:256].rearrange("p (g h d) -> p g h d", g=2, h=2), R=[bc], W=[bvb[j]])
        P.store(D['Vnat_d'][tt * 128:(tt + 1) * 128, :], vn[j].rearrange("p h d -> p (h d)"), bvn[j])
        for g in range(2):
            P.store(D['Vb_d'][g, tt * 128:(tt + 1) * 128, :], vb[j][:, g].rearrange("p h d -> p (h d)"), bvb[j])
    ar.release(m0)
    P.barrier()
    P.retire(nb0)


def run_pipeline(items, LA, s0, s1):
    n = len(items)
    pend = {}
    cur = [0]

    def defer(delay, fn):
        pend.setdefault(cur[0] + delay, []).append(fn)

    for t in range(n + LA):
        cur[0] = t
        if t < n:
            s0(items[t], t)
        if t >= LA:
            s1(items[t - LA], t - LA, defer)
        for fn in pend.pop(t, []):
            fn()
    while pend:
        t2 = min(pend)
        cur[0] = t2
        for fn in pend.pop(t2):
            fn()


def finalize_norm(C, acc, bacc, n, dst, bdst, shape3=None, esink=None, tagk=0, defer=None, after=None, dl=(1, 3, 5, 6)):
    P = C.P
    k = tagk % 2
    r, rhi, rlo = C.fr[k], C.frhi[k], C.frlo[k]
    br = C.bfr[k]
    bc, bbc = C.pb[7], C.bpb[7]
    bcs, bbcs = C.fbcs[k], C.bfbcs[k]
    src = acc[64:65, 0:n]

    def g0():
        if esink is not None:
            j, q = shape3
            P.tt('dve', r[64:65, 0:n].rearrange("p (j q) -> p j q", j=j), src.rearrange("p (j q) -> p j q", j=j),
                 esink, ALU.add, R=[bacc, C.b_esk], W=[br])
            P.act(r[64:65, 0:n], r[64:65, 0:n], AF.Ln, R=[br], W=[br])
        else:
            P.act(r[64:65, 0:n], src, AF.Ln, R=[bacc], W=[br])
        P.act(r[64:65, 0:n], r[64:65, 0:n], AF.Exp, R=[br], W=[br], scale=-1.0)
        P.cp('dve', rhi[64:65, 0:n], r[64:65, 0:n], R=[br], W=[br])
        P.tt('dve', rlo[64:65, 0:n], r[64:65, 0:n], rhi[64:65, 0:n], ALU.subtract, R=[br], W=[br])

    def g1():
        P.mm(bc[0:64, 0:n], C.ones_bf[64:65, 0:64], rhi[64:65, 0:n], start=True, stop=False, R=[br, C.b_const], W=[bbc])
        P.mm(bc[0:64, 0:n], C.ones_bf[64:65, 0:64], rlo[64:65, 0:n], start=False, stop=True, R=[br, C.b_const], W=[bbc])

    def g2():
        P.cp('act', bcs[0:64, 0:n], bc[0:64, 0:n], R=[bbc], W=[bbcs])

    def g3():
        a0 = acc[0:64, 0:n]
        b0 = bcs[0:64, 0:n]
        if shape3 is not None:
            j, q = shape3
            a0 = a0.rearrange("p (j q) -> p j q", j=j)
            b0 = b0.rearrange("p (j q) -> p j q", j=j)
        P.tt('dve', dst, a0, b0, ALU.mult, R=[bacc, bbcs], W=[bdst])
        if after is not None:
            after()

    if defer is None:
        g0(); g1(); g2(); g3()
    else:
        defer(dl[0], g0); defer(dl[1], g1); defer(dl[2], g2); defer(dl[3], g3)


def alloc_fin(C):
    ar, P = C.ar, C.P
    C.fr = [ar.alloc([512], F32) for _ in range(2)]
    C.frhi = [ar.alloc([512], BF16) for _ in range(2)]
    C.frlo = [ar.alloc([512], BF16) for _ in range(2)]
    C.bfr = P.bufs_n('fr', 2)
    C.fbcs = [ar.alloc([512], F32) for _ in range(2)]
    C.bfbcs = P.bufs_n('fbcs', 2)


def mixer_a(C, l):
    P, ar, D = C.P, C.ar, C.D
    nb0 = len(P.bufs)
    m0 = ar.mark()
    alloc_fin(C)
    QT = ar.alloc([4, S], BF16)
    bQT = P.buf('aQT')
    KT = ar.alloc([2, S], BF16)
    bKT = P.buf('aKT')
    V = ar.alloc([NT, 130], BF16)
    bV = P.buf('aV')
    yst = ar.alloc([4, S], BF16)
    byst = P.buf('ayst')
    esk = ar.alloc([8], F32)
    C.b_esk = P.buf('esk')
    P.load(esk, D['sinkb'][l], C.b_esk)
    P.act(esk, esk, AF.Exp, R=[C.b_esk], W=[C.b_esk])
    for g in range(2):
        for j in range(4):
            h = 4 * g + j
            P.load(QT[g * 64:(g + 1) * 64, j, :], D['QKT_d'][h * 64:(h + 1) * 64, :], bQT)
    P.memset('pool', KT, 0.0, W=[bKT])
    for g in range(2):
        P.load(KT[g * 64:(g + 1) * 64, g, :], D['QKT_d'][512 + g * 64:512 + (g + 1) * 64, :], bKT)
    P.load(V, D['Vnat_d'].rearrange("(t p) c -> p t c", p=128)[:, :, 0:130], bV)
    NP = 4
    pt = [ar.alloc([512], BF16) for _ in range(NP)]
    bpt = P.bufs_n('apt', NP)
    mlo = C.cbf[:, C_MLO:C_MLO + 128].unsqueeze(1).broadcast_to([128, 4, 128])
    mhi = C.cbf[:, C_MHI:C_MHI + 128].unsqueeze(1).broadcast_to([128, 4, 128])
    items = []
    fi = 0
    for g in range(2):
        for n in range(NT):
            ms = [m for m in (n - 1, n, n + 1) if 0 <= m < NT]
            for idx, m in enumerate(ms):
                items.append((g, n, m, idx == 0, idx == len(ms) - 1, fi))
            fi += 1

    def s0(itm, t):
        g, n, m, first, last, f = itm
        ps = slice(g * 64, (g + 1) * 64)
        st, bst = C.pb[t % 4], C.bpb[t % 4]
        P.mm(st[:, :].rearrange("p (j q) -> p j q", j=4), KT[:, g, m * 128:(m + 1) * 128], QT[:, :, n * 128:(n + 1) * 128],
             start=True, stop=(m == n), R=[bKT, bQT], W=[bst])
        if m != n:
            nk = C_NLO4 if m < n else C_NHI4
            P.mm(st[:, :], C.ident, C.cbf[:, nk:nk + 512], start=False, stop=True, R=[C.b_const], W=[bst])

    def s1(itm, t, defer):
        g, n, m, first, last, f = itm
        k = t % NP
        st, bst = C.pb[t % 4], C.bpb[t % 4]
        acc, bacc = C.pb[4 + (f % 3)], C.bpb[4 + (f % 3)]
        P.act(pt[k], st[:, :], AF.Exp, R=[bst], W=[bpt[k]], scale=0.125)
        P.mm(acc[0:65, :], V[:, m, g * 65:(g + 1) * 65], pt[k], start=first, stop=last, R=[bV, bpt[k]], W=[bacc])
        if last:
            es = esk[64:65, 4 * g:4 * g + 4].unsqueeze(2).broadcast_to([1, 4, 128])
            def after(g=g, n=n):
                if n == NT - 1:
                    for j in range(4):
                        h = 4 * g + j
                        P.store(D['YT_d'][h * 64:(h + 1) * 64, :], yst[0:64, j, :], byst)
            finalize_norm(C, acc, bacc, 512, yst[0:64, :, n * 128:(n + 1) * 128], byst, shape3=(4, 128), esink=es, tagk=f,
                          defer=defer, after=after, dl=(1, 2, 3, 4))

    run_pipeline(items, 3, s0, s1)
    ar.release(m0)
    P.barrier()
    P.retire(nb0)


def mixer_b(C, l):
    P, ar, D = C.P, C.ar, C.D
    nb0 = len(P.bufs)
    m0 = ar.mark()
    alloc_fin(C)
    accN = ar.alloc([2, S], F32)
    baccN = P.buf('baccN')
    yst = ar.alloc([2, S], BF16)
    byst = P.buf('byst')
    QTs = [ar.alloc([S], BF16) for _ in range(3)]
    KTs = [ar.alloc([2, S], BF16) for _ in range(3)]
    Vs = [ar.alloc([NT, 130], BF16) for _ in range(3)]
    bQ = P.bufs_n('bQT', 3)
    bK = P.bufs_n('bKT', 3)
    bVv = P.bufs_n('bV', 3)
    NP = 4
    pt = [ar.alloc([512], BF16) for _ in range(NP)]
    bpt = P.bufs_n('bpt', NP)
    items = []
    ai = 0
    for gi, d in enumerate(B_DIL):
        L = S // d
        nb = L // 128
        QT, KT, V = QTs[gi], KTs[gi], Vs[gi]
        bq, bk, bv = bQ[gi], bK[gi], bVv[gi]
        P.load(QT, D['QKT_d'][(5 + gi) * 128:(6 + gi) * 128, :], bq)
        P.memset('pool' if gi % 2 else 'dve', KT, 0.0, W=[bk])
        for jh_ in range(2):
            P.load(KT[jh_ * 64:(jh_ + 1) * 64, jh_, :], D['QKT_d'][(8 + gi) * 128 + jh_ * 64:(8 + gi) * 128 + (jh_ + 1) * 64, :], bk)
        if gi == 0:
            P.load(V, D['Vnat_d'].rearrange("(t p) c -> p t c", p=128)[:, :, 650:780], bv)
        else:
            P.load(V, D['Vb_d'][gi - 1].rearrange("(t p) c -> p t c", p=128), bv)
        for jh in range(2):
            for r in range(d):
                base = r * L
                qblocks = []
                for n_ in range(-1, nb):
                    q0 = 128 * n_ + 64
                    qa, qb = max(q0, 0), min(q0 + 128, L)
                    tiles = []
                    if n_ >= 0:
                        tiles.append((n_, C_MLO))
                    if n_ + 1 < nb:
                        tiles.append((n_ + 1, C_MHI))
                    qblocks.append((qa, qb - qa, qa - q0, tiles))
                for g0 in range(0, len(qblocks), 4):
                    grp = qblocks[g0:g0 + 4]
                    ncols = sum(q[1] for q in grp)
                    pstart = grp[0][0]
                    col = 0
                    nsub = (len(grp) + 1) // 2
                    for bi in range(0, len(grp), 2):
                        sub = grp[bi:bi + 2]
                        plist = []
                        sc = 0
                        for (qa, nq, aoff, tiles) in sub:
                            for ti, (m, mk) in enumerate(tiles):
                                plist.append((sc, nq, aoff, mk, base // 128 + m, col, ti == 0, ti == len(tiles) - 1, base + qa))
                                sc += nq
                            col += nq
                        endinfo = None
                        if bi // 2 == nsub - 1:
                            endinfo = (gi, d, r, pstart, ncols)
                        items.append((QT, KT, V, bq, bk, bv, jh, plist, sc, ai, endinfo))
                    ai += 1

    def s0(itm, t):
        QT, KT, V, bq, bk, bv, jh, plist, sc, a_, endinfo = itm
        ps = slice(jh * 64, (jh + 1) * 64)
        st, bst = C.pb[t % 4], C.bpb[t % 4]
        for (s0_, nq, aoff, mk, tg, c0, first, last, qpos) in plist:
            P.mm(st[:, s0_:s0_ + nq], KT[:, jh, tg * 128:(tg + 1) * 128], QT[:, qpos:qpos + nq], start=True, stop=False, R=[bk, bq], W=[bst])
            nk = C_NLO4 if mk == C_MLO else C_NHI4
            P.mm(st[:, s0_:s0_ + nq], C.ident, C.cbf[:, nk + aoff:nk + aoff + nq], start=False, stop=True, R=[C.b_const], W=[bst])

    def s1(itm, t, defer):
        QT, KT, V, bq, bk, bv, jh, plist, sc, a_, endinfo = itm
        k = t % NP
        st, bst = C.pb[t % 4], C.bpb[t % 4]
        acc, bacc = C.pb[4 + (a_ % 3)], C.bpb[4 + (a_ % 3)]
        P.act(pt[k][:, 0:sc], st[:, 0:sc], AF.Exp, R=[bst], W=[bpt[k]], scale=0.125)
        for (s0_, nq, aoff, mk, tg, c0, first, last, _) in plist:
            P.mm(acc[0:65, c0:c0 + nq], V[:, tg, jh * 65:(jh + 1) * 65], pt[k][:, s0_:s0_ + nq], start=first, stop=last,
                 R=[bv, bpt[k]], W=[bacc])
        if endinfo is not None:
            gi, d, r, pstart, ncols = endinfo
            if gi == 0:
                P.cp('act', accN[0:65, jh, pstart:pstart + ncols], acc[0:65, 0:ncols], R=[bacc], W=[baccN])
            else:
                t0 = r + d * pstart
                view = accN[0:65, jh, sl(t0, ncols, d)]
                P.tt('dve', view, view, acc[0:65, 0:ncols], ALU.add, R=[bacc, baccN], W=[baccN])

    run_pipeline(items, 3, s0, s1)
    fi = 0
    for jh in range(2):
        for ch in range(8):
            cs = slice(ch * 512, (ch + 1) * 512)
            k = fi % 2
            r, rhi, rlo, br = C.fr[k], C.frhi[k], C.frlo[k], C.bfr[k]
            bc, bbc = C.pb[7], C.bpb[7]
            P.act(r[64:65, :], accN[64:65, jh, cs], AF.Ln, R=[baccN], W=[br])
            P.act(r[64:65, :], r[64:65, :], AF.Exp, R=[br], W=[br], scale=-1.0)
            P.cp('dve', rhi[64:65, :], r[64:65, :], R=[br], W=[br])
            P.tt('dve', rlo[64:65, :], r[64:65, :], rhi[64:65, :], ALU.subtract, R=[br], W=[br])
            P.mm(bc[0:64, :], C.ones_bf[64:65, 0:64], rhi[64:65, :], start=True, stop=False, R=[br, C.b_const], W=[bbc])
            P.mm(bc[0:64, :], C.ones_bf[64:65, 0:64], rlo[64:65, :], start=False, stop=True, R=[br, C.b_const], W=[bbc])
            P.tt('dve', yst[0:64, jh, cs], accN[0:64, jh, cs], bc[0:64, :], ALU.mult, R=[baccN, bbc], W=[byst])
            fi += 1
        P.store(D['YT_d'][512 + jh * 64:512 + (jh + 1) * 64, :], yst[0:64, jh, :], byst)
    ar.release(m0)
    P.barrier()
    P.retire(nb0)


def mixer_d(C, l):
    P, ar, D = C.P, C.ar, C.D
    nb0 = len(P.bufs)
    m0 = ar.mark()
    alloc_fin(C)
    QT = ar.alloc([2, S], BF16)
    KT = ar.alloc([4, S], BF16)
    V = ar.alloc([NT, 260], BF16)
    bQT, bKT, bV = P.buf('dQT'), P.buf('dKT'), P.buf('dV')
    yst = ar.alloc([4, S], BF16)
    byst = P.buf('dyst')
    EB = ar.alloc([4, NTAB * 128], BF16)
    bEB = P.buf('dEB')
    tmpb = [ar.alloc([NTAB * 128], F32) for _ in range(2)]
    btmp = P.bufs_n('dtmp', 2)
    P.memset('pool', KT[:, 0:2, :], 0.0, W=[bKT])
    P.memset('dve', KT[:, 2:4, :], 0.0, W=[bKT])
    for i in range(2):
        P.load(QT[:, i, :], D['QKT_d'][(15 + i) * 128:(16 + i) * 128, :], bQT)
    for h_ in range(4):
        r0_ = (h_ % 2) * 64
        P.load(KT[r0_:r0_ + 64, h_, :], D['QKT_d'][17 * 128 + h_ * 64:17 * 128 + (h_ + 1) * 64, :], bKT)
    P.load(V, D['Vnat_d'].rearrange("(t p) c -> p t c", p=128)[:, :, 390:650], bV)
    for h in range(4):
        P.load(tmpb[h % 2], D['dbias'][l, h], btmp[h % 2])
        P.act(tmpb[h % 2], tmpb[h % 2], AF.Exp, R=[btmp[h % 2]], W=[btmp[h % 2]])
        P.tt('dve', EB[:, h, :], tmpb[h % 2], C.cbf[:, C_DVALID:C_DVALID + NTAB * 128], ALU.mult, R=[btmp[h % 2], C.b_const], W=[bEB])
    NP = 4
    pt = [ar.alloc([512], BF16) for _ in range(NP)]
    bpt = P.bufs_n('dpt', NP)
    items = []
    fi = 0
    for h in range(4):
        for n4 in range(8):
            plist = []
            for n in range(n4 * 4, n4 * 4 + 4):
                pl = D_PAIRS[n]
                for pi, (m, tab) in enumerate(pl):
                    plist.append((n, m, tab, pi == 0, pi == len(pl) - 1))
            nch = (len(plist) + 3) // 4
            for ci in range(nch):
                items.append((h, n4, plist[ci * 4:ci * 4 + 4], fi, ci == nch - 1))
            fi += 1

    def s0(itm, t):
        h, n4, chunk, f, endg = itm
        bq = h // 2
        ps = slice((h % 2) * 64, (h % 2 + 1) * 64)
        st, bst = C.pb[t % 4], C.bpb[t % 4]
        for i, (n, m, tab, first, last) in enumerate(chunk):
            P.mm(st[:, i * 128:(i + 1) * 128], KT[:, h, m * 128:(m + 1) * 128], QT[:, bq, n * 128:(n + 1) * 128],
                 R=[bKT, bQT], W=[bst])

    def s1(itm, t, defer):
        h, n4, chunk, f, endg = itm
        k = t % NP
        st, bst = C.pb[t % 4], C.bpb[t % 4]
        acc, bacc = C.pb[4 + (f % 3)], C.bpb[4 + (f % 3)]
        used = len(chunk) * 128
        P.act(pt[k][:, 0:used], st[:, 0:used], AF.Exp, R=[bst], W=[bpt[k]], scale=0.125)
        for i, (n, m, tab, first, last) in enumerate(chunk):
            P.tt('pool' if i % 2 else 'dve', pt[k][:, i * 128:(i + 1) * 128], pt[k][:, i * 128:(i + 1) * 128],
                 EB[:, h, tab * 128:(tab + 1) * 128], ALU.mult, R=[bpt[k], bEB], W=[bpt[k]])
        for i, (n, m, tab, first, last) in enumerate(chunk):
            qc = (n % 4) * 128
            P.mm(acc[0:65, qc:qc + 128], V[:, m, h * 65:(h + 1) * 65], pt[k][:, i * 128:(i + 1) * 128], start=first, stop=last,
                 R=[bV, bpt[k]], W=[bacc])
        if endg:
            def after(h=h, n4=n4):
                if n4 == 7:
                    P.store(D['YT_d'][896 + h * 64:896 + (h + 1) * 64, :], yst[0:64, h, :], byst)
            finalize_norm(C, acc, bacc, 512, yst[0:64, h, n4 * 512:(n4 + 1) * 512], byst, tagk=f, defer=defer, after=after, dl=(1, 2, 3, 4))

    run_pipeline(items, 3, s0, s1)
    ar.release(m0)
    P.barrier()
    P.retire(nb0)


def mixer_c(C, l):
    P, ar, D = C.P, C.ar, C.D
    lam_init = 0.8 - 0.6 * math.exp(-0.3 * l)
    nb0 = len(P.bufs)
    m0 = ar.mark()
    alloc_fin(C)
    QT = ar.alloc([2, S], BF16)
    KT = ar.alloc([8, S], BF16)
    V = ar.alloc([NT, 260], BF16)
    bQT, bKT, bV = P.buf('cQT'), P.buf('cKT'), P.buf('cV')
    yst = ar.alloc([4, S], BF16)
    byst = P.buf('cyst')
    P.memset('pool', KT[:, 0:4, :], 0.0, W=[bKT])
    P.memset('dve', KT[:, 4:8, :], 0.0, W=[bKT])
    for g2 in range(2):
        P.load(QT[:, g2, :], D['QKT_d'][(11 + g2) * 128:(12 + g2) * 128, :], bQT)
    for b in range(8):
        sl_ = (b % 4) * 32
        row = (b // 4) * 128 + sl_
        P.load(KT[sl_:sl_ + 32, b, :], D['QKT_d'][13 * 128 + row:13 * 128 + row + 32, :], bKT)
    P.load(V, D['Vnat_d'].rearrange("(t p) c -> p t c", p=128)[:, :, 130:390], bV)
    lamb = ar.alloc([128], F32)
    blam = P.buf('lam')
    lt = ar.alloc([2, 32], F32)
    l2 = ar.alloc([2], F32)
    nlam = ar.alloc([1], F32)
    gsc = ar.alloc([1], F32)
    bgsc = P.buf('gsc')
    P.load(lamb, D['lamb'][l], blam)
    P.load(gsc, D['gsub'][l], bgsc)
    lv = lamb.rearrange("p (a b c) -> p a b c", a=2, b=2)
    P.tt('dve', lt, lv[:, :, 0, :], lv[:, :, 1, :], ALU.mult, R=[blam], W=[blam])
    P.op('dve', lambda e: e.reduce_sum(out=l2, in_=lt, axis=mybir.AxisListType.X), R=[blam], W=[blam])
    P.act(l2, l2, AF.Exp, R=[blam], W=[blam])
    P.tt('dve', nlam, l2[:, 0:1], l2[:, 1:2], ALU.subtract, R=[blam], W=[blam])
    P.ts('dve', nlam, nlam, lam_init, -1.0, ALU.add, ALU.mult, R=[blam], W=[blam])
    P.ts('dve', gsc, gsc, 1.0 - lam_init, None, ALU.mult, R=[bgsc], W=[bgsc])
    NP = 4
    pt = [ar.alloc([512], BF16) for _ in range(NP)]
    bpt = P.bufs_n('cpt', NP)
    to = [ar.alloc([512], F32) for _ in range(2)]
    t1 = [ar.alloc([512], F32) for _ in range(2)]
    sqb = [ar.alloc([512], BF16) for _ in range(2)]
    rsd = [ar.alloc([512], F32) for _ in range(2)]
    bto = P.bufs_n('cto', 2)
    scale = 32.0 ** -0.5
    items = []
    fi = 0
    for h in range(4):
        for Q in range(8):
            for c in range(2):
                for kt in range(NT):
                    items.append((h, Q, c, kt, fi))
            fi += 1

    def s0(itm, t):
        h, Q, c, kt, f = itm
        b = 2 * h + c
        st, bst = C.pb[t % 3], C.bpb[t % 3]
        P.mm(st[:, :], KT[:, b, kt * 128:(kt + 1) * 128], QT[:, b // 4, Q * 512:(Q + 1) * 512], R=[bKT, bQT], W=[bst])

    def s1(itm, t, defer):
        h, Q, c, kt, f = itm
        qs = slice(Q * 512, (Q + 1) * 512)
        k = t % NP
        st, bst = C.pb[t % 3], C.bpb[t % 3]
        a0 = 3 + 2 * (f % 2)
        accs = [C.pb[a0], C.pb[a0 + 1]]
        baccs = [C.bpb[a0], C.bpb[a0 + 1]]
        P.act(pt[k], st[:, :], AF.Exp, R=[bst], W=[bpt[k]], scale=scale)
        P.mm(accs[c][0:65, :], V[:, kt, h * 65:(h + 1) * 65], pt[k], start=(kt == 0), stop=(kt == NT - 1),
             R=[bV, bpt[k]], W=[baccs[c]])
        if not (c == 1 and kt == NT - 1):
            return
        k2 = f % 2
        bt = bto[k2]
        bc, bbc = C.pb[7], C.bpb[7]

        def f0():
            for c_ in range(2):
                r, rhi, rlo, br = C.fr[c_], C.frhi[c_], C.frlo[c_], C.bfr[c_]
                P.recip(r[64:65, :], accs[c_][64:65, :], R=[baccs[c_]], W=[br])
                if c_ == 1:
                    P.ts('dve', r[64:65, :], r[64:65, :], nlam[64:65, 0:1], None, ALU.mult, R=[br, blam], W=[br])
                P.cp('dve', rhi[64:65, :], r[64:65, :], R=[br], W=[br])
                P.tt('dve', rlo[64:65, :], r[64:65, :], rhi[64:65, :], ALU.subtract, R=[br], W=[br])

        def f1(c_):
            def fn():
                r, rhi, rlo, br = C.fr[c_], C.frhi[c_], C.frlo[c_], C.bfr[c_]
                P.mm(bc[0:64, :], C.ones_bf[64:65, 0:64], rhi[64:65, :], start=True, stop=False, R=[br, C.b_const], W=[bbc])
                P.mm(bc[0:64, :], C.ones_bf[64:65, 0:64], rlo[64:65, :], start=False, stop=True, R=[br, C.b_const], W=[bbc])
            return fn

        def f2(c_):
            def fn():
                P.cp('act', C.fbcs[c_][0:64, :], bc[0:64, :], R=[bbc], W=[C.bfbcs[c_]])
            return fn

        def f3():
            P.tt('dve', to[k2][0:64, :], accs[0][0:64, :], C.fbcs[0][0:64, :], ALU.mult, R=[baccs[0], C.bfbcs[0]], W=[bt])
            P.tt('dve', t1[k2][0:64, :], accs[1][0:64, :], C.fbcs[1][0:64, :], ALU.mult, R=[baccs[1], C.bfbcs[1], bt], W=[bt])
            P.tt('pool', to[k2][0:64, :], to[k2][0:64, :], t1[k2][0:64, :], ALU.add, R=[bt], W=[bt])

        def f4():
            P.act(sqb[k2][0:64, :], to[k2][0:64, :], AF.Square, R=[bt], W=[bt])

        def f5():
            P.mm(bc[0:64, :], C.ones_bf[0:64, 0:64], sqb[k2][0:64, :], R=[bt, C.b_const], W=[bbc])

        def f6():
            P.act(rsd[k2][0:64, :], bc[0:64, :], AF.Ln, R=[bbc], W=[bt], scale=1.0 / 64, bias=EPS)
            P.act(rsd[k2][0:64, :], rsd[k2][0:64, :], AF.Exp, R=[bt], W=[bt], scale=-0.5)

        def f7():
            P.stt(yst[0:64, h, qs], to[k2][0:64, :], gsc[0:64, 0:1], rsd[k2][0:64, :], ALU.mult, ALU.mult, R=[bt, bgsc], W=[byst])
            if Q == 7:
                P.store(D['YT_d'][640 + h * 64:640 + (h + 1) * 64, :], yst[0:64, h, :], byst)

        defer(1, f0)
        defer(4, f1(0))
        defer(5, f2(0))
        defer(6, f1(1))
        defer(7, f2(1))
        defer(9, f3)
        defer(12, f4)
        defer(13, f5)
        defer(15, f6)
        defer(17, f7)

    run_pipeline(items, 2, s0, s1)
    ar.release(m0)
    P.barrier()
    P.retire(nb0)


def phase_p3a(C, l, xin, xout):
    P, ar, D = C.P, C.ar, C.D
    nb0 = len(P.bufs)
    m0 = ar.mark()
    Wg = ar.alloc([8, 4096], BF16)
    Wb = ar.alloc([9, DM], BF16)
    Wo = ar.alloc([8, DM], BF16)
    bWg, bWb, bWo = P.buf('Wg'), P.buf('Wb'), P.buf('Wo')
    w_in = D['w_in'][l].rearrange("(kc p) c -> p kc c", p=128)
    for kc in range(8):
        P.dma('pool', Wg[:, kc, :], w_in[:, kc, GATE_OFF:GATE_OFF + 4096], bWg, W=(bWg,))
    P.dma('pool', Wb[:, 0:4, :], D['w_ba'][l].rearrange("(kc p) c -> p kc c", p=128), bWb, W=(bWb,))
    P.dma('pool', Wb[:, 4, :], D['w_bb'][l], bWb, W=(bWb,))
    P.dma('pool', Wb[:, 5:7, :], D['w_bc'][l].rearrange("(kc p) c -> p kc c", p=128), bWb, W=(bWb,))
    P.dma('pool', Wb[:, 7:9, :], D['w_bd'][l].rearrange("(kc p) c -> p kc c", p=128), bWb, W=(bWb,))
    P.dma('pool', Wo, D['w_out'][l].rearrange("(kc p) c -> p kc c", p=128), bWo, W=(bWo,))
    hTc = [ar.alloc([8, 512], BF16) for _ in range(2)]
    YTc = [ar.alloc([9, 512], BF16) for _ in range(2)]
    bh = P.bufs_n('hTc', 2)
    by = P.bufs_n('YTc', 2)
    mT = [ar.alloc([8, 512], BF16) for _ in range(2)]
    bmT = P.bufs_n('mT', 2)
    sig = [ar.alloc([512], F32) for _ in range(3)]
    bsig = P.bufs_n('sig', 3)
    tmp = [ar.alloc([512], F32) for _ in range(2)]
    btmp = P.bufs_n('mtmp', 2)
    macc = [ar.alloc([512], F32) for _ in range(2)]
    bmacc = P.bufs_n('macc', 2)
    xt = [ar.alloc([DM], F32) for _ in range(2)]
    bxt = P.bufs_n('x3', 2)
    xo = [ar.alloc([DM], F32) for _ in range(2)]
    bxo = P.bufs_n('xo3', 2)
    hT_v = D['hT_d'].rearrange("(kc p) t -> p kc t", p=128)
    YT_v = D['YT_d'].rearrange("(kc p) t -> p kc t", p=128)
    branches = [(0, 4), (4, 5), (5, 7), (7, 9)]
    cn = {'it': 0, 'si': 0, 'ti': 0}

    def gates(tg):
        g2 = tg % 2
        ts_ = slice(tg * 512, (tg + 1) * 512)
        P.load(hTc[g2], hT_v[:, :, ts_], bh[g2])
        P.load(YTc[g2], YT_v[:, :, ts_], by[g2])
        for ct in range(8):
            ma, bma = macc[ct % 2], bmacc[ct % 2]
            for br in range(4):
                it = cn['it']
                cn['it'] += 1
                pg, bpg = C.pb[(it % 2) * 2], C.bpb[(it % 2) * 2]
                py, bpy = C.pb[(it % 2) * 2 + 1], C.bpb[(it % 2) * 2 + 1]
                c0 = br * 1024 + ct * 128
                for kc in range(8):
                    P.mm(pg[:, :], Wg[:, kc, c0:c0 + 128], hTc[g2][:, kc, :], start=(kc == 0), stop=(kc == 7), R=[bWg, bh[g2]], W=[bpg])
                b0, b1 = branches[br]
                for bi in range(b0, b1):
                    P.mm(py[:, :], Wb[:, bi, ct * 128:(ct + 1) * 128], YTc[g2][:, bi, :], start=(bi == b0), stop=(bi == b1 - 1),
                         R=[bWb, by[g2]], W=[bpy])
                s_, bs_ = sig[cn['si'] % 3], bsig[cn['si'] % 3]
                cn['si'] += 1
                P.act(s_, pg[:, :], AF.Sigmoid, R=[bpg], W=[bs_])
                if br == 0:
                    P.tt('dve', ma, s_, py[:, :], ALU.mult, R=[bs_, bpy], W=[bma])
                else:
                    t_, bt_ = tmp[cn['ti'] % 2], btmp[cn['ti'] % 2]
                    cn['ti'] += 1
                    P.tt('dve', t_, s_, py[:, :], ALU.mult, R=[bs_, bpy], W=[bt_])
                    if br < 3:
                        P.tt('pool', ma, ma, t_, ALU.add, R=[bma, bt_], W=[bma])
                    else:
                        P.tt('pool', mT[g2][:, ct, :], ma, t_, ALU.add, R=[bma, bt_], W=[bmT[g2]])

    def wout(tg):
        g2 = tg % 2
        for tt in range(4):
            tok0 = tg * 512 + tt * 128
            xi = (tg * 4 + tt) % 2
            P.load(xt[xi], xin[tok0:tok0 + 128, :], bxt[xi])
            for cg in range(2):
                po, bpo = C.pb[4 + (cg + 2 * tt) % 4], C.bpb[4 + (cg + 2 * tt) % 4]
                for kc in range(8):
                    P.mm(po[:, :], mT[g2][:, kc, tt * 128:(tt + 1) * 128], Wo[:, kc, cg * 512:(cg + 1) * 512], start=(kc == 0), stop=(kc == 7),
                         R=[bmT[g2], bWo], W=[bpo])
                P.tt('dve', xo[xi][:, cg * 512:(cg + 1) * 512], po[:, :], xt[xi][:, cg * 512:(cg + 1) * 512], ALU.add,
                     R=[bpo, bxt[xi]], W=[bxo[xi]])
            P.store(xout[tok0:tok0 + 128, :], xo[xi], bxo[xi])
    gates(0)
    for tg in range(8):
        if tg + 1 < 8:
            gates(tg + 1)
        wout(tg)
    ar.release(m0)
    P.barrier()
    P.retire(nb0)


def phase_p3b(C, l, xin, xout):
    P, ar, D = C.P, C.ar, C.D
    nb0 = len(P.bufs)
    m0 = ar.mark()
    Wu = ar.alloc([8, 4096], BF16)
    Wd = ar.alloc([32, DM], BF16)
    gb = ar.alloc([DM], F32)
    bWu, bWd, bgb = P.buf('Wu'), P.buf('Wd'), P.buf('gbm')
    wu_v = D['w_up'][l].rearrange("(kc p) c -> p kc c", p=128)
    wd_v = D['w_down'][l].rearrange("(kc p) c -> p kc c", p=128)
    P.load(gb, D['gb_mlp'][l], bgb)
    bWuc = P.bufs_n('Wuc', 8)
    for cb in range(8):
        P.dma('pool', Wu[:, :, cb * 512:(cb + 1) * 512], wu_v[:, :, cb * 512:(cb + 1) * 512], bWuc[cb], W=(bWuc[cb],))
    for k4 in range(8):
        P.dma('pool', Wd[:, k4 * 4:(k4 + 1) * 4, :], wd_v[:, k4 * 4:(k4 + 1) * 4, :], bWd, W=(bWd,))
    xt = [ar.alloc([DM], F32) for _ in range(2)]
    bxt = P.bufs_n('x4', 2)
    xo = [ar.alloc([DM], F32) for _ in range(1)]
    bxo = P.bufs_n('xo4', 1)
    ss = [ar.alloc([1], F32) for _ in range(2)]
    bss = P.bufs_n('ss4', 2)
    hb = [ar.alloc([DM], BF16) for _ in range(2)]
    bhb = P.bufs_n('hb4', 2)
    hmT = [ar.alloc([8, 512], BF16) for _ in range(2)]
    bhm = P.bufs_n('hmT', 2)
    uT = ar.alloc([32, 512], BF16)
    buT = P.buf('uT')
    rl = [ar.alloc([512], F32) for _ in range(2)]
    brl = P.bufs_n('rl', 2)
    st_ = {'xi': 0, 'it': 0}

    def norm_tile(tg, tt, part):
        g2 = tg % 2
        tok0 = tg * 512 + tt * 128
        if part == 0:
            i = j = st_['xi'] % 2
            st_['xi'] += 1
            st_['nj'] = j
            P.load(xt[i], xin[tok0:tok0 + 128, :], bxt[i])
            P.act(hb[j], xt[i], AF.Square, R=[bxt[i]], W=[bhb[j], bss[j]], accum_out=ss[j])
            P.act(ss[j], ss[j], AF.Ln, R=[bss[j]], W=[bss[j]], scale=1.0 / DM, bias=EPS)
            P.act(ss[j], ss[j], AF.Exp, R=[bss[j]], W=[bss[j]], scale=-0.5)
            P.stt(hb[j], xt[i], ss[j], gb, ALU.mult, ALU.mult, R=[bxt[i], bss[j], bgb], W=[bhb[j]])
        else:
            j = st_['nj']
            pbv = C.pb[6 + j].bitcast(BF16)
            for kc in range(8):
                P.tr(pbv[:, kc * 128:(kc + 1) * 128], hb[j][:, kc * 128:(kc + 1) * 128], C.ident, R=[bhb[j], C.b_const], W=[C.bpb[6 + j]])
            P.cp('dve', hmT[g2][:, :, tt * 128:(tt + 1) * 128], pbv.rearrange("p (k t) -> p k t", k=8), R=[C.bpb[6 + j]], W=[bhm[g2]])

    def norm(tg):
        for tt in range(4):
            norm_tile(tg, tt, 0)
            norm_tile(tg, tt, 1)

    def up(tg):
        g2 = tg % 2
        for mt in range(32):
            it = st_['it']
            st_['it'] += 1
            pu, bpu = C.pb[it % 3], C.bpb[it % 3]
            r_, br_ = rl[it % 2], brl[it % 2]
            for kc in range(8):
                P.mm(pu[:, :], Wu[:, kc, mt * 128:(mt + 1) * 128], hmT[g2][:, kc, :], start=(kc == 0), stop=(kc == 7), R=[bWuc[mt // 4], bhm[g2]], W=[bpu])
            P.act(r_, pu[:, :], AF.Relu, R=[bpu], W=[br_])
            P.tt('pool' if mt % 2 else 'dve', uT[:, mt, :], r_, r_, ALU.mult, R=[br_], W=[buT])

    def down(tg):
        for tt in range(4):
            if tg + 1 < 8:
                norm_tile(tg + 1, tt, 0)
            tok0 = tg * 512 + tt * 128
            i = st_['xi'] % 2
            st_['xi'] += 1
            P.load(xt[i], xin[tok0:tok0 + 128, :], bxt[i])
            for cg in range(2):
                pd, bpd = C.pb[3 + (cg + 2 * tt) % 3], C.bpb[3 + (cg + 2 * tt) % 3]
                for mt in range(32):
                    P.mm(pd[:, :], uT[:, mt, tt * 128:(tt + 1) * 128], Wd[:, mt, cg * 512:(cg + 1) * 512], start=(mt == 0), stop=(mt == 31),
                         R=[buT, bWd], W=[bpd])
                P.tt('dve', xo[0][:, cg * 512:(cg + 1) * 512], pd[:, :], xt[i][:, cg * 512:(cg + 1) * 512], ALU.add,
                     R=[bpd, bxt[i]], W=[bxo[0]])
            P.store(xout[tok0:tok0 + 128, :], xo[0], bxo[0])
            if tg + 1 < 8:
                norm_tile(tg + 1, tt, 1)

    norm(0)
    for tg in range(8):
        up(tg)
        down(tg)
    ar.release(m0)
    P.barrier()
    P.retire(nb0)


ARENA = 207 * 1024 + 512
NSEM_POOL = 96


class SemPool:
    def __init__(self, nc, stack, n):
        self.sems = [stack.enter_context(nc.semaphore(f"sm{i}")) for i in range(n)]
        self.i = 0

        self.free = []

    def get(self):
        if self.free:
            return self.free.pop()
        s_ = self.sems[self.i]
        self.i += 1
        return (s_, 0)

    def put(self, sem, cnt):
        self.free.append((sem, cnt))


def build(n_layers=2, phases=None, dbg=False):
    nc = bass.Bass("TRN2", target_bir_lowering=False)
    D = {}
    x = nc.dram_tensor("x", [S, DM], F32, kind="ExternalInput").ap()
    for name, shp in PARAM_SHAPES.items():
        D[name] = nc.dram_tensor(name, shp, F32, kind="ExternalInput").ap()
    y = nc.dram_tensor("y", [S, DM], F32, kind="ExternalOutput").ap()
    sk = "ExternalOutput" if dbg else "Internal"
    D['hT_d'] = nc.dram_tensor("hT_d", [DM, S], BF16, kind=sk).ap()
    D['QKT_d'] = nc.dram_tensor("QKT_d", [NQKB * 128, S], BF16, kind=sk).ap()
    D['Vnat_d'] = nc.dram_tensor("Vnat_d", [S, 780], BF16, kind=sk).ap()
    D['Vb_d'] = nc.dram_tensor("Vb_d", [2, S, 130], BF16, kind=sk).ap()
    D['YT_d'] = nc.dram_tensor("YT_d", [1152, S], BF16, kind=sk).ap()
    x1_d = nc.dram_tensor("x1_d", [S, DM], F32, kind=sk).ap()
    x2_d = nc.dram_tensor("x2_d", [S, DM], F32, kind=sk).ap()
    with ExitStack() as stack:
        arena_t = stack.enter_context(nc.sbuf_tensor("arena", [128, ARENA], U8))
        pbs = [stack.enter_context(nc.psum_tensor(f"pb{i}", [128, 512], F32)) for i in range(8)]
        sp_ = SemPool(nc, stack, NSEM_POOL)

        class _St:
            def enter_context(self, cm):
                raise RuntimeError

        P = Prog.__new__(Prog)
        P.nc = nc
        P.ops = {e: [] for e in ENGS}
        P.bufs = []
        P.esem = {e: sp_.get()[0] for e in ENGS}
        P.nsem = len(ENGS)

        def dma(eng, out, in_, owner, R=(), W=()):
            if owner.sem is None:
                owner.sem, owner.cnt = sp_.get()
            owner.cnt += 16
            o = Op(eng, lambda e: e.dma_start(out=out, in_=in_))
            o.dma_sem = owner.sem
            P._track(o, ('dma', owner.sem, owner.cnt), 'dma', R, W)
            P.ops[eng].append(o)
            return o
        P.dma = dma

        def retire(nb0):
            for b in P.bufs[nb0:]:
                if b.sem is not None:
                    sp_.put(b.sem, b.cnt)
                    b.sem = None
            del P.bufs[nb0:]
        P.retire = retire
        block = stack.enter_context(nc.Block())
        C = Ctx()
        C.P, C.D = P, D
        C.ar = Arena(arena_t, ARENA)
        C.pb = [p[:, :] for p in pbs]
        C.bpb = P.bufs_n('pb', 8)
        C.cbf = C.ar.alloc([NCONST], BF16)
        C.b_const = P.buf('consts')
        P.dma('pool', C.cbf, D['consts'], C.b_const, W=(C.b_const,))
        C.ident = C.cbf[:, C_IDENT:C_IDENT + 128]
        C.bd64 = C.cbf[:, C_BD64:C_BD64 + 128]
        C.bd32 = C.cbf[:, C_BD32:C_BD32 + 128]
        C.psw64 = C.cbf[:, C_PSW64:C_PSW64 + 128]
        C.psw32 = C.cbf[:, C_PSW32:C_PSW32 + 128]
        C.ones_bf = C.cbf[:, C_ONES:C_ONES + 128]
        all_ph = ['p1', 'a', 'b', 'c', 'd', 'p3a', 'p3b']
        phases = phases or all_ph
        for l in range(n_layers):
            xin = x if l == 0 else x2_d
            xfin = y if l == n_layers - 1 else x2_d
            if 'p1' in phases:
                phase_p1(C, l, xin)
            if 'a' in phases:
                mixer_a(C, l)
            if 'b' in phases:
                mixer_b(C, l)
            if 'c' in phases:
                mixer_c(C, l)
            if 'd' in phases:
                mixer_d(C, l)
            if 'p3a' in phases:
                phase_p3a(C, l, xin, x1_d)
            if 'p3b' in phases:
                phase_p3b(C, l, x1_d, xfin)
        P.barrier()
        P.emit(block)
        C.nsem = sp_.i
    return nc, C


_CACHE = {}


def kernel(**inputs):
    x = np.ascontiguousarray(np.asarray(inputs['x'], dtype=np.float32))
    params = host_prep(inputs)
    if 'nc' not in _CACHE:
        _CACHE['nc'] = build()[0]
    nc = _CACHE['nc']
    in_maps = []
    for b in range(8):
        m = {'x': x[b]}
        m.update(params)
        in_maps.append(m)
    res = run_bass_kernel_spmd(nc, in_maps, core_ids=list(range(8)))
    return np.stack([np.asarray(r['y'], dtype=np.float32) for r in res.results], axis=0)
```

```python
import math
from contextlib import ExitStack
import numpy as np
import concourse.bass as bass
import concourse.mybir as mybir
from concourse.bass_utils import run_bass_kernel_spmd

F32 = mybir.dt.float32
BF16 = mybir.dt.bfloat16
U8 = mybir.dt.uint8
AF = mybir.ActivationFunctionType
ALU = mybir.AluOpType

S = 4096
DM = 1024
NT = 32
EPS = 1e-6
INC = 7552
ENGS = ['pe', 'act', 'dve', 'pool', 'sp']
SAME_ENG_SYNC = ('act', 'dve', 'pool')


class Buf:
    __slots__ = ('name', 'w', 'rs', 'sem', 'cnt')

    def __init__(self, name):
        self.name = name
        self.w = None
        self.rs = {}
        self.sem = None
        self.cnt = 0


class Op:
    __slots__ = ('eng', 'fn', 'deps', 'dwaits', 'needs_inc', 'semval', 'dma_sem')

    def __init__(self, eng, fn):
        self.eng = eng
        self.fn = fn
        self.deps = set()
        self.dwaits = {}
        self.needs_inc = False
        self.semval = 0
        self.dma_sem = None


class Prog:
    def __init__(self, nc, stack):
        self.nc = nc
        self.stack = stack
        self.ops = {e: [] for e in ENGS}
        self.bufs = []
        self.esem = {e: stack.enter_context(nc.semaphore("s_" + e)) for e in ENGS}
        self.nsem = len(ENGS)

    def buf(self, name):
        b = Buf(name)
        self.bufs.append(b)
        return b

    def bufs_n(self, name, n):
        return [self.buf(f"{name}{i}") for i in range(n)]

    def _add_ev(self, o, ev):
        if ev is None:
            return
        if ev[0] == 'op':
            d = ev[1]
            if d.eng == o.eng and d.eng not in SAME_ENG_SYNC:
                return
            o.deps.add(d)
            d.needs_inc = True
        else:
            _, sem, val = ev
            cur = o.dwaits.get(id(sem))
            if cur is None or cur[1] < val:
                o.dwaits[id(sem)] = (sem, val)

    def _track(self, o, ev, key, R, W):
        for b in R:
            self._add_ev(o, b.w)
        for b in W:
            self._add_ev(o, b.w)
            for e2 in b.rs.values():
                self._add_ev(o, e2)
        for b in R:
            b.rs[key] = ev
        for b in W:
            b.w = ev
            b.rs = {}

    def op(self, eng, fn, R=(), W=()):
        o = Op(eng, fn)
        self._track(o, ('op', o), eng, R, W)
        self.ops[eng].append(o)
        return o

    def dma(self, eng, out, in_, owner, R=(), W=()):
        if owner.sem is None:
            owner.sem = self.stack.enter_context(self.nc.semaphore("d_" + owner.name))
            self.nsem += 1
        owner.cnt += 16
        o = Op(eng, lambda e: e.dma_start(out=out, in_=in_))
        o.dma_sem = owner.sem
        self._track(o, ('dma', owner.sem, owner.cnt), 'dma', R, W)
        self.ops[eng].append(o)
        return o

    def load(self, out, in_, owner, eng='sp'):
        return self.dma(eng, out, in_, owner, R=(), W=(owner,))

    def store(self, out, in_, owner, eng='sp'):
        return self.dma(eng, out, in_, owner, R=(owner,), W=())

    def barrier(self):
        o = Op('sp', lambda e: e.nop())
        for E in ENGS:
            if self.ops[E]:
                last = None
                for c in reversed(self.ops[E]):
                    if c.fn is not None and c.dma_sem is None:
                        last = c
                        break
                if last is not None and E != 'sp':
                    o.deps.add(last)
                    last.needs_inc = True
        for b in self.bufs:
            if b.sem is not None and b.cnt > 0:
                o.dwaits[id(b.sem)] = (b.sem, b.cnt)
        o.needs_inc = True
        self.ops['sp'].append(o)
        for E in ENGS:
            if E != 'sp':
                w = Op(E, None)
                w.deps.add(o)
                self.ops[E].append(w)
        for b in self.bufs:
            b.w = None
            b.rs = {}

    def mm(self, out, lhsT, rhs, start=True, stop=True, R=(), W=()):
        return self.op('pe', lambda e: e.matmul(out, lhsT, rhs, start=start, stop=stop), R, W)

    def tr(self, out, in_, ident, R=(), W=()):
        return self.op('pe', lambda e: e.transpose(out, in_, ident), R, W)

    def act(self, out, in_, func, R=(), W=(), **kw):
        return self.op('act', lambda e: e.activation(out=out, in_=in_, func=func, **kw), R, W)

    def tt(self, eng, out, in0, in1, op, R=(), W=()):
        return self.op(eng, lambda e: e.tensor_tensor(out=out, in0=in0, in1=in1, op=op), R, W)

    def ts(self, eng, out, in0, s1, s2, op0, op1=None, R=(), W=()):
        if op1 is None:
            return self.op(eng, lambda e: e.tensor_scalar(out=out, in0=in0, scalar1=s1, scalar2=None, op0=op0), R, W)
        return self.op(eng, lambda e: e.tensor_scalar(out=out, in0=in0, scalar1=s1, scalar2=s2, op0=op0, op1=op1), R, W)

    def stt(self, out, in0, scalar, in1, op0, op1, R=(), W=()):
        return self.op('dve', lambda e: e.scalar_tensor_tensor(out=out, in0=in0, scalar=scalar, in1=in1, op0=op0, op1=op1), R, W)

    def cp(self, eng, out, in_, R=(), W=()):
        if eng == 'act':
            return self.op('act', lambda e: e.copy(out=out, in_=in_), R, W)
        return self.op(eng, lambda e: e.tensor_copy(out=out, in_=in_), R, W)

    def recip(self, out, in_, R=(), W=()):
        return self.op('dve', lambda e: e.reciprocal(out=out, in_=in_), R, W)

    def memset(self, eng, ap, val, W=()):
        return self.op(eng, lambda e: e.memset(ap, val), (), W)

    def emit(self, block):
        for E in ENGS:
            c = 0
            for o in self.ops[E]:
                if o.needs_inc and o.dma_sem is None:
                    c += 1
                    o.semval = c
        esem = self.esem

        def run(E, eng):
            waited = {}
            for o in self.ops[E]:
                needs = []
                for d in o.deps:
                    needs.append((esem[d.eng], d.semval))
                for sem, val in o.dwaits.values():
                    needs.append((sem, val))
                for sem, val in needs:
                    k = id(sem)
                    if waited.get(k, 0) < val:
                        eng.wait_ge(sem, val)
                        waited[k] = val
                if o.fn is not None:
                    ins = o.fn(eng)
                    if o.dma_sem is not None:
                        ins.then_inc(o.dma_sem, 16)
                    elif o.needs_inc:
                        ins.then_inc(esem[E], 1)

        @block.tensor
        def _(e):
            run('pe', e)

        @block.scalar
        def _(e):
            run('act', e)

        @block.vector
        def _(e):
            run('dve', e)

        @block.gpsimd
        def _(e):
            run('pool', e)

        @block.sync
        def _(e):
            run('sp', e)


class Arena:
    def __init__(self, t, size):
        self.t = t
        self.size = size
        self.off = 0

    def alloc(self, free_shape, dtype):
        es = 4 if dtype == F32 else (2 if dtype == BF16 else 1)
        n = 1
        for s_ in free_shape:
            n *= s_
        nb = (n * es + 31) // 32 * 32
        assert self.off + nb <= self.size, f"arena overflow {self.off}+{nb}>{self.size}"
        ap = self.t[:, self.off:self.off + n * es].bitcast(dtype)
        self.off += nb
        if len(free_shape) == 2:
            ap = ap.rearrange("p (a b) -> p a b", a=free_shape[0])
        elif len(free_shape) == 3:
            ap = ap.rearrange("p (a b c) -> p a b c", a=free_shape[0], b=free_shape[1])
        return ap

    def mark(self):
        return self.off

    def release(self, m):
        self.off = m


QK_BLOCKS = []
for i in range(4):
    QK_BLOCKS.append((i * 128, 'n64'))
QK_BLOCKS.append((512, 'n64'))
for g, kind in enumerate(['n64', 'p4', 'p16']):
    QK_BLOCKS.append((768 + g * 128, kind))
for g, kind in enumerate(['n64', 'p4', 'p16']):
    QK_BLOCKS.append((1152 + g * 128, kind))
for i in range(2):
    QK_BLOCKS.append((1920 + i * 128, 'n32'))
for i in range(2):
    QK_BLOCKS.append((2176 + i * 128, 'n32'))
for i in range(2):
    QK_BLOCKS.append((2688 + i * 128, 'd'))
for i in range(2):
    QK_BLOCKS.append((2944 + i * 128, 'd'))
NQKB = len(QK_BLOCKS)
VNAT_COLS = [(640, 128), (2432, 256), (3200, 256), (1536, 128)]
GATE_OFF = 3456
B_DIL = [1, 4, 16]


def perm_tokens(d):
    L = S // d
    j = np.arange(S)
    return (j % L) * d + (j // L)


def rope_tabs(dim):
    half = dim // 2
    inv = np.power(np.float32(10000.0), -(np.arange(0, dim, 2, dtype=np.float32) / np.float32(dim))).astype(np.float32)
    ang = (np.arange(S, dtype=np.float32)[:, None] * inv[None, :]).astype(np.float32)
    c = np.cos(ang).astype(np.float32)
    s_ = np.sin(ang).astype(np.float32)
    p = np.arange(128) % dim
    cosT = c[:, p % half].T.copy()
    sgn = np.where(p < half, -1.0, 1.0).astype(np.float32)
    sinT = (s_[:, p % half] * sgn[None, :]).T.copy()
    return cosT, sinT


def d_tables():
    rows = 64
    r0 = np.clip(np.arange(rows) - 4, 0, rows - 8)
    cj = np.arange(64)
    c0 = np.clip(cj - 8, 0, 48)
    col_ok = (cj[None, :] >= c0[:, None]) & (cj[None, :] < c0[:, None] + 16)
    dc = np.clip(cj[None, :] - cj[:, None], -15, 15) + 15
    tabs = {}
    tab_list = []
    pairs = []
    for n in range(32):
        lo = r0[2 * n] // 2
        hi = (r0[2 * n + 1] + 7) // 2
        pl = []
        for m in range(lo, hi + 1):
            valid = np.zeros((128, 128), dtype=bool)
            dr = np.zeros((128, 128), dtype=np.int64)
            for a in range(2):
                for b in range(2):
                    rho = 2 * m + a
                    i = 2 * n + b
                    ok = (r0[i] <= rho) and (rho <= r0[i] + 7)
                    if ok:
                        valid[a * 64:(a + 1) * 64, b * 64:(b + 1) * 64] = col_ok.T
                        dr[a * 64:(a + 1) * 64, b * 64:(b + 1) * 64] = rho - i + 7
            key = (m - n, valid.tobytes())
            if key not in tabs:
                tabs[key] = len(tab_list)
                tab_list.append((dr, valid))
            pl.append((m, tabs[key]))
        pairs.append(pl)
    dcidx = np.zeros((128, 128), dtype=np.int64)
    for a in range(2):
        for b in range(2):
            dcidx[a * 64:(a + 1) * 64, b * 64:(b + 1) * 64] = dc.T
    return tab_list, pairs, dcidx


D_TABS, D_PAIRS, D_DC = d_tables()
NTAB = len(D_TABS)

C_IDENT = 0
C_BD64 = 128
C_BD32 = 256
C_PSW64 = 384
C_PSW32 = 512
C_ONES = 640
C_MLO = 768
C_MHI = 896
C_NLO4 = 1024
C_NHI4 = 1536
C_DVALID = 2048
NCONST = C_DVALID + NTAB * 128


def make_consts():
    c = np.zeros((128, NCONST), dtype=np.float32)
    p = np.arange(128)
    c[:, C_IDENT:C_IDENT + 128] = np.eye(128, dtype=np.float32)
    c[:, C_BD64:C_BD64 + 128] = (p[:, None] // 64 == p[None, :] // 64)
    c[:, C_BD32:C_BD32 + 128] = (p[:, None] // 32 == p[None, :] // 32)
    part64 = (p // 64) * 64 + (p % 64 + 32) % 64
    part32 = (p // 32) * 32 + (p % 32 + 16) % 32
    c[:, C_PSW64:C_PSW64 + 128] = (p[:, None] == part64[None, :])
    c[:, C_PSW32:C_PSW32 + 128] = (p[:, None] == part32[None, :])
    c[:, C_ONES:C_ONES + 128] = 1.0
    c[:, C_MLO:C_MLO + 128] = (p[:, None] >= p[None, :])
    c[:, C_MHI:C_MHI + 128] = (p[:, None] <= p[None, :])
    for r_ in range(4):
        c[:, C_NLO4 + r_ * 128:C_NLO4 + (r_ + 1) * 128] = (c[:, C_MLO:C_MLO + 128] - 1.0) * 30000.0
        c[:, C_NHI4 + r_ * 128:C_NHI4 + (r_ + 1) * 128] = (c[:, C_MHI:C_MHI + 128] - 1.0) * 30000.0
    for t, (dr, valid) in enumerate(D_TABS):
        c[:, C_DVALID + t * 128:C_DVALID + (t + 1) * 128] = valid
    return c


def host_prep(inp):
    f = lambda a: np.ascontiguousarray(np.asarray(a, dtype=np.float32))
    out = {}
    out['w_in'] = f(inp['w_in'])
    out['w_ba'] = f(inp['w_branch_a'])
    out['w_bb'] = f(inp['w_branch_b'])
    out['w_bc'] = f(inp['w_branch_c'])
    out['w_bd'] = f(inp['w_branch_d'])
    out['w_out'] = f(inp['w_out'])
    out['w_up'] = f(inp['w_up'])
    out['w_down'] = f(inp['w_down'])
    out['gb_attn'] = f(np.broadcast_to(f(inp['attn_norm_g'])[:, None, :], (2, 128, DM)))
    out['gb_mlp'] = f(np.broadcast_to(f(inp['mlp_norm_g'])[:, None, :], (2, 128, DM)))
    gcol = np.zeros((2, 128, NQKB), dtype=np.float32)
    aq, bq, cq, dq = f(inp['a_qk_norm_g']), f(inp['b_qk_norm_g']), f(inp['c_qk_norm_g']), f(inp['d_qk_norm_g'])
    for l in range(2):
        for b in range(4):
            gcol[l, :, b] = np.tile(aq[l, 0], 2)
        gcol[l, :, 4] = np.tile(aq[l, 1], 2)
        for b in range(5, 8):
            gcol[l, :, b] = np.tile(bq[l, 0], 2)
        for b in range(8, 11):
            gcol[l, :, b] = np.tile(bq[l, 1], 2)
        for b in range(11, 13):
            gcol[l, :, b] = np.tile(cq[l, 0], 4)
        for b in range(13, 15):
            gcol[l, :, b] = np.tile(cq[l, 1], 4)
        for b in range(15, 17):
            gcol[l, :, b] = np.tile(dq[l, 0], 2)
        for b in range(17, 19):
            gcol[l, :, b] = np.tile(dq[l, 1], 2)
    out['gcol'] = gcol
    out['sinkb'] = f(np.broadcast_to(f(inp['a_sink'])[:, None, :], (2, 128, 8)))
    out['lamb'] = f(np.broadcast_to(f(inp['c_lambda']).reshape(2, 1, 128), (2, 128, 128)))
    out['gsub'] = f(np.tile(f(inp['c_subln_g']), (1, 2)).reshape(2, 128, 1))
    rpb = f(inp['d_rel_bias'])
    db = np.zeros((2, 4, 128, NTAB, 128), dtype=np.float32)
    for t, (dr, valid) in enumerate(D_TABS):
        db[:, :, :, t, :] = rpb[:, :, dr, D_DC]
    out['dbias'] = db.reshape(2, 4, 128, NTAB * 128)
    c64, s64 = rope_tabs(64)
    c32, s32 = rope_tabs(32)
    p4, p16 = perm_tokens(4), perm_tokens(16)
    out['rope'] = np.ascontiguousarray(np.stack([c64, s64, c64[:, p4], s64[:, p4], c64[:, p16], s64[:, p16], c32, s32], 0))
    out['consts'] = make_consts()
    return out


PARAM_SHAPES = {
    'w_in': [2, DM, INC], 'w_ba': [2, 512, DM], 'w_bb': [2, 128, DM], 'w_bc': [2, 256, DM], 'w_bd': [2, 256, DM],
    'w_out': [2, DM, DM], 'w_up': [2, DM, 4096], 'w_down': [2, 4096, DM],
    'gb_attn': [2, 128, DM], 'gb_mlp': [2, 128, DM], 'gcol': [2, 128, NQKB], 'sinkb': [2, 128, 8],
    'lamb': [2, 128, 128], 'gsub': [2, 128, 1], 'dbias': [2, 4, 128, NTAB * 128],
    'rope': [8, 128, S], 'consts': [128, NCONST],
}


class Ctx:
    pass


def sl(start, n, step):
    return slice(start, start + (n - 1) * step + 1, step)


def ring(lst, i):
    return lst[i % len(lst)]


def phase_p1(C, l, xin):
    P, ar, D = C.P, C.ar, C.D
    nb0 = len(P.bufs)
    m0 = ar.mark()
    hT = ar.alloc([8, S], BF16)
    b_hT = P.buf('hT')
    gb = ar.alloc([DM], F32)
    b_gb = P.buf('gb')
    gcol = ar.alloc([NQKB], F32)
    b_gcol = P.buf('gcol')
    P.load(gb, D['gb_attn'][l], b_gb)
    P.load(gcol, D['gcol'][l], b_gcol)
    m1 = ar.mark()
    xt = [ar.alloc([DM], F32) for _ in range(3)]
    bx = P.bufs_n('xt', 3)
    junk = ar.alloc([DM], BF16)
    b_junk = P.buf('junk')
    ss = [ar.alloc([1], F32) for _ in range(4)]
    bss = P.bufs_n('ss', 4)
    hb = [ar.alloc([DM], BF16) for _ in range(4)]
    bhb = P.bufs_n('hb', 4)
    for tt in range(NT):
        i, j = tt % 3, tt % 4
        P.load(xt[i], xin[tt * 128:(tt + 1) * 128, :], bx[i])
        P.act(junk, xt[i], AF.Square, R=[bx[i]], W=[b_junk, bss[j]], accum_out=ss[j])
        P.act(ss[j], ss[j], AF.Ln, R=[bss[j]], W=[bss[j]], scale=1.0 / DM, bias=EPS)
        P.act(ss[j], ss[j], AF.Exp, R=[bss[j]], W=[bss[j]], scale=-0.5)
        P.stt(hb[j], xt[i], ss[j], gb, ALU.mult, ALU.mult, R=[bx[i], bss[j], b_gb], W=[bhb[j]])
        pbv = C.pb[j].bitcast(BF16)
        for kc in range(8):
            P.tr(pbv[:, kc * 128:(kc + 1) * 128], hb[j][:, kc * 128:(kc + 1) * 128], C.ident, R=[bhb[j], C.b_const], W=[C.bpb[j]])
        P.cp('act' if tt % 2 else 'dve', hT[:, :, tt * 128:(tt + 1) * 128], pbv.rearrange("p (k t) -> p k t", k=8), R=[C.bpb[j]], W=[b_hT])
    for kc in range(8):
        P.store(D['hT_d'][kc * 128:(kc + 1) * 128, :], hT[:, kc, :], b_hT)
    ar.release(m1)
    tabs2 = [ar.alloc([2, S], F32) for _ in range(2)]
    b_tabs2 = P.bufs_n('ropetab', 2)
    wq = [ar.alloc([8, 128], BF16) for _ in range(2)]
    bwq = P.bufs_n('wq', 2)
    NB = 4
    sq = [ar.alloc([512], BF16) for _ in range(NB)]
    bsq = P.bufs_n('sq', NB)
    xg = [ar.alloc([512], BF16) for _ in range(NB)]
    bxg = P.bufs_n('xg', NB)
    rs = [ar.alloc([512], F32) for _ in range(NB)]
    brs = P.bufs_n('rs', NB)
    ta = [ar.alloc([512], F32) for _ in range(NB)]
    bta = P.bufs_n('ta', NB)
    tb = [ar.alloc([512], F32) for _ in range(NB)]
    btb = P.bufs_n('tb', NB)
    ob = [ar.alloc([512], BF16) for _ in range(NB)]
    bob = P.bufs_n('ob', NB)
    w_in = D['w_in'][l].rearrange("(kc p) c -> p kc c", p=128)
    blk_order = [0, 1, 2, 3, 4, 5, 8, 6, 9, 7, 10, 11, 12, 13, 14, 15, 16, 17, 18]
    items = [(blk, tc) for blk in blk_order for tc in range(8)]
    TABK = {'n64': 0, 'p4': 2, 'p16': 4, 'n32': 6, 'd': None}
    variants = [0, 2, 4, 6]
    state = {'vi': -1}

    def load_tab(vi):
        if vi < len(variants):
            tb_, bt_ = tabs2[vi % 2], b_tabs2[vi % 2]
            P.load(tb_[:, 0, :], D['rope'][variants[vi]], bt_)
            P.load(tb_[:, 1, :], D['rope'][variants[vi] + 1], bt_)
    load_tab(0)

    def tokf(kind, tc):
        if kind == 'p4':
            r, h0 = tc // 2, (tc % 2) * 512
            return lambda kc: hT[:, kc, sl(r + 4 * h0, 512, 4)]
        if kind == 'p16':
            return lambda kc: hT[:, kc, :].rearrange("p (m r) -> p r m", r=16)[:, 2 * tc:2 * tc + 2, :]
        return lambda kc: hT[:, kc, tc * 512:(tc + 1) * 512]

    def s0(itm, t):
        blk, tc = itm
        coff, kind = QK_BLOCKS[blk]
        wi = blk % 2
        if tc == 0:
            P.dma('pool', wq[wi], w_in[:, :, coff:coff + 128], bwq[wi], W=(bwq[wi],))
        tok = tokf(kind, tc)
        pA, bA = C.pb[t % 3], C.bpb[t % 3]
        for kc in range(8):
            P.mm(pA[:, :], wq[wi][:, kc, :], tok(kc), start=(kc == 0), stop=(kc == 7), R=[bwq[wi], b_hT], W=[bA])

    def s1(itm, t, defer):
        blk, tc = itm
        coff, kind = QK_BLOCKS[blk]
        tabkind = TABK[kind]
        if tabkind is not None:
            vi = variants.index(tabkind)
            if vi != state['vi']:
                state['vi'] = vi
                load_tab(vi + 1)
            tabs, b_tabs = tabs2[vi % 2], b_tabs2[vi % 2]
        dh = 32 if kind == 'n32' else 64
        bd = C.bd32 if dh == 32 else C.bd64
        psw = C.psw32 if dh == 32 else C.psw64
        k = t % NB
        pA, pS, pR = C.pb[t % 3], C.pb[3 + (t % 2)], C.pb[5 + (t % 3)]
        bA, bS, bR = C.bpb[t % 3], C.bpb[3 + (t % 2)], C.bpb[5 + (t % 3)]
        P.act(sq[k], pA[:, :], AF.Square, R=[bA], W=[bsq[k]])
        P.act(xg[k], pA[:, :], AF.Copy, R=[bA, b_gcol], W=[bxg[k]], scale=gcol[:, blk:blk + 1])
        P.mm(pS[:, :], bd, sq[k], R=[bsq[k], C.b_const], W=[bS])
        if kind != 'd':
            P.mm(pR[:, :], psw, xg[k], R=[bxg[k], C.b_const], W=[bR])
        csl = slice(tc * 512, (tc + 1) * 512)
        if kind != 'd':
            P.tt('pool', ta[k], xg[k], tabs[:, 0, csl], ALU.mult, R=[bxg[k], b_tabs], W=[bta[k]])
            P.tt('dve', tb[k], pR[:, :], tabs[:, 1, csl], ALU.mult, R=[bR, b_tabs], W=[btb[k]])

        def s1b():
            P.act(rs[k], pS[:, :], AF.Ln, R=[bS], W=[brs[k]], scale=1.0 / dh, bias=EPS)
            P.act(rs[k], rs[k], AF.Exp, R=[brs[k]], W=[brs[k]], scale=-0.5)
            if kind == 'd':
                P.tt('dve', ob[k], xg[k], rs[k], ALU.mult, R=[bxg[k], brs[k]], W=[bob[k]])
            else:
                P.tt('dve', ta[k], ta[k], tb[k], ALU.add, R=[bta[k], btb[k]], W=[bta[k]])
                P.tt('dve', ob[k], ta[k], rs[k], ALU.mult, R=[bta[k], brs[k]], W=[bob[k]])
            P.store(D['QKT_d'][blk * 128:(blk + 1) * 128, csl], ob[k], bob[k])
        defer(1, s1b)

    run_pipeline(items, 2, s0, s1)
    ar.release(m1)
    wv = ar.alloc([8, 1024], BF16)
    b_wv = P.buf('wv')
    o = 0
    for (coff, n) in VNAT_COLS + [(1536 + 128, 256)]:
        P.dma('pool', wv[:, :, o:o + n], w_in[:, :, coff:coff + n], b_wv, W=(b_wv,))
        o += n
    vn = [ar.alloc([12, 65], BF16) for _ in range(2)]
    bvn = P.bufs_n('vn', 2)
    vb = [ar.alloc([2, 2, 65], BF16) for _ in range(2)]
    bvb = P.bufs_n('vbp', 2)
    for j in range(2):
        P.memset('pool', vn[j][:, :, 64:65], 1.0, W=[bvn[j]])
        P.memset('pool', vb[j][:, :, :, 64:65], 1.0, W=[bvb[j]])
    p4, p16 = perm_tokens(4), perm_tokens(16)
    for tt in range(NT):
        j = tt % 2
        pa, pbk, pc = C.pb[j * 3], C.pb[j * 3 + 1], C.pb[j * 3 + 2]
        ba, bb_, bc = C.bpb[j * 3], C.bpb[j * 3 + 1], C.bpb[j * 3 + 2]
        for kc in range(8):
            P.mm(pa[:, :], hT[:, kc, tt * 128:(tt + 1) * 128], wv[:, kc, 0:512], start=(kc == 0), stop=(kc == 7), R=[b_hT, b_wv], W=[ba])
        for kc in range(8):
            P.mm(pbk[:, 0:256], hT[:, kc, tt * 128:(tt + 1) * 128], wv[:, kc, 512:768], start=(kc == 0), stop=(kc == 7), R=[b_hT, b_wv], W=[bb_])
        t4 = int(p4[tt * 128])
        t16 = int(p16[tt * 128])
        for kc in range(8):
            P.mm(pc[:, 0:128], hT[:, kc, sl(t4, 128, 4)], wv[:, kc, 768:896], start=(kc == 0), stop=(kc == 7), R=[b_hT, b_wv], W=[bc])
        for kc in range(8):
            P.mm(pc[:, 128:256], hT[:, kc, sl(t16, 128, 16)], wv[:, kc, 896:1024], start=(kc == 0), stop=(kc == 7), R=[b_hT, b_wv], W=[bc])
        P.cp('act', vn[j][:, 0:8, 0:64], pa[:, :].rearrange("p (h d) -> p h d", h=8), R=[ba], W=[bvn[j]])
        P.cp('dve', vn[j][:, 8:12, 0:64], pbk[:, 0:256].rearrange("p (h d) -> p h d", h=4), R=[bb_], W=[bvn[j]])
        P.cp('dve', vb[j][:, :, :, 0:64], pc[:, 0:256].rearrange("p (g h d) -> p g h d", g=2, h=2), R=[bc], W=[bvb[j]])
        P.store(D['Vnat_d'][tt * 128:(tt + 1) * 128, :], vn[j].rearrange("p h d -> p (h d)"), bvn[j])
        for g in range(2):
            P.store(D['Vb_d'][g, tt * 128:(tt + 1) * 128, :], vb[j][:, g].rearrange("p h d -> p (h d)"), bvb[j])
    ar.release(m0)
    P.barrier()
    P.retire(nb0)


def run_pipeline(items, LA, s0, s1):
    n = len(items)
    pend = {}
    cur = [0]

    def defer(delay, fn):
        pend.setdefault(cur[0] + delay, []).append(fn)

    for t in range(n + LA):
        cur[0] = t
        if t < n:
            s0(items[t], t)
        if t >= LA:
            s1(items[t - LA], t - LA, defer)
        for fn in pend.pop(t, []):
            fn()
    while pend:
        t2 = min(pend)
        cur[0] = t2
        for fn in pend.pop(t2):
            fn()


def finalize_norm(C, acc, bacc, n, dst, bdst, shape3=None, esink=None, tagk=0, defer=None, after=None, dl=(1, 3, 5, 6)):
    P = C.P
    k = tagk % 2
    r, rhi, rlo = C.fr[k], C.frhi[k], C.frlo[k]
    br = C.bfr[k]
    bc, bbc = C.pb[7], C.bpb[7]
    bcs, bbcs = C.fbcs[k], C.bfbcs[k]
    src = acc[64:65, 0:n]

    def g0():
        if esink is not None:
            j, q = shape3
            P.tt('dve', r[64:65, 0:n].rearrange("p (j q) -> p j q", j=j), src.rearrange("p (j q) -> p j q", j=j),
                 esink, ALU.add, R=[bacc, C.b_esk], W=[br])
            P.act(r[64:65, 0:n], r[64:65, 0:n], AF.Ln, R=[br], W=[br])
        else:
            P.act(r[64:65, 0:n], src, AF.Ln, R=[bacc], W=[br])
        P.act(r[64:65, 0:n], r[64:65, 0:n], AF.Exp, R=[br], W=[br], scale=-1.0)
        P.cp('dve', rhi[64:65, 0:n], r[64:65, 0:n], R=[br], W=[br])
        P.tt('dve', rlo[64:65, 0:n], r[64:65, 0:n], rhi[64:65, 0:n], ALU.subtract, R=[br], W=[br])

    def g1():
        P.mm(bc[0:64, 0:n], C.ones_bf[64:65, 0:64], rhi[64:65, 0:n], start=True, stop=False, R=[br, C.b_const], W=[bbc])
        P.mm(bc[0:64, 0:n], C.ones_bf[64:65, 0:64], rlo[64:65, 0:n], start=False, stop=True, R=[br, C.b_const], W=[bbc])

    def g2():
        P.cp('act', bcs[0:64, 0:n], bc[0:64, 0:n], R=[bbc], W=[bbcs])

    def g3():
        a0 = acc[0:64, 0:n]
        b0 = bcs[0:64, 0:n]
        if shape3 is not None:
            j, q = shape3
            a0 = a0.rearrange("p (j q) -> p j q", j=j)
            b0 = b0.rearrange("p (j q) -> p j q", j=j)
        P.tt('dve', dst, a0, b0, ALU.mult, R=[bacc, bbcs], W=[bdst])
        if after is not None:
            after()

    if defer is None:
        g0(); g1(); g2(); g3()
    else:
        defer(dl[0], g0); defer(dl[1], g1); defer(dl[2], g2); defer(dl[3], g3)


def alloc_fin(C):
    ar, P = C.ar, C.P
    C.fr = [ar.alloc([512], F32) for _ in range(2)]
    C.frhi = [ar.alloc([512], BF16) for _ in range(2)]
    C.frlo = [ar.alloc([512], BF16) for _ in range(2)]
    C.bfr = P.bufs_n('fr', 2)
    C.fbcs = [ar.alloc([512], F32) for _ in range(2)]
    C.bfbcs = P.bufs_n('fbcs', 2)


def mixer_a(C, l):
    P, ar, D = C.P, C.ar, C.D
    nb0 = len(P.bufs)
    m0 = ar.mark()
    alloc_fin(C)
    QT = ar.alloc([4, S], BF16)
    bQT = P.buf('aQT')
    KT = ar.alloc([2, S], BF16)
    bKT = P.buf('aKT')
    V = ar.alloc([NT, 130], BF16)
    bV = P.buf('aV')
    yst = ar.alloc([4, S], BF16)
    byst = P.buf('ayst')
    esk = ar.alloc([8], F32)
    C.b_esk = P.buf('esk')
    P.load(esk, D['sinkb'][l], C.b_esk)
    P.act(esk, esk, AF.Exp, R=[C.b_esk], W=[C.b_esk])
    for g in range(2):
        for j in range(4):
            h = 4 * g + j
            P.load(QT[g * 64:(g + 1) * 64, j, :], D['QKT_d'][h * 64:(h + 1) * 64, :], bQT)
    P.memset('pool', KT, 0.0, W=[bKT])
    for g in range(2):
        P.load(KT[g * 64:(g + 1) * 64, g, :], D['QKT_d'][512 + g * 64:512 + (g + 1) * 64, :], bKT)
    P.load(V, D['Vnat_d'].rearrange("(t p) c -> p t c", p=128)[:, :, 0:130], bV)
    NP = 4
    pt = [ar.alloc([512], BF16) for _ in range(NP)]
    bpt = P.bufs_n('apt', NP)
    mlo = C.cbf[:, C_MLO:C_MLO + 128].unsqueeze(1).broadcast_to([128, 4, 128])
    mhi = C.cbf[:, C_MHI:C_MHI + 128].unsqueeze(1).broadcast_to([128, 4, 128])
    items = []
    fi = 0
    for g in range(2):
        for n in range(NT):
            ms = [m for m in (n - 1, n, n + 1) if 0 <= m < NT]
            for idx, m in enumerate(ms):
                items.append((g, n, m, idx == 0, idx == len(ms) - 1, fi))
            fi += 1

    def s0(itm, t):
        g, n, m, first, last, f = itm
        ps = slice(g * 64, (g + 1) * 64)
        st, bst = C.pb[t % 4], C.bpb[t % 4]
        P.mm(st[:, :].rearrange("p (j q) -> p j q", j=4), KT[:, g, m * 128:(m + 1) * 128], QT[:, :, n * 128:(n + 1) * 128],
             start=True, stop=(m == n), R=[bKT, bQT], W=[bst])
        if m != n:
            nk = C_NLO4 if m < n else C_NHI4
            P.mm(st[:, :], C.ident, C.cbf[:, nk:nk + 512], start=False, stop=True, R=[C.b_const], W=[bst])

    def s1(itm, t, defer):
        g, n, m, first, last, f = itm
        k = t % NP
        st, bst = C.pb[t % 4], C.bpb[t % 4]
        acc, bacc = C.pb[4 + (f % 3)], C.bpb[4 + (f % 3)]
        P.act(pt[k], st[:, :], AF.Exp, R=[bst], W=[bpt[k]], scale=0.125)
        P.mm(acc[0:65, :], V[:, m, g * 65:(g + 1) * 65], pt[k], start=first, stop=last, R=[bV, bpt[k]], W=[bacc])
        if last:
            es = esk[64:65, 4 * g:4 * g + 4].unsqueeze(2).broadcast_to([1, 4, 128])
            def after(g=g, n=n):
                if n == NT - 1:
                    for j in range(4):
                        h = 4 * g + j
                        P.store(D['YT_d'][h * 64:(h + 1) * 64, :], yst[0:64, j, :], byst)
            finalize_norm(C, acc, bacc, 512, yst[0:64, :, n * 128:(n + 1) * 128], byst, shape3=(4, 128), esink=es, tagk=f,
                          defer=defer, after=after, dl=(1, 2, 3, 4))

    run_pipeline(items, 3, s0, s1)
    ar.release(m0)
    P.barrier()
    P.retire(nb0)


def mixer_b(C, l):
    P, ar, D = C.P, C.ar, C.D
    nb0 = len(P.bufs)
    m0 = ar.mark()
    alloc_fin(C)
    accN = ar.alloc([2, S], F32)
    baccN = P.buf('baccN')
    yst = ar.alloc([2, S], BF16)
    byst = P.buf('byst')
    QTs = [ar.alloc([S], BF16) for _ in range(3)]
    KTs = [ar.alloc([2, S], BF16) for _ in range(3)]
    Vs = [ar.alloc([NT, 130], BF16) for _ in range(3)]
    bQ = P.bufs_n('bQT', 3)
    bK = P.bufs_n('bKT', 3)
    bVv = P.bufs_n('bV', 3)
    NP = 4
    pt = [ar.alloc([512], BF16) for _ in range(NP)]
    bpt = P.bufs_n('bpt', NP)
    items = []
    ai = 0
    for gi, d in enumerate(B_DIL):
        L = S // d
        nb = L // 128
        QT, KT, V = QTs[gi], KTs[gi], Vs[gi]
        bq, bk, bv = bQ[gi], bK[gi], bVv[gi]
        P.load(QT, D['QKT_d'][(5 + gi) * 128:(6 + gi) * 128, :], bq)
        P.memset('pool' if gi % 2 else 'dve', KT, 0.0, W=[bk])
        for jh_ in range(2):
            P.load(KT[jh_ * 64:(jh_ + 1) * 64, jh_, :], D['QKT_d'][(8 + gi) * 128 + jh_ * 64:(8 + gi) * 128 + (jh_ + 1) * 64, :], bk)
        if gi == 0:
            P.load(V, D['Vnat_d'].rearrange("(t p) c -> p t c", p=128)[:, :, 650:780], bv)
        else:
            P.load(V, D['Vb_d'][gi - 1].rearrange("(t p) c -> p t c", p=128), bv)
        for jh in range(2):
            for r in range(d):
                base = r * L
                qblocks = []
                for n_ in range(-1, nb):
                    q0 = 128 * n_ + 64
                    qa, qb = max(q0, 0), min(q0 + 128, L)
                    tiles = []
                    if n_ >= 0:
                        tiles.append((n_, C_MLO))
                    if n_ + 1 < nb:
                        tiles.append((n_ + 1, C_MHI))
                    qblocks.append((qa, qb - qa, qa - q0, tiles))
                for g0 in range(0, len(qblocks), 4):
                    grp = qblocks[g0:g0 + 4]
                    ncols = sum(q[1] for q in grp)
                    pstart = grp[0][0]
                    col = 0
                    nsub = (len(grp) + 1) // 2
                    for bi in range(0, len(grp), 2):
                        sub = grp[bi:bi + 2]
                        plist = []
                        sc = 0
                        for (qa, nq, aoff, tiles) in sub:
                            for ti, (m, mk) in enumerate(tiles):
                                plist.append((sc, nq, aoff, mk, base // 128 + m, col, ti == 0, ti == len(tiles) - 1, base + qa))
                                sc += nq
                            col += nq
                        endinfo = None
                        if bi // 2 == nsub - 1:
                            endinfo = (gi, d, r, pstart, ncols)
                        items.append((QT, KT, V, bq, bk, bv, jh, plist, sc, ai, endinfo))
                    ai += 1

    def s0(itm, t):
        QT, KT, V, bq, bk, bv, jh, plist, sc, a_, endinfo = itm
        ps = slice(jh * 64, (jh + 1) * 64)
        st, bst = C.pb[t % 4], C.bpb[t % 4]
        for (s0_, nq, aoff, mk, tg, c0, first, last, qpos) in plist:
            P.mm(st[:, s0_:s0_ + nq], KT[:, jh, tg * 128:(tg + 1) * 128], QT[:, qpos:qpos + nq], start=True, stop=False, R=[bk, bq], W=[bst])
            nk = C_NLO4 if mk == C_MLO else C_NHI4
            P.mm(st[:, s0_:s0_ + nq], C.ident, C.cbf[:, nk + aoff:nk + aoff + nq], start=False, stop=True, R=[C.b_const], W=[bst])

    def s1(itm, t, defer):
        QT, KT, V, bq, bk, bv, jh, plist, sc, a_, endinfo = itm
        k = t % NP
        st, bst = C.pb[t % 4], C.bpb[t % 4]
        acc, bacc = C.pb[4 + (a_ % 3)], C.bpb[4 + (a_ % 3)]
        P.act(pt[k][:, 0:sc], st[:, 0:sc], AF.Exp, R=[bst], W=[bpt[k]], scale=0.125)
        for (s0_, nq, aoff, mk, tg, c0, first, last, _) in plist:
            P.mm(acc[0:65, c0:c0 + nq], V[:, tg, jh * 65:(jh + 1) * 65], pt[k][:, s0_:s0_ + nq], start=first, stop=last,
                 R=[bv, bpt[k]], W=[bacc])
        if endinfo is not None:
            gi, d, r, pstart, ncols = endinfo
            if gi == 0:
                P.cp('act', accN[0:65, jh, pstart:pstart + ncols], acc[0:65, 0:ncols], R=[bacc], W=[baccN])
            else:
                t0 = r + d * pstart
                view = accN[0:65, jh, sl(t0, ncols, d)]
                P.tt('dve', view, view, acc[0:65, 0:ncols], ALU.add, R=[bacc, baccN], W=[baccN])

    run_pipeline(items, 3, s0, s1)
    fi = 0
    for jh in range(2):
        for ch in range(8):
            cs = slice(ch * 512, (ch + 1) * 512)
            k = fi % 2
            r, rhi, rlo, br = C.fr[k], C.frhi[k], C.frlo[k], C.bfr[k]
            bc, bbc = C.pb[7], C.bpb[7]
            P.act(r[64:65, :], accN[64:65, jh, cs], AF.Ln, R=[baccN], W=[br])
            P.act(r[64:65, :], r[64:65, :], AF.Exp, R=[br], W=[br], scale=-1.0)
            P.cp('dve', rhi[64:65, :], r[64:65, :], R=[br], W=[br])
            P.tt('dve', rlo[64:65, :], r[64:65, :], rhi[64:65, :], ALU.subtract, R=[br], W=[br])
            P.mm(bc[0:64, :], C.ones_bf[64:65, 0:64], rhi[64:65, :], start=True, stop=False, R=[br, C.b_const], W=[bbc])
            P.mm(bc[0:64, :], C.ones_bf[64:65, 0:64], rlo[64:65, :], start=False, stop=True, R=[br, C.b_const], W=[bbc])
            P.tt('dve', yst[0:64, jh, cs], accN[0:64, jh, cs], bc[0:64, :], ALU.mult, R=[baccN, bbc], W=[byst])
            fi += 1
        P.store(D['YT_d'][512 + jh * 64:512 + (jh + 1) * 64, :], yst[0:64, jh, :], byst)
    ar.release(m0)
    P.barrier()
    P.retire(nb0)


def mixer_d(C, l):
    P, ar, D = C.P, C.ar, C.D
    nb0 = len(P.bufs)
    m0 = ar.mark()
    alloc_fin(C)
    QT = ar.alloc([2, S], BF16)
    KT = ar.alloc([4, S], BF16)
    V = ar.alloc([NT, 260], BF16)
    bQT, bKT, bV = P.buf('dQT'), P.buf('dKT'), P.buf('dV')
    yst = ar.alloc([4, S], BF16)
    byst = P.buf('dyst')
    EB = ar.alloc([4, NTAB * 128], BF16)
    bEB = P.buf('dEB')
    tmpb = [ar.alloc([NTAB * 128], F32) for _ in range(2)]
    btmp = P.bufs_n('dtmp', 2)
    P.memset('pool', KT[:, 0:2, :], 0.0, W=[bKT])
    P.memset('dve', KT[:, 2:4, :], 0.0, W=[bKT])
    for i in range(2):
        P.load(QT[:, i, :], D['QKT_d'][(15 + i) * 128:(16 + i) * 128, :], bQT)
    for h_ in range(4):
        r0_ = (h_ % 2) * 64
        P.load(KT[r0_:r0_ + 64, h_, :], D['QKT_d'][17 * 128 + h_ * 64:17 * 128 + (h_ + 1) * 64, :], bKT)
    P.load(V, D['Vnat_d'].rearrange("(t p) c -> p t c", p=128)[:, :, 390:650], bV)
    for h in range(4):
        P.load(tmpb[h % 2], D['dbias'][l, h], btmp[h % 2])
        P.act(tmpb[h % 2], tmpb[h % 2], AF.Exp, R=[btmp[h % 2]], W=[btmp[h % 2]])
        P.tt('dve', EB[:, h, :], tmpb[h % 2], C.cbf[:, C_DVALID:C_DVALID + NTAB * 128], ALU.mult, R=[btmp[h % 2], C.b_const], W=[bEB])
    NP = 4
    pt = [ar.alloc([512], BF16) for _ in range(NP)]
    bpt = P.bufs_n('dpt', NP)
    items = []
    fi = 0
    for h in range(4):
        for n4 in range(8):
            plist = []
            for n in range(n4 * 4, n4 * 4 + 4):
                pl = D_PAIRS[n]
                for pi, (m, tab) in enumerate(pl):
                    plist.append((n, m, tab, pi == 0, pi == len(pl) - 1))
            nch = (len(plist) + 3) // 4
            for ci in range(nch):
                items.append((h, n4, plist[ci * 4:ci * 4 + 4], fi, ci == nch - 1))
            fi += 1

    def s0(itm, t):
        h, n4, chunk, f, endg = itm
        bq = h // 2
        ps = slice((h % 2) * 64, (h % 2 + 1) * 64)
        st, bst = C.pb[t % 4], C.bpb[t % 4]
        for i, (n, m, tab, first, last) in enumerate(chunk):
            P.mm(st[:, i * 128:(i + 1) * 128], KT[:, h, m * 128:(m + 1) * 128], QT[:, bq, n * 128:(n + 1) * 128],
                 R=[bKT, bQT], W=[bst])

    def s1(itm, t, defer):
        h, n4, chunk, f, endg = itm
        k = t % NP
        st, bst = C.pb[t % 4], C.bpb[t % 4]
        acc, bacc = C.pb[4 + (f % 3)], C.bpb[4 + (f % 3)]
        used = len(chunk) * 128
        P.act(pt[k][:, 0:used], st[:, 0:used], AF.Exp, R=[bst], W=[bpt[k]], scale=0.125)
        for i, (n, m, tab, first, last) in enumerate(chunk):
            P.tt('pool' if i % 2 else 'dve', pt[k][:, i * 128:(i + 1) * 128], pt[k][:, i * 128:(i + 1) * 128],
                 EB[:, h, tab * 128:(tab + 1) * 128], ALU.mult, R=[bpt[k], bEB], W=[bpt[k]])
        for i, (n, m, tab, first, last) in enumerate(chunk):
            qc = (n % 4) * 128
            P.mm(acc[0:65, qc:qc + 128], V[:, m, h * 65:(h + 1) * 65], pt[k][:, i * 128:(i + 1) * 128], start=first, stop=last,
                 R=[bV, bpt[k]], W=[bacc])
        if endg:
            def after(h=h, n4=n4):
                if n4 == 7:
                    P.store(D['YT_d'][896 + h * 64:896 + (h + 1) * 64, :], yst[0:64, h, :], byst)
            finalize_norm(C, acc, bacc, 512, yst[0:64, h, n4 * 512:(n4 + 1) * 512], byst, tagk=f, defer=defer, after=after, dl=(1, 2, 3, 4))

    run_pipeline(items, 3, s0, s1)
    ar.release(m0)
    P.barrier()
    P.retire(nb0)


def mixer_c(C, l):
    P, ar, D = C.P, C.ar, C.D
    lam_init = 0.8 - 0.6 * math.exp(-0.3 * l)
    nb0 = len(P.bufs)
    m0 = ar.mark()
    alloc_fin(C)
    QT = ar.alloc([2, S], BF16)
    KT = ar.alloc([8, S], BF16)
    V = ar.alloc([NT, 260], BF16)
    bQT, bKT, bV = P.buf('cQT'), P.buf('cKT'), P.buf('cV')
    yst = ar.alloc([4, S], BF16)
    byst = P.buf('cyst')
    P.memset('pool', KT[:, 0:4, :], 0.0, W=[bKT])
    P.memset('dve', KT[:, 4:8, :], 0.0, W=[bKT])
    for g2 in range(2):
        P.load(QT[:, g2, :], D['QKT_d'][(11 + g2) * 128:(12 + g2) * 128, :], bQT)
    for b in range(8):
        sl_ = (b % 4) * 32
        row = (b // 4) * 128 + sl_
        P.load(KT[sl_:sl_ + 32, b, :], D['QKT_d'][13 * 128 + row:13 * 128 + row + 32, :], bKT)
    P.load(V, D['Vnat_d'].rearrange("(t p) c -> p t c", p=128)[:, :, 130:390], bV)
    lamb = ar.alloc([128], F32)
    blam = P.buf('lam')
    lt = ar.alloc([2, 32], F32)
    l2 = ar.alloc([2], F32)
    nlam = ar.alloc([1], F32)
    gsc = ar.alloc([1], F32)
    bgsc = P.buf('gsc')
    P.load(lamb, D['lamb'][l], blam)
    P.load(gsc, D['gsub'][l], bgsc)
    lv = lamb.rearrange("p (a b c) -> p a b c", a=2, b=2)
    P.tt('dve', lt, lv[:, :, 0, :], lv[:, :, 1, :], ALU.mult, R=[blam], W=[blam])
    P.op('dve', lambda e: e.reduce_sum(out=l2, in_=lt, axis=mybir.AxisListType.X), R=[blam], W=[blam])
    P.act(l2, l2, AF.Exp, R=[blam], W=[blam])
    P.tt('dve', nlam, l2[:, 0:1], l2[:, 1:2], ALU.subtract, R=[blam], W=[blam])
    P.ts('dve', nlam, nlam, lam_init, -1.0, ALU.add, ALU.mult, R=[blam], W=[blam])
    P.ts('dve', gsc, gsc, 1.0 - lam_init, None, ALU.mult, R=[bgsc], W=[bgsc])
    NP = 4
    pt = [ar.alloc([512], BF16) for _ in range(NP)]
    bpt = P.bufs_n('cpt', NP)
    to = [ar.alloc([512], F32) for _ in range(2)]
    t1 = [ar.alloc([512], F32) for _ in range(2)]
    sqb = [ar.alloc([512], BF16) for _ in range(2)]
    rsd = [ar.alloc([512], F32) for _ in range(2)]
    bto = P.bufs_n('cto', 2)
    scale = 32.0 ** -0.5
    items = []
    fi = 0
    for h in range(4):
        for Q in range(8):
            for c in range(2):
                for kt in range(NT):
                    items.append((h, Q, c, kt, fi))
            fi += 1

    def s0(itm, t):
        h, Q, c, kt, f = itm
        b = 2 * h + c
        st, bst = C.pb[t % 3], C.bpb[t % 3]
        P.mm(st[:, :], KT[:, b, kt * 128:(kt + 1) * 128], QT[:, b // 4, Q * 512:(Q + 1) * 512], R=[bKT, bQT], W=[bst])

    def s1(itm, t, defer):
        h, Q, c, kt, f = itm
        qs = slice(Q * 512, (Q + 1) * 512)
        k = t % NP
        st, bst = C.pb[t % 3], C.bpb[t % 3]
        a0 = 3 + 2 * (f % 2)
        accs = [C.pb[a0], C.pb[a0 + 1]]
        baccs = [C.bpb[a0], C.bpb[a0 + 1]]
        P.act(pt[k], st[:, :], AF.Exp, R=[bst], W=[bpt[k]], scale=scale)
        P.mm(accs[c][0:65, :], V[:, kt, h * 65:(h + 1) * 65], pt[k], start=(kt == 0), stop=(kt == NT - 1),
             R=[bV, bpt[k]], W=[baccs[c]])
        if not (c == 1 and kt == NT - 1):
            return
        k2 = f % 2
        bt = bto[k2]
        bc, bbc = C.pb[7], C.bpb[7]

        def f0():
            for c_ in range(2):
                r, rhi, rlo, br = C.fr[c_], C.frhi[c_], C.frlo[c_], C.bfr[c_]
                P.recip(r[64:65, :], accs[c_][64:65, :], R=[baccs[c_]], W=[br])
                if c_ == 1:
                    P.ts('dve', r[64:65, :], r[64:65, :], nlam[64:65, 0:1], None, ALU.mult, R=[br, blam], W=[br])
                P.cp('dve', rhi[64:65, :], r[64:65, :], R=[br], W=[br])
                P.tt('dve', rlo[64:65, :], r[64:65, :], rhi[64:65, :], ALU.subtract, R=[br], W=[br])

        def f1(c_):
            def fn():
                r, rhi, rlo, br = C.fr[c_], C.frhi[c_], C.frlo[c_], C.bfr[c_]
                P.mm(bc[0:64, :], C.ones_bf[64:65, 0:64], rhi[64:65, :], start=True, stop=False, R=[br, C.b_const], W=[bbc])
                P.mm(bc[0:64, :], C.ones_bf[64:65, 0:64], rlo[64:65, :], start=False, stop=True, R=[br, C.b_const], W=[bbc])
            return fn

        def f2(c_):
            def fn():
                P.cp('dve', C.fbcs[c_][0:64, :], bc[0:64, :], R=[bbc], W=[C.bfbcs[c_]])
            return fn

        def f3():
            P.tt('dve', to[k2][0:64, :], accs[0][0:64, :], C.fbcs[0][0:64, :], ALU.mult, R=[baccs[0], C.bfbcs[0]], W=[bt])
            P.tt('dve', t1[k2][0:64, :], accs[1][0:64, :], C.fbcs[1][0:64, :], ALU.mult, R=[baccs[1], C.bfbcs[1], bt], W=[bt])
            P.tt('pool', to[k2][0:64, :], to[k2][0:64, :], t1[k2][0:64, :], ALU.add, R=[bt], W=[bt])

        def f4():
            P.tt('pool', sqb[k2][0:64, :], to[k2][0:64, :], to[k2][0:64, :], ALU.mult, R=[bt], W=[bt])

        def f5():
            P.mm(bc[0:64, :], C.ones_bf[0:64, 0:64], sqb[k2][0:64, :], R=[bt, C.b_const], W=[bbc])

        def f6():
            P.act(rsd[k2][0:64, :], bc[0:64, :], AF.Ln, R=[bbc], W=[bt], scale=1.0 / 64, bias=EPS)
            P.act(rsd[k2][0:64, :], rsd[k2][0:64, :], AF.Exp, R=[bt], W=[bt], scale=-0.5)

        def f7():
            P.stt(yst[0:64, h, qs], to[k2][0:64, :], gsc[0:64, 0:1], rsd[k2][0:64, :], ALU.mult, ALU.mult, R=[bt, bgsc], W=[byst])
            if Q == 7:
                P.store(D['YT_d'][640 + h * 64:640 + (h + 1) * 64, :], yst[0:64, h, :], byst)

        defer(1, f0)
        defer(4, f1(0))
        defer(5, f2(0))
        defer(6, f1(1))
        defer(7, f2(1))
        defer(9, f3)
        defer(12, f4)
        defer(13, f5)
        defer(15, f6)
        defer(17, f7)

    run_pipeline(items, 2, s0, s1)
    ar.release(m0)
    P.barrier()
    P.retire(nb0)


def phase_p3a(C, l, xin, xout):
    P, ar, D = C.P, C.ar, C.D
    nb0 = len(P.bufs)
    m0 = ar.mark()
    Wg = ar.alloc([8, 4096], BF16)
    Wb = ar.alloc([9, DM], BF16)
    Wo = ar.alloc([8, DM], BF16)
    bWg, bWb, bWo = P.buf('Wg'), P.buf('Wb'), P.buf('Wo')
    w_in = D['w_in'][l].rearrange("(kc p) c -> p kc c", p=128)
    for kc in range(8):
        P.dma('pool', Wg[:, kc, :], w_in[:, kc, GATE_OFF:GATE_OFF + 4096], bWg, W=(bWg,))
    P.dma('pool', Wb[:, 0:4, :], D['w_ba'][l].rearrange("(kc p) c -> p kc c", p=128), bWb, W=(bWb,))
    P.dma('pool', Wb[:, 4, :], D['w_bb'][l], bWb, W=(bWb,))
    P.dma('pool', Wb[:, 5:7, :], D['w_bc'][l].rearrange("(kc p) c -> p kc c", p=128), bWb, W=(bWb,))
    P.dma('pool', Wb[:, 7:9, :], D['w_bd'][l].rearrange("(kc p) c -> p kc c", p=128), bWb, W=(bWb,))
    P.dma('pool', Wo, D['w_out'][l].rearrange("(kc p) c -> p kc c", p=128), bWo, W=(bWo,))
    hTc = [ar.alloc([8, 512], BF16) for _ in range(2)]
    YTc = [ar.alloc([9, 512], BF16) for _ in range(2)]
    bh = P.bufs_n('hTc', 2)
    by = P.bufs_n('YTc', 2)
    mT = [ar.alloc([8, 512], BF16) for _ in range(2)]
    bmT = P.bufs_n('mT', 2)
    sig = [ar.alloc([512], F32) for _ in range(3)]
    bsig = P.bufs_n('sig', 3)
    tmp = [ar.alloc([512], F32) for _ in range(2)]
    btmp = P.bufs_n('mtmp', 2)
    macc = [ar.alloc([512], F32) for _ in range(2)]
    bmacc = P.bufs_n('macc', 2)
    xt = [ar.alloc([DM], F32) for _ in range(2)]
    bxt = P.bufs_n('x3', 2)
    xo = [ar.alloc([DM], F32) for _ in range(2)]
    bxo = P.bufs_n('xo3', 2)
    hT_v = D['hT_d'].rearrange("(kc p) t -> p kc t", p=128)
    YT_v = D['YT_d'].rearrange("(kc p) t -> p kc t", p=128)
    branches = [(0, 4), (4, 5), (5, 7), (7, 9)]
    cn = {'it': 0, 'si': 0, 'ti': 0}

    def gates(tg):
        g2 = tg % 2
        ts_ = slice(tg * 512, (tg + 1) * 512)
        P.load(hTc[g2], hT_v[:, :, ts_], bh[g2])
        P.load(YTc[g2], YT_v[:, :, ts_], by[g2])
        for ct in range(8):
            ma, bma = macc[ct % 2], bmacc[ct % 2]
            for br in range(4):
                it = cn['it']
                cn['it'] += 1
                pg, bpg = C.pb[(it % 2) * 2], C.bpb[(it % 2) * 2]
                py, bpy = C.pb[(it % 2) * 2 + 1], C.bpb[(it % 2) * 2 + 1]
                c0 = br * 1024 + ct * 128
                for kc in range(8):
                    P.mm(pg[:, :], Wg[:, kc, c0:c0 + 128], hTc[g2][:, kc, :], start=(kc == 0), stop=(kc == 7), R=[bWg, bh[g2]], W=[bpg])
                b0, b1 = branches[br]
                for bi in range(b0, b1):
                    P.mm(py[:, :], Wb[:, bi, ct * 128:(ct + 1) * 128], YTc[g2][:, bi, :], start=(bi == b0), stop=(bi == b1 - 1),
                         R=[bWb, by[g2]], W=[bpy])
                s_, bs_ = sig[cn['si'] % 3], bsig[cn['si'] % 3]
                cn['si'] += 1
                P.act(s_, pg[:, :], AF.Sigmoid, R=[bpg], W=[bs_])
                if br == 0:
                    P.tt('dve', ma, s_, py[:, :], ALU.mult, R=[bs_, bpy], W=[bma])
                else:
                    t_, bt_ = tmp[cn['ti'] % 2], btmp[cn['ti'] % 2]
                    cn['ti'] += 1
                    P.tt('dve', t_, s_, py[:, :], ALU.mult, R=[bs_, bpy], W=[bt_])
                    if br < 3:
                        P.tt('pool', ma, ma, t_, ALU.add, R=[bma, bt_], W=[bma])
                    else:
                        P.tt('pool', mT[g2][:, ct, :], ma, t_, ALU.add, R=[bma, bt_], W=[bmT[g2]])

    def wout(tg):
        g2 = tg % 2
        for tt in range(4):
            tok0 = tg * 512 + tt * 128
            xi = (tg * 4 + tt) % 2
            P.load(xt[xi], xin[tok0:tok0 + 128, :], bxt[xi])
            for cg in range(2):
                po, bpo = C.pb[4 + (cg + 2 * tt) % 4], C.bpb[4 + (cg + 2 * tt) % 4]
                for kc in range(8):
                    P.mm(po[:, :], mT[g2][:, kc, tt * 128:(tt + 1) * 128], Wo[:, kc, cg * 512:(cg + 1) * 512], start=(kc == 0), stop=(kc == 7),
                         R=[bmT[g2], bWo], W=[bpo])
                P.tt('dve', xo[xi][:, cg * 512:(cg + 1) * 512], po[:, :], xt[xi][:, cg * 512:(cg + 1) * 512], ALU.add,
                     R=[bpo, bxt[xi]], W=[bxo[xi]])
            P.store(xout[tok0:tok0 + 128, :], xo[xi], bxo[xi])
    gates(0)
    for tg in range(8):
        if tg + 1 < 8:
            gates(tg + 1)
        wout(tg)
    ar.release(m0)
    P.barrier()
    P.retire(nb0)


def phase_p3b(C, l, xin, xout):
    P, ar, D = C.P, C.ar, C.D
    nb0 = len(P.bufs)
    m0 = ar.mark()
    Wu = ar.alloc([8, 4096], BF16)
    Wd = ar.alloc([32, DM], BF16)
    gb = ar.alloc([DM], F32)
    bWu, bWd, bgb = P.buf('Wu'), P.buf('Wd'), P.buf('gbm')
    wu_v = D['w_up'][l].rearrange("(kc p) c -> p kc c", p=128)
    wd_v = D['w_down'][l].rearrange("(kc p) c -> p kc c", p=128)
    P.load(gb, D['gb_mlp'][l], bgb)
    bWuc = P.bufs_n('Wuc', 8)
    for cb in range(8):
        P.dma('pool', Wu[:, :, cb * 512:(cb + 1) * 512], wu_v[:, :, cb * 512:(cb + 1) * 512], bWuc[cb], W=(bWuc[cb],))
    for k4 in range(8):
        P.dma('pool', Wd[:, k4 * 4:(k4 + 1) * 4, :], wd_v[:, k4 * 4:(k4 + 1) * 4, :], bWd, W=(bWd,))
    xt = [ar.alloc([DM], F32) for _ in range(2)]
    bxt = P.bufs_n('x4', 2)
    xo = [ar.alloc([DM], F32) for _ in range(1)]
    bxo = P.bufs_n('xo4', 1)
    ss = [ar.alloc([1], F32) for _ in range(2)]
    bss = P.bufs_n('ss4', 2)
    hb = [ar.alloc([DM], BF16) for _ in range(2)]
    bhb = P.bufs_n('hb4', 2)
    hmT = [ar.alloc([8, 512], BF16) for _ in range(2)]
    bhm = P.bufs_n('hmT', 2)
    uT = ar.alloc([32, 512], BF16)
    buT = P.buf('uT')
    rl = [ar.alloc([512], F32) for _ in range(2)]
    brl = P.bufs_n('rl', 2)
    st_ = {'xi': 0, 'it': 0}

    def norm_tile(tg, tt, part):
        g2 = tg % 2
        tok0 = tg * 512 + tt * 128
        if part == 0:
            i = j = st_['xi'] % 2
            st_['xi'] += 1
            st_['nj'] = j
            P.load(xt[i], xin[tok0:tok0 + 128, :], bxt[i])
            P.act(hb[j], xt[i], AF.Square, R=[bxt[i]], W=[bhb[j], bss[j]], accum_out=ss[j])
            P.act(ss[j], ss[j], AF.Ln, R=[bss[j]], W=[bss[j]], scale=1.0 / DM, bias=EPS)
            P.act(ss[j], ss[j], AF.Exp, R=[bss[j]], W=[bss[j]], scale=-0.5)
            P.stt(hb[j], xt[i], ss[j], gb, ALU.mult, ALU.mult, R=[bxt[i], bss[j], bgb], W=[bhb[j]])
        else:
            j = st_['nj']
            pbv = C.pb[6 + j].bitcast(BF16)
            for kc in range(8):
                P.tr(pbv[:, kc * 128:(kc + 1) * 128], hb[j][:, kc * 128:(kc + 1) * 128], C.ident, R=[bhb[j], C.b_const], W=[C.bpb[6 + j]])
            P.cp('dve', hmT[g2][:, :, tt * 128:(tt + 1) * 128], pbv.rearrange("p (k t) -> p k t", k=8), R=[C.bpb[6 + j]], W=[bhm[g2]])

    def norm(tg):
        for tt in range(4):
            norm_tile(tg, tt, 0)
            norm_tile(tg, tt, 1)

    def up(tg):
        g2 = tg % 2
        for mt in range(32):
            it = st_['it']
            st_['it'] += 1
            pu, bpu = C.pb[it % 3], C.bpb[it % 3]
            r_, br_ = rl[it % 2], brl[it % 2]
            for kc in range(8):
                P.mm(pu[:, :], Wu[:, kc, mt * 128:(mt + 1) * 128], hmT[g2][:, kc, :], start=(kc == 0), stop=(kc == 7), R=[bWuc[mt // 4], bhm[g2]], W=[bpu])
            P.act(r_, pu[:, :], AF.Relu, R=[bpu], W=[br_])
            P.tt('pool' if mt % 2 else 'dve', uT[:, mt, :], r_, r_, ALU.mult, R=[br_], W=[buT])

    def down(tg):
        for tt in range(4):
            if tg + 1 < 8:
                norm_tile(tg + 1, tt, 0)
            tok0 = tg * 512 + tt * 128
            i = st_['xi'] % 2
            st_['xi'] += 1
            P.load(xt[i], xin[tok0:tok0 + 128, :], bxt[i])
            for cg in range(2):
                pd, bpd = C.pb[3 + (cg + 2 * tt) % 3], C.bpb[3 + (cg + 2 * tt) % 3]
                for mt in range(32):
                    P.mm(pd[:, :], uT[:, mt, tt * 128:(tt + 1) * 128], Wd[:, mt, cg * 512:(cg + 1) * 512], start=(mt == 0), stop=(mt == 31),
                         R=[buT, bWd], W=[bpd])
                P.tt('dve', xo[0][:, cg * 512:(cg + 1) * 512], pd[:, :], xt[i][:, cg * 512:(cg + 1) * 512], ALU.add,
                     R=[bpd, bxt[i]], W=[bxo[0]])
            P.store(xout[tok0:tok0 + 128, :], xo[0], bxo[0])
            if tg + 1 < 8:
                norm_tile(tg + 1, tt, 1)

    norm(0)
    for tg in range(8):
        up(tg)
        down(tg)
    ar.release(m0)
    P.barrier()
    P.retire(nb0)


ARENA = 207 * 1024 + 512
NSEM_POOL = 96


class SemPool:
    def __init__(self, nc, stack, n):
        self.sems = [stack.enter_context(nc.semaphore(f"sm{i}")) for i in range(n)]
        self.i = 0

        self.free = []

    def get(self):
        if self.free:
            return self.free.pop()
        s_ = self.sems[self.i]
        self.i += 1
        return (s_, 0)

    def put(self, sem, cnt):
        self.free.append((sem, cnt))


def build(n_layers=2, phases=None, dbg=False):
    nc = bass.Bass("TRN2", target_bir_lowering=False)
    D = {}
    x = nc.dram_tensor("x", [S, DM], F32, kind="ExternalInput").ap()
    for name, shp in PARAM_SHAPES.items():
        D[name] = nc.dram_tensor(name, shp, F32, kind="ExternalInput").ap()
    y = nc.dram_tensor("y", [S, DM], F32, kind="ExternalOutput").ap()
    sk = "ExternalOutput" if dbg else "Internal"
    D['hT_d'] = nc.dram_tensor("hT_d", [DM, S], BF16, kind=sk).ap()
    D['QKT_d'] = nc.dram_tensor("QKT_d", [NQKB * 128, S], BF16, kind=sk).ap()
    D['Vnat_d'] = nc.dram_tensor("Vnat_d", [S, 780], BF16, kind=sk).ap()
    D['Vb_d'] = nc.dram_tensor("Vb_d", [2, S, 130], BF16, kind=sk).ap()
    D['YT_d'] = nc.dram_tensor("YT_d", [1152, S], BF16, kind=sk).ap()
    x1_d = nc.dram_tensor("x1_d", [S, DM], F32, kind=sk).ap()
    x2_d = nc.dram_tensor("x2_d", [S, DM], F32, kind=sk).ap()
    with ExitStack() as stack:
        arena_t = stack.enter_context(nc.sbuf_tensor("arena", [128, ARENA], U8))
        pbs = [stack.enter_context(nc.psum_tensor(f"pb{i}", [128, 512], F32)) for i in range(8)]
        sp_ = SemPool(nc, stack, NSEM_POOL)

        class _St:
            def enter_context(self, cm):
                raise RuntimeError

        P = Prog.__new__(Prog)
        P.nc = nc
        P.ops = {e: [] for e in ENGS}
        P.bufs = []
        P.esem = {e: sp_.get()[0] for e in ENGS}
        P.nsem = len(ENGS)

        def dma(eng, out, in_, owner, R=(), W=()):
            if owner.sem is None:
                owner.sem, owner.cnt = sp_.get()
            owner.cnt += 16
            o = Op(eng, lambda e: e.dma_start(out=out, in_=in_))
            o.dma_sem = owner.sem
            P._track(o, ('dma', owner.sem, owner.cnt), 'dma', R, W)
            P.ops[eng].append(o)
            return o
        P.dma = dma

        def retire(nb0):
            for b in P.bufs[nb0:]:
                if b.sem is not None:
                    sp_.put(b.sem, b.cnt)
                    b.sem = None
            del P.bufs[nb0:]
        P.retire = retire
        block = stack.enter_context(nc.Block())
        C = Ctx()
        C.P, C.D = P, D
        C.ar = Arena(arena_t, ARENA)
        C.pb = [p[:, :] for p in pbs]
        C.bpb = P.bufs_n('pb', 8)
        C.cbf = C.ar.alloc([NCONST], BF16)
        C.b_const = P.buf('consts')
        P.dma('pool', C.cbf, D['consts'], C.b_const, W=(C.b_const,))
        C.ident = C.cbf[:, C_IDENT:C_IDENT + 128]
        C.bd64 = C.cbf[:, C_BD64:C_BD64 + 128]
        C.bd32 = C.cbf[:, C_BD32:C_BD32 + 128]
        C.psw64 = C.cbf[:, C_PSW64:C_PSW64 + 128]
        C.psw32 = C.cbf[:, C_PSW32:C_PSW32 + 128]
        C.ones_bf = C.cbf[:, C_ONES:C_ONES + 128]
        all_ph = ['p1', 'a', 'b', 'c', 'd', 'p3a', 'p3b']
        phases = phases or all_ph
        for l in range(n_layers):
            xin = x if l == 0 else x2_d
            xfin = y if l == n_layers - 1 else x2_d
            if 'p1' in phases:
                phase_p1(C, l, xin)
            if 'a' in phases:
                mixer_a(C, l)
            if 'b' in phases:
                mixer_b(C, l)
            if 'c' in phases:
                mixer_c(C, l)
            if 'd' in phases:
                mixer_d(C, l)
            if 'p3a' in phases:
                phase_p3a(C, l, xin, x1_d)
            if 'p3b' in phases:
                phase_p3b(C, l, x1_d, xfin)
        P.barrier()
        P.emit(block)
        C.nsem = sp_.i
    return nc, C


_CACHE = {}


def kernel(**inputs):
    x = np.ascontiguousarray(np.asarray(inputs['x'], dtype=np.float32))
    params = host_prep(inputs)
    if 'nc' not in _CACHE:
        _CACHE['nc'] = build()[0]
    nc = _CACHE['nc']
    in_maps = []
    for b in range(8):
        m = {'x': x[b]}
        m.update(params)
        in_maps.append(m)
    res = run_bass_kernel_spmd(nc, in_maps, core_ids=list(range(8)))
    return np.stack([np.asarray(r['y'], dtype=np.float32) for r in res.results], axis=0)
```

```python
import math
from contextlib import ExitStack
import numpy as np
import concourse.bass as bass
import concourse.mybir as mybir
from concourse.bass_utils import run_bass_kernel_spmd

F32 = mybir.dt.float32
BF16 = mybir.dt.bfloat16
U8 = mybir.dt.uint8
AF = mybir.ActivationFunctionType
ALU = mybir.AluOpType

S = 4096
DM = 1024
NT = 32
EPS = 1e-6
INC = 7552
ENGS = ['pe', 'act', 'dve', 'pool', 'sp']
SAME_ENG_SYNC = ('act', 'dve', 'pool')


class Buf:
    __slots__ = ('name', 'w', 'rs', 'sem', 'cnt')

    def __init__(self, name):
        self.name = name
        self.w = None
        self.rs = {}
        self.sem = None
        self.cnt = 0


class Op:
    __slots__ = ('eng', 'fn', 'deps', 'dwaits', 'needs_inc', 'semval', 'dma_sem')

    def __init__(self, eng, fn):
        self.eng = eng
        self.fn = fn
        self.deps = set()
        self.dwaits = {}
        self.needs_inc = False
        self.semval = 0
        self.dma_sem = None


class Prog:
    def __init__(self, nc, stack):
        self.nc = nc
        self.stack = stack
        self.ops = {e: [] for e in ENGS}
        self.bufs = []
        self.esem = {e: stack.enter_context(nc.semaphore("s_" + e)) for e in ENGS}
        self.nsem = len(ENGS)

    def buf(self, name):
        b = Buf(name)
        self.bufs.append(b)
        return b

    def bufs_n(self, name, n):
        return [self.buf(f"{name}{i}") for i in range(n)]

    def _add_ev(self, o, ev):
        if ev is None:
            return
        if ev[0] == 'op':
            d = ev[1]
            if d.eng == o.eng and d.eng not in SAME_ENG_SYNC:
                return
            o.deps.add(d)
            d.needs_inc = True
        else:
            _, sem, val = ev
            cur = o.dwaits.get(id(sem))
            if cur is None or cur[1] < val:
                o.dwaits[id(sem)] = (sem, val)

    def _track(self, o, ev, key, R, W):
        for b in R:
            self._add_ev(o, b.w)
        for b in W:
            self._add_ev(o, b.w)
            for e2 in b.rs.values():
                self._add_ev(o, e2)
        for b in R:
            b.rs[key] = ev
        for b in W:
            b.w = ev
            b.rs = {}

    def op(self, eng, fn, R=(), W=()):
        o = Op(eng, fn)
        self._track(o, ('op', o), eng, R, W)
        self.ops[eng].append(o)
        return o

    def dma(self, eng, out, in_, owner, R=(), W=()):
        if owner.sem is None:
            owner.sem = self.stack.enter_context(self.nc.semaphore("d_" + owner.name))
            self.nsem += 1
        owner.cnt += 16
        o = Op(eng, lambda e: e.dma_start(out=out, in_=in_))
        o.dma_sem = owner.sem
        self._track(o, ('dma', owner.sem, owner.cnt), 'dma', R, W)
        self.ops[eng].append(o)
        return o

    def load(self, out, in_, owner, eng='sp'):
        return self.dma(eng, out, in_, owner, R=(), W=(owner,))

    def store(self, out, in_, owner, eng='sp'):
        return self.dma(eng, out, in_, owner, R=(owner,), W=())

    def barrier(self):
        o = Op('sp', lambda e: e.nop())
        for E in ENGS:
            if self.ops[E]:
                last = None
                for c in reversed(self.ops[E]):
                    if c.fn is not None and c.dma_sem is None:
                        last = c
                        break
                if last is not None and E != 'sp':
                    o.deps.add(last)
                    last.needs_inc = True
        for b in self.bufs:
            if b.sem is not None and b.cnt > 0:
                o.dwaits[id(b.sem)] = (b.sem, b.cnt)
        o.needs_inc = True
        self.ops['sp'].append(o)
        for E in ENGS:
            if E != 'sp':
                w = Op(E, None)
                w.deps.add(o)
                self.ops[E].append(w)
        for b in self.bufs:
            b.w = None
            b.rs = {}

    def mm(self, out, lhsT, rhs, start=True, stop=True, R=(), W=()):
        return self.op('pe', lambda e: e.matmul(out, lhsT, rhs, start=start, stop=stop), R, W)

    def tr(self, out, in_, ident, R=(), W=()):
        return self.op('pe', lambda e: e.transpose(out, in_, ident), R, W)

    def act(self, out, in_, func, R=(), W=(), **kw):
        return self.op('act', lambda e: e.activation(out=out, in_=in_, func=func, **kw), R, W)

    def tt(self, eng, out, in0, in1, op, R=(), W=()):
        return self.op(eng, lambda e: e.tensor_tensor(out=out, in0=in0, in1=in1, op=op), R, W)

    def ts(self, eng, out, in0, s1, s2, op0, op1=None, R=(), W=()):
        if op1 is None:
            return self.op(eng, lambda e: e.tensor_scalar(out=out, in0=in0, scalar1=s1, scalar2=None, op0=op0), R, W)
        return self.op(eng, lambda e: e.tensor_scalar(out=out, in0=in0, scalar1=s1, scalar2=s2, op0=op0, op1=op1), R, W)

    def stt(self, out, in0, scalar, in1, op0, op1, R=(), W=()):
        return self.op('dve', lambda e: e.scalar_tensor_tensor(out=out, in0=in0, scalar=scalar, in1=in1, op0=op0, op1=op1), R, W)

    def cp(self, eng, out, in_, R=(), W=()):
        if eng == 'act':
            return self.op('act', lambda e: e.copy(out=out, in_=in_), R, W)
        return self.op(eng, lambda e: e.tensor_copy(out=out, in_=in_), R, W)

    def recip(self, out, in_, R=(), W=()):
        return self.op('dve', lambda e: e.reciprocal(out=out, in_=in_), R, W)

    def memset(self, eng, ap, val, W=()):
        return self.op(eng, lambda e: e.memset(ap, val), (), W)

    def emit(self, block):
        for E in ENGS:
            c = 0
            for o in self.ops[E]:
                if o.needs_inc and o.dma_sem is None:
                    c += 1
                    o.semval = c
        esem = self.esem

        def run(E, eng):
            waited = {}
            for o in self.ops[E]:
                needs = []
                for d in o.deps:
                    needs.append((esem[d.eng], d.semval))
                for sem, val in o.dwaits.values():
                    needs.append((sem, val))
                for sem, val in needs:
                    k = id(sem)
                    if waited.get(k, 0) < val:
                        eng.wait_ge(sem, val)
                        waited[k] = val
                if o.fn is not None:
                    ins = o.fn(eng)
                    if o.dma_sem is not None:
                        ins.then_inc(o.dma_sem, 16)
                    elif o.needs_inc:
                        ins.then_inc(esem[E], 1)

        @block.tensor
        def _(e):
            run('pe', e)

        @block.scalar
        def _(e):
            run('act', e)

        @block.vector
        def _(e):
            run('dve', e)

        @block.gpsimd
        def _(e):
            run('pool', e)

        @block.sync
        def _(e):
            run('sp', e)


class Arena:
    def __init__(self, t, size):
        self.t = t
        self.size = size
        self.off = 0

    def alloc(self, free_shape, dtype):
        es = 4 if dtype == F32 else (2 if dtype == BF16 else 1)
        n = 1
        for s_ in free_shape:
            n *= s_
        nb = (n * es + 31) // 32 * 32
        assert self.off + nb <= self.size, f"arena overflow {self.off}+{nb}>{self.size}"
        ap = self.t[:, self.off:self.off + n * es].bitcast(dtype)
        self.off += nb
        if len(free_shape) == 2:
            ap = ap.rearrange("p (a b) -> p a b", a=free_shape[0])
        elif len(free_shape) == 3:
            ap = ap.rearrange("p (a b c) -> p a b c", a=free_shape[0], b=free_shape[1])
        return ap

    def mark(self):
        return self.off

    def release(self, m):
        self.off = m


QK_BLOCKS = []
for i in range(4):
    QK_BLOCKS.append((i * 128, 'n64'))
QK_BLOCKS.append((512, 'n64'))
for g, kind in enumerate(['n64', 'p4', 'p16']):
    QK_BLOCKS.append((768 + g * 128, kind))
for g, kind in enumerate(['n64', 'p4', 'p16']):
    QK_BLOCKS.append((1152 + g * 128, kind))
for i in range(2):
    QK_BLOCKS.append((1920 + i * 128, 'n32'))
for i in range(2):
    QK_BLOCKS.append((2176 + i * 128, 'n32'))
for i in range(2):
    QK_BLOCKS.append((2688 + i * 128, 'd'))
for i in range(2):
    QK_BLOCKS.append((2944 + i * 128, 'd'))
NQKB = len(QK_BLOCKS)
VNAT_COLS = [(640, 128), (2432, 256), (3200, 256), (1536, 128)]
GATE_OFF = 3456
B_DIL = [1, 4, 16]


def perm_tokens(d):
    L = S // d
    j = np.arange(S)
    return (j % L) * d + (j // L)


def rope_tabs(dim):
    half = dim // 2
    inv = np.power(np.float32(10000.0), -(np.arange(0, dim, 2, dtype=np.float32) / np.float32(dim))).astype(np.float32)
    ang = (np.arange(S, dtype=np.float32)[:, None] * inv[None, :]).astype(np.float32)
    c = np.cos(ang).astype(np.float32)
    s_ = np.sin(ang).astype(np.float32)
    p = np.arange(128) % dim
    cosT = c[:, p % half].T.copy()
    sgn = np.where(p < half, -1.0, 1.0).astype(np.float32)
    sinT = (s_[:, p % half] * sgn[None, :]).T.copy()
    return cosT, sinT


def d_tables():
    rows = 64
    r0 = np.clip(np.arange(rows) - 4, 0, rows - 8)
    cj = np.arange(64)
    c0 = np.clip(cj - 8, 0, 48)
    col_ok = (cj[None, :] >= c0[:, None]) & (cj[None, :] < c0[:, None] + 16)
    dc = np.clip(cj[None, :] - cj[:, None], -15, 15) + 15
    tabs = {}
    tab_list = []
    pairs = []
    for n in range(32):
        lo = r0[2 * n] // 2
        hi = (r0[2 * n + 1] + 7) // 2
        pl = []
        for m in range(lo, hi + 1):
            valid = np.zeros((128, 128), dtype=bool)
            dr = np.zeros((128, 128), dtype=np.int64)
            for a in range(2):
                for b in range(2):
                    rho = 2 * m + a
                    i = 2 * n + b
                    ok = (r0[i] <= rho) and (rho <= r0[i] + 7)
                    if ok:
                        valid[a * 64:(a + 1) * 64, b * 64:(b + 1) * 64] = col_ok.T
                        dr[a * 64:(a + 1) * 64, b * 64:(b + 1) * 64] = rho - i + 7
            key = (m - n, valid.tobytes())
            if key not in tabs:
                tabs[key] = len(tab_list)
                tab_list.append((dr, valid))
            pl.append((m, tabs[key]))
        pairs.append(pl)
    dcidx = np.zeros((128, 128), dtype=np.int64)
    for a in range(2):
        for b in range(2):
            dcidx[a * 64:(a + 1) * 64, b * 64:(b + 1) * 64] = dc.T
    return tab_list, pairs, dcidx


D_TABS, D_PAIRS, D_DC = d_tables()
NTAB = len(D_TABS)

C_IDENT = 0
C_BD64 = 128
C_BD32 = 256
C_PSW64 = 384
C_PSW32 = 512
C_ONES = 640
C_MLO = 768
C_MHI = 896
C_NLO4 = 1024
C_NHI4 = 1536
C_DVALID = 2048
NCONST = C_DVALID + NTAB * 128


def make_consts():
    c = np.zeros((128, NCONST), dtype=np.float32)
    p = np.arange(128)
    c[:, C_IDENT:C_IDENT + 128] = np.eye(128, dtype=np.float32)
    c[:, C_BD64:C_BD64 + 128] = (p[:, None] // 64 == p[None, :] // 64)
    c[:, C_BD32:C_BD32 + 128] = (p[:, None] // 32 == p[None, :] // 32)
    part64 = (p // 64) * 64 + (p % 64 + 32) % 64
    part32 = (p // 32) * 32 + (p % 32 + 16) % 32
    c[:, C_PSW64:C_PSW64 + 128] = (p[:, None] == part64[None, :])
    c[:, C_PSW32:C_PSW32 + 128] = (p[:, None] == part32[None, :])
    c[:, C_ONES:C_ONES + 128] = 1.0
    c[:, C_MLO:C_MLO + 128] = (p[:, None] >= p[None, :])
    c[:, C_MHI:C_MHI + 128] = (p[:, None] <= p[None, :])
    for r_ in range(4):
        c[:, C_NLO4 + r_ * 128:C_NLO4 + (r_ + 1) * 128] = (c[:, C_MLO:C_MLO + 128] - 1.0) * 30000.0
        c[:, C_NHI4 + r_ * 128:C_NHI4 + (r_ + 1) * 128] = (c[:, C_MHI:C_MHI + 128] - 1.0) * 30000.0
    for t, (dr, valid) in enumerate(D_TABS):
        c[:, C_DVALID + t * 128:C_DVALID + (t + 1) * 128] = valid
    return c


def host_prep(inp):
    f = lambda a: np.ascontiguousarray(np.asarray(a, dtype=np.float32))
    out = {}
    out['w_in'] = f(inp['w_in'])
    out['w_ba'] = f(inp['w_branch_a'])
    out['w_bb'] = f(inp['w_branch_b'])
    out['w_bc'] = f(inp['w_branch_c'])
    out['w_bd'] = f(inp['w_branch_d'])
    out['w_out'] = f(inp['w_out'])
    out['w_up'] = f(inp['w_up'])
    out['w_down'] = f(inp['w_down'])
    out['gb_attn'] = f(np.broadcast_to(f(inp['attn_norm_g'])[:, None, :], (2, 128, DM)))
    out['gb_mlp'] = f(np.broadcast_to(f(inp['mlp_norm_g'])[:, None, :], (2, 128, DM)))
    gcol = np.zeros((2, 128, NQKB), dtype=np.float32)
    aq, bq, cq, dq = f(inp['a_qk_norm_g']), f(inp['b_qk_norm_g']), f(inp['c_qk_norm_g']), f(inp['d_qk_norm_g'])
    for l in range(2):
        for b in range(4):
            gcol[l, :, b] = np.tile(aq[l, 0], 2)
        gcol[l, :, 4] = np.tile(aq[l, 1], 2)
        for b in range(5, 8):
            gcol[l, :, b] = np.tile(bq[l, 0], 2)
        for b in range(8, 11):
            gcol[l, :, b] = np.tile(bq[l, 1], 2)
        for b in range(11, 13):
            gcol[l, :, b] = np.tile(cq[l, 0], 4)
        for b in range(13, 15):
            gcol[l, :, b] = np.tile(cq[l, 1], 4)
        for b in range(15, 17):
            gcol[l, :, b] = np.tile(dq[l, 0], 2)
        for b in range(17, 19):
            gcol[l, :, b] = np.tile(dq[l, 1], 2)
    out['gcol'] = gcol
    out['sinkb'] = f(np.broadcast_to(f(inp['a_sink'])[:, None, :], (2, 128, 8)))
    out['lamb'] = f(np.broadcast_to(f(inp['c_lambda']).reshape(2, 1, 128), (2, 128, 128)))
    out['gsub'] = f(np.tile(f(inp['c_subln_g']), (1, 2)).reshape(2, 128, 1))
    rpb = f(inp['d_rel_bias'])
    db = np.zeros((2, 4, 128, NTAB, 128), dtype=np.float32)
    for t, (dr, valid) in enumerate(D_TABS):
        db[:, :, :, t, :] = rpb[:, :, dr, D_DC]
    out['dbias'] = db.reshape(2, 4, 128, NTAB * 128)
    c64, s64 = rope_tabs(64)
    c32, s32 = rope_tabs(32)
    p4, p16 = perm_tokens(4), perm_tokens(16)
    out['rope'] = np.ascontiguousarray(np.stack([c64, s64, c64[:, p4], s64[:, p4], c64[:, p16], s64[:, p16], c32, s32], 0))
    out['consts'] = make_consts()
    return out


PARAM_SHAPES = {
    'w_in': [2, DM, INC], 'w_ba': [2, 512, DM], 'w_bb': [2, 128, DM], 'w_bc': [2, 256, DM], 'w_bd': [2, 256, DM],
    'w_out': [2, DM, DM], 'w_up': [2, DM, 4096], 'w_down': [2, 4096, DM],
    'gb_attn': [2, 128, DM], 'gb_mlp': [2, 128, DM], 'gcol': [2, 128, NQKB], 'sinkb': [2, 128, 8],
    'lamb': [2, 128, 128], 'gsub': [2, 128, 1], 'dbias': [2, 4, 128, NTAB * 128],
    'rope': [8, 128, S], 'consts': [128, NCONST],
}


class Ctx:
    pass


def sl(start, n, step):
    return slice(start, start + (n - 1) * step + 1, step)


def ring(lst, i):
    return lst[i % len(lst)]


def phase_p1(C, l, xin):
    P, ar, D = C.P, C.ar, C.D
    nb0 = len(P.bufs)
    m0 = ar.mark()
    hT = ar.alloc([8, S], BF16)
    b_hT = P.buf('hT')
    gb = ar.alloc([DM], F32)
    b_gb = P.buf('gb')
    gcol = ar.alloc([NQKB], F32)
    b_gcol = P.buf('gcol')
    P.load(gb, D['gb_attn'][l], b_gb)
    P.load(gcol, D['gcol'][l], b_gcol)
    m1 = ar.mark()
    xt = [ar.alloc([DM], F32) for _ in range(3)]
    bx = P.bufs_n('xt', 3)
    junk = ar.alloc([DM], BF16)
    b_junk = P.buf('junk')
    ss = [ar.alloc([1], F32) for _ in range(4)]
    bss = P.bufs_n('ss', 4)
    hb = [ar.alloc([DM], BF16) for _ in range(4)]
    bhb = P.bufs_n('hb', 4)
    for tt in range(NT):
        i, j = tt % 3, tt % 4
        P.load(xt[i], xin[tt * 128:(tt + 1) * 128, :], bx[i])
        P.act(junk, xt[i], AF.Square, R=[bx[i]], W=[b_junk, bss[j]], accum_out=ss[j])
        P.act(ss[j], ss[j], AF.Ln, R=[bss[j]], W=[bss[j]], scale=1.0 / DM, bias=EPS)
        P.act(ss[j], ss[j], AF.Exp, R=[bss[j]], W=[bss[j]], scale=-0.5)
        P.stt(hb[j], xt[i], ss[j], gb, ALU.mult, ALU.mult, R=[bx[i], bss[j], b_gb], W=[bhb[j]])
        pbv = C.pb[j].bitcast(BF16)
        for kc in range(8):
            P.tr(pbv[:, kc * 128:(kc + 1) * 128], hb[j][:, kc * 128:(kc + 1) * 128], C.ident, R=[bhb[j], C.b_const], W=[C.bpb[j]])
        P.cp('act' if tt % 2 else 'dve', hT[:, :, tt * 128:(tt + 1) * 128], pbv.rearrange("p (k t) -> p k t", k=8), R=[C.bpb[j]], W=[b_hT])
    for kc in range(8):
        P.store(D['hT_d'][kc * 128:(kc + 1) * 128, :], hT[:, kc, :], b_hT)
    ar.release(m1)
    tabs2 = [ar.alloc([2, S], F32) for _ in range(2)]
    b_tabs2 = P.bufs_n('ropetab', 2)
    wq = [ar.alloc([8, 128], BF16) for _ in range(2)]
    bwq = P.bufs_n('wq', 2)
    NB = 4
    sq = [ar.alloc([512], BF16) for _ in range(NB)]
    bsq = P.bufs_n('sq', NB)
    xg = [ar.alloc([512], BF16) for _ in range(NB)]
    bxg = P.bufs_n('xg', NB)
    rs = [ar.alloc([512], F32) for _ in range(NB)]
    brs = P.bufs_n('rs', NB)
    ta = [ar.alloc([512], F32) for _ in range(NB)]
    bta = P.bufs_n('ta', NB)
    tb = [ar.alloc([512], F32) for _ in range(NB)]
    btb = P.bufs_n('tb', NB)
    ob = [ar.alloc([512], BF16) for _ in range(NB)]
    bob = P.bufs_n('ob', NB)
    w_in = D['w_in'][l].rearrange("(kc p) c -> p kc c", p=128)
    blk_order = [0, 1, 2, 3, 4, 5, 8, 6, 9, 7, 10, 11, 12, 13, 14, 15, 16, 17, 18]
    items = [(blk, tc) for blk in blk_order for tc in range(8)]
    TABK = {'n64': 0, 'p4': 2, 'p16': 4, 'n32': 6, 'd': None}
    variants = [0, 2, 4, 6]
    state = {'vi': -1}

    def load_tab(vi):
        if vi < len(variants):
            tb_, bt_ = tabs2[vi % 2], b_tabs2[vi % 2]
            P.load(tb_[:, 0, :], D['rope'][variants[vi]], bt_)
            P.load(tb_[:, 1, :], D['rope'][variants[vi] + 1], bt_)
    load_tab(0)

    def tokf(kind, tc):
        if kind == 'p4':
            r, h0 = tc // 2, (tc % 2) * 512
            return lambda kc: hT[:, kc, sl(r + 4 * h0, 512, 4)]
        if kind == 'p16':
            return lambda kc: hT[:, kc, :].rearrange("p (m r) -> p r m", r=16)[:, 2 * tc:2 * tc + 2, :]
        return lambda kc: hT[:, kc, tc * 512:(tc + 1) * 512]

    def s0(itm, t):
        blk, tc = itm
        coff, kind = QK_BLOCKS[blk]
        wi = blk % 2
        if tc == 0:
            P.dma('pool', wq[wi], w_in[:, :, coff:coff + 128], bwq[wi], W=(bwq[wi],))
        tok = tokf(kind, tc)
        pA, bA = C.pb[t % 3], C.bpb[t % 3]
        for kc in range(8):
            P.mm(pA[:, :], wq[wi][:, kc, :], tok(kc), start=(kc == 0), stop=(kc == 7), R=[bwq[wi], b_hT], W=[bA])

    def s1(itm, t, defer):
        blk, tc = itm
        coff, kind = QK_BLOCKS[blk]
        tabkind = TABK[kind]
        if tabkind is not None:
            vi = variants.index(tabkind)
            if vi != state['vi']:
                state['vi'] = vi
                load_tab(vi + 1)
            tabs, b_tabs = tabs2[vi % 2], b_tabs2[vi % 2]
        dh = 32 if kind == 'n32' else 64
        bd = C.bd32 if dh == 32 else C.bd64
        psw = C.psw32 if dh == 32 else C.psw64
        k = t % NB
        pA, pS, pR = C.pb[t % 3], C.pb[3 + (t % 2)], C.pb[5 + (t % 3)]
        bA, bS, bR = C.bpb[t % 3], C.bpb[3 + (t % 2)], C.bpb[5 + (t % 3)]
        P.act(sq[k], pA[:, :], AF.Square, R=[bA], W=[bsq[k]])
        P.act(xg[k], pA[:, :], AF.Copy, R=[bA, b_gcol], W=[bxg[k]], scale=gcol[:, blk:blk + 1])
        P.mm(pS[:, :], bd, sq[k], R=[bsq[k], C.b_const], W=[bS])
        if kind != 'd':
            P.mm(pR[:, :], psw, xg[k], R=[bxg[k], C.b_const], W=[bR])
        csl = slice(tc * 512, (tc + 1) * 512)
        if kind != 'd':
            P.tt('pool', ta[k], xg[k], tabs[:, 0, csl], ALU.mult, R=[bxg[k], b_tabs], W=[bta[k]])
            P.tt('dve', tb[k], pR[:, :], tabs[:, 1, csl], ALU.mult, R=[bR, b_tabs], W=[btb[k]])

        def s1b():
            P.act(rs[k], pS[:, :], AF.Ln, R=[bS], W=[brs[k]], scale=1.0 / dh, bias=EPS)
            P.act(rs[k], rs[k], AF.Exp, R=[brs[k]], W=[brs[k]], scale=-0.5)
            if kind == 'd':
                P.tt('dve', ob[k], xg[k], rs[k], ALU.mult, R=[bxg[k], brs[k]], W=[bob[k]])
            else:
                P.tt('dve', ta[k], ta[k], tb[k], ALU.add, R=[bta[k], btb[k]], W=[bta[k]])
                P.tt('dve', ob[k], ta[k], rs[k], ALU.mult, R=[bta[k], brs[k]], W=[bob[k]])
            P.store(D['QKT_d'][blk * 128:(blk + 1) * 128, csl], ob[k], bob[k])
        defer(1, s1b)

    run_pipeline(items, 2, s0, s1)
    ar.release(m1)
    wv = ar.alloc([8, 1024], BF16)
    b_wv = P.buf('wv')
    o = 0
    for (coff, n) in VNAT_COLS + [(1536 + 128, 256)]:
        P.dma('pool', wv[:, :, o:o + n], w_in[:, :, coff:coff + n], b_wv, W=(b_wv,))
        o += n
    vn = [ar.alloc([12, 65], BF16) for _ in range(2)]
    bvn = P.bufs_n('vn', 2)
    vb = [ar.alloc([2, 2, 65], BF16) for _ in range(2)]
    bvb = P.bufs_n('vbp', 2)
    for j in range(2):
        P.memset('pool', vn[j][:, :, 64:65], 1.0, W=[bvn[j]])
        P.memset('pool', vb[j][:, :, :, 64:65], 1.0, W=[bvb[j]])
    p4, p16 = perm_tokens(4), perm_tokens(16)
    for tt in range(NT):
        j = tt % 2
        pa, pbk, pc = C.pb[j * 3], C.pb[j * 3 + 1], C.pb[j * 3 + 2]
        ba, bb_, bc = C.bpb[j * 3], C.bpb[j * 3 + 1], C.bpb[j * 3 + 2]
        for kc in range(8):
            P.mm(pa[:, :], hT[:, kc, tt * 128:(tt + 1) * 128], wv[:, kc, 0:512], start=(kc == 0), stop=(kc == 7), R=[b_hT, b_wv], W=[ba])
        for kc in range(8):
            P.mm(pbk[:, 0:256], hT[:, kc, tt * 128:(tt + 1) * 128], wv[:, kc, 512:768], start=(kc == 0), stop=(kc == 7), R=[b_hT, b_wv], W=[bb_])
        t4 = int(p4[tt * 128])
        t16 = int(p16[tt * 128])
        for kc in range(8):
            P.mm(pc[:, 0:128], hT[:, kc, sl(t4, 128, 4)], wv[:, kc, 768:896], start=(kc == 0), stop=(kc == 7), R=[b_hT, b_wv], W=[bc])
        for kc in range(8):
            P.mm(pc[:, 128:256], hT[:, kc, sl(t16, 128, 16)], wv[:, kc, 896:1024], start=(kc == 0), stop=(kc == 7), R=[b_hT, b_wv], W=[bc])
        P.cp('act', vn[j][:, 0:8, 0:64], pa[:, :].rearrange("p (h d) -> p h d", h=8), R=[ba], W=[bvn[j]])
        P.cp('dve', vn[j][:, 8:12, 0:64], pbk[:, 0:256].rearrange("p (h d) -> p h d", h=4), R=[bb_], W=[bvn[j]])
        P.cp('dve', vb[j][:, :, :, 0:64], pc[:, 0:256].rearrange("p (g h d) -> p g h d", g=2, h=2), R=[bc], W=[bvb[j]])
        P.store(D['Vnat_d'][tt * 128:(tt + 1) * 128, :], vn[j].rearrange("p h d -> p (h d)"), bvn[j])
        for g in range(2):
            P.store(D['Vb_d'][g, tt * 128:(tt + 1) * 128, :], vb[j][:, g].rearrange("p h d -> p (h d)"), bvb[j])
    ar.release(m0)
    P.barrier()
    P.retire(nb0)


def run_pipeline(items, LA, s0, s1):
    n = len(items)
    pend = {}
    cur = [0]

    def defer(delay, fn):
        pend.setdefault(cur[0] + delay, []).append(fn)

    for t in range(n + LA):
        cur[0] = t
        if t < n:
            s0(items[t], t)
        if t >= LA:
            s1(items[t - LA], t - LA, defer)
        for fn in pend.pop(t, []):
            fn()
    while pend:
        t2 = min(pend)
        cur[0] = t2
        for fn in pend.pop(t2):
            fn()


def finalize_norm(C, acc, bacc, n, dst, bdst, shape3=None, esink=None, tagk=0, defer=None, after=None, dl=(1, 3, 5, 6)):
    P = C.P
    k = tagk % 2
    r, rhi, rlo = C.fr[k], C.frhi[k], C.frlo[k]
    br = C.bfr[k]
    bc, bbc = C.pb[7], C.bpb[7]
    bcs, bbcs = C.fbcs[k], C.bfbcs[k]
    src = acc[64:65, 0:n]

    def g0():
        if esink is not None:
            j, q = shape3
            P.tt('dve', r[64:65, 0:n].rearrange("p (j q) -> p j q", j=j), src.rearrange("p (j q) -> p j q", j=j),
                 esink, ALU.add, R=[bacc, C.b_esk], W=[br])
            P.act(r[64:65, 0:n], r[64:65, 0:n], AF.Ln, R=[br], W=[br])
        else:
            P.act(r[64:65, 0:n], src, AF.Ln, R=[bacc], W=[br])
        P.act(r[64:65, 0:n], r[64:65, 0:n], AF.Exp, R=[br], W=[br], scale=-1.0)
        P.cp('dve', rhi[64:65, 0:n], r[64:65, 0:n], R=[br], W=[br])
        P.tt('dve', rlo[64:65, 0:n], r[64:65, 0:n], rhi[64:65, 0:n], ALU.subtract, R=[br], W=[br])

    def g1():
        P.mm(bc[0:64, 0:n], C.ones_bf[64:65, 0:64], rhi[64:65, 0:n], start=True, stop=False, R=[br, C.b_const], W=[bbc])
        P.mm(bc[0:64, 0:n], C.ones_bf[64:65, 0:64], rlo[64:65, 0:n], start=False, stop=True, R=[br, C.b_const], W=[bbc])

    def g2():
        P.cp('act', bcs[0:64, 0:n], bc[0:64, 0:n], R=[bbc], W=[bbcs])

    def g3():
        a0 = acc[0:64, 0:n]
        b0 = bcs[0:64, 0:n]
        if shape3 is not None:
            j, q = shape3
            a0 = a0.rearrange("p (j q) -> p j q", j=j)
            b0 = b0.rearrange("p (j q) -> p j q", j=j)
        P.tt('dve', dst, a0, b0, ALU.mult, R=[bacc, bbcs], W=[bdst])
        if after is not None:
            after()

    if defer is None:
        g0(); g1(); g2(); g3()
    else:
        defer(dl[0], g0); defer(dl[1], g1); defer(dl[2], g2); defer(dl[3], g3)


def alloc_fin(C):
    ar, P = C.ar, C.P
    C.fr = [ar.alloc([512], F32) for _ in range(2)]
    C.frhi = [ar.alloc([512], BF16) for _ in range(2)]
    C.frlo = [ar.alloc([512], BF16) for _ in range(2)]
    C.bfr = P.bufs_n('fr', 2)
    C.fbcs = [ar.alloc([512], F32) for _ in range(2)]
    C.bfbcs = P.bufs_n('fbcs', 2)


def mixer_a(C, l):
    P, ar, D = C.P, C.ar, C.D
    nb0 = len(P.bufs)
    m0 = ar.mark()
    alloc_fin(C)
    QT = ar.alloc([4, S], BF16)
    bQT = P.buf('aQT')
    KT = ar.alloc([2, S], BF16)
    bKT = P.buf('aKT')
    V = ar.alloc([NT, 130], BF16)
    bV = P.buf('aV')
    yst = ar.alloc([4, S], BF16)
    byst = P.buf('ayst')
    esk = ar.alloc([8], F32)
    C.b_esk = P.buf('esk')
    P.load(esk, D['sinkb'][l], C.b_esk)
    P.act(esk, esk, AF.Exp, R=[C.b_esk], W=[C.b_esk])
    for g in range(2):
        for j in range(4):
            h = 4 * g + j
            P.load(QT[g * 64:(g + 1) * 64, j, :], D['QKT_d'][h * 64:(h + 1) * 64, :], bQT)
    P.memset('pool', KT, 0.0, W=[bKT])
    for g in range(2):
        P.load(KT[g * 64:(g + 1) * 64, g, :], D['QKT_d'][512 + g * 64:512 + (g + 1) * 64, :], bKT)
    P.load(V, D['Vnat_d'].rearrange("(t p) c -> p t c", p=128)[:, :, 0:130], bV)
    NP = 4
    pt = [ar.alloc([512], BF16) for _ in range(NP)]
    bpt = P.bufs_n('apt', NP)
    mlo = C.cbf[:, C_MLO:C_MLO + 128].unsqueeze(1).broadcast_to([128, 4, 128])
    mhi = C.cbf[:, C_MHI:C_MHI + 128].unsqueeze(1).broadcast_to([128, 4, 128])
    items = []
    fi = 0
    for g in range(2):
        for n in range(NT):
            ms = [m for m in (n - 1, n, n + 1) if 0 <= m < NT]
            for idx, m in enumerate(ms):
                items.append((g, n, m, idx == 0, idx == len(ms) - 1, fi))
            fi += 1

    def s0(itm, t):
        g, n, m, first, last, f = itm
        ps = slice(g * 64, (g + 1) * 64)
        st, bst = C.pb[t % 4], C.bpb[t % 4]
        P.mm(st[:, :].rearrange("p (j q) -> p j q", j=4), KT[:, g, m * 128:(m + 1) * 128], QT[:, :, n * 128:(n + 1) * 128],
             start=True, stop=(m == n), R=[bKT, bQT], W=[bst])
        if m != n:
            nk = C_NLO4 if m < n else C_NHI4
            P.mm(st[:, :], C.ident, C.cbf[:, nk:nk + 512], start=False, stop=True, R=[C.b_const], W=[bst])

    def s1(itm, t, defer):
        g, n, m, first, last, f = itm
        k = t % NP
        st, bst = C.pb[t % 4], C.bpb[t % 4]
        acc, bacc = C.pb[4 + (f % 3)], C.bpb[4 + (f % 3)]
        P.act(pt[k], st[:, :], AF.Exp, R=[bst], W=[bpt[k]], scale=0.125)
        P.mm(acc[0:65, :], V[:, m, g * 65:(g + 1) * 65], pt[k], start=first, stop=last, R=[bV, bpt[k]], W=[bacc])
        if last:
            es = esk[64:65, 4 * g:4 * g + 4].unsqueeze(2).broadcast_to([1, 4, 128])
            def after(g=g, n=n):
                if n == NT - 1:
                    for j in range(4):
                        h = 4 * g + j
                        P.store(D['YT_d'][h * 64:(h + 1) * 64, :], yst[0:64, j, :], byst)
            finalize_norm(C, acc, bacc, 512, yst[0:64, :, n * 128:(n + 1) * 128], byst, shape3=(4, 128), esink=es, tagk=f,
                          defer=defer, after=after, dl=(1, 2, 3, 4))

    run_pipeline(items, 3, s0, s1)
    ar.release(m0)
    P.barrier()
    P.retire(nb0)


def mixer_b(C, l):
    P, ar, D = C.P, C.ar, C.D
    nb0 = len(P.bufs)
    m0 = ar.mark()
    alloc_fin(C)
    accN = ar.alloc([2, S], F32)
    baccN = P.buf('baccN')
    yst = ar.alloc([2, S], BF16)
    byst = P.buf('byst')
    QTs = [ar.alloc([S], BF16) for _ in range(3)]
    KTs = [ar.alloc([2, S], BF16) for _ in range(3)]
    Vs = [ar.alloc([NT, 130], BF16) for _ in range(3)]
    bQ = P.bufs_n('bQT', 3)
    bK = P.bufs_n('bKT', 3)
    bVv = P.bufs_n('bV', 3)
    NP = 4
    pt = [ar.alloc([512], BF16) for _ in range(NP)]
    bpt = P.bufs_n('bpt', NP)
    items = []
    ai = 0
    for gi, d in enumerate(B_DIL):
        L = S // d
        nb = L // 128
        QT, KT, V = QTs[gi], KTs[gi], Vs[gi]
        bq, bk, bv = bQ[gi], bK[gi], bVv[gi]
        P.load(QT, D['QKT_d'][(5 + gi) * 128:(6 + gi) * 128, :], bq)
        P.memset('pool' if gi % 2 else 'dve', KT, 0.0, W=[bk])
        for jh_ in range(2):
            P.load(KT[jh_ * 64:(jh_ + 1) * 64, jh_, :], D['QKT_d'][(8 + gi) * 128 + jh_ * 64:(8 + gi) * 128 + (jh_ + 1) * 64, :], bk)
        if gi == 0:
            P.load(V, D['Vnat_d'].rearrange("(t p) c -> p t c", p=128)[:, :, 650:780], bv)
        else:
            P.load(V, D['Vb_d'][gi - 1].rearrange("(t p) c -> p t c", p=128), bv)
        for jh in range(2):
            for r in range(d):
                base = r * L
                qblocks = []
                for n_ in range(-1, nb):
                    q0 = 128 * n_ + 64
                    qa, qb = max(q0, 0), min(q0 + 128, L)
                    tiles = []
                    if n_ >= 0:
                        tiles.append((n_, C_MLO))
                    if n_ + 1 < nb:
                        tiles.append((n_ + 1, C_MHI))
                    qblocks.append((qa, qb - qa, qa - q0, tiles))
                for g0 in range(0, len(qblocks), 4):
                    grp = qblocks[g0:g0 + 4]
                    ncols = sum(q[1] for q in grp)
                    pstart = grp[0][0]
                    col = 0
                    nsub = (len(grp) + 1) // 2
                    for bi in range(0, len(grp), 2):
                        sub = grp[bi:bi + 2]
                        plist = []
                        sc = 0
                        for (qa, nq, aoff, tiles) in sub:
                            for ti, (m, mk) in enumerate(tiles):
                                plist.append((sc, nq, aoff, mk, base // 128 + m, col, ti == 0, ti == len(tiles) - 1, base + qa))
                                sc += nq
                            col += nq
                        endinfo = None
                        if bi // 2 == nsub - 1:
                            endinfo = (gi, d, r, pstart, ncols)
                        items.append((QT, KT, V, bq, bk, bv, jh, plist, sc, ai, endinfo))
                    ai += 1

    def s0(itm, t):
        QT, KT, V, bq, bk, bv, jh, plist, sc, a_, endinfo = itm
        ps = slice(jh * 64, (jh + 1) * 64)
        st, bst = C.pb[t % 4], C.bpb[t % 4]
        for (s0_, nq, aoff, mk, tg, c0, first, last, qpos) in plist:
            P.mm(st[:, s0_:s0_ + nq], KT[:, jh, tg * 128:(tg + 1) * 128], QT[:, qpos:qpos + nq], start=True, stop=False, R=[bk, bq], W=[bst])
            nk = C_NLO4 if mk == C_MLO else C_NHI4
            P.mm(st[:, s0_:s0_ + nq], C.ident, C.cbf[:, nk + aoff:nk + aoff + nq], start=False, stop=True, R=[C.b_const], W=[bst])

    def s1(itm, t, defer):
        QT, KT, V, bq, bk, bv, jh, plist, sc, a_, endinfo = itm
        k = t % NP
        st, bst = C.pb[t % 4], C.bpb[t % 4]
        acc, bacc = C.pb[4 + (a_ % 3)], C.bpb[4 + (a_ % 3)]
        P.act(pt[k][:, 0:sc], st[:, 0:sc], AF.Exp, R=[bst], W=[bpt[k]], scale=0.125)
        for (s0_, nq, aoff, mk, tg, c0, first, last, _) in plist:
            P.mm(acc[0:65, c0:c0 + nq], V[:, tg, jh * 65:(jh + 1) * 65], pt[k][:, s0_:s0_ + nq], start=first, stop=last,
                 R=[bv, bpt[k]], W=[bacc])
        if endinfo is not None:
            gi, d, r, pstart, ncols = endinfo
            if gi == 0:
                P.cp('act', accN[0:65, jh, pstart:pstart + ncols], acc[0:65, 0:ncols], R=[bacc], W=[baccN])
            else:
                t0 = r + d * pstart
                view = accN[0:65, jh, sl(t0, ncols, d)]
                P.tt('dve', view, view, acc[0:65, 0:ncols], ALU.add, R=[bacc, baccN], W=[baccN])

    run_pipeline(items, 3, s0, s1)
    fi = 0
    for jh in range(2):
        for ch in range(8):
            cs = slice(ch * 512, (ch + 1) * 512)
            k = fi % 2
            r, rhi, rlo, br = C.fr[k], C.frhi[k], C.frlo[k], C.bfr[k]
            bc, bbc = C.pb[7], C.bpb[7]
            P.act(r[64:65, :], accN[64:65, jh, cs], AF.Ln, R=[baccN], W=[br])
            P.act(r[64:65, :], r[64:65, :], AF.Exp, R=[br], W=[br], scale=-1.0)
            P.cp('dve', rhi[64:65, :], r[64:65, :], R=[br], W=[br])
            P.tt('dve', rlo[64:65, :], r[64:65, :], rhi[64:65, :], ALU.subtract, R=[br], W=[br])
            P.mm(bc[0:64, :], C.ones_bf[64:65, 0:64], rhi[64:65, :], start=True, stop=False, R=[br, C.b_const], W=[bbc])
            P.mm(bc[0:64, :], C.ones_bf[64:65, 0:64], rlo[64:65, :], start=False, stop=True, R=[br, C.b_const], W=[bbc])
            P.tt('dve', yst[0:64, jh, cs], accN[0:64, jh, cs], bc[0:64, :], ALU.mult, R=[baccN, bbc], W=[byst])
            fi += 1
        P.store(D['YT_d'][512 + jh * 64:512 + (jh + 1) * 64, :], yst[0:64, jh, :], byst)
    ar.release(m0)
    P.barrier()
    P.retire(nb0)


def mixer_d(C, l):
    P, ar, D = C.P, C.ar, C.D
    nb0 = len(P.bufs)
    m0 = ar.mark()
    alloc_fin(C)
    QT = ar.alloc([2, S], BF16)
    KT = ar.alloc([4, S], BF16)
    V = ar.alloc([NT, 260], BF16)
    bQT, bKT, bV = P.buf('dQT'), P.buf('dKT'), P.buf('dV')
    yst = ar.alloc([4, S], BF16)
    byst = P.buf('dyst')
    EB = ar.alloc([4, NTAB * 128], BF16)
    bEB = P.buf('dEB')
    tmpb = [ar.alloc([NTAB * 128], F32) for _ in range(2)]
    btmp = P.bufs_n('dtmp', 2)
    P.memset('pool', KT[:, 0:2, :], 0.0, W=[bKT])
    P.memset('dve', KT[:, 2:4, :], 0.0, W=[bKT])
    for i in range(2):
        P.load(QT[:, i, :], D['QKT_d'][(15 + i) * 128:(16 + i) * 128, :], bQT)
    for h_ in range(4):
        r0_ = (h_ % 2) * 64
        P.load(KT[r0_:r0_ + 64, h_, :], D['QKT_d'][17 * 128 + h_ * 64:17 * 128 + (h_ + 1) * 64, :], bKT)
    P.load(V, D['Vnat_d'].rearrange("(t p) c -> p t c", p=128)[:, :, 390:650], bV)
    for h in range(4):
        P.load(tmpb[h % 2], D['dbias'][l, h], btmp[h % 2])
        P.act(tmpb[h % 2], tmpb[h % 2], AF.Exp, R=[btmp[h % 2]], W=[btmp[h % 2]])
        P.tt('dve', EB[:, h, :], tmpb[h % 2], C.cbf[:, C_DVALID:C_DVALID + NTAB * 128], ALU.mult, R=[btmp[h % 2], C.b_const], W=[bEB])
    NP = 4
    pt = [ar.alloc([512], BF16) for _ in range(NP)]
    bpt = P.bufs_n('dpt', NP)
    items = []
    fi = 0
    for h in range(4):
        for n4 in range(8):
            plist = []
            for n in range(n4 * 4, n4 * 4 + 4):
                pl = D_PAIRS[n]
                for pi, (m, tab) in enumerate(pl):
                    plist.append((n, m, tab, pi == 0, pi == len(pl) - 1))
            nch = (len(plist) + 3) // 4
            for ci in range(nch):
                items.append((h, n4, plist[ci * 4:ci * 4 + 4], fi, ci == nch - 1))
            fi += 1

    def s0(itm, t):
        h, n4, chunk, f, endg = itm
        bq = h // 2
        ps = slice((h % 2) * 64, (h % 2 + 1) * 64)
        st, bst = C.pb[t % 4], C.bpb[t % 4]
        for i, (n, m, tab, first, last) in enumerate(chunk):
            P.mm(st[:, i * 128:(i + 1) * 128], KT[:, h, m * 128:(m + 1) * 128], QT[:, bq, n * 128:(n + 1) * 128],
                 R=[bKT, bQT], W=[bst])

    def s1(itm, t, defer):
        h, n4, chunk, f, endg = itm
        k = t % NP
        st, bst = C.pb[t % 4], C.bpb[t % 4]
        acc, bacc = C.pb[4 + (f % 3)], C.bpb[4 + (f % 3)]
        used = len(chunk) * 128
        P.act(pt[k][:, 0:used], st[:, 0:used], AF.Exp, R=[bst], W=[bpt[k]], scale=0.125)
        for i, (n, m, tab, first, last) in enumerate(chunk):
            P.tt('pool' if i % 2 else 'dve', pt[k][:, i * 128:(i + 1) * 128], pt[k][:, i * 128:(i + 1) * 128],
                 EB[:, h, tab * 128:(tab + 1) * 128], ALU.mult, R=[bpt[k], bEB], W=[bpt[k]])
        for i, (n, m, tab, first, last) in enumerate(chunk):
            qc = (n % 4) * 128
            P.mm(acc[0:65, qc:qc + 128], V[:, m, h * 65:(h + 1) * 65], pt[k][:, i * 128:(i + 1) * 128], start=first, stop=last,
                 R=[bV, bpt[k]], W=[bacc])
        if endg:
            def after(h=h, n4=n4):
                if n4 == 7:
                    P.store(D['YT_d'][896 + h * 64:896 + (h + 1) * 64, :], yst[0:64, h, :], byst)
            finalize_norm(C, acc, bacc, 512, yst[0:64, h, n4 * 512:(n4 + 1) * 512], byst, tagk=f, defer=defer, after=after, dl=(1, 2, 3, 4))

    run_pipeline(items, 3, s0, s1)
    ar.release(m0)
    P.barrier()
    P.retire(nb0)


def mixer_c(C, l):
    P, ar, D = C.P, C.ar, C.D
    lam_init = 0.8 - 0.6 * math.exp(-0.3 * l)
    nb0 = len(P.bufs)
    m0 = ar.mark()
    alloc_fin(C)
    QT = ar.alloc([2, S], BF16)
    KT = ar.alloc([8, S], BF16)
    V = ar.alloc([NT, 260], BF16)
    bQT, bKT, bV = P.buf('cQT'), P.buf('cKT'), P.buf('cV')
    yst = ar.alloc([4, S], BF16)
    byst = P.buf('cyst')
    P.memset('pool', KT[:, 0:4, :], 0.0, W=[bKT])
    P.memset('dve', KT[:, 4:8, :], 0.0, W=[bKT])
    for g2 in range(2):
        P.load(QT[:, g2, :], D['QKT_d'][(11 + g2) * 128:(12 + g2) * 128, :], bQT)
    for b in range(8):
        sl_ = (b % 4) * 32
        row = (b // 4) * 128 + sl_
        P.load(KT[sl_:sl_ + 32, b, :], D['QKT_d'][13 * 128 + row:13 * 128 + row + 32, :], bKT)
    P.load(V, D['Vnat_d'].rearrange("(t p) c -> p t c", p=128)[:, :, 130:390], bV)
    lamb = ar.alloc([128], F32)
    blam = P.buf('lam')
    lt = ar.alloc([2, 32], F32)
    l2 = ar.alloc([2], F32)
    nlam = ar.alloc([1], F32)
    gsc = ar.alloc([1], F32)
    bgsc = P.buf('gsc')
    P.load(lamb, D['lamb'][l], blam)
    P.load(gsc, D['gsub'][l], bgsc)
    lv = lamb.rearrange("p (a b c) -> p a b c", a=2, b=2)
    P.tt('dve', lt, lv[:, :, 0, :], lv[:, :, 1, :], ALU.mult, R=[blam], W=[blam])
    P.op('dve', lambda e: e.reduce_sum(out=l2, in_=lt, axis=mybir.AxisListType.X), R=[blam], W=[blam])
    P.act(l2, l2, AF.Exp, R=[blam], W=[blam])
    P.tt('dve', nlam, l2[:, 0:1], l2[:, 1:2], ALU.subtract, R=[blam], W=[blam])
    P.ts('dve', nlam, nlam, lam_init, -1.0, ALU.add, ALU.mult, R=[blam], W=[blam])
    P.ts('dve', gsc, gsc, 1.0 - lam_init, None, ALU.mult, R=[bgsc], W=[bgsc])
    NP = 4
    pt = [ar.alloc([512], BF16) for _ in range(NP)]
    bpt = P.bufs_n('cpt', NP)
    to = [ar.alloc([512], F32) for _ in range(2)]
    t1 = [ar.alloc([512], F32) for _ in range(2)]
    sqb = [ar.alloc([512], BF16) for _ in range(2)]
    rsd = [ar.alloc([512], F32) for _ in range(2)]
    bto = P.bufs_n('cto', 2)
    scale = 32.0 ** -0.5
    items = []
    fi = 0
    for h in range(4):
        for Q in range(8):
            for c in range(2):
                for kt in range(NT):
                    items.append((h, Q, c, kt, fi))
            fi += 1

    def s0(itm, t):
        h, Q, c, kt, f = itm
        b = 2 * h + c
        st, bst = C.pb[t % 3], C.bpb[t % 3]
        P.mm(st[:, :], KT[:, b, kt * 128:(kt + 1) * 128], QT[:, b // 4, Q * 512:(Q + 1) * 512], R=[bKT, bQT], W=[bst])

    def s1(itm, t, defer):
        h, Q, c, kt, f = itm
        qs = slice(Q * 512, (Q + 1) * 512)
        k = t % NP
        st, bst = C.pb[t % 3], C.bpb[t % 3]
        a0 = 3 + 2 * (f % 2)
        accs = [C.pb[a0], C.pb[a0 + 1]]
        baccs = [C.bpb[a0], C.bpb[a0 + 1]]
        P.act(pt[k], st[:, :], AF.Exp, R=[bst], W=[bpt[k]], scale=scale)
        P.mm(accs[c][0:65, :], V[:, kt, h * 65:(h + 1) * 65], pt[k], start=(kt == 0), stop=(kt == NT - 1),
             R=[bV, bpt[k]], W=[baccs[c]])
        if not (c == 1 and kt == NT - 1):
            return
        k2 = f % 2
        bt = bto[k2]
        bc, bbc = C.pb[7], C.bpb[7]

        def f0():
            for c_ in range(2):
                r, rhi, rlo, br = C.fr[c_], C.frhi[c_], C.frlo[c_], C.bfr[c_]
                P.recip(r[64:65, :], accs[c_][64:65, :], R=[baccs[c_]], W=[br])
                if c_ == 1:
                    P.ts('dve', r[64:65, :], r[64:65, :], nlam[64:65, 0:1], None, ALU.mult, R=[br, blam], W=[br])
                P.cp('dve', rhi[64:65, :], r[64:65, :], R=[br], W=[br])
                P.tt('dve', rlo[64:65, :], r[64:65, :], rhi[64:65, :], ALU.subtract, R=[br], W=[br])

        def f1(c_):
            def fn():
                r, rhi, rlo, br = C.fr[c_], C.frhi[c_], C.frlo[c_], C.bfr[c_]
                P.mm(bc[0:64, :], C.ones_bf[64:65, 0:64], rhi[64:65, :], start=True, stop=False, R=[br, C.b_const], W=[bbc])
                P.mm(bc[0:64, :], C.ones_bf[64:65, 0:64], rlo[64:65, :], start=False, stop=True, R=[br, C.b_const], W=[bbc])
            return fn

        def f2(c_):
            def fn():
                P.cp('act', C.fbcs[c_][0:64, :], bc[0:64, :], R=[bbc], W=[C.bfbcs[c_]])
            return fn

        def f3():
            P.tt('dve', to[k2][0:64, :], accs[0][0:64, :], C.fbcs[0][0:64, :], ALU.mult, R=[baccs[0], C.bfbcs[0]], W=[bt])
            P.tt('dve', t1[k2][0:64, :], accs[1][0:64, :], C.fbcs[1][0:64, :], ALU.mult, R=[baccs[1], C.bfbcs[1], bt], W=[bt])
            P.tt('pool', to[k2][0:64, :], to[k2][0:64, :], t1[k2][0:64, :], ALU.add, R=[bt], W=[bt])

        def f4():
            P.act(sqb[k2][0:64, :], to[k2][0:64, :], AF.Square, R=[bt], W=[bt])

        def f5():
            P.mm(bc[0:64, :], C.ones_bf[0:64, 0:64], sqb[k2][0:64, :], R=[bt, C.b_const], W=[bbc])

        def f6():
            P.act(rsd[k2][0:64, :], bc[0:64, :], AF.Ln, R=[bbc], W=[bt], scale=1.0 / 64, bias=EPS)
            P.act(rsd[k2][0:64, :], rsd[k2][0:64, :], AF.Exp, R=[bt], W=[bt], scale=-0.5)

        def f7():
            P.stt(yst[0:64, h, qs], to[k2][0:64, :], gsc[0:64, 0:1], rsd[k2][0:64, :], ALU.mult, ALU.mult, R=[bt, bgsc], W=[byst])
            if Q == 7:
                P.store(D['YT_d'][640 + h * 64:640 + (h + 1) * 64, :], yst[0:64, h, :], byst)

        defer(1, f0)
        defer(4, f1(0))
        defer(5, f2(0))
        defer(6, f1(1))
        defer(7, f2(1))
        defer(9, f3)
        defer(12, f4)
        defer(13, f5)
        defer(15, f6)
        defer(17, f7)

    run_pipeline(items, 2, s0, s1)
    ar.release(m0)
    P.barrier()
    P.retire(nb0)


def phase_p3a(C, l, xin, xout):
    P, ar, D = C.P, C.ar, C.D
    nb0 = len(P.bufs)
    m0 = ar.mark()
    Wg = ar.alloc([8, 4096], BF16)
    Wb = ar.alloc([9, DM], BF16)
    Wo = ar.alloc([8, DM], BF16)
    bWg, bWb, bWo = P.buf('Wg'), P.buf('Wb'), P.buf('Wo')
    w_in = D['w_in'][l].rearrange("(kc p) c -> p kc c", p=128)
    for kc in range(8):
        P.dma('pool', Wg[:, kc, :], w_in[:, kc, GATE_OFF:GATE_OFF + 4096], bWg, W=(bWg,))
    P.dma('pool', Wb[:, 0:4, :], D['w_ba'][l].rearrange("(kc p) c -> p kc c", p=128), bWb, W=(bWb,))
    P.dma('pool', Wb[:, 4, :], D['w_bb'][l], bWb, W=(bWb,))
    P.dma('pool', Wb[:, 5:7, :], D['w_bc'][l].rearrange("(kc p) c -> p kc c", p=128), bWb, W=(bWb,))
    P.dma('pool', Wb[:, 7:9, :], D['w_bd'][l].rearrange("(kc p) c -> p kc c", p=128), bWb, W=(bWb,))
    P.dma('pool', Wo, D['w_out'][l].rearrange("(kc p) c -> p kc c", p=128), bWo, W=(bWo,))
    hTc = [ar.alloc([8, 512], BF16) for _ in range(2)]
    YTc = [ar.alloc([9, 512], BF16) for _ in range(2)]
    bh = P.bufs_n('hTc', 2)
    by = P.bufs_n('YTc', 2)
    mT = [ar.alloc([8, 512], BF16) for _ in range(2)]
    bmT = P.bufs_n('mT', 2)
    sig = [ar.alloc([512], F32) for _ in range(3)]
    bsig = P.bufs_n('sig', 3)
    tmp = [ar.alloc([512], F32) for _ in range(2)]
    btmp = P.bufs_n('mtmp', 2)
    macc = [ar.alloc([512], F32) for _ in range(2)]
    bmacc = P.bufs_n('macc', 2)
    xt = [ar.alloc([DM], F32) for _ in range(2)]
    bxt = P.bufs_n('x3', 2)
    xo = [ar.alloc([DM], F32) for _ in range(2)]
    bxo = P.bufs_n('xo3', 2)
    hT_v = D['hT_d'].rearrange("(kc p) t -> p kc t", p=128)
    YT_v = D['YT_d'].rearrange("(kc p) t -> p kc t", p=128)
    branches = [(0, 4), (4, 5), (5, 7), (7, 9)]
    cn = {'it': 0, 'si': 0, 'ti': 0}

    def gates(tg):
        g2 = tg % 2
        ts_ = slice(tg * 512, (tg + 1) * 512)
        P.load(hTc[g2], hT_v[:, :, ts_], bh[g2])
        P.load(YTc[g2], YT_v[:, :, ts_], by[g2])
        for ct in range(8):
            ma, bma = macc[ct % 2], bmacc[ct % 2]
            for br in range(4):
                it = cn['it']
                cn['it'] += 1
                pg, bpg = C.pb[(it % 2) * 2], C.bpb[(it % 2) * 2]
                py, bpy = C.pb[(it % 2) * 2 + 1], C.bpb[(it % 2) * 2 + 1]
                c0 = br * 1024 + ct * 128
                for kc in range(8):
                    P.mm(pg[:, :], Wg[:, kc, c0:c0 + 128], hTc[g2][:, kc, :], start=(kc == 0), stop=(kc == 7), R=[bWg, bh[g2]], W=[bpg])
                b0, b1 = branches[br]
                for bi in range(b0, b1):
                    P.mm(py[:, :], Wb[:, bi, ct * 128:(ct + 1) * 128], YTc[g2][:, bi, :], start=(bi == b0), stop=(bi == b1 - 1),
                         R=[bWb, by[g2]], W=[bpy])
                s_, bs_ = sig[cn['si'] % 3], bsig[cn['si'] % 3]
                cn['si'] += 1
                P.act(s_, pg[:, :], AF.Sigmoid, R=[bpg], W=[bs_])
                if br == 0:
                    P.tt('dve', ma, s_, py[:, :], ALU.mult, R=[bs_, bpy], W=[bma])
                else:
                    t_, bt_ = tmp[cn['ti'] % 2], btmp[cn['ti'] % 2]
                    cn['ti'] += 1
                    P.tt('dve', t_, s_, py[:, :], ALU.mult, R=[bs_, bpy], W=[bt_])
                    if br < 3:
                        P.tt('pool', ma, ma, t_, ALU.add, R=[bma, bt_], W=[bma])
                    else:
                        P.tt('pool', mT[g2][:, ct, :], ma, t_, ALU.add, R=[bma, bt_], W=[bmT[g2]])

    def wout(tg):
        g2 = tg % 2
        for tt in range(4):
            tok0 = tg * 512 + tt * 128
            xi = (tg * 4 + tt) % 2
            P.load(xt[xi], xin[tok0:tok0 + 128, :], bxt[xi])
            for cg in range(2):
                po, bpo = C.pb[4 + (cg + 2 * tt) % 4], C.bpb[4 + (cg + 2 * tt) % 4]
                for kc in range(8):
                    P.mm(po[:, :], mT[g2][:, kc, tt * 128:(tt + 1) * 128], Wo[:, kc, cg * 512:(cg + 1) * 512], start=(kc == 0), stop=(kc == 7),
                         R=[bmT[g2], bWo], W=[bpo])
                P.tt('dve', xo[xi][:, cg * 512:(cg + 1) * 512], po[:, :], xt[xi][:, cg * 512:(cg + 1) * 512], ALU.add,
                     R=[bpo, bxt[xi]], W=[bxo[xi]])
            P.store(xout[tok0:tok0 + 128, :], xo[xi], bxo[xi])
    gates(0)
    for tg in range(8):
        if tg + 1 < 8:
            gates(tg + 1)
        wout(tg)
    ar.release(m0)
    P.barrier()
    P.retire(nb0)


def phase_p3b(C, l, xin, xout):
    P, ar, D = C.P, C.ar, C.D
    nb0 = len(P.bufs)
    m0 = ar.mark()
    Wu = ar.alloc([8, 4096], BF16)
    Wd = ar.alloc([32, DM], BF16)
    gb = ar.alloc([DM], F32)
    bWu, bWd, bgb = P.buf('Wu'), P.buf('Wd'), P.buf('gbm')
    wu_v = D['w_up'][l].rearrange("(kc p) c -> p kc c", p=128)
    wd_v = D['w_down'][l].rearrange("(kc p) c -> p kc c", p=128)
    P.load(gb, D['gb_mlp'][l], bgb)
    bWuc = P.bufs_n('Wuc', 8)
    for cb in range(8):
        P.dma('pool', Wu[:, :, cb * 512:(cb + 1) * 512], wu_v[:, :, cb * 512:(cb + 1) * 512], bWuc[cb], W=(bWuc[cb],))
    for k4 in range(8):
        P.dma('pool', Wd[:, k4 * 4:(k4 + 1) * 4, :], wd_v[:, k4 * 4:(k4 + 1) * 4, :], bWd, W=(bWd,))
    xt = [ar.alloc([DM], F32) for _ in range(2)]
    bxt = P.bufs_n('x4', 2)
    xo = [ar.alloc([DM], F32) for _ in range(1)]
    bxo = P.bufs_n('xo4', 1)
    ss = [ar.alloc([1], F32) for _ in range(2)]
    bss = P.bufs_n('ss4', 2)
    hb = [ar.alloc([DM], BF16) for _ in range(2)]
    bhb = P.bufs_n('hb4', 2)
    hmT = [ar.alloc([8, 512], BF16) for _ in range(2)]
    bhm = P.bufs_n('hmT', 2)
    uT = ar.alloc([32, 512], BF16)
    buT = P.buf('uT')
    rl = [ar.alloc([512], F32) for _ in range(2)]
    brl = P.bufs_n('rl', 2)
    st_ = {'xi': 0, 'it': 0}

    def norm_tile(tg, tt, part):
        g2 = tg % 2
        tok0 = tg * 512 + tt * 128
        if part == 0:
            i = j = st_['xi'] % 2
            st_['xi'] += 1
            st_['nj'] = j
            P.load(xt[i], xin[tok0:tok0 + 128, :], bxt[i])
            P.act(hb[j], xt[i], AF.Square, R=[bxt[i]], W=[bhb[j], bss[j]], accum_out=ss[j])
            P.act(ss[j], ss[j], AF.Ln, R=[bss[j]], W=[bss[j]], scale=1.0 / DM, bias=EPS)
            P.act(ss[j], ss[j], AF.Exp, R=[bss[j]], W=[bss[j]], scale=-0.5)
            P.stt(hb[j], xt[i], ss[j], gb, ALU.mult, ALU.mult, R=[bxt[i], bss[j], bgb], W=[bhb[j]])
        else:
            j = st_['nj']
            pbv = C.pb[6 + j].bitcast(BF16)
            for kc in range(8):
                P.tr(pbv[:, kc * 128:(kc + 1) * 128], hb[j][:, kc * 128:(kc + 1) * 128], C.ident, R=[bhb[j], C.b_const], W=[C.bpb[6 + j]])
            P.cp('dve', hmT[g2][:, :, tt * 128:(tt + 1) * 128], pbv.rearrange("p (k t) -> p k t", k=8), R=[C.bpb[6 + j]], W=[bhm[g2]])

    def norm(tg):
        for tt in range(4):
            norm_tile(tg, tt, 0)
            norm_tile(tg, tt, 1)

    def up(tg):
        g2 = tg % 2
        for mt in range(32):
            it = st_['it']
            st_['it'] += 1
            pu, bpu = C.pb[it % 3], C.bpb[it % 3]
            r_, br_ = rl[it % 2], brl[it % 2]
            for kc in range(8):
                P.mm(pu[:, :], Wu[:, kc, mt * 128:(mt + 1) * 128], hmT[g2][:, kc, :], start=(kc == 0), stop=(kc == 7), R=[bWuc[mt // 4], bhm[g2]], W=[bpu])
            P.act(r_, pu[:, :], AF.Relu, R=[bpu], W=[br_])
            P.tt('pool' if mt % 2 else 'dve', uT[:, mt, :], r_, r_, ALU.mult, R=[br_], W=[buT])

    def down(tg):
        for tt in range(4):
            if tg + 1 < 8:
                norm_tile(tg + 1, tt, 0)
            tok0 = tg * 512 + tt * 128
            i = st_['xi'] % 2
            st_['xi'] += 1
            P.load(xt[i], xin[tok0:tok0 + 128, :], bxt[i])
            for cg in range(2):
                pd, bpd = C.pb[3 + (cg + 2 * tt) % 3], C.bpb[3 + (cg + 2 * tt) % 3]
                for mt in range(32):
                    P.mm(pd[:, :], uT[:, mt, tt * 128:(tt + 1) * 128], Wd[:, mt, cg * 512:(cg + 1) * 512], start=(mt == 0), stop=(mt == 31),
                         R=[buT, bWd], W=[bpd])
                P.tt('dve', xo[0][:, cg * 512:(cg + 1) * 512], pd[:, :], xt[i][:, cg * 512:(cg + 1) * 512], ALU.add,
                     R=[bpd, bxt[i]], W=[bxo[0]])
            P.store(xout[tok0:tok0 + 128, :], xo[0], bxo[0])
            if tg + 1 < 8:
                norm_tile(tg + 1, tt, 1)

    norm(0)
    for tg in range(8):
        up(tg)
        down(tg)
    ar.release(m0)
    P.barrier()
    P.retire(nb0)


ARENA = 207 * 1024 + 512
NSEM_POOL = 96


class SemPool:
    def __init__(self, nc, stack, n):
        self.sems = [stack.enter_context(nc.semaphore(f"sm{i}")) for i in range(n)]
        self.i = 0

        self.free = []

    def get(self):
        if self.free:
            return self.free.pop()
        s_ = self.sems[self.i]
        self.i += 1
        return (s_, 0)

    def put(self, sem, cnt):
        self.free.append((sem, cnt))


def build(n_layers=2, phases=None, dbg=False):
    nc = bass.Bass("TRN2", target_bir_lowering=False)
    D = {}
    x = nc.dram_tensor("x", [S, DM], F32, kind="ExternalInput").ap()
    for name, shp in PARAM_SHAPES.items():
        D[name] = nc.dram_tensor(name, shp, F32, kind="ExternalInput").ap()
    y = nc.dram_tensor("y", [S, DM], F32, kind="ExternalOutput").ap()
    sk = "ExternalOutput" if dbg else "Internal"
    D['hT_d'] = nc.dram_tensor("hT_d", [DM, S], BF16, kind=sk).ap()
    D['QKT_d'] = nc.dram_tensor("QKT_d", [NQKB * 128, S], BF16, kind=sk).ap()
    D['Vnat_d'] = nc.dram_tensor("Vnat_d", [S, 780], BF16, kind=sk).ap()
    D['Vb_d'] = nc.dram_tensor("Vb_d", [2, S, 130], BF16, kind=sk).ap()
    D['YT_d'] = nc.dram_tensor("YT_d", [1152, S], BF16, kind=sk).ap()
    x1_d = nc.dram_tensor("x1_d", [S, DM], F32, kind=sk).ap()
    x2_d = nc.dram_tensor("x2_d", [S, DM], F32, kind=sk).ap()
    with ExitStack() as stack:
        arena_t = stack.enter_context(nc.sbuf_tensor("arena", [128, ARENA], U8))
        pbs = [stack.enter_context(nc.psum_tensor(f"pb{i}", [128, 512], F32)) for i in range(8)]
        sp_ = SemPool(nc, stack, NSEM_POOL)

        class _St:
            def enter_context(self, cm):
                raise RuntimeError

        P = Prog.__new__(Prog)
        P.nc = nc
        P.ops = {e: [] for e in ENGS}
        P.bufs = []
        P.esem = {e: sp_.get()[0] for e in ENGS}
        P.nsem = len(ENGS)

        def dma(eng, out, in_, owner, R=(), W=()):
            if owner.sem is None:
                owner.sem, owner.cnt = sp_.get()
            owner.cnt += 16
            o = Op(eng, lambda e: e.dma_start(out=out, in_=in_))
            o.dma_sem = owner.sem
            P._track(o, ('dma', owner.sem, owner.cnt), 'dma', R, W)
            P.ops[eng].append(o)
            return o
        P.dma = dma

        def retire(nb0):
            for b in P.bufs[nb0:]:
                if b.sem is not None:
                    sp_.put(b.sem, b.cnt)
                    b.sem = None
            del P.bufs[nb0:]
        P.retire = retire
        block = stack.enter_context(nc.Block())
        C = Ctx()
        C.P, C.D = P, D
        C.ar = Arena(arena_t, ARENA)
        C.pb = [p[:, :] for p in pbs]
        C.bpb = P.bufs_n('pb', 8)
        C.cbf = C.ar.alloc([NCONST], BF16)
        C.b_const = P.buf('consts')
        P.dma('pool', C.cbf, D['consts'], C.b_const, W=(C.b_const,))
        C.ident = C.cbf[:, C_IDENT:C_IDENT + 128]
        C.bd64 = C.cbf[:, C_BD64:C_BD64 + 128]
        C.bd32 = C.cbf[:, C_BD32:C_BD32 + 128]
        C.psw64 = C.cbf[:, C_PSW64:C_PSW64 + 128]
        C.psw32 = C.cbf[:, C_PSW32:C_PSW32 + 128]
        C.ones_bf = C.cbf[:, C_ONES:C_ONES + 128]
        all_ph = ['p1', 'a', 'b', 'c', 'd', 'p3a', 'p3b']
        phases = phases or all_ph
        for l in range(n_layers):
            xin = x if l == 0 else x2_d
            xfin = y if l == n_layers - 1 else x2_d
            if 'p1' in phases:
                phase_p1(C, l, xin)
            if 'a' in phases:
                mixer_a(C, l)
            if 'b' in phases:
                mixer_b(C, l)
            if 'c' in phases:
                mixer_c(C, l)
            if 'd' in phases:
                mixer_d(C, l)
            if 'p3a' in phases:
                phase_p3a(C, l, xin, x1_d)
            if 'p3b' in phases:
                phase_p3b(C, l, x1_d, xfin)
        P.barrier()
        P.emit(block)
        C.nsem = sp_.i
    return nc, C


_CACHE = {}


def kernel(**inputs):
    x = np.ascontiguousarray(np.asarray(inputs['x'], dtype=np.float32))
    params = host_prep(inputs)
    if 'nc' not in _CACHE:
        _CACHE['nc'] = build()[0]
    nc = _CACHE['nc']
    in_maps = []
    for b in range(8):
        m = {'x': x[b]}
        m.update(params)
        in_maps.append(m)
    res = run_bass_kernel_spmd(nc, in_maps, core_ids=list(range(8)))
    return np.stack([np.asarray(r['y'], dtype=np.float32) for r in res.results], axis=0)
```

```python
import math
from contextlib import ExitStack
import numpy as np
import concourse.bass as bass
import concourse.mybir as mybir
from concourse.bass_utils import run_bass_kernel_spmd

F32 = mybir.dt.float32
BF16 = mybir.dt.bfloat16
U8 = mybir.dt.uint8
AF = mybir.ActivationFunctionType
ALU = mybir.AluOpType

S = 4096
DM = 1024
NT = 32
EPS = 1e-6
INC = 7552
ENGS = ['pe', 'act', 'dve', 'pool', 'sp']
SAME_ENG_SYNC = ('act', 'dve', 'pool')
EMBED_WAIT = True


class Buf:
    __slots__ = ('name', 'w', 'rs', 'sem', 'cnt')

    def __init__(self, name):
        self.name = name
        self.w = None
        self.rs = {}
        self.sem = None
        self.cnt = 0


class Op:
    __slots__ = ('eng', 'fn', 'deps', 'dwaits', 'needs_inc', 'semval', 'dma_sem')

    def __init__(self, eng, fn):
        self.eng = eng
        self.fn = fn
        self.deps = set()
        self.dwaits = {}
        self.needs_inc = False
        self.semval = 0
        self.dma_sem = None


class Prog:
    def __init__(self, nc, stack):
        self.nc = nc
        self.stack = stack
        self.ops = {e: [] for e in ENGS}
        self.bufs = []
        self.esem = {e: stack.enter_context(nc.semaphore("s_" + e)) for e in ENGS}
        self.nsem = len(ENGS)

    def buf(self, name):
        b = Buf(name)
        self.bufs.append(b)
        return b

    def bufs_n(self, name, n):
        return [self.buf(f"{name}{i}") for i in range(n)]

    def _add_ev(self, o, ev):
        if ev is None:
            return
        if ev[0] == 'op':
            d = ev[1]
            if d.eng == o.eng and d.eng not in SAME_ENG_SYNC:
                return
            o.deps.add(d)
            d.needs_inc = True
        else:
            _, sem, val = ev
            cur = o.dwaits.get(id(sem))
            if cur is None or cur[1] < val:
                o.dwaits[id(sem)] = (sem, val)

    def _track(self, o, ev, key, R, W):
        for b in R:
            self._add_ev(o, b.w)
        for b in W:
            self._add_ev(o, b.w)
            for e2 in b.rs.values():
                self._add_ev(o, e2)
        for b in R:
            b.rs[key] = ev
        for b in W:
            b.w = ev
            b.rs = {}

    def op(self, eng, fn, R=(), W=()):
        o = Op(eng, fn)
        self._track(o, ('op', o), eng, R, W)
        self.ops[eng].append(o)
        return o

    def dma(self, eng, out, in_, owner, R=(), W=()):
        if owner.sem is None:
            owner.sem = self.stack.enter_context(self.nc.semaphore("d_" + owner.name))
            self.nsem += 1
        owner.cnt += 16
        o = Op(eng, lambda e: e.dma_start(out=out, in_=in_))
        o.dma_sem = owner.sem
        self._track(o, ('dma', owner.sem, owner.cnt), 'dma', R, W)
        self.ops[eng].append(o)
        return o

    def load(self, out, in_, owner, eng='sp'):
        return self.dma(eng, out, in_, owner, R=(), W=(owner,))

    def store(self, out, in_, owner, eng='sp'):
        return self.dma(eng, out, in_, owner, R=(owner,), W=())

    def barrier(self):
        o = Op('sp', lambda e: e.nop())
        for E in ENGS:
            if self.ops[E]:
                last = None
                for c in reversed(self.ops[E]):
                    if c.fn is not None and c.dma_sem is None:
                        last = c
                        break
                if last is not None and E != 'sp':
                    o.deps.add(last)
                    last.needs_inc = True
        for b in self.bufs:
            if b.sem is not None and b.cnt > 0:
                o.dwaits[id(b.sem)] = (b.sem, b.cnt)
        o.needs_inc = True
        self.ops['sp'].append(o)
        for E in ENGS:
            if E != 'sp':
                w = Op(E, None)
                w.deps.add(o)
                self.ops[E].append(w)
        for b in self.bufs:
            b.w = None
            b.rs = {}

    def mm(self, out, lhsT, rhs, start=True, stop=True, R=(), W=()):
        return self.op('pe', lambda e: e.matmul(out, lhsT, rhs, start=start, stop=stop), R, W)

    def tr(self, out, in_, ident, R=(), W=()):
        return self.op('pe', lambda e: e.transpose(out, in_, ident), R, W)

    def act(self, out, in_, func, R=(), W=(), **kw):
        return self.op('act', lambda e: e.activation(out=out, in_=in_, func=func, **kw), R, W)

    def tt(self, eng, out, in0, in1, op, R=(), W=()):
        return self.op(eng, lambda e: e.tensor_tensor(out=out, in0=in0, in1=in1, op=op), R, W)

    def ts(self, eng, out, in0, s1, s2, op0, op1=None, R=(), W=()):
        if op1 is None:
            return self.op(eng, lambda e: e.tensor_scalar(out=out, in0=in0, scalar1=s1, scalar2=None, op0=op0), R, W)
        return self.op(eng, lambda e: e.tensor_scalar(out=out, in0=in0, scalar1=s1, scalar2=s2, op0=op0, op1=op1), R, W)

    def stt(self, out, in0, scalar, in1, op0, op1, R=(), W=()):
        return self.op('dve', lambda e: e.scalar_tensor_tensor(out=out, in0=in0, scalar=scalar, in1=in1, op0=op0, op1=op1), R, W)

    def cp(self, eng, out, in_, R=(), W=()):
        if eng == 'act':
            return self.op('act', lambda e: e.copy(out=out, in_=in_), R, W)
        return self.op(eng, lambda e: e.tensor_copy(out=out, in_=in_), R, W)

    def recip(self, out, in_, R=(), W=()):
        return self.op('dve', lambda e: e.reciprocal(out=out, in_=in_), R, W)

    def memset(self, eng, ap, val, W=()):
        return self.op(eng, lambda e: e.memset(ap, val), (), W)

    def emit(self, block):
        for E in ENGS:
            c = 0
            for o in self.ops[E]:
                if o.needs_inc and o.dma_sem is None:
                    c += 1
                    o.semval = c
        esem = self.esem

        def run(E, eng):
            waited = {}
            for o in self.ops[E]:
                need = {}
                for d in o.deps:
                    sem = esem[d.eng]
                    if need.get(id(sem), (None, 0))[1] < d.semval:
                        need[id(sem)] = (sem, d.semval)
                for sem, val in o.dwaits.values():
                    if need.get(id(sem), (None, 0))[1] < val:
                        need[id(sem)] = (sem, val)
                todo = []
                for k, (sem, val) in need.items():
                    if waited.get(k, 0) < val:
                        todo.append((sem, val))
                        waited[k] = val
                emb = None
                if EMBED_WAIT and o.fn is not None and todo:
                    emb = todo.pop()
                for sem, val in todo:
                    eng.wait_ge(sem, val)
                if o.fn is not None:
                    ins = o.fn(eng)
                    if emb is not None:
                        ins._wait_ge(emb[0], emb[1])
                    if o.dma_sem is not None:
                        ins.then_inc(o.dma_sem, 16)
                    elif o.needs_inc:
                        ins.then_inc(esem[E], 1)

        @block.tensor
        def _(e):
            run('pe', e)

        @block.scalar
        def _(e):
            run('act', e)

        @block.vector
        def _(e):
            run('dve', e)

        @block.gpsimd
        def _(e):
            run('pool', e)

        @block.sync
        def _(e):
            run('sp', e)


class Arena:
    def __init__(self, t, size):
        self.t = t
        self.size = size
        self.off = 0

    def alloc(self, free_shape, dtype):
        es = 4 if dtype == F32 else (2 if dtype == BF16 else 1)
        n = 1
        for s_ in free_shape:
            n *= s_
        nb = (n * es + 31) // 32 * 32
        assert self.off + nb <= self.size, f"arena overflow {self.off}+{nb}>{self.size}"
        ap = self.t[:, self.off:self.off + n * es].bitcast(dtype)
        self.off += nb
        if len(free_shape) == 2:
            ap = ap.rearrange("p (a b) -> p a b", a=free_shape[0])
        elif len(free_shape) == 3:
            ap = ap.rearrange("p (a b c) -> p a b c", a=free_shape[0], b=free_shape[1])
        return ap

    def mark(self):
        return self.off

    def release(self, m):
        self.off = m


QK_BLOCKS = []
for i in range(4):
    QK_BLOCKS.append((i * 128, 'n64'))
QK_BLOCKS.append((512, 'n64'))
for g, kind in enumerate(['n64', 'p4', 'p16']):
    QK_BLOCKS.append((768 + g * 128, kind))
for g, kind in enumerate(['n64', 'p4', 'p16']):
    QK_BLOCKS.append((1152 + g * 128, kind))
for i in range(2):
    QK_BLOCKS.append((1920 + i * 128, 'n32'))
for i in range(2):
    QK_BLOCKS.append((2176 + i * 128, 'n32'))
for i in range(2):
    QK_BLOCKS.append((2688 + i * 128, 'd'))
for i in range(2):
    QK_BLOCKS.append((2944 + i * 128, 'd'))
NQKB = len(QK_BLOCKS)
VNAT_COLS = [(640, 128), (2432, 256), (3200, 256), (1536, 128)]
GATE_OFF = 3456
B_DIL = [1, 4, 16]


def perm_tokens(d):
    L = S // d
    j = np.arange(S)
    return (j % L) * d + (j // L)


def rope_tabs(dim):
    half = dim // 2
    inv = np.power(np.float32(10000.0), -(np.arange(0, dim, 2, dtype=np.float32) / np.float32(dim))).astype(np.float32)
    ang = (np.arange(S, dtype=np.float32)[:, None] * inv[None, :]).astype(np.float32)
    c = np.cos(ang).astype(np.float32)
    s_ = np.sin(ang).astype(np.float32)
    p = np.arange(128) % dim
    cosT = c[:, p % half].T.copy()
    sgn = np.where(p < half, -1.0, 1.0).astype(np.float32)
    sinT = (s_[:, p % half] * sgn[None, :]).T.copy()
    return cosT, sinT


def d_tables():
    rows = 64
    r0 = np.clip(np.arange(rows) - 4, 0, rows - 8)
    cj = np.arange(64)
    c0 = np.clip(cj - 8, 0, 48)
    col_ok = (cj[None, :] >= c0[:, None]) & (cj[None, :] < c0[:, None] + 16)
    dc = np.clip(cj[None, :] - cj[:, None], -15, 15) + 15
    tabs = {}
    tab_list = []
    pairs = []
    for n in range(32):
        lo = r0[2 * n] // 2
        hi = (r0[2 * n + 1] + 7) // 2
        pl = []
        for m in range(lo, hi + 1):
            valid = np.zeros((128, 128), dtype=bool)
            dr = np.zeros((128, 128), dtype=np.int64)
            for a in range(2):
                for b in range(2):
                    rho = 2 * m + a
                    i = 2 * n + b
                    ok = (r0[i] <= rho) and (rho <= r0[i] + 7)
                    if ok:
                        valid[a * 64:(a + 1) * 64, b * 64:(b + 1) * 64] = col_ok.T
                        dr[a * 64:(a + 1) * 64, b * 64:(b + 1) * 64] = rho - i + 7
            key = (m - n, valid.tobytes())
            if key not in tabs:
                tabs[key] = len(tab_list)
                tab_list.append((dr, valid))
            pl.append((m, tabs[key]))
        pairs.append(pl)
    dcidx = np.zeros((128, 128), dtype=np.int64)
    for a in range(2):
        for b in range(2):
            dcidx[a * 64:(a + 1) * 64, b * 64:(b + 1) * 64] = dc.T
    return tab_list, pairs, dcidx


D_TABS, D_PAIRS, D_DC = d_tables()
NTAB = len(D_TABS)

C_IDENT = 0
C_BD64 = 128
C_BD32 = 256
C_PSW64 = 384
C_PSW32 = 512
C_ONES = 640
C_MLO = 768
C_MHI = 896
C_NLO4 = 1024
C_NHI4 = 1536
C_DVALID = 2048
NCONST = C_DVALID + NTAB * 128


def make_consts():
    c = np.zeros((128, NCONST), dtype=np.float32)
    p = np.arange(128)
    c[:, C_IDENT:C_IDENT + 128] = np.eye(128, dtype=np.float32)
    c[:, C_BD64:C_BD64 + 128] = (p[:, None] // 64 == p[None, :] // 64)
    c[:, C_BD32:C_BD32 + 128] = (p[:, None] // 32 == p[None, :] // 32)
    part64 = (p // 64) * 64 + (p % 64 + 32) % 64
    part32 = (p // 32) * 32 + (p % 32 + 16) % 32
    c[:, C_PSW64:C_PSW64 + 128] = (p[:, None] == part64[None, :])
    c[:, C_PSW32:C_PSW32 + 128] = (p[:, None] == part32[None, :])
    c[:, C_ONES:C_ONES + 128] = 1.0
    c[:, C_MLO:C_MLO + 128] = (p[:, None] >= p[None, :])
    c[:, C_MHI:C_MHI + 128] = (p[:, None] <= p[None, :])
    for r_ in range(4):
        c[:, C_NLO4 + r_ * 128:C_NLO4 + (r_ + 1) * 128] = (c[:, C_MLO:C_MLO + 128] - 1.0) * 30000.0
        c[:, C_NHI4 + r_ * 128:C_NHI4 + (r_ + 1) * 128] = (c[:, C_MHI:C_MHI + 128] - 1.0) * 30000.0
    for t, (dr, valid) in enumerate(D_TABS):
        c[:, C_DVALID + t * 128:C_DVALID + (t + 1) * 128] = valid
    return c


def host_prep(inp):
    f = lambda a: np.ascontiguousarray(np.asarray(a, dtype=np.float32))
    out = {}
    out['w_in'] = f(inp['w_in'])
    out['w_ba'] = f(inp['w_branch_a'])
    out['w_bb'] = f(inp['w_branch_b'])
    out['w_bc'] = f(inp['w_branch_c'])
    out['w_bd'] = f(inp['w_branch_d'])
    out['w_out'] = f(inp['w_out'])
    out['w_up'] = f(inp['w_up'])
    out['w_down'] = f(inp['w_down'])
    out['gb_attn'] = f(np.broadcast_to(f(inp['attn_norm_g'])[:, None, :], (2, 128, DM)))
    out['gb_mlp'] = f(np.broadcast_to(f(inp['mlp_norm_g'])[:, None, :], (2, 128, DM)))
    gcol = np.zeros((2, 128, NQKB), dtype=np.float32)
    aq, bq, cq, dq = f(inp['a_qk_norm_g']), f(inp['b_qk_norm_g']), f(inp['c_qk_norm_g']), f(inp['d_qk_norm_g'])
    for l in range(2):
        for b in range(4):
            gcol[l, :, b] = np.tile(aq[l, 0], 2)
        gcol[l, :, 4] = np.tile(aq[l, 1], 2)
        for b in range(5, 8):
            gcol[l, :, b] = np.tile(bq[l, 0], 2)
        for b in range(8, 11):
            gcol[l, :, b] = np.tile(bq[l, 1], 2)
        for b in range(11, 13):
            gcol[l, :, b] = np.tile(cq[l, 0], 4)
        for b in range(13, 15):
            gcol[l, :, b] = np.tile(cq[l, 1], 4)
        for b in range(15, 17):
            gcol[l, :, b] = np.tile(dq[l, 0], 2)
        for b in range(17, 19):
            gcol[l, :, b] = np.tile(dq[l, 1], 2)
    out['gcol'] = gcol
    out['sinkb'] = f(np.broadcast_to(f(inp['a_sink'])[:, None, :], (2, 128, 8)))
    out['lamb'] = f(np.broadcast_to(f(inp['c_lambda']).reshape(2, 1, 128), (2, 128, 128)))
    out['gsub'] = f(np.tile(f(inp['c_subln_g']), (1, 2)).reshape(2, 128, 1))
    rpb = f(inp['d_rel_bias'])
    db = np.zeros((2, 4, 128, NTAB, 128), dtype=np.float32)
    for t, (dr, valid) in enumerate(D_TABS):
        db[:, :, :, t, :] = rpb[:, :, dr, D_DC]
    out['dbias'] = db.reshape(2, 4, 128, NTAB * 128)
    c64, s64 = rope_tabs(64)
    c32, s32 = rope_tabs(32)
    p4, p16 = perm_tokens(4), perm_tokens(16)
    out['rope'] = np.ascontiguousarray(np.stack([c64, s64, c64[:, p4], s64[:, p4], c64[:, p16], s64[:, p16], c32, s32], 0))
    out['consts'] = make_consts()
    return out


PARAM_SHAPES = {
    'w_in': [2, DM, INC], 'w_ba': [2, 512, DM], 'w_bb': [2, 128, DM], 'w_bc': [2, 256, DM], 'w_bd': [2, 256, DM],
    'w_out': [2, DM, DM], 'w_up': [2, DM, 4096], 'w_down': [2, 4096, DM],
    'gb_attn': [2, 128, DM], 'gb_mlp': [2, 128, DM], 'gcol': [2, 128, NQKB], 'sinkb': [2, 128, 8],
    'lamb': [2, 128, 128], 'gsub': [2, 128, 1], 'dbias': [2, 4, 128, NTAB * 128],
    'rope': [8, 128, S], 'consts': [128, NCONST],
}


class Ctx:
    pass


def sl(start, n, step):
    return slice(start, start + (n - 1) * step + 1, step)


def ring(lst, i):
    return lst[i % len(lst)]


def phase_p1(C, l, xin):
    P, ar, D = C.P, C.ar, C.D
    nb0 = len(P.bufs)
    m0 = ar.mark()
    hT = ar.alloc([8, S], BF16)
    b_hT = P.buf('hT')
    gb = ar.alloc([DM], F32)
    b_gb = P.buf('gb')
    gcol = ar.alloc([NQKB], F32)
    b_gcol = P.buf('gcol')
    P.load(gb, D['gb_attn'][l], b_gb)
    P.load(gcol, D['gcol'][l], b_gcol)
    m1 = ar.mark()
    xt = [ar.alloc([DM], F32) for _ in range(3)]
    bx = P.bufs_n('xt', 3)
    junk = ar.alloc([DM], BF16)
    b_junk = P.buf('junk')
    ss = [ar.alloc([1], F32) for _ in range(4)]
    bss = P.bufs_n('ss', 4)
    hb = [ar.alloc([DM], BF16) for _ in range(4)]
    bhb = P.bufs_n('hb', 4)
    for tt in range(NT):
        i, j = tt % 3, tt % 4
        P.load(xt[i], xin[tt * 128:(tt + 1) * 128, :], bx[i])
        P.act(junk, xt[i], AF.Square, R=[bx[i]], W=[b_junk, bss[j]], accum_out=ss[j])
        P.act(ss[j], ss[j], AF.Ln, R=[bss[j]], W=[bss[j]], scale=1.0 / DM, bias=EPS)
        P.act(ss[j], ss[j], AF.Exp, R=[bss[j]], W=[bss[j]], scale=-0.5)
        P.stt(hb[j], xt[i], ss[j], gb, ALU.mult, ALU.mult, R=[bx[i], bss[j], b_gb], W=[bhb[j]])
        pbv = C.pb[j].bitcast(BF16)
        for kc in range(8):
            P.tr(pbv[:, kc * 128:(kc + 1) * 128], hb[j][:, kc * 128:(kc + 1) * 128], C.ident, R=[bhb[j], C.b_const], W=[C.bpb[j]])
        P.cp('act' if tt % 2 else 'dve', hT[:, :, tt * 128:(tt + 1) * 128], pbv.rearrange("p (k t) -> p k t", k=8), R=[C.bpb[j]], W=[b_hT])
    for kc in range(8):
        P.store(D['hT_d'][kc * 128:(kc + 1) * 128, :], hT[:, kc, :], b_hT)
    ar.release(m1)
    tabs2 = [ar.alloc([2, S], F32) for _ in range(2)]
    b_tabs2 = P.bufs_n('ropetab', 2)
    wq = [ar.alloc([8, 128], BF16) for _ in range(2)]
    bwq = P.bufs_n('wq', 2)
    NB = 4
    sq = [ar.alloc([512], BF16) for _ in range(NB)]
    bsq = P.bufs_n('sq', NB)
    xg = [ar.alloc([512], BF16) for _ in range(NB)]
    bxg = P.bufs_n('xg', NB)
    rs = [ar.alloc([512], F32) for _ in range(NB)]
    brs = P.bufs_n('rs', NB)
    ta = [ar.alloc([512], F32) for _ in range(NB)]
    bta = P.bufs_n('ta', NB)
    tb = [ar.alloc([512], F32) for _ in range(NB)]
    btb = P.bufs_n('tb', NB)
    ob = [ar.alloc([512], BF16) for _ in range(NB)]
    bob = P.bufs_n('ob', NB)
    w_in = D['w_in'][l].rearrange("(kc p) c -> p kc c", p=128)
    blk_order = [0, 1, 2, 3, 4, 5, 8, 6, 9, 7, 10, 11, 12, 13, 14, 15, 16, 17, 18]
    items = [(blk, tc) for blk in blk_order for tc in range(8)]
    TABK = {'n64': 0, 'p4': 2, 'p16': 4, 'n32': 6, 'd': None}
    variants = [0, 2, 4, 6]
    state = {'vi': -1}

    def load_tab(vi):
        if vi < len(variants):
            tb_, bt_ = tabs2[vi % 2], b_tabs2[vi % 2]
            P.load(tb_[:, 0, :], D['rope'][variants[vi]], bt_)
            P.load(tb_[:, 1, :], D['rope'][variants[vi] + 1], bt_)
    load_tab(0)

    def tokf(kind, tc):
        if kind == 'p4':
            r, h0 = tc // 2, (tc % 2) * 512
            return lambda kc: hT[:, kc, sl(r + 4 * h0, 512, 4)]
        if kind == 'p16':
            return lambda kc: hT[:, kc, :].rearrange("p (m r) -> p r m", r=16)[:, 2 * tc:2 * tc + 2, :]
        return lambda kc: hT[:, kc, tc * 512:(tc + 1) * 512]

    def s0(itm, t):
        blk, tc = itm
        coff, kind = QK_BLOCKS[blk]
        wi = blk % 2
        if tc == 0:
            P.dma('pool', wq[wi], w_in[:, :, coff:coff + 128], bwq[wi], W=(bwq[wi],))
        tok = tokf(kind, tc)
        pA, bA = C.pb[t % 3], C.bpb[t % 3]
        for kc in range(8):
            P.mm(pA[:, :], wq[wi][:, kc, :], tok(kc), start=(kc == 0), stop=(kc == 7), R=[bwq[wi], b_hT], W=[bA])

    def s1(itm, t, defer):
        blk, tc = itm
        coff, kind = QK_BLOCKS[blk]
        tabkind = TABK[kind]
        if tabkind is not None:
            vi = variants.index(tabkind)
            if vi != state['vi']:
                state['vi'] = vi
                load_tab(vi + 1)
            tabs, b_tabs = tabs2[vi % 2], b_tabs2[vi % 2]
        dh = 32 if kind == 'n32' else 64
        bd = C.bd32 if dh == 32 else C.bd64
        psw = C.psw32 if dh == 32 else C.psw64
        k = t % NB
        pA, pS, pR = C.pb[t % 3], C.pb[3 + (t % 2)], C.pb[5 + (t % 3)]
        bA, bS, bR = C.bpb[t % 3], C.bpb[3 + (t % 2)], C.bpb[5 + (t % 3)]
        P.act(sq[k], pA[:, :], AF.Square, R=[bA], W=[bsq[k]])
        P.act(xg[k], pA[:, :], AF.Copy, R=[bA, b_gcol], W=[bxg[k]], scale=gcol[:, blk:blk + 1])
        P.mm(pS[:, :], bd, sq[k], R=[bsq[k], C.b_const], W=[bS])
        if kind != 'd':
            P.mm(pR[:, :], psw, xg[k], R=[bxg[k], C.b_const], W=[bR])
        csl = slice(tc * 512, (tc + 1) * 512)
        if kind != 'd':
            P.tt('pool', ta[k], xg[k], tabs[:, 0, csl], ALU.mult, R=[bxg[k], b_tabs], W=[bta[k]])
            P.tt('dve', tb[k], pR[:, :], tabs[:, 1, csl], ALU.mult, R=[bR, b_tabs], W=[btb[k]])

        def s1b():
            P.act(rs[k], pS[:, :], AF.Ln, R=[bS], W=[brs[k]], scale=1.0 / dh, bias=EPS)
            P.act(rs[k], rs[k], AF.Exp, R=[brs[k]], W=[brs[k]], scale=-0.5)
            if kind == 'd':
                P.tt('dve', ob[k], xg[k], rs[k], ALU.mult, R=[bxg[k], brs[k]], W=[bob[k]])
            else:
                P.tt('dve', ta[k], ta[k], tb[k], ALU.add, R=[bta[k], btb[k]], W=[bta[k]])
                P.tt('dve', ob[k], ta[k], rs[k], ALU.mult, R=[bta[k], brs[k]], W=[bob[k]])
            P.store(D['QKT_d'][blk * 128:(blk + 1) * 128, csl], ob[k], bob[k])
        defer(1, s1b)

    run_pipeline(items, 2, s0, s1)
    ar.release(m1)
    wv = ar.alloc([8, 1024], BF16)
    b_wv = P.buf('wv')
    o = 0
    for (coff, n) in VNAT_COLS + [(1536 + 128, 256)]:
        P.dma('pool', wv[:, :, o:o + n], w_in[:, :, coff:coff + n], b_wv, W=(b_wv,))
        o += n
    vn = [ar.alloc([12, 65], BF16) for _ in range(2)]
    bvn = P.bufs_n('vn', 2)
    vb = [ar.alloc([2, 2, 65], BF16) for _ in range(2)]
    bvb = P.bufs_n('vbp', 2)
    for j in range(2):
        P.memset('pool', vn[j][:, :, 64:65], 1.0, W=[bvn[j]])
        P.memset('pool', vb[j][:, :, :, 64:65], 1.0, W=[bvb[j]])
    p4, p16 = perm_tokens(4), perm_tokens(16)
    for tt in range(NT):
        j = tt % 2
        pa, pbk, pc = C.pb[j * 3], C.pb[j * 3 + 1], C.pb[j * 3 + 2]
        ba, bb_, bc = C.bpb[j * 3], C.bpb[j * 3 + 1], C.bpb[j * 3 + 2]
        for kc in range(8):
            P.mm(pa[:, :], hT[:, kc, tt * 128:(tt + 1) * 128], wv[:, kc, 0:512], start=(kc == 0), stop=(kc == 7), R=[b_hT, b_wv], W=[ba])
        for kc in range(8):
            P.mm(pbk[:, 0:256], hT[:, kc, tt * 128:(tt + 1) * 128], wv[:, kc, 512:768], start=(kc == 0), stop=(kc == 7), R=[b_hT, b_wv], W=[bb_])
        t4 = int(p4[tt * 128])
        t16 = int(p16[tt * 128])
        for kc in range(8):
            P.mm(pc[:, 0:128], hT[:, kc, sl(t4, 128, 4)], wv[:, kc, 768:896], start=(kc == 0), stop=(kc == 7), R=[b_hT, b_wv], W=[bc])
        for kc in range(8):
            P.mm(pc[:, 128:256], hT[:, kc, sl(t16, 128, 16)], wv[:, kc, 896:1024], start=(kc == 0), stop=(kc == 7), R=[b_hT, b_wv], W=[bc])
        P.cp('act', vn[j][:, 0:8, 0:64], pa[:, :].rearrange("p (h d) -> p h d", h=8), R=[ba], W=[bvn[j]])
        P.cp('dve', vn[j][:, 8:12, 0:64], pbk[:, 0:256].rearrange("p (h d) -> p h d", h=4), R=[bb_], W=[bvn[j]])
        P.cp('dve', vb[j][:, :, :, 0:64], pc[:, 0:256].rearrange("p (g h d) -> p g h d", g=2, h=2), R=[bc], W=[bvb[j]])
        P.store(D['Vnat_d'][tt * 128:(tt + 1) * 128, :], vn[j].rearrange("p h d -> p (h d)"), bvn[j])
        for g in range(2):
            P.store(D['Vb_d'][g, tt * 128:(tt + 1) * 128, :], vb[j][:, g].rearrange("p h d -> p (h d)"), bvb[j])
    ar.release(m0)
    P.barrier()
    P.retire(nb0)


def run_pipeline(items, LA, s0, s1):
    n = len(items)
    pend = {}
    cur = [0]

    def defer(delay, fn):
        pend.setdefault(cur[0] + delay, []).append(fn)

    for t in range(n + LA):
        cur[0] = t
        if t < n:
            s0(items[t], t)
        if t >= LA:
            s1(items[t - LA], t - LA, defer)
        for fn in pend.pop(t, []):
            fn()
    while pend:
        t2 = min(pend)
        cur[0] = t2
        for fn in pend.pop(t2):
            fn()


def finalize_norm(C, acc, bacc, n, dst, bdst, shape3=None, esink=None, tagk=0, defer=None, after=None, dl=(1, 3, 5, 6)):
    P = C.P
    k = tagk % 2
    r, rhi, rlo = C.fr[k], C.frhi[k], C.frlo[k]
    br = C.bfr[k]
    bc, bbc = C.pb[7], C.bpb[7]
    bcs, bbcs = C.fbcs[k], C.bfbcs[k]
    src = acc[64:65, 0:n]

    def g0():
        if esink is not None:
            j, q = shape3
            P.tt('dve', r[64:65, 0:n].rearrange("p (j q) -> p j q", j=j), src.rearrange("p (j q) -> p j q", j=j),
                 esink, ALU.add, R=[bacc, C.b_esk], W=[br])
            P.act(r[64:65, 0:n], r[64:65, 0:n], AF.Ln, R=[br], W=[br])
        else:
            P.act(r[64:65, 0:n], src, AF.Ln, R=[bacc], W=[br])
        P.act(r[64:65, 0:n], r[64:65, 0:n], AF.Exp, R=[br], W=[br], scale=-1.0)
        P.cp('dve', rhi[64:65, 0:n], r[64:65, 0:n], R=[br], W=[br])
        P.tt('dve', rlo[64:65, 0:n], r[64:65, 0:n], rhi[64:65, 0:n], ALU.subtract, R=[br], W=[br])

    def g1():
        P.mm(bc[0:64, 0:n], C.ones_bf[64:65, 0:64], rhi[64:65, 0:n], start=True, stop=False, R=[br, C.b_const], W=[bbc])
        P.mm(bc[0:64, 0:n], C.ones_bf[64:65, 0:64], rlo[64:65, 0:n], start=False, stop=True, R=[br, C.b_const], W=[bbc])

    def g2():
        P.cp('act', bcs[0:64, 0:n], bc[0:64, 0:n], R=[bbc], W=[bbcs])

    def g3():
        a0 = acc[0:64, 0:n]
        b0 = bcs[0:64, 0:n]
        if shape3 is not None:
            j, q = shape3
            a0 = a0.rearrange("p (j q) -> p j q", j=j)
            b0 = b0.rearrange("p (j q) -> p j q", j=j)
        P.tt('dve', dst, a0, b0, ALU.mult, R=[bacc, bbcs], W=[bdst])
        if after is not None:
            after()

    if defer is None:
        g0(); g1(); g2(); g3()
    else:
        defer(dl[0], g0); defer(dl[1], g1); defer(dl[2], g2); defer(dl[3], g3)


def alloc_fin(C):
    ar, P = C.ar, C.P
    C.fr = [ar.alloc([512], F32) for _ in range(2)]
    C.frhi = [ar.alloc([512], BF16) for _ in range(2)]
    C.frlo = [ar.alloc([512], BF16) for _ in range(2)]
    C.bfr = P.bufs_n('fr', 2)
    C.fbcs = [ar.alloc([512], F32) for _ in range(2)]
    C.bfbcs = P.bufs_n('fbcs', 2)


def mixer_a(C, l):
    P, ar, D = C.P, C.ar, C.D
    nb0 = len(P.bufs)
    m0 = ar.mark()
    alloc_fin(C)
    QT = ar.alloc([4, S], BF16)
    bQT = P.buf('aQT')
    KT = ar.alloc([2, S], BF16)
    bKT = P.buf('aKT')
    V = ar.alloc([NT, 130], BF16)
    bV = P.buf('aV')
    yst = ar.alloc([4, S], BF16)
    byst = P.buf('ayst')
    esk = ar.alloc([8], F32)
    C.b_esk = P.buf('esk')
    P.load(esk, D['sinkb'][l], C.b_esk)
    P.act(esk, esk, AF.Exp, R=[C.b_esk], W=[C.b_esk])
    for g in range(2):
        for j in range(4):
            h = 4 * g + j
            P.load(QT[g * 64:(g + 1) * 64, j, :], D['QKT_d'][h * 64:(h + 1) * 64, :], bQT)
    P.memset('pool', KT, 0.0, W=[bKT])
    for g in range(2):
        P.load(KT[g * 64:(g + 1) * 64, g, :], D['QKT_d'][512 + g * 64:512 + (g + 1) * 64, :], bKT)
    P.load(V, D['Vnat_d'].rearrange("(t p) c -> p t c", p=128)[:, :, 0:130], bV)
    NP = 4
    pt = [ar.alloc([512], BF16) for _ in range(NP)]
    bpt = P.bufs_n('apt', NP)
    mlo = C.cbf[:, C_MLO:C_MLO + 128].unsqueeze(1).broadcast_to([128, 4, 128])
    mhi = C.cbf[:, C_MHI:C_MHI + 128].unsqueeze(1).broadcast_to([128, 4, 128])
    items = []
    fi = 0
    for g in range(2):
        for n in range(NT):
            ms = [m for m in (n - 1, n, n + 1) if 0 <= m < NT]
            for idx, m in enumerate(ms):
                items.append((g, n, m, idx == 0, idx == len(ms) - 1, fi))
            fi += 1

    def s0(itm, t):
        g, n, m, first, last, f = itm
        ps = slice(g * 64, (g + 1) * 64)
        st, bst = C.pb[t % 4], C.bpb[t % 4]
        P.mm(st[:, :].rearrange("p (j q) -> p j q", j=4), KT[:, g, m * 128:(m + 1) * 128], QT[:, :, n * 128:(n + 1) * 128],
             start=True, stop=(m == n), R=[bKT, bQT], W=[bst])
        if m != n:
            nk = C_NLO4 if m < n else C_NHI4
            P.mm(st[:, :], C.ident, C.cbf[:, nk:nk + 512], start=False, stop=True, R=[C.b_const], W=[bst])

    def s1(itm, t, defer):
        g, n, m, first, last, f = itm
        k = t % NP
        st, bst = C.pb[t % 4], C.bpb[t % 4]
        acc, bacc = C.pb[4 + (f % 3)], C.bpb[4 + (f % 3)]
        P.act(pt[k], st[:, :], AF.Exp, R=[bst], W=[bpt[k]], scale=0.125)
        P.mm(acc[0:65, :], V[:, m, g * 65:(g + 1) * 65], pt[k], start=first, stop=last, R=[bV, bpt[k]], W=[bacc])
        if last:
            es = esk[64:65, 4 * g:4 * g + 4].unsqueeze(2).broadcast_to([1, 4, 128])
            def after(g=g, n=n):
                if n == NT - 1:
                    for j in range(4):
                        h = 4 * g + j
                        P.store(D['YT_d'][h * 64:(h + 1) * 64, :], yst[0:64, j, :], byst)
            finalize_norm(C, acc, bacc, 512, yst[0:64, :, n * 128:(n + 1) * 128], byst, shape3=(4, 128), esink=es, tagk=f,
                          defer=defer, after=after, dl=(1, 2, 3, 4))

    run_pipeline(items, 3, s0, s1)
    ar.release(m0)
    P.barrier()
    P.retire(nb0)


def mixer_b(C, l):
    P, ar, D = C.P, C.ar, C.D
    nb0 = len(P.bufs)
    m0 = ar.mark()
    alloc_fin(C)
    accN = ar.alloc([2, S], F32)
    baccN = P.buf('baccN')
    yst = ar.alloc([2, S], BF16)
    byst = P.buf('byst')
    QTs = [ar.alloc([S], BF16) for _ in range(3)]
    KTs = [ar.alloc([2, S], BF16) for _ in range(3)]
    Vs = [ar.alloc([NT, 130], BF16) for _ in range(3)]
    bQ = P.bufs_n('bQT', 3)
    bK = P.bufs_n('bKT', 3)
    bVv = P.bufs_n('bV', 3)
    NP = 4
    pt = [ar.alloc([512], BF16) for _ in range(NP)]
    bpt = P.bufs_n('bpt', NP)
    items = []
    ai = 0
    for gi, d in enumerate(B_DIL):
        L = S // d
        nb = L // 128
        QT, KT, V = QTs[gi], KTs[gi], Vs[gi]
        bq, bk, bv = bQ[gi], bK[gi], bVv[gi]
        P.load(QT, D['QKT_d'][(5 + gi) * 128:(6 + gi) * 128, :], bq)
        P.memset('pool' if gi % 2 else 'dve', KT, 0.0, W=[bk])
        for jh_ in range(2):
            P.load(KT[jh_ * 64:(jh_ + 1) * 64, jh_, :], D['QKT_d'][(8 + gi) * 128 + jh_ * 64:(8 + gi) * 128 + (jh_ + 1) * 64, :], bk)
        if gi == 0:
            P.load(V, D['Vnat_d'].rearrange("(t p) c -> p t c", p=128)[:, :, 650:780], bv)
        else:
            P.load(V, D['Vb_d'][gi - 1].rearrange("(t p) c -> p t c", p=128), bv)
        for jh in range(2):
            for r in range(d):
                base = r * L
                qblocks = []
                for n_ in range(-1, nb):
                    q0 = 128 * n_ + 64
                    qa, qb = max(q0, 0), min(q0 + 128, L)
                    tiles = []
                    if n_ >= 0:
                        tiles.append((n_, C_MLO))
                    if n_ + 1 < nb:
                        tiles.append((n_ + 1, C_MHI))
                    qblocks.append((qa, qb - qa, qa - q0, tiles))
                for g0 in range(0, len(qblocks), 4):
                    grp = qblocks[g0:g0 + 4]
                    ncols = sum(q[1] for q in grp)
                    pstart = grp[0][0]
                    col = 0
                    nsub = (len(grp) + 1) // 2
                    for bi in range(0, len(grp), 2):
                        sub = grp[bi:bi + 2]
                        plist = []
                        sc = 0
                        for (qa, nq, aoff, tiles) in sub:
                            for ti, (m, mk) in enumerate(tiles):
                                plist.append((sc, nq, aoff, mk, base // 128 + m, col, ti == 0, ti == len(tiles) - 1, base + qa))
                                sc += nq
                            col += nq
                        endinfo = None
                        if bi // 2 == nsub - 1:
                            endinfo = (gi, d, r, pstart, ncols)
                        items.append((QT, KT, V, bq, bk, bv, jh, plist, sc, ai, endinfo))
                    ai += 1

    def s0(itm, t):
        QT, KT, V, bq, bk, bv, jh, plist, sc, a_, endinfo = itm
        ps = slice(jh * 64, (jh + 1) * 64)
        st, bst = C.pb[t % 4], C.bpb[t % 4]
        for (s0_, nq, aoff, mk, tg, c0, first, last, qpos) in plist:
            P.mm(st[:, s0_:s0_ + nq], KT[:, jh, tg * 128:(tg + 1) * 128], QT[:, qpos:qpos + nq], start=True, stop=False, R=[bk, bq], W=[bst])
            nk = C_NLO4 if mk == C_MLO else C_NHI4
            P.mm(st[:, s0_:s0_ + nq], C.ident, C.cbf[:, nk + aoff:nk + aoff + nq], start=False, stop=True, R=[C.b_const], W=[bst])

    def s1(itm, t, defer):
        QT, KT, V, bq, bk, bv, jh, plist, sc, a_, endinfo = itm
        k = t % NP
        st, bst = C.pb[t % 4], C.bpb[t % 4]
        acc, bacc = C.pb[4 + (a_ % 3)], C.bpb[4 + (a_ % 3)]
        P.act(pt[k][:, 0:sc], st[:, 0:sc], AF.Exp, R=[bst], W=[bpt[k]], scale=0.125)
        for (s0_, nq, aoff, mk, tg, c0, first, last, _) in plist:
            P.mm(acc[0:65, c0:c0 + nq], V[:, tg, jh * 65:(jh + 1) * 65], pt[k][:, s0_:s0_ + nq], start=first, stop=last,
                 R=[bv, bpt[k]], W=[bacc])
        if endinfo is not None:
            gi, d, r, pstart, ncols = endinfo
            if gi == 0:
                P.cp('act', accN[0:65, jh, pstart:pstart + ncols], acc[0:65, 0:ncols], R=[bacc], W=[baccN])
            else:
                t0 = r + d * pstart
                view = accN[0:65, jh, sl(t0, ncols, d)]
                P.tt('dve', view, view, acc[0:65, 0:ncols], ALU.add, R=[bacc, baccN], W=[baccN])

    run_pipeline(items, 3, s0, s1)
    fi = 0
    for jh in range(2):
        for ch in range(8):
            cs = slice(ch * 512, (ch + 1) * 512)
            k = fi % 2
            r, rhi, rlo, br = C.fr[k], C.frhi[k], C.frlo[k], C.bfr[k]
            bc, bbc = C.pb[7], C.bpb[7]
            P.act(r[64:65, :], accN[64:65, jh, cs], AF.Ln, R=[baccN], W=[br])
            P.act(r[64:65, :], r[64:65, :], AF.Exp, R=[br], W=[br], scale=-1.0)
            P.cp('dve', rhi[64:65, :], r[64:65, :], R=[br], W=[br])
            P.tt('dve', rlo[64:65, :], r[64:65, :], rhi[64:65, :], ALU.subtract, R=[br], W=[br])
            P.mm(bc[0:64, :], C.ones_bf[64:65, 0:64], rhi[64:65, :], start=True, stop=False, R=[br, C.b_const], W=[bbc])
            P.mm(bc[0:64, :], C.ones_bf[64:65, 0:64], rlo[64:65, :], start=False, stop=True, R=[br, C.b_const], W=[bbc])
            P.tt('dve', yst[0:64, jh, cs], accN[0:64, jh, cs], bc[0:64, :], ALU.mult, R=[baccN, bbc], W=[byst])
            fi += 1
        P.store(D['YT_d'][512 + jh * 64:512 + (jh + 1) * 64, :], yst[0:64, jh, :], byst)
    ar.release(m0)
    P.barrier()
    P.retire(nb0)


def mixer_d(C, l):
    P, ar, D = C.P, C.ar, C.D
    nb0 = len(P.bufs)
    m0 = ar.mark()
    alloc_fin(C)
    QT = ar.alloc([2, S], BF16)
    KT = ar.alloc([4, S], BF16)
    V = ar.alloc([NT, 260], BF16)
    bQT, bKT, bV = P.buf('dQT'), P.buf('dKT'), P.buf('dV')
    yst = ar.alloc([4, S], BF16)
    byst = P.buf('dyst')
    EB = ar.alloc([4, NTAB * 128], BF16)
    bEB = P.buf('dEB')
    tmpb = [ar.alloc([NTAB * 128], F32) for _ in range(2)]
    btmp = P.bufs_n('dtmp', 2)
    P.memset('pool', KT[:, 0:2, :], 0.0, W=[bKT])
    P.memset('dve', KT[:, 2:4, :], 0.0, W=[bKT])
    for i in range(2):
        P.load(QT[:, i, :], D['QKT_d'][(15 + i) * 128:(16 + i) * 128, :], bQT)
    for h_ in range(4):
        r0_ = (h_ % 2) * 64
        P.load(KT[r0_:r0_ + 64, h_, :], D['QKT_d'][17 * 128 + h_ * 64:17 * 128 + (h_ + 1) * 64, :], bKT)
    P.load(V, D['Vnat_d'].rearrange("(t p) c -> p t c", p=128)[:, :, 390:650], bV)
    for h in range(4):
        P.load(tmpb[h % 2], D['dbias'][l, h], btmp[h % 2])
        P.act(tmpb[h % 2], tmpb[h % 2], AF.Exp, R=[btmp[h % 2]], W=[btmp[h % 2]])
        P.tt('dve', EB[:, h, :], tmpb[h % 2], C.cbf[:, C_DVALID:C_DVALID + NTAB * 128], ALU.mult, R=[btmp[h % 2], C.b_const], W=[bEB])
    NP = 4
    pt = [ar.alloc([512], BF16) for _ in range(NP)]
    bpt = P.bufs_n('dpt', NP)
    items = []
    fi = 0
    for h in range(4):
        for n4 in range(8):
            plist = []
            for n in range(n4 * 4, n4 * 4 + 4):
                pl = D_PAIRS[n]
                for pi, (m, tab) in enumerate(pl):
                    plist.append((n, m, tab, pi == 0, pi == len(pl) - 1))
            nch = (len(plist) + 3) // 4
            for ci in range(nch):
                items.append((h, n4, plist[ci * 4:ci * 4 + 4], fi, ci == nch - 1))
            fi += 1

    def s0(itm, t):
        h, n4, chunk, f, endg = itm
        bq = h // 2
        ps = slice((h % 2) * 64, (h % 2 + 1) * 64)
        st, bst = C.pb[t % 4], C.bpb[t % 4]
        for i, (n, m, tab, first, last) in enumerate(chunk):
            P.mm(st[:, i * 128:(i + 1) * 128], KT[:, h, m * 128:(m + 1) * 128], QT[:, bq, n * 128:(n + 1) * 128],
                 R=[bKT, bQT], W=[bst])

    def s1(itm, t, defer):
        h, n4, chunk, f, endg = itm
        k = t % NP
        st, bst = C.pb[t % 4], C.bpb[t % 4]
        acc, bacc = C.pb[4 + (f % 3)], C.bpb[4 + (f % 3)]
        used = len(chunk) * 128
        P.act(pt[k][:, 0:used], st[:, 0:used], AF.Exp, R=[bst], W=[bpt[k]], scale=0.125)
        for i, (n, m, tab, first, last) in enumerate(chunk):
            P.tt('pool' if i % 2 else 'dve', pt[k][:, i * 128:(i + 1) * 128], pt[k][:, i * 128:(i + 1) * 128],
                 EB[:, h, tab * 128:(tab + 1) * 128], ALU.mult, R=[bpt[k], bEB], W=[bpt[k]])
        for i, (n, m, tab, first, last) in enumerate(chunk):
            qc = (n % 4) * 128
            P.mm(acc[0:65, qc:qc + 128], V[:, m, h * 65:(h + 1) * 65], pt[k][:, i * 128:(i + 1) * 128], start=first, stop=last,
                 R=[bV, bpt[k]], W=[bacc])
        if endg:
            def after(h=h, n4=n4):
                if n4 == 7:
                    P.store(D['YT_d'][896 + h * 64:896 + (h + 1) * 64, :], yst[0:64, h, :], byst)
            finalize_norm(C, acc, bacc, 512, yst[0:64, h, n4 * 512:(n4 + 1) * 512], byst, tagk=f, defer=defer, after=after, dl=(1, 2, 3, 4))

    run_pipeline(items, 3, s0, s1)
    ar.release(m0)
    P.barrier()
    P.retire(nb0)


def mixer_c(C, l):
    P, ar, D = C.P, C.ar, C.D
    lam_init = 0.8 - 0.6 * math.exp(-0.3 * l)
    nb0 = len(P.bufs)
    m0 = ar.mark()
    alloc_fin(C)
    QT = ar.alloc([2, S], BF16)
    KT = ar.alloc([8, S], BF16)
    V = ar.alloc([NT, 260], BF16)
    bQT, bKT, bV = P.buf('cQT'), P.buf('cKT'), P.buf('cV')
    yst = ar.alloc([4, S], BF16)
    byst = P.buf('cyst')
    P.memset('pool', KT[:, 0:4, :], 0.0, W=[bKT])
    P.memset('dve', KT[:, 4:8, :], 0.0, W=[bKT])
    for g2 in range(2):
        P.load(QT[:, g2, :], D['QKT_d'][(11 + g2) * 128:(12 + g2) * 128, :], bQT)
    for b in range(8):
        sl_ = (b % 4) * 32
        row = (b // 4) * 128 + sl_
        P.load(KT[sl_:sl_ + 32, b, :], D['QKT_d'][13 * 128 + row:13 * 128 + row + 32, :], bKT)
    P.load(V, D['Vnat_d'].rearrange("(t p) c -> p t c", p=128)[:, :, 130:390], bV)
    lamb = ar.alloc([128], F32)
    blam = P.buf('lam')
    lt = ar.alloc([2, 32], F32)
    l2 = ar.alloc([2], F32)
    nlam = ar.alloc([1], F32)
    gsc = ar.alloc([1], F32)
    bgsc = P.buf('gsc')
    P.load(lamb, D['lamb'][l], blam)
    P.load(gsc, D['gsub'][l], bgsc)
    lv = lamb.rearrange("p (a b c) -> p a b c", a=2, b=2)
    P.tt('dve', lt, lv[:, :, 0, :], lv[:, :, 1, :], ALU.mult, R=[blam], W=[blam])
    P.op('dve', lambda e: e.reduce_sum(out=l2, in_=lt, axis=mybir.AxisListType.X), R=[blam], W=[blam])
    P.act(l2, l2, AF.Exp, R=[blam], W=[blam])
    P.tt('dve', nlam, l2[:, 0:1], l2[:, 1:2], ALU.subtract, R=[blam], W=[blam])
    P.ts('dve', nlam, nlam, lam_init, -1.0, ALU.add, ALU.mult, R=[blam], W=[blam])
    P.ts('dve', gsc, gsc, 1.0 - lam_init, None, ALU.mult, R=[bgsc], W=[bgsc])
    NP = 4
    pt = [ar.alloc([512], BF16) for _ in range(NP)]
    bpt = P.bufs_n('cpt', NP)
    to = [ar.alloc([512], F32) for _ in range(2)]
    t1 = [ar.alloc([512], F32) for _ in range(2)]
    sqb = [ar.alloc([512], BF16) for _ in range(2)]
    rsd = [ar.alloc([512], F32) for _ in range(2)]
    bto = P.bufs_n('cto', 2)
    scale = 32.0 ** -0.5
    items = []
    fi = 0
    for h in range(4):
        for Q in range(8):
            for c in range(2):
                for kt in range(NT):
                    items.append((h, Q, c, kt, fi))
            fi += 1

    def s0(itm, t):
        h, Q, c, kt, f = itm
        b = 2 * h + c
        st, bst = C.pb[t % 3], C.bpb[t % 3]
        P.mm(st[:, :], KT[:, b, kt * 128:(kt + 1) * 128], QT[:, b // 4, Q * 512:(Q + 1) * 512], R=[bKT, bQT], W=[bst])

    def s1(itm, t, defer):
        h, Q, c, kt, f = itm
        qs = slice(Q * 512, (Q + 1) * 512)
        k = t % NP
        st, bst = C.pb[t % 3], C.bpb[t % 3]
        a0 = 3 + 2 * (f % 2)
        accs = [C.pb[a0], C.pb[a0 + 1]]
        baccs = [C.bpb[a0], C.bpb[a0 + 1]]
        P.act(pt[k], st[:, :], AF.Exp, R=[bst], W=[bpt[k]], scale=scale)
        P.mm(accs[c][0:65, :], V[:, kt, h * 65:(h + 1) * 65], pt[k], start=(kt == 0), stop=(kt == NT - 1),
             R=[bV, bpt[k]], W=[baccs[c]])
        if not (c == 1 and kt == NT - 1):
            return
        k2 = f % 2
        bt = bto[k2]
        bc, bbc = C.pb[7], C.bpb[7]

        def f0():
            for c_ in range(2):
                r, rhi, rlo, br = C.fr[c_], C.frhi[c_], C.frlo[c_], C.bfr[c_]
                P.recip(r[64:65, :], accs[c_][64:65, :], R=[baccs[c_]], W=[br])
                if c_ == 1:
                    P.ts('dve', r[64:65, :], r[64:65, :], nlam[64:65, 0:1], None, ALU.mult, R=[br, blam], W=[br])
                P.cp('dve', rhi[64:65, :], r[64:65, :], R=[br], W=[br])
                P.tt('dve', rlo[64:65, :], r[64:65, :], rhi[64:65, :], ALU.subtract, R=[br], W=[br])

        def f1(c_):
            def fn():
                r, rhi, rlo, br = C.fr[c_], C.frhi[c_], C.frlo[c_], C.bfr[c_]
                P.mm(bc[0:64, :], C.ones_bf[64:65, 0:64], rhi[64:65, :], start=True, stop=False, R=[br, C.b_const], W=[bbc])
                P.mm(bc[0:64, :], C.ones_bf[64:65, 0:64], rlo[64:65, :], start=False, stop=True, R=[br, C.b_const], W=[bbc])
            return fn

        def f2(c_):
            def fn():
                P.cp('act', C.fbcs[c_][0:64, :], bc[0:64, :], R=[bbc], W=[C.bfbcs[c_]])
            return fn

        def f3():
            P.tt('dve', to[k2][0:64, :], accs[0][0:64, :], C.fbcs[0][0:64, :], ALU.mult, R=[baccs[0], C.bfbcs[0]], W=[bt])
            P.tt('dve', t1[k2][0:64, :], accs[1][0:64, :], C.fbcs[1][0:64, :], ALU.mult, R=[baccs[1], C.bfbcs[1], bt], W=[bt])
            P.tt('pool', to[k2][0:64, :], to[k2][0:64, :], t1[k2][0:64, :], ALU.add, R=[bt], W=[bt])

        def f4():
            P.act(sqb[k2][0:64, :], to[k2][0:64, :], AF.Square, R=[bt], W=[bt])

        def f5():
            P.mm(bc[0:64, :], C.ones_bf[0:64, 0:64], sqb[k2][0:64, :], R=[bt, C.b_const], W=[bbc])

        def f6():
            P.act(rsd[k2][0:64, :], bc[0:64, :], AF.Ln, R=[bbc], W=[bt], scale=1.0 / 64, bias=EPS)
            P.act(rsd[k2][0:64, :], rsd[k2][0:64, :], AF.Exp, R=[bt], W=[bt], scale=-0.5)

        def f7():
            P.stt(yst[0:64, h, qs], to[k2][0:64, :], gsc[0:64, 0:1], rsd[k2][0:64, :], ALU.mult, ALU.mult, R=[bt, bgsc], W=[byst])
            if Q == 7:
                P.store(D['YT_d'][640 + h * 64:640 + (h + 1) * 64, :], yst[0:64, h, :], byst)

        defer(1, f0)
        defer(4, f1(0))
        defer(5, f2(0))
        defer(6, f1(1))
        defer(7, f2(1))
        defer(9, f3)
        defer(12, f4)
        defer(13, f5)
        defer(15, f6)
        defer(17, f7)

    run_pipeline(items, 2, s0, s1)
    ar.release(m0)
    P.barrier()
    P.retire(nb0)


def phase_p3a(C, l, xin, xout):
    P, ar, D = C.P, C.ar, C.D
    nb0 = len(P.bufs)
    m0 = ar.mark()
    Wg = ar.alloc([8, 4096], BF16)
    Wb = ar.alloc([9, DM], BF16)
    Wo = ar.alloc([8, DM], BF16)
    bWg, bWb, bWo = P.buf('Wg'), P.buf('Wb'), P.buf('Wo')
    w_in = D['w_in'][l].rearrange("(kc p) c -> p kc c", p=128)
    for kc in range(8):
        P.dma('pool', Wg[:, kc, :], w_in[:, kc, GATE_OFF:GATE_OFF + 4096], bWg, W=(bWg,))
    P.dma('pool', Wb[:, 0:4, :], D['w_ba'][l].rearrange("(kc p) c -> p kc c", p=128), bWb, W=(bWb,))
    P.dma('pool', Wb[:, 4, :], D['w_bb'][l], bWb, W=(bWb,))
    P.dma('pool', Wb[:, 5:7, :], D['w_bc'][l].rearrange("(kc p) c -> p kc c", p=128), bWb, W=(bWb,))
    P.dma('pool', Wb[:, 7:9, :], D['w_bd'][l].rearrange("(kc p) c -> p kc c", p=128), bWb, W=(bWb,))
    P.dma('pool', Wo, D['w_out'][l].rearrange("(kc p) c -> p kc c", p=128), bWo, W=(bWo,))
    hTc = [ar.alloc([8, 512], BF16) for _ in range(2)]
    YTc = [ar.alloc([9, 512], BF16) for _ in range(2)]
    bh = P.bufs_n('hTc', 2)
    by = P.bufs_n('YTc', 2)
    mT = [ar.alloc([8, 512], BF16) for _ in range(2)]
    bmT = P.bufs_n('mT', 2)
    sig = [ar.alloc([512], F32) for _ in range(3)]
    bsig = P.bufs_n('sig', 3)
    tmp = [ar.alloc([512], F32) for _ in range(2)]
    btmp = P.bufs_n('mtmp', 2)
    macc = [ar.alloc([512], F32) for _ in range(2)]
    bmacc = P.bufs_n('macc', 2)
    xt = [ar.alloc([DM], F32) for _ in range(2)]
    bxt = P.bufs_n('x3', 2)
    xo = [ar.alloc([DM], F32) for _ in range(2)]
    bxo = P.bufs_n('xo3', 2)
    hT_v = D['hT_d'].rearrange("(kc p) t -> p kc t", p=128)
    YT_v = D['YT_d'].rearrange("(kc p) t -> p kc t", p=128)
    branches = [(0, 4), (4, 5), (5, 7), (7, 9)]
    cn = {'it': 0, 'si': 0, 'ti': 0}

    def gates(tg):
        g2 = tg % 2
        ts_ = slice(tg * 512, (tg + 1) * 512)
        P.load(hTc[g2], hT_v[:, :, ts_], bh[g2])
        P.load(YTc[g2], YT_v[:, :, ts_], by[g2])
        for ct in range(8):
            ma, bma = macc[ct % 2], bmacc[ct % 2]
            for br in range(4):
                it = cn['it']
                cn['it'] += 1
                pg, bpg = C.pb[(it % 2) * 2], C.bpb[(it % 2) * 2]
                py, bpy = C.pb[(it % 2) * 2 + 1], C.bpb[(it % 2) * 2 + 1]
                c0 = br * 1024 + ct * 128
                for kc in range(8):
                    P.mm(pg[:, :], Wg[:, kc, c0:c0 + 128], hTc[g2][:, kc, :], start=(kc == 0), stop=(kc == 7), R=[bWg, bh[g2]], W=[bpg])
                b0, b1 = branches[br]
                for bi in range(b0, b1):
                    P.mm(py[:, :], Wb[:, bi, ct * 128:(ct + 1) * 128], YTc[g2][:, bi, :], start=(bi == b0), stop=(bi == b1 - 1),
                         R=[bWb, by[g2]], W=[bpy])
                s_, bs_ = sig[cn['si'] % 3], bsig[cn['si'] % 3]
                cn['si'] += 1
                P.act(s_, pg[:, :], AF.Sigmoid, R=[bpg], W=[bs_])
                if br == 0:
                    P.tt('dve', ma, s_, py[:, :], ALU.mult, R=[bs_, bpy], W=[bma])
                else:
                    t_, bt_ = tmp[cn['ti'] % 2], btmp[cn['ti'] % 2]
                    cn['ti'] += 1
                    P.tt('dve', t_, s_, py[:, :], ALU.mult, R=[bs_, bpy], W=[bt_])
                    if br < 3:
                        P.tt('pool', ma, ma, t_, ALU.add, R=[bma, bt_], W=[bma])
                    else:
                        P.tt('pool', mT[g2][:, ct, :], ma, t_, ALU.add, R=[bma, bt_], W=[bmT[g2]])

    def wout(tg):
        g2 = tg % 2
        for tt in range(4):
            tok0 = tg * 512 + tt * 128
            xi = (tg * 4 + tt) % 2
            P.load(xt[xi], xin[tok0:tok0 + 128, :], bxt[xi])
            for cg in range(2):
                po, bpo = C.pb[4 + (cg + 2 * tt) % 4], C.bpb[4 + (cg + 2 * tt) % 4]
                for kc in range(8):
                    P.mm(po[:, :], mT[g2][:, kc, tt * 128:(tt + 1) * 128], Wo[:, kc, cg * 512:(cg + 1) * 512], start=(kc == 0), stop=(kc == 7),
                         R=[bmT[g2], bWo], W=[bpo])
                P.tt('dve', xo[xi][:, cg * 512:(cg + 1) * 512], po[:, :], xt[xi][:, cg * 512:(cg + 1) * 512], ALU.add,
                     R=[bpo, bxt[xi]], W=[bxo[xi]])
            P.store(xout[tok0:tok0 + 128, :], xo[xi], bxo[xi])
    gates(0)
    for tg in range(8):
        if tg + 1 < 8:
            gates(tg + 1)
        wout(tg)
    ar.release(m0)
    P.barrier()
    P.retire(nb0)


def phase_p3b(C, l, xin, xout):
    P, ar, D = C.P, C.ar, C.D
    nb0 = len(P.bufs)
    m0 = ar.mark()
    Wu = ar.alloc([8, 4096], BF16)
    Wd = ar.alloc([32, DM], BF16)
    gb = ar.alloc([DM], F32)
    bWu, bWd, bgb = P.buf('Wu'), P.buf('Wd'), P.buf('gbm')
    wu_v = D['w_up'][l].rearrange("(kc p) c -> p kc c", p=128)
    wd_v = D['w_down'][l].rearrange("(kc p) c -> p kc c", p=128)
    P.load(gb, D['gb_mlp'][l], bgb)
    bWuc = P.bufs_n('Wuc', 8)
    for cb in range(8):
        P.dma('pool', Wu[:, :, cb * 512:(cb + 1) * 512], wu_v[:, :, cb * 512:(cb + 1) * 512], bWuc[cb], W=(bWuc[cb],))
    for k4 in range(8):
        P.dma('pool', Wd[:, k4 * 4:(k4 + 1) * 4, :], wd_v[:, k4 * 4:(k4 + 1) * 4, :], bWd, W=(bWd,))
    xt = [ar.alloc([DM], F32) for _ in range(2)]
    bxt = P.bufs_n('x4', 2)
    xo = [ar.alloc([DM], F32) for _ in range(1)]
    bxo = P.bufs_n('xo4', 1)
    ss = [ar.alloc([1], F32) for _ in range(2)]
    bss = P.bufs_n('ss4', 2)
    hb = [ar.alloc([DM], BF16) for _ in range(2)]
    bhb = P.bufs_n('hb4', 2)
    hmT = [ar.alloc([8, 512], BF16) for _ in range(2)]
    bhm = P.bufs_n('hmT', 2)
    uT = ar.alloc([32, 512], BF16)
    buT = P.buf('uT')
    rl = [ar.alloc([512], F32) for _ in range(2)]
    brl = P.bufs_n('rl', 2)
    st_ = {'xi': 0, 'it': 0}

    def norm_tile(tg, tt, part):
        g2 = tg % 2
        tok0 = tg * 512 + tt * 128
        if part == 0:
            i = j = st_['xi'] % 2
            st_['xi'] += 1
            st_['nj'] = j
            P.load(xt[i], xin[tok0:tok0 + 128, :], bxt[i])
            P.act(hb[j], xt[i], AF.Square, R=[bxt[i]], W=[bhb[j], bss[j]], accum_out=ss[j])
            P.act(ss[j], ss[j], AF.Ln, R=[bss[j]], W=[bss[j]], scale=1.0 / DM, bias=EPS)
            P.act(ss[j], ss[j], AF.Exp, R=[bss[j]], W=[bss[j]], scale=-0.5)
            P.stt(hb[j], xt[i], ss[j], gb, ALU.mult, ALU.mult, R=[bxt[i], bss[j], bgb], W=[bhb[j]])
        else:
            j = st_['nj']
            pbv = C.pb[6 + j].bitcast(BF16)
            for kc in range(8):
                P.tr(pbv[:, kc * 128:(kc + 1) * 128], hb[j][:, kc * 128:(kc + 1) * 128], C.ident, R=[bhb[j], C.b_const], W=[C.bpb[6 + j]])
            P.cp('dve', hmT[g2][:, :, tt * 128:(tt + 1) * 128], pbv.rearrange("p (k t) -> p k t", k=8), R=[C.bpb[6 + j]], W=[bhm[g2]])

    def norm(tg):
        for tt in range(4):
            norm_tile(tg, tt, 0)
            norm_tile(tg, tt, 1)

    def up(tg):
        g2 = tg % 2
        for mt in range(32):
            it = st_['it']
            st_['it'] += 1
            pu, bpu = C.pb[it % 3], C.bpb[it % 3]
            r_, br_ = rl[it % 2], brl[it % 2]
            for kc in range(8):
                P.mm(pu[:, :], Wu[:, kc, mt * 128:(mt + 1) * 128], hmT[g2][:, kc, :], start=(kc == 0), stop=(kc == 7), R=[bWuc[mt // 4], bhm[g2]], W=[bpu])
            P.act(r_, pu[:, :], AF.Relu, R=[bpu], W=[br_])
            P.tt('pool' if mt % 2 else 'dve', uT[:, mt, :], r_, r_, ALU.mult, R=[br_], W=[buT])

    def down(tg):
        for tt in range(4):
            if tg + 1 < 8:
                norm_tile(tg + 1, tt, 0)
            tok0 = tg * 512 + tt * 128
            i = st_['xi'] % 2
            st_['xi'] += 1
            P.load(xt[i], xin[tok0:tok0 + 128, :], bxt[i])
            for cg in range(2):
                pd, bpd = C.pb[3 + (cg + 2 * tt) % 3], C.bpb[3 + (cg + 2 * tt) % 3]
                for mt in range(32):
                    P.mm(pd[:, :], uT[:, mt, tt * 128:(tt + 1) * 128], Wd[:, mt, cg * 512:(cg + 1) * 512], start=(mt == 0), stop=(mt == 31),
                         R=[buT, bWd], W=[bpd])
                P.tt('dve', xo[0][:, cg * 512:(cg + 1) * 512], pd[:, :], xt[i][:, cg * 512:(cg + 1) * 512], ALU.add,
                     R=[bpd, bxt[i]], W=[bxo[0]])
            P.store(xout[tok0:tok0 + 128, :], xo[0], bxo[0])
            if tg + 1 < 8:
                norm_tile(tg + 1, tt, 1)

    norm(0)
    for tg in range(8):
        up(tg)
        down(tg)
    ar.release(m0)
    P.barrier()
    P.retire(nb0)


ARENA = 207 * 1024 + 512
NSEM_POOL = 96


class SemPool:
    def __init__(self, nc, stack, n):
        self.sems = [stack.enter_context(nc.semaphore(f"sm{i}")) for i in range(n)]
        self.i = 0

        self.free = {True: [], False: []}
        self.kind = {}

    def get(self, sw=False):
        if self.free[sw]:
            return self.free[sw].pop()
        s_ = self.sems[self.i]
        self.i += 1
        self.kind[id(s_)] = sw
        return (s_, 0)

    def put(self, sem, cnt):
        self.free[self.kind[id(sem)]].append((sem, cnt))


def build(n_layers=2, phases=None, dbg=False):
    nc = bass.Bass("TRN2", target_bir_lowering=False)
    D = {}
    x = nc.dram_tensor("x", [S, DM], F32, kind="ExternalInput").ap()
    for name, shp in PARAM_SHAPES.items():
        D[name] = nc.dram_tensor(name, shp, F32, kind="ExternalInput").ap()
    y = nc.dram_tensor("y", [S, DM], F32, kind="ExternalOutput").ap()
    sk = "ExternalOutput" if dbg else "Internal"
    D['hT_d'] = nc.dram_tensor("hT_d", [DM, S], BF16, kind=sk).ap()
    D['QKT_d'] = nc.dram_tensor("QKT_d", [NQKB * 128, S], BF16, kind=sk).ap()
    D['Vnat_d'] = nc.dram_tensor("Vnat_d", [S, 780], BF16, kind=sk).ap()
    D['Vb_d'] = nc.dram_tensor("Vb_d", [2, S, 130], BF16, kind=sk).ap()
    D['YT_d'] = nc.dram_tensor("YT_d", [1152, S], BF16, kind=sk).ap()
    x1_d = nc.dram_tensor("x1_d", [S, DM], F32, kind=sk).ap()
    x2_d = nc.dram_tensor("x2_d", [S, DM], F32, kind=sk).ap()
    with ExitStack() as stack:
        arena_t = stack.enter_context(nc.sbuf_tensor("arena", [128, ARENA], U8))
        pbs = [stack.enter_context(nc.psum_tensor(f"pb{i}", [128, 512], F32)) for i in range(8)]
        sp_ = SemPool(nc, stack, NSEM_POOL)

        class _St:
            def enter_context(self, cm):
                raise RuntimeError

        P = Prog.__new__(Prog)
        P.nc = nc
        P.ops = {e: [] for e in ENGS}
        P.bufs = []
        P.esem = {e: sp_.get()[0] for e in ENGS}
        P.nsem = len(ENGS)

        def dma(eng, out, in_, owner, R=(), W=()):
            if owner.sem is None:
                owner.sem, owner.cnt = sp_.get(eng == 'pool')
            else:
                assert sp_.kind[id(owner.sem)] == (eng == 'pool'), owner.name
            owner.cnt += 16
            o = Op(eng, lambda e: e.dma_start(out=out, in_=in_))
            o.dma_sem = owner.sem
            P._track(o, ('dma', owner.sem, owner.cnt), 'dma', R, W)
            P.ops[eng].append(o)
            return o
        P.dma = dma

        def retire(nb0):
            for b in P.bufs[nb0:]:
                if b.sem is not None:
                    sp_.put(b.sem, b.cnt)
                    b.sem = None
            del P.bufs[nb0:]
        P.retire = retire
        block = stack.enter_context(nc.Block())
        C = Ctx()
        C.P, C.D = P, D
        C.ar = Arena(arena_t, ARENA)
        C.pb = [p[:, :] for p in pbs]
        C.bpb = P.bufs_n('pb', 8)
        C.cbf = C.ar.alloc([NCONST], BF16)
        C.b_const = P.buf('consts')
        P.dma('pool', C.cbf, D['consts'], C.b_const, W=(C.b_const,))
        C.ident = C.cbf[:, C_IDENT:C_IDENT + 128]
        C.bd64 = C.cbf[:, C_BD64:C_BD64 + 128]
        C.bd32 = C.cbf[:, C_BD32:C_BD32 + 128]
        C.psw64 = C.cbf[:, C_PSW64:C_PSW64 + 128]
        C.psw32 = C.cbf[:, C_PSW32:C_PSW32 + 128]
        C.ones_bf = C.cbf[:, C_ONES:C_ONES + 128]
        all_ph = ['p1', 'a', 'b', 'c', 'd', 'p3a', 'p3b']
        phases = phases or all_ph
        for l in range(n_layers):
            xin = x if l == 0 else x2_d
            xfin = y if l == n_layers - 1 else x2_d
            if 'p1' in phases:
                phase_p1(C, l, xin)
            if 'a' in phases:
                mixer_a(C, l)
            if 'b' in phases:
                mixer_b(C, l)
            if 'c' in phases:
                mixer_c(C, l)
            if 'd' in phases:
                mixer_d(C, l)
            if 'p3a' in phases:
                phase_p3a(C, l, xin, x1_d)
            if 'p3b' in phases:
                phase_p3b(C, l, x1_d, xfin)
        P.barrier()
        P.emit(block)
        C.nsem = sp_.i
    return nc, C


_CACHE = {}


def kernel(**inputs):
    x = np.ascontiguousarray(np.asarray(inputs['x'], dtype=np.float32))
    params = host_prep(inputs)
    if 'nc' not in _CACHE:
        _CACHE['nc'] = build()[0]
    nc = _CACHE['nc']
    in_maps = []
    for b in range(8):
        m = {'x': x[b]}
        m.update(params)
        in_maps.append(m)
    res = run_bass_kernel_spmd(nc, in_maps, core_ids=list(range(8)))
    return np.stack([np.asarray(r['y'], dtype=np.float32) for r in res.results], axis=0)
```
